# Optimizing a Trainium2 kernel written in Bass

```python
import jax, jax.numpy as jnp
from jax import lax
import numpy as np

D_MODEL = 1024
BATCH = 8
SEQ = 8192
DEPTH = 1

NSA_HEADS = 8
NSA_KV_HEADS = 2
NSA_GROUP = NSA_HEADS // NSA_KV_HEADS
NSA_HEAD_DIM = 128
CMP_BLOCK = 32
CMP_STRIDE = 16
CMP_HIDDEN = 256
SEL_BLOCK = 64
SEL_TOPK = 16
WINDOW = 512
NSA_Q_BLOCK = 64
MLA_HEADS = 8
MLA_Q_RANK = 384
MLA_KV_RANK = 256
MLA_NOPE_DIM = 128
MLA_ROPE_DIM = 64
MLA_QK_DIM = MLA_NOPE_DIM + MLA_ROPE_DIM
MLA_V_DIM = 128
MLA_Q_BLOCK = 128
D_FF = 2816
CONV_WIDTH = 3
ROPE_THETA = 10000.0
NORM_EPS = 1e-6
NEG_INF = -1e30
SEL_FORCE = 1e9

W_NSA_Q = NSA_HEADS * NSA_HEAD_DIM
W_NSA_KV = NSA_KV_HEADS * NSA_HEAD_DIM
IN_SPLITS = (W_NSA_Q,
             W_NSA_KV, W_NSA_KV,
             W_NSA_KV, W_NSA_KV,
             W_NSA_KV, W_NSA_KV,
             3 * NSA_HEADS,
             MLA_Q_RANK, MLA_KV_RANK,
             MLA_ROPE_DIM,
             2 * D_MODEL)
IN_WIDTH = sum(IN_SPLITS)
IN_OFFSETS = tuple(int(v) for v in np.cumsum(IN_SPLITS)[:-1])

kernel_name = 'hybrid_nsa_mla_convffn_adaln'


def rms_norm(x, gain):
    xf = x.astype(jnp.float32)
    y = xf * lax.rsqrt(jnp.mean(xf * xf, axis=-1, keepdims=True) + NORM_EPS)
    return (y * gain.astype(jnp.float32)).astype(x.dtype)


def rope(x, pos):
    dim = x.shape[-1]
    half = dim // 2
    inv = ROPE_THETA ** (-jnp.arange(half, dtype=jnp.float32) * 2.0 / dim)
    ang = pos[:, None] * inv[None, :]
    cos = jnp.cos(ang)[None, :, None, :]
    sin = jnp.sin(ang)[None, :, None, :]
    xf = x.astype(jnp.float32)
    x1, x2 = xf[..., :half], xf[..., half:]
    return jnp.concatenate([x1 * cos - x2 * sin, x1 * sin + x2 * cos], axis=-1).astype(x.dtype)


def masked_softmax(s, mask):
    p = jax.nn.softmax(jnp.where(mask, s.astype(jnp.float32), NEG_INF), axis=-1)
    return jnp.where(mask, p, 0.0)


def compress(x, pe, w1, w2):
    B, S, G, hd = x.shape
    n_chunks = S // CMP_STRIDE
    r = CMP_BLOCK // CMP_STRIDE
    n_cmp = n_chunks - r + 1
    chunks = x.reshape(B, n_chunks, CMP_STRIDE, G, hd)
    blocks = jnp.concatenate([chunks[:, j:j + n_cmp] for j in range(r)], axis=2)
    blocks = blocks + pe[None, None, :, None, :].astype(x.dtype)
    flat = blocks.transpose(0, 1, 3, 2, 4).reshape(B, n_cmp, G, CMP_BLOCK * hd)
    return jax.nn.silu(flat @ w1) @ w2


def nsa_mixer(q, kc, vc, ks, vs, kw, vw, gates):
    B, S, H, hd = q.shape
    G, R, Cq = NSA_KV_HEADS, NSA_GROUP, NSA_Q_BLOCK
    scale = hd ** -0.5
    n_cmp = kc.shape[1]
    n_sel = S // SEL_BLOCK
    k_top = min(SEL_TOPK, n_sel)
    cmp_start = jnp.arange(n_cmp) * CMP_STRIDE
    cmp_end = cmp_start + CMP_BLOCK - 1
    sel_idx = jnp.arange(n_sel)
    sel_start = sel_idx * SEL_BLOCK
    overlap = ((cmp_start[:, None] <= (sel_start + SEL_BLOCK - 1)[None, :])
               & (cmp_end[:, None] >= sel_start[None, :])).astype(jnp.float32)
    ks_blk = ks.reshape(B, n_sel, SEL_BLOCK, G, hd).transpose(0, 3, 1, 2, 4)
    vs_blk = vs.reshape(B, n_sel, SEL_BLOCK, G, hd).transpose(0, 3, 1, 2, 4)
    kw_pad = jnp.pad(kw, ((0, 0), (WINDOW, 0), (0, 0), (0, 0)))
    vw_pad = jnp.pad(vw, ((0, 0), (WINDOW, 0), (0, 0), (0, 0)))
    b_ix = jnp.arange(B)[:, None, None, None]
    g_ix = jnp.arange(G)[None, :, None, None]

    def block(n):
        q0 = n * Cq
        t = q0 + jnp.arange(Cq)
        qb = lax.dynamic_slice_in_dim(q, q0, Cq, axis=1).reshape(B, Cq, G, R, hd)
        gb = lax.dynamic_slice_in_dim(gates, q0, Cq, axis=1).reshape(B, Cq, G, R, 3)
        s = jnp.einsum('bqgrd,bngd->bgrqn', qb, kc) * scale
        p_cmp = masked_softmax(s, cmp_end[None, :] <= t[:, None])
        o_cmp = jnp.einsum('bgrqn,bngd->bqgrd', p_cmp.astype(vc.dtype), vc)
        imp = jnp.einsum('bgqn,nj->bgqj', p_cmp.sum(axis=2), overlap)
        cur = t // SEL_BLOCK
        forced = ((sel_idx[None, :] == 0) | (sel_idx[None, :] == cur[:, None])
                  | (sel_idx[None, :] == cur[:, None] - 1))
        allowed = sel_start[None, :] <= t[:, None]
        score = jnp.where(forced, SEL_FORCE, jnp.where(allowed, imp, NEG_INF))
        _, idx = lax.top_k(score, k_top)
        k_sel = ks_blk[b_ix, g_ix, idx]
        v_sel = vs_blk[b_ix, g_ix, idx]
        kpos = idx[..., None] * SEL_BLOCK + jnp.arange(SEL_BLOCK)
        sel_mask = (kpos <= t[None, None, :, None, None]).reshape(B, G, 1, Cq, k_top * SEL_BLOCK)
        s = jnp.einsum('bqgrd,bgqkld->bgrqkl', qb, k_sel) * scale
        p = masked_softmax(s.reshape(B, G, R, Cq, k_top * SEL_BLOCK), sel_mask)
        p = p.reshape(B, G, R, Cq, k_top, SEL_BLOCK)
        o_slc = jnp.einsum('bgrqkl,bgqkld->bqgrd', p.astype(vs.dtype), v_sel)
        k_w = lax.dynamic_slice_in_dim(kw_pad, q0, Cq + WINDOW, axis=1)
        v_w = lax.dynamic_slice_in_dim(vw_pad, q0, Cq + WINDOW, axis=1)
        wpos = q0 - WINDOW + jnp.arange(Cq + WINDOW)
        win_mask = ((wpos[None, :] <= t[:, None]) & (wpos[None, :] > t[:, None] - WINDOW)
                    & (wpos[None, :] >= 0))
        s = jnp.einsum('bqgrd,bkgd->bgrqk', qb, k_w) * scale
        p = masked_softmax(s, win_mask)
        o_win = jnp.einsum('bgrqk,bkgd->bqgrd', p.astype(vw.dtype), v_w)
        o = gb[..., 0:1] * o_cmp + gb[..., 1:2] * o_slc + gb[..., 2:3] * o_win
        return o.reshape(B, Cq, H * hd)

    out = lax.map(block, jnp.arange(S // Cq))
    return out.transpose(1, 0, 2, 3).reshape(B, S, H * hd)


def mla_mixer(q, k, v):
    B, S, H, _ = q.shape
    scale = MLA_QK_DIM ** -0.5
    kpos = jnp.arange(S)

    def block(n):
        q0 = n * MLA_Q_BLOCK
        t = q0 + jnp.arange(MLA_Q_BLOCK)
        qb = lax.dynamic_slice_in_dim(q, q0, MLA_Q_BLOCK, axis=1)
        s = jnp.einsum('bqhd,bkhd->bhqk', qb, k) * scale
        p = masked_softmax(s, kpos[None, :] <= t[:, None])
        o = jnp.einsum('bhqk,bkhd->bqhd', p.astype(v.dtype), v)
        return o.reshape(B, MLA_Q_BLOCK, H * MLA_V_DIM)

    out = lax.map(block, jnp.arange(S // MLA_Q_BLOCK))
    return out.transpose(1, 0, 2, 3).reshape(B, S, H * MLA_V_DIM)


def hybrid_layer(x, c, pos, w_ada, b_ada, attn_norm, ffn_norm, w_in,
                 nsa_q_norm, nsa_kc_norm, nsa_ks_norm, nsa_kw_norm,
                 cmp_k_pe, cmp_k_w1, cmp_k_w2, cmp_v_pe, cmp_v_w1, cmp_v_w2,
                 mla_cq_norm, mla_ckv_norm, w_uq, w_ukv, mla_q_norm, mla_k_norm,
                 w_o, w_up, w_conv, b_conv, w_down):
    B, S, D = x.shape
    G = NSA_KV_HEADS
    mod = (jax.nn.silu(c) @ w_ada + b_ada)[:, None, :]
    shift_a, scale_a, gate_a, shift_f, scale_f, gate_f = jnp.split(mod, 6, axis=-1)

    h = rms_norm(x, attn_norm) * (1.0 + scale_a) + shift_a
    (nq, nkc, nvc, nks, nvs, nkw, nvw, ngate,
     cq, ckv, kr, bgate) = jnp.split(h @ w_in, IN_OFFSETS, axis=-1)

    def heads(t, n):
        return t.reshape(B, S, n, -1)

    q_a = rope(rms_norm(heads(nq, NSA_HEADS), nsa_q_norm), pos)
    kc = rms_norm(compress(rope(heads(nkc, G), pos), cmp_k_pe, cmp_k_w1, cmp_k_w2), nsa_kc_norm)
    vc = compress(heads(nvc, G), cmp_v_pe, cmp_v_w1, cmp_v_w2)
    ks = rope(rms_norm(heads(nks, G), nsa_ks_norm), pos)
    kw = rope(rms_norm(heads(nkw, G), nsa_kw_norm), pos)
    g_nsa = jax.nn.sigmoid(heads(ngate, NSA_HEADS))
    o_a = nsa_mixer(q_a, kc, vc, ks, heads(nvs, G), kw, heads(nvw, G), g_nsa)

    q_b = (rms_norm(cq, mla_cq_norm) @ w_uq).reshape(B, S, MLA_HEADS, MLA_QK_DIM)
    kv_b = (rms_norm(ckv, mla_ckv_norm) @ w_ukv).reshape(B, S, MLA_HEADS, MLA_NOPE_DIM + MLA_V_DIM)
    k_nope, v_b = jnp.split(kv_b, [MLA_NOPE_DIM], axis=-1)
    k_b = jnp.concatenate(
        [k_nope, jnp.broadcast_to(kr[:, :, None, :], (B, S, MLA_HEADS, MLA_ROPE_DIM))], axis=-1)
    q_b = rms_norm(q_b, mla_q_norm)
    k_b = rms_norm(k_b, mla_k_norm)
    q_b = jnp.concatenate([q_b[..., :MLA_NOPE_DIM], rope(q_b[..., MLA_NOPE_DIM:], pos)], axis=-1)
    k_b = jnp.concatenate([k_b[..., :MLA_NOPE_DIM], rope(k_b[..., MLA_NOPE_DIM:], pos)], axis=-1)
    o_b = mla_mixer(q_b, k_b, v_b)

    g_a, g_b = jnp.split(jax.nn.sigmoid(bgate), 2, axis=-1)
    x = x + gate_a * ((g_a * o_a + g_b * o_b) @ w_o)

    h = rms_norm(x, ffn_norm) * (1.0 + scale_f) + shift_f
    u = lax.conv_general_dilated(
        h @ w_up, w_conv[:, None, :], window_strides=(1,), padding=[(CONV_WIDTH - 1, 0)],
        dimension_numbers=('NWC', 'WIO', 'NWC'), feature_group_count=2 * D_FF) + b_conv
    val, gt = jnp.split(u, 2, axis=-1)
    x = x + gate_f * ((jax.nn.silu(gt) * val) @ w_down)
    return x


def setup_inputs(seed: int = 0) -> dict:
    key = jax.random.key(seed)
    k = jax.random.split(key, 32)
    L, hd = DEPTH, NSA_HEAD_DIM

    def nrm(kk, shape, scale):
        return jax.random.normal(kk, shape, jnp.float32) * scale

    def gain(kk, n):
        return 1.0 + nrm(kk, (L, n), 0.02)

    return {
        'x': nrm(k[0], (BATCH, SEQ, D_MODEL), 1.0),
        'c': nrm(k[1], (BATCH, D_MODEL), 1.0),
        'w_ada': nrm(k[2], (L, D_MODEL, 6 * D_MODEL), D_MODEL ** -0.5),
        'b_ada': nrm(k[3], (L, 6 * D_MODEL), 0.01),
        'attn_norm': gain(k[4], D_MODEL),
        'ffn_norm': gain(k[5], D_MODEL),
        'w_in': nrm(k[6], (L, D_MODEL, IN_WIDTH), D_MODEL ** -0.5),
        'nsa_q_norm': gain(k[7], hd),
        'nsa_kc_norm': gain(k[8], hd),
        'nsa_ks_norm': gain(k[9], hd),
        'nsa_kw_norm': gain(k[10], hd),
        'cmp_k_pe': nrm(k[11], (L, CMP_BLOCK, hd), 0.5),
        'cmp_k_w1': nrm(k[12], (L, CMP_BLOCK * hd, CMP_HIDDEN), (CMP_BLOCK * hd) ** -0.5),
        'cmp_k_w2': nrm(k[13], (L, CMP_HIDDEN, hd), CMP_HIDDEN ** -0.5),
        'cmp_v_pe': nrm(k[14], (L, CMP_BLOCK, hd), 0.5),
        'cmp_v_w1': nrm(k[15], (L, CMP_BLOCK * hd, CMP_HIDDEN), (CMP_BLOCK * hd) ** -0.5),
        'cmp_v_w2': nrm(k[16], (L, CMP_HIDDEN, hd), CMP_HIDDEN ** -0.5),
        'mla_cq_norm': gain(k[17], MLA_Q_RANK),
        'mla_ckv_norm': gain(k[18], MLA_KV_RANK),
        'w_uq': nrm(k[19], (L, MLA_Q_RANK, MLA_HEADS * MLA_QK_DIM), MLA_Q_RANK ** -0.5),
        'w_ukv': nrm(k[20], (L, MLA_KV_RANK, MLA_HEADS * (MLA_NOPE_DIM + MLA_V_DIM)), MLA_KV_RANK ** -0.5),
        'mla_q_norm': gain(k[21], MLA_QK_DIM),
        'mla_k_norm': gain(k[22], MLA_QK_DIM),
        'w_o': nrm(k[23], (L, D_MODEL, D_MODEL), D_MODEL ** -0.5),
        'w_up': nrm(k[24], (L, D_MODEL, 2 * D_FF), D_MODEL ** -0.5),
        'w_conv': nrm(k[25], (L, CONV_WIDTH, 2 * D_FF), CONV_WIDTH ** -0.5),
        'b_conv': nrm(k[26], (L, 2 * D_FF), 0.01),
        'w_down': nrm(k[27], (L, D_FF, D_MODEL), D_FF ** -0.5),
    }


def reference(x, c, w_ada, b_ada, attn_norm, ffn_norm, w_in,
              nsa_q_norm, nsa_kc_norm, nsa_ks_norm, nsa_kw_norm,
              cmp_k_pe, cmp_k_w1, cmp_k_w2, cmp_v_pe, cmp_v_w1, cmp_v_w2,
              mla_cq_norm, mla_ckv_norm, w_uq, w_ukv, mla_q_norm, mla_k_norm,
              w_o, w_up, w_conv, b_conv, w_down):
    pos = jnp.arange(x.shape[1], dtype=jnp.float32)
    for l in range(DEPTH):
        x = hybrid_layer(x, c, pos, w_ada[l], b_ada[l], attn_norm[l], ffn_norm[l], w_in[l],
                         nsa_q_norm[l], nsa_kc_norm[l], nsa_ks_norm[l], nsa_kw_norm[l],
                         cmp_k_pe[l], cmp_k_w1[l], cmp_k_w2[l], cmp_v_pe[l], cmp_v_w1[l], cmp_v_w2[l],
                         mla_cq_norm[l], mla_ckv_norm[l], w_uq[l], w_ukv[l], mla_q_norm[l], mla_k_norm[l],
                         w_o[l], w_up[l], w_conv[l], b_conv[l], w_down[l])
    return x
```

```python
import contextlib
import numpy as np
import ml_dtypes
import concourse.bass as bass
import concourse.mybir as mybir
from concourse.bass_utils import run_bass_kernel_spmd

F32 = mybir.dt.float32
BF16 = mybir.dt.bfloat16
AF = mybir.ActivationFunctionType
ALU = mybir.AluOpType
AX = mybir.AxisListType

EPS = 1e-6
NEGB = -30000.0
EXPB = -4.0


class Buf:
    __slots__ = ("w", "r")

    def __init__(self):
        self.w = {}
        self.r = {}


class T:
    def __init__(self, t):
        self.t = t
        self.b = Buf()

    def __getitem__(self, k):
        return self.t[k]


class TO(T):
    def __init__(self, t, off):
        super().__init__(t)
        self.off = off

    def __getitem__(self, k):
        p, c = k
        return self.t[p, slice(c.start - self.off, c.stop - self.off)]


def _b(x):
    return x.b if isinstance(x, T) else x


class FW:
    ROT = 12000
    NQ = 24

    def __init__(self, nc, es):
        self.nc, self.es = nc, es
        self.E = {"pe": nc.tensor, "act": nc.scalar, "dve": nc.vector, "pool": nc.gpsimd, "sp": nc.sync}
        self.sem, self.cnt = {}, {}
        self.nsem = 0
        for e in self.E:
            self._newsem(e)
        self.waited = {e: {} for e in self.E}
        self.dq = {}
        self.n = 0
        self.rec = None

    def _newsem(self, e):
        s = self.es.enter_context(self.nc.semaphore(f"s{e}{self.nsem}"))
        self.nsem += 1
        self.sem[e] = s
        self.cnt[e] = 0

    def _wait(self, e, deps):
        for s, (v, pe) in deps.items():
            if self.waited[e].get(s, 0) >= v:
                continue
            self.E[e].wait_ge(s, v)
            self.waited[e][s] = v
            self.n += 1

    def _deps(self, e, r, w):
        deps = {}

        def add(d, raw):
            for s, (v, pe) in d.items():
                if pe == e and e != "dma":
                    if e == "pe" or not raw:
                        continue
                if deps.get(s, (0,))[0] < v:
                    deps[s] = (v, pe)
        for b in r:
            add(_b(b).w, True)
        for b in w:
            add(_b(b).w, False)
            add(_b(b).r, False)
        return deps

    def I(self, e, fn, r=(), w=()):
        if self.rec is not None:
            r, w = list(r), list(w)
            self.rec.append(lambda: self._I(e, fn, r, w))
            return None
        return self._I(e, fn, r, w)

    def _I(self, e, fn, r=(), w=()):
        self._wait(e, self._deps(e, r, w))
        if self.cnt[e] >= self.ROT:
            self._newsem(e)
        inst = fn()
        s = self.sem[e]
        inst.then_inc(s, 1)
        self.cnt[e] += 1
        self.n += 1
        tok = (self.cnt[e], e)
        for b in w:
            b = _b(b)
            b.w = {s: tok}
            b.r = {}
        for b in r:
            _b(b).r[s] = tok
        return inst

    def dma(self, out, in_, r=(), w=(), q="sp"):
        if self.rec is not None:
            r, w = list(r), list(w)
            self.rec.append(lambda: self._dma(out, in_, r, w, q))
            return None
        return self._dma(out, in_, r, w, q)

    def record(self, fn):
        self.rec = []
        fn()
        lst, self.rec = self.rec, None
        return lst

    @staticmethod
    def interleave_skewed(streams):
        n = len(streams)
        L = max(len(st_) for st_ in streams)
        pos = [0] * n
        start = [0] + [0] * (n - 1)
        i = 0
        while any(pos[k] < len(streams[k]) for k in range(n)):
            for k in range(n):
                if i >= start[k] and pos[k] < len(streams[k]):
                    streams[k][pos[k]]()
                    pos[k] += 1
            i += 1

    @staticmethod
    def interleave(lists):
        for i in range(max(len(l) for l in lists)):
            for l in lists:
                if i < len(l):
                    l[i]()

    def _dma(self, out, in_, r=(), w=(), q="sp"):
        self._wait(q, self._deps("dma", r, w))
        d = self.dq.setdefault(q, {"sems": [], "i": 0})
        if len(d["sems"]) < self.NQ:
            s = self.es.enter_context(self.nc.semaphore(f"d{q}{len(d['sems'])}"))
            ent = [s, 0]
            d["sems"].append(ent)
        else:
            ent = d["sems"][d["i"] % self.NQ]
            d["i"] += 1
            self._wait(q, {ent[0]: (16 * ent[1], "dma")})
        inst = self.E[q].dma_start(out=out, in_=in_)
        inst.then_inc(ent[0], 16)
        ent[1] += 1
        self.n += 1
        tok = (16 * ent[1], "dma")
        for b in w:
            b = _b(b)
            b.w = {ent[0]: tok}
            b.r = {}
        for b in r:
            _b(b).r[ent[0]] = tok

    def barrier(self):
        toks = {}
        for e in self.E:
            if self.cnt[e] > 0:
                toks[self.sem[e]] = (self.cnt[e], "x")
        for q, d in self.dq.items():
            for s, c in d["sems"]:
                if c:
                    toks[s] = (16 * c, "dma")
        for e in self.E:
            self._wait(e, toks)


O_NQ, O_NKC, O_NVC, O_NKS, O_NVS, O_NKW, O_NVW, O_NG, O_CQ, O_CKV, O_KR, O_BG = (
    0, 1024, 1280, 1536, 1792, 2048, 2304, 2560, 2584, 2968, 3224, 3288)
IN_W = 5336


def host_consts(S):
    NT = S // 128
    NSEL = S // 64
    n_cmp = S // 16 - 1
    pos = np.arange(S, dtype=np.float32)
    inv128 = (10000.0 ** (-np.arange(64, dtype=np.float32) * 2.0 / 128)).astype(np.float32)
    inv64 = (10000.0 ** (-np.arange(32, dtype=np.float32) * 2.0 / 64)).astype(np.float32)
    a128 = pos[:, None] * inv128[None, :]
    a64 = pos[:, None] * inv64[None, :]
    c = {}
    c["cs128"] = np.concatenate([np.cos(a128), np.sin(a128)], axis=1).astype(np.float32)
    c["cs64"] = np.concatenate([np.cos(a64), np.sin(a64)], axis=1).astype(np.float32)
    p = np.arange(128)[:, None]
    f = np.arange(128)[None, :]
    c["ident"] = (p == f).astype(ml_dtypes.bfloat16)
    c["identf"] = (p == f).astype(np.float32)
    c["tri"] = (p <= f).astype(ml_dtypes.bfloat16)
    c["atri"] = (p > f).astype(ml_dtypes.bfloat16)
    W0 = 512 + 8 * (NT - 1)
    cc = np.arange(W0)[None, :]
    m = cc - 8 * (NT - 1)
    c["m0ext"] = ((16 * m + 31) <= p).astype(np.float32)
    CO = 2 * (NT - 1)
    W1 = NSEL + CO
    cc = np.arange(W1)[None, :]
    d = cc - CO
    hi = (p >= 64).astype(np.int64)
    c["aext"] = (d <= hi - 2).astype(np.float32)
    forced = (d == hi) | (d == hi - 1)
    c["fext"] = np.where(forced, 1e9, np.where(d > hi, -1.0, 0.0)).astype(np.float32)
    KJ = min(128, NSEL)
    E = np.zeros((KJ, NT, 128), dtype=np.float32)
    for kt in range(NT):
        E[2 * kt, kt, :64] = 1.0
        E[2 * kt + 1, kt, 64:] = 1.0
    c["esel"] = E.astype(ml_dtypes.bfloat16)
    return c


def build(S, dbg=False, upto=99):
    NT = S // 128
    NSEL = S // 64
    KJ = min(128, NSEL)
    n_cmp = S // 16 - 1
    NCT = (n_cmp + 127) // 128
    NCP = NCT * 128
    CO = 2 * (NT - 1)
    nc = bass.Bass("TRN2", target_bir_lowering=False)
    okind = "ExternalOutput"

    def din(name, shape, dt=F32):
        return nc.dram_tensor(name, list(shape), dt, kind="ExternalInput").ap()

    def dscr(name, shape, dt):
        return nc.dram_tensor(name, list(shape), dt, kind=okind).ap()

    x = din("x", [S, 1024])
    ccol = din("ccol", [128, 8])
    w_ada = din("w_ada", [1024, 6144])
    b_ada = din("b_ada", [6144])
    attn_norm = din("attn_norm", [1024])
    ffn_norm = din("ffn_norm", [1024])
    w_in = din("w_in", [1024, IN_W])
    g_q = din("nsa_q_norm", [128])
    g_kc = din("nsa_kc_norm", [128])
    g_ks = din("nsa_ks_norm", [128])
    g_kw = din("nsa_kw_norm", [128])
    pe_kT = din("pe_kT", [128, 32])
    k_w1 = din("cmp_k_w1", [4096, 256])
    k_w2 = din("cmp_k_w2", [256, 128])
    pe_vT = din("pe_vT", [128, 32])
    v_w1 = din("cmp_v_w1", [4096, 256])
    v_w2 = din("cmp_v_w2", [256, 128])
    g_cq = din("mla_cq_norm", [384])
    g_ckv = din("mla_ckv_norm", [256])
    w_uq = din("w_uq", [384, 1536])
    w_ukv = din("w_ukv", [256, 2048])
    g_mq = din("mla_q_norm", [192])
    g_mk = din("mla_k_norm", [192])
    w_o = din("w_o", [1024, 1024])
    w_up = din("w_up", [1024, 5632])
    wconv_l = din("wconv_l", [128, 44, 3])
    bconv_l = din("bconv_l", [128, 44])
    w_down = din("w_down", [2816, 1024])
    cs128 = din("cs128", [S, 128])
    cs64 = din("cs64", [S, 64])
    c_ident = din("ident", [128, 128], BF16)
    c_identf = din("identf", [128, 128])
    c_tri = din("tri", [128, 128], BF16)
    c_atri = din("atri", [128, 128], BF16)
    c_m0 = din("m0ext", [128, 512 + 8 * (NT - 1)])
    c_aext = din("aext", [128, NSEL + CO])
    c_fext = din("fext", [128, NSEL + CO])
    c_esel = din("esel", [KJ, NT, 128], BF16)
    out = nc.dram_tensor("out", [S, 1024], F32, kind="ExternalOutput").ap()

    QaT = dscr("QaT", [8, 128, S], BF16)
    KcT = dscr("KcT", [2, 128, S], BF16)
    VcT = dscr("VcT", [2, 128, S], BF16)
    KsT = dscr("KsT", [2, 128, S], BF16)
    KwT = dscr("KwT", [2, 128, S], BF16)
    Vs = dscr("Vs", [S, 2, 128], BF16)
    Vw = dscr("Vw", [S, 2, 128], BF16)
    Gn = dscr("Gn", [S, 24], F32)
    BG = dscr("BG", [S, 2048], BF16)
    QbN = dscr("QbN", [8, 128, S], BF16)
    QbR = dscr("QbR", [8, 64, S], BF16)
    KbN = dscr("KbN", [8, 128, S], BF16)
    KbR = dscr("KbR", [8, 64, S], BF16)
    Vb = dscr("Vb", [S, 8, 128], BF16)
    Oc = dscr("Oc", [S, 1024], F32)
    BT = dscr("BT", [2, KJ, S], BF16)
    Ya = dscr("Ya", [S, 1024], F32)
    Yb = dscr("Yb", [S, 1024], BF16)
    MODB = dscr("MODB", [128, 6144], F32)

    with contextlib.ExitStack() as es:
        fw = FW(nc, es)
        global LASTFW
        LASTFW = fw
        I = fw.I
        dma = fw.dma
        V, G, A, PE = "dve", "pool", "act", "pe"

        def sb(st, name, shape, dt):
            return T(st.enter_context(nc.sbuf_tensor("sb_" + name, list(shape), dt)))

        def ps(st, name, shape, dt=F32):
            return T(st.enter_context(nc.psum_tensor("ps_" + name, list(shape), dt)))

        ident = sb(es, "ident", [128, 128], BF16)
        tri = sb(es, "tri", [128, 128], BF16)
        atri = sb(es, "atri", [128, 128], BF16)
        dma(ident[:], c_ident, w=[ident])
        dma(tri[:], c_tri, w=[tri])
        identf = sb(es, "identf", [128, 128], F32)
        dma(identf[:], c_identf, w=[identf])
        ones_f = sb(es, "ones_f", [128, 2], F32)
        I(V, lambda: nc.vector.memset(ones_f[:], 1.0), w=[ones_f])
        dma(atri[:], c_atri, w=[atri])
        SH_A, A_ATT, G_A, SH_F, A_FFN, G_F = [slice(i * 1024, (i + 1) * 1024) for i in range(6)]

        def rsqrt_ms(ssT, ss_ap, rsT, rs_ap, inv_n, rows=128):
            I(A, lambda: nc.scalar.activation(out=rs_ap, in_=ss_ap, func=AF.Sqrt, bias=eps_t[0:rows, 0:1], scale=inv_n),
              r=[ssT, eps_t], w=[rsT])
            I(V, lambda: nc.vector.reciprocal(out=rs_ap, in_=rs_ap), r=[rsT], w=[rsT])

        eps_t = sb(es, "eps_t", [128, 2], F32)
        I(V, lambda: nc.vector.memset(eps_t[:, 0:1], EPS), w=[eps_t])
        I(V, lambda: nc.vector.memset(eps_t[:, 1:2], EXPB), w=[eps_t])

        with contextlib.ExitStack() as st:
            modb = sb(st, "modb", [128, 6144], F32)
            cs_t = sb(st, "cs_t", [128, 8], F32)
            sc_t = sb(st, "sc_t", [128, 8], F32)
            scb = sb(st, "scb", [128, 8, 128], F32)
            wst = [sb(st, f"wst{i}", [128, 3072], F32) for i in range(2)]
            gtmp = sb(st, "gtmp", [128, 1024], F32)
            pm = [ps(st, f"pm{i}", [128, 512]) for i in range(6)]
            dma(cs_t[:], ccol, w=[cs_t])
            dma(modb[:], b_ada.partition_broadcast(128), w=[modb])
            I(A, lambda: nc.scalar.activation(out=sc_t[:], in_=cs_t[:], func=AF.Silu), r=[cs_t], w=[sc_t])
            I(V, lambda: nc.vector.tensor_copy(out=scb[:], in_=sc_t[:].unsqueeze(2).broadcast_to([128, 8, 128])),
              r=[sc_t], w=[scb])
            for half in range(2):
                for kc in range(8):
                    wb = wst[kc % 2]
                    dma(wb[:], w_ada[kc * 128:(kc + 1) * 128, half * 3072:(half + 1) * 3072], w=[wb])
                    for j in range(6):
                        I(PE, lambda j=j, wb=wb, kc=kc: nc.tensor.matmul(
                            pm[j][:], lhsT=scb[:, kc, :], rhs=wb[:, j * 512:(j + 1) * 512],
                            start=(kc == 0), stop=(kc == 7)), r=[scb, wb], w=[pm[j]])
                for j in range(6):
                    cs = slice(half * 3072 + j * 512, half * 3072 + (j + 1) * 512)
                    I(V, lambda j=j, cs=cs: nc.vector.tensor_tensor(out=modb[:, cs], in0=pm[j][:], in1=modb[:, cs],
                                                                    op=ALU.add), r=[pm[j], modb], w=[modb])
            for gsrc, sl in ((attn_norm, A_ATT), (ffn_norm, A_FFN)):
                dma(gtmp[:], gsrc.partition_broadcast(128), w=[gtmp])
                I(V, lambda sl=sl: nc.vector.scalar_tensor_tensor(out=modb[:, sl], in0=modb[:, sl], scalar=1.0,
                                                                  in1=gtmp[:], op0=ALU.add, op1=ALU.mult),
                  r=[modb, gtmp], w=[modb])
            dma(MODB, modb[:], r=[modb])
            fw.barrier()

        def load_cast(st_scratch, dst, dst_ap_fn, src_ap_fn, nchunks, shape, engs=(V, G, A)):
            for i in range(nchunks):
                stg = st_scratch[i % len(st_scratch)]
                dma(stg[tuple(slice(0, s_) for s_ in shape)] if False else stg_view(stg, shape), src_ap_fn(i), w=[stg])
                e = engs[i % len(engs)]
                if e == A:
                    I(A, lambda i=i, stg=stg: nc.scalar.copy(out=dst_ap_fn(i), in_=stg_view(stg, shape)), r=[stg], w=[dst])
                elif e == V:
                    I(V, lambda i=i, stg=stg: nc.vector.tensor_copy(out=dst_ap_fn(i), in_=stg_view(stg, shape)), r=[stg], w=[dst])
                else:
                    I(G, lambda i=i, stg=stg: nc.gpsimd.tensor_copy(out=dst_ap_fn(i), in_=stg_view(stg, shape)), r=[stg], w=[dst])

        def stg_view(stg, shape):
            n = 1
            for s_ in shape[1:]:
                n *= s_
            v = stg[0:shape[0], 0:n]
            if len(shape) == 3:
                v = v.rearrange("p (a b) -> p a b", a=shape[1])
            return v

        def bcast_load(st, name, src, n):
            t = sb(st, name, [128, n], F32)
            dma(t[:], src.partition_broadcast(128), w=[t])
            return t

        with contextlib.ExitStack() as st:
            winb = sb(st, "winb", [128, 8, IN_W], BF16)
            wuqb = sb(st, "wuqb", [128, 3, 1536], BF16)
            wukvb = sb(st, "wukvb", [128, 2, 2048], BF16)
            with contextlib.ExitStack() as st2:
                stg = [sb(st2, f"stg{i}", [128, IN_W], F32) for i in range(2)]
                load_cast(stg, winb, lambda i: winb[:, i, :], lambda i: w_in[i * 128:(i + 1) * 128, :], 8, [128, IN_W])
                load_cast(stg, wuqb, lambda i: wuqb[:, i, :], lambda i: w_uq[i * 128:(i + 1) * 128, :], 3, [128, 1536])
                load_cast(stg, wukvb, lambda i: wukvb[:, i, :], lambda i: w_ukv[i * 128:(i + 1) * 128, :], 2, [128, 2048])
                fw.barrier()
            modb = TO(st.enter_context(nc.sbuf_tensor("sb_mod1", [128, 2048], F32)), 0)
            dma(modb.t[:], MODB[:, 0:2048], w=[modb])
            gq_t = bcast_load(st, "gq_t", g_q, 128)
            gks_t = bcast_load(st, "gks_t", g_ks, 128)
            gkw_t = bcast_load(st, "gkw_t", g_kw, 128)
            gcq_t = bcast_load(st, "gcq_t", g_cq, 384)
            gckv_t = bcast_load(st, "gckv_t", g_ckv, 256)
            gmq_t = bcast_load(st, "gmq_t", g_mq, 192)
            gmk_t = bcast_load(st, "gmk_t", g_mk, 192)

            def mk1(u):
                B = {}
                B["cst2"] = [sb(st, f"cst_{u}{i}", [128, 192], F32) for i in range(2)]
                for nm, shp, dt in (("xt", [128, 1024], F32), ("ss1", [128, 1], F32), ("rs1", [128, 1], F32),
                                    ("nb", [128, 8, 192], BF16), ("hT", [128, 8, 128], BF16), ("f_a", [128, 1536], F32),
                                    ("f_b", [128, 1536], F32), ("f_d", [128, 1024], F32), ("ssn", [128, 8], F32), ("rsn", [128, 8], F32),
                                    ("vbt", [128, 8, 128], BF16), ("gnt", [128, 24], F32), ("cqT", [128, 3, 128], BF16),
                                    ("ckvT", [128, 2, 128], BF16), ("krf", [128, 64], F32)):
                    B[nm] = sb(st, f"{nm}_{u}", shp, dt)
                B["bgt"] = [sb(st, f"bgt_{u}{i}", [128, 512], BF16) for i in range(2)]
                B["sgA"] = [sb(st, f"sgA_{u}{i}", [128, 8, 128], BF16) for i in range(2)]
                B["sgR"] = [sb(st, f"sgR_{u}{i}", [64, 8, 128], BF16) for i in range(2)]
                B["pp"] = [ps(st, f"pp_{u}{i}", [128, 512]) for i in range(2)]
                B["pT"] = ps(st, f"pT_{u}", [128, 1024], BF16)
                B["npp"] = 0
                B["nA"] = 0
                B["nR"] = 0
                return B
            sets1 = [mk1(0), mk1(1)]

            def load_tile(t):
                B = sets1[t % 2]
                dma(B["xt"][:], x[t * 128:(t + 1) * 128, :], w=[B["xt"]])
                c_ = B["cst2"][(t // 2) % 2]
                dma(c_[:, 0:128], cs128[t * 128:(t + 1) * 128, :], w=[c_])
                dma(c_[:, 128:192], cs64[t * 128:(t + 1) * 128, :], w=[c_])

            def p1_tile(t):
                B = sets1[t % 2]
                xtt, ss1, rs1, nb, hT, f_a, f_b, f_d = (B[k] for k in ("xt", "ss1", "rs1", "nb", "hT", "f_a", "f_b", "f_d"))
                cs_ = B["cst2"][(t // 2) % 2]
                ssn, rsn, vbt, gnt, cqT, ckvT, krf, pT_ = (B[k] for k in ("ssn", "rsn", "vbt", "gnt", "cqT", "ckvT", "krf", "pT"))
                tok = slice(t * 128, (t + 1) * 128)
                nbf = nb[:].rearrange("p h d -> p (h d)")

                def proj(lhsT_t, lhs_fn, nk, w_t, c0, c1):
                    p = B["pp"][B["npp"] % 2]
                    B["npp"] += 1
                    for kc in range(nk):
                        I(PE, lambda kc=kc: nc.tensor.matmul(p[:, 0:c1 - c0], lhsT=lhs_fn(kc), rhs=w_t[:, kc, c0:c1],
                                                             start=(kc == 0), stop=(kc == nk - 1)), r=[lhsT_t, w_t], w=[p])
                    return p

                def evac(p, c, dstT, dst_ap, eng=A):
                    if eng == A:
                        I(A, lambda: nc.scalar.copy(out=dst_ap, in_=p[:, 0:c]), r=[p], w=[dstT])
                    else:
                        I(V, lambda: nc.vector.tensor_copy(out=dst_ap, in_=p[:, 0:c]), r=[p], w=[dstT])

                def norm_rope(src, src_ap, nh, hd, gain_t, do_norm, rope_off, rope_half, cs_off, dst_ap, dstT):
                    if do_norm:
                        sq = f_b[:, 0:nh * hd].rearrange("p (h d) -> p h d", h=nh)
                        I(A, lambda: nc.scalar.activation(out=sq, in_=src_ap, func=AF.Square), r=[src], w=[f_b])
                        I(V, lambda: nc.vector.tensor_reduce(out=ssn[:, 0:nh], in_=sq, axis=AX.X, op=ALU.add), r=[f_b], w=[ssn])
                        rsqrt_ms(ssn, ssn[:, 0:nh], rsn, rsn[:, 0:nh], 1.0 / hd)
                        I(V, lambda: nc.vector.tensor_tensor(out=src_ap, in0=src_ap, in1=rsn[:, 0:nh].unsqueeze(2).broadcast_to([128, nh, hd]),
                                                             op=ALU.mult), r=[src, rsn], w=[src])
                        I(G, lambda: nc.gpsimd.tensor_tensor(out=src_ap, in0=src_ap, in1=gain_t[:, 0:hd].unsqueeze(1).broadcast_to([128, nh, hd]),
                                                             op=ALU.mult), r=[src, gain_t], w=[src])
                    if rope_half == 0:
                        I(V, lambda: nc.vector.tensor_copy(out=dst_ap, in_=src_ap), r=[src], w=[dstT])
                        return
                    if rope_off > 0:
                        I(G, lambda: nc.gpsimd.tensor_copy(out=dst_ap[:, :, 0:rope_off], in_=src_ap[:, :, 0:rope_off]), r=[src], w=[dstT])
                    hh_ = rope_half
                    x1 = src_ap[:, :, rope_off:rope_off + hh_]
                    x2 = src_ap[:, :, rope_off + hh_:rope_off + 2 * hh_]
                    cb = cs_[:, cs_off:cs_off + hh_].unsqueeze(1).broadcast_to([128, nh, hh_])
                    sbb = cs_[:, cs_off + hh_:cs_off + 2 * hh_].unsqueeze(1).broadcast_to([128, nh, hh_])
                    t1 = f_d[:, 0:nh * hh_].rearrange("p (h d) -> p h d", h=nh)
                    t2 = f_d[:, 512:512 + nh * hh_].rearrange("p (h d) -> p h d", h=nh)
                    t3 = f_b[:, 0:nh * hh_].rearrange("p (h d) -> p h d", h=nh)
                    t4 = f_b[:, 512:512 + nh * hh_].rearrange("p (h d) -> p h d", h=nh)
                    I(V, lambda: nc.vector.tensor_tensor(out=t1, in0=x1, in1=cb, op=ALU.mult), r=[src, cs_], w=[f_d])
                    I(V, lambda: nc.vector.tensor_tensor(out=t2, in0=x2, in1=sbb, op=ALU.mult), r=[src, cs_], w=[f_d])
                    I(G, lambda: nc.gpsimd.tensor_tensor(out=t3, in0=x1, in1=sbb, op=ALU.mult), r=[src, cs_], w=[f_b])
                    I(G, lambda: nc.gpsimd.tensor_tensor(out=t4, in0=x2, in1=cb, op=ALU.mult), r=[src, cs_], w=[f_b])
                    I(V, lambda: nc.vector.tensor_tensor(out=dst_ap[:, :, rope_off:rope_off + hh_], in0=t1, in1=t2, op=ALU.subtract),
                      r=[f_d], w=[dstT])
                    I(G, lambda: nc.gpsimd.tensor_tensor(out=dst_ap[:, :, rope_off + hh_:rope_off + 2 * hh_], in0=t3, in1=t4, op=ALU.add),
                      r=[f_b], w=[dstT])

                def transposes(src_t, src_ap_fn, n, rows, dstT, dst_ap):
                    for i in range(n):
                        I(PE, lambda i=i: nc.tensor.transpose(out=pT_[0:rows, i * 128:(i + 1) * 128], in_=src_ap_fn(i), identity=ident[:]),
                          r=[src_t, ident], w=[pT_])
                    I(A, lambda: nc.scalar.copy(out=dst_ap, in_=pT_[0:rows, 0:n * 128].rearrange("p (a b) -> p a b", a=n)),
                      r=[pT_], w=[dstT])

                def slotA():
                    sg = B["sgA"][B["nA"] % 2]
                    B["nA"] += 1
                    return sg

                def slotR():
                    sg = B["sgR"][B["nR"] % 2]
                    B["nR"] += 1
                    return sg

                def outT(dst, sg, b0, nblk):
                    dma(dst[:, :, tok].rearrange("h d t -> d h t"), sg[:, b0:b0 + nblk, :], r=[sg])

                if t + 2 < NT:
                    pass
                I(A, lambda: nc.scalar.activation(out=nbf[:, 0:1024], in_=xtt[:], func=AF.Square, accum_out=ss1[:, 0:1]), r=[xtt], w=[nb, ss1])
                rsqrt_ms(ss1, ss1[:, 0:1], rs1, rs1[:, 0:1], 1.0 / 1024)
                I(V, lambda: nc.vector.scalar_tensor_tensor(out=f_b[:, 0:1024], in0=xtt[:], scalar=rs1[:, 0:1], in1=modb[:, A_ATT],
                                                            op0=ALU.mult, op1=ALU.mult), r=[xtt, rs1, modb], w=[f_b])
                I(G, lambda: nc.gpsimd.tensor_tensor(out=nbf[:, 0:1024], in0=f_b[:, 0:1024], in1=modb[:, SH_A], op=ALU.add), r=[f_b, modb], w=[nb])
                transposes(nb, lambda i: nbf[:, i * 128:(i + 1) * 128], 8, 128, hT, hT[:])
                if t + 2 < NT:
                    load_tile(t + 2)
                lh = lambda kc: hT[:, kc, :]
                for gq in range(2):
                    p = proj(hT, lh, 8, winb, O_NQ + gq * 512, O_NQ + (gq + 1) * 512)
                    evac(p, 512, f_a, f_a[:, gq * 512:(gq + 1) * 512], eng=(A if gq else V))
                norm_rope(f_a, f_a[:, 0:1024].rearrange("p (h d) -> p h d", h=8), 8, 128, gq_t, True, 0, 64, 0, nb[:, :, 0:128], nb)
                sg = slotA()
                transposes(nb, lambda i: nb[:, i, 0:128], 8, 128, sg, sg[:, 0:8, :])
                outT(QaT, sg, 0, 8)
                p = proj(hT, lh, 8, winb, O_NKC, O_NKC + 512)
                evac(p, 512, f_a, f_a[:, 0:512])
                norm_rope(f_a, f_a[:, 0:256].rearrange("p (h d) -> p h d", h=2), 2, 128, None, False, 0, 64, 0, nb[:, 0:2, 0:128], nb)
                I(V, lambda: nc.vector.tensor_copy(out=nb[:, 2:4, 0:128], in_=f_a[:, 256:512].rearrange("p (h d) -> p h d", h=2)),
                  r=[f_a], w=[nb])
                sg = slotA()
                transposes(nb, lambda i: nb[:, i, 0:128], 4, 128, sg, sg[:, 0:4, :])
                outT(KcT, sg, 0, 2)
                outT(VcT, sg, 2, 2)
                sg = slotA()
                for wi, (off, gt_) in enumerate(((O_NKS, gks_t), (O_NKW, gkw_t))):
                    p = proj(hT, lh, 8, winb, off, off + 512)
                    evac(p, 512, f_a, f_a[:, 0:512])
                    norm_rope(f_a, f_a[:, 0:256].rearrange("p (h d) -> p h d", h=2), 2, 128, gt_, True, 0, 64, 0, nb[:, 0:2, 0:128], nb)
                    I(V, lambda wi=wi: nc.vector.tensor_copy(out=vbt[:, 2 * wi:2 * wi + 2, :], in_=f_a[:, 256:512].rearrange("p (h d) -> p h d", h=2)),
                      r=[f_a], w=[vbt])
                    transposes(nb, lambda i: nb[:, i, 0:128], 2, 128, sg, sg[:, 2 * wi:2 * wi + 2, :])
                outT(KsT, sg, 0, 2)
                outT(KwT, sg, 2, 2)
                dma(Vs[tok, :, :], vbt[:, 0:2, :], r=[vbt])
                dma(Vw[tok, :, :], vbt[:, 2:4, :], r=[vbt])
                p = proj(hT, lh, 8, winb, O_NG, O_CKV)
                I(A, lambda: nc.scalar.activation(out=gnt[:], in_=p[:, 0:24], func=AF.Sigmoid), r=[p], w=[gnt])
                evac(p, 408, f_a, f_a[:, 0:408], eng=V)
                dma(Gn[tok, :], gnt[:], r=[gnt])
                norm_rope(f_a, f_a[:, 24:408].rearrange("p (h d) -> p h d", h=1), 1, 384, gcq_t, True, 0, 0, 0,
                          nbf[:, 0:384].rearrange("p (h d) -> p h d", h=1), nb)
                transposes(nb, lambda i: nbf[:, i * 128:(i + 1) * 128], 3, 128, cqT, cqT[:])
                p = proj(hT, lh, 8, winb, O_CKV, O_BG)
                evac(p, 320, f_a, f_a[:, 0:320], eng=V)
                I(G, lambda: nc.gpsimd.tensor_copy(out=krf[:], in_=f_a[:, 256:320]), r=[f_a], w=[krf])
                norm_rope(f_a, f_a[:, 0:256].rearrange("p (h d) -> p h d", h=1), 1, 256, gckv_t, True, 0, 0, 0,
                          nbf[:, 512:768].rearrange("p (h d) -> p h d", h=1), nb)
                transposes(nb, lambda i: nbf[:, 512 + i * 128:512 + (i + 1) * 128], 2, 128, ckvT, ckvT[:])
                for j in range(4):
                    p = proj(hT, lh, 8, winb, O_BG + j * 512, O_BG + (j + 1) * 512)
                    bg_ = B["bgt"][j % 2]
                    I(A, lambda bg_=bg_, p=p: nc.scalar.activation(out=bg_[:], in_=p[:, 0:512], func=AF.Sigmoid), r=[p], w=[bg_])
                    dma(BG[tok, j * 512:(j + 1) * 512], bg_[:], r=[bg_])
                for j in range(3):
                    p = proj(cqT, lambda kc: cqT[:, kc, :], 3, wuqb, j * 512, (j + 1) * 512)
                    evac(p, 512, f_a, f_a[:, j * 512:(j + 1) * 512], eng=(A if j % 2 else V))
                norm_rope(f_a, f_a[:, 0:1536].rearrange("p (h d) -> p h d", h=8), 8, 192, gmq_t, True, 128, 32, 128, nb[:, :, :], nb)
                sg = slotA()
                transposes(nb, lambda i: nb[:, i, 0:128], 8, 128, sg, sg[:, 0:8, :])
                outT(QbN, sg, 0, 8)
                sgr = slotR()
                transposes(nb, lambda i: nb[:, i, 128:192], 8, 64, sgr, sgr[:, 0:8, :])
                outT(QbR, sgr, 0, 8)
                for half in range(2):
                    for j in range(2):
                        c0 = half * 1024 + j * 512
                        p = proj(ckvT, lambda kc: ckvT[:, kc, :], 2, wukvb, c0, c0 + 512)
                        evac(p, 512, f_a, f_a[:, j * 512:(j + 1) * 512], eng=(A if j % 2 else V))
                    kvv = f_a[:, 0:1024].rearrange("p (h d) -> p h d", h=4)
                    I(G, lambda half=half, kvv=kvv: nc.gpsimd.tensor_copy(out=vbt[:, 4 * half:4 * half + 4, :], in_=kvv[:, :, 128:256]), r=[f_a], w=[vbt])
                    I(V, lambda kvv=kvv: nc.vector.tensor_copy(out=kvv[:, :, 128:192], in_=krf[:].unsqueeze(1).broadcast_to([128, 4, 64])),
                      r=[krf, f_a], w=[f_a])
                    norm_rope(f_a, kvv[:, :, 0:192], 4, 192, gmk_t, True, 128, 32, 128, nb[:, 4 * half:4 * half + 4, :], nb)
                dma(Vb[tok, :, :], vbt[:], r=[vbt])
                sg = slotA()
                transposes(nb, lambda i: nb[:, i, 0:128], 8, 128, sg, sg[:, 0:8, :])
                outT(KbN, sg, 0, 8)
                sgr = slotR()
                transposes(nb, lambda i: nb[:, i, 128:192], 8, 64, sgr, sgr[:, 0:8, :])
                outT(KbR, sgr, 0, 8)

            load_tile(0)
            if NT > 1:
                load_tile(1)
            str0, str1 = [], []
            for t in range(0, NT, 2):
                str0 += fw.record(lambda t=t: p1_tile(t))
                if t + 1 < NT:
                    str1 += fw.record(lambda t=t: p1_tile(t + 1))
            skew = (len(str0) // max(1, (NT + 1) // 2)) // 2
            fw.interleave([str0, [(lambda: None)] * skew + str1])
            fw.barrier()
            if upto == 1:
                return nc

        SC128 = 128.0 ** -0.5
        SC192 = 192.0 ** -0.5

        def tr_generic(pbuf, src_t, src_ap_fn, n, rows, dstT, dst_ap, eng=A):
            for i in range(n):
                I(PE, lambda i=i: nc.tensor.transpose(out=pbuf[0:rows, i * 128:(i + 1) * 128], in_=src_ap_fn(i), identity=ident[:]),
                  r=[src_t, ident], w=[pbuf])
            if eng == A:
                I(A, lambda: nc.scalar.copy(out=dst_ap, in_=pbuf[0:rows, 0:n * 128].rearrange("p (a b) -> p a b", a=n)), r=[pbuf], w=[dstT])
            else:
                I(V, lambda: nc.vector.tensor_copy(out=dst_ap, in_=pbuf[0:rows, 0:n * 128].rearrange("p (a b) -> p a b", a=n)), r=[pbuf], w=[dstT])

        with contextlib.ExitStack() as st23:
            kcT = [sb(st23, f"kcT{g}", [128, NCP], BF16) for g in range(2)]
            vcb = [sb(st23, f"vcb{g}", [128, NCT, 128], BF16) for g in range(2)]
            for g in range(2):
                I(V, lambda g=g: nc.vector.memset(kcT[g][:], 0.0), w=[kcT[g]])
                I(G, lambda g=g: nc.gpsimd.memset(vcb[g][:], 0.0), w=[vcb[g]])
            with contextlib.ExitStack() as st:
                XT = sb(st, "XT", [128, S], BF16)
                Xl = sb(st, "Xl", [128, 32, 512], BF16)
                w1s = sb(st, "w1s", [128, 32 * 256], F32)
                w1b = sb(st, "w1b", [128, 32, 256], BF16)
                w2s = sb(st, "w2s", [128, 256], F32)
                w2b = sb(st, "w2b", [128, 2, 128], BF16)
                peT = sb(st, "peT", [128, 32], F32)
                hTc = sb(st, "hTc", [128, 2, 512], BF16)
                gkc_t = bcast_load(st, "gkc_t", g_kc, 128)
                kf = sb(st, "kf", [128, 128], F32)
                kf2 = sb(st, "kf2", [128, 128], F32)
                kb = sb(st, "kb", [128, 128], BF16)
                ssk = sb(st, "ssk", [128, 1], F32)
                rsk = sb(st, "rsk", [128, 1], F32)
                pc = [ps(st, f"pc{i}", [128, 512]) for i in range(2)]
                pk = ps(st, "pk", [128, 128])
                pkT = ps(st, "pkT", [128, 128], BF16)
                for kv in range(2):
                    w1, w2, pe_ = (k_w1, k_w2, pe_kT) if kv == 0 else (v_w1, v_w2, pe_vT)
                    dma(w1s[:].rearrange("p (l c) -> p l c", l=32), w1.rearrange("(l d) c -> d l c", d=128), w=[w1s])
                    I(V, lambda: nc.vector.tensor_copy(out=w1b[:], in_=w1s[:].rearrange("p (l c) -> p l c", l=32)), r=[w1s], w=[w1b])
                    dma(w2s[:].rearrange("p (k d) -> p k d", k=2), w2.rearrange("(k c) d -> c k d", c=128), w=[w2s])
                    I(G, lambda: nc.gpsimd.tensor_copy(out=w2b[:], in_=w2s[:].rearrange("p (k d) -> p k d", k=2)), r=[w2s], w=[w2b])
                    dma(peT[:], pe_, w=[peT])
                    for g in range(2):
                        src = KcT if kv == 0 else VcT
                        dma(XT[:], src[g, :, :], w=[XT])
                        for l in range(32):
                            xin = XT[:, l:l + 16 * (n_cmp - 1) + 1:16]
                            if l % 2 == 0:
                                I(V, lambda l=l, xin=xin: nc.vector.tensor_scalar(out=Xl[:, l, 0:n_cmp], in0=xin, scalar1=peT[:, l:l + 1],
                                                                                  scalar2=None, op0=ALU.add), r=[XT, peT], w=[Xl])
                            else:
                                I(A, lambda l=l, xin=xin: nc.scalar.activation(out=Xl[:, l, 0:n_cmp], in_=xin, func=AF.Identity,
                                                                               bias=peT[:, l:l + 1], scale=1.0), r=[XT, peT], w=[Xl])
                        for ch in range(2):
                            p = pc[ch]
                            for l in range(32):
                                I(PE, lambda l=l, p=p, ch=ch: nc.tensor.matmul(p[:, 0:n_cmp], lhsT=w1b[:, l, ch * 128:(ch + 1) * 128],
                                                                               rhs=Xl[:, l, 0:n_cmp], start=(l == 0), stop=(l == 31)),
                                  r=[w1b, Xl], w=[p])
                            I(A, lambda p=p, ch=ch: nc.scalar.activation(out=hTc[:, ch, 0:n_cmp], in_=p[:, 0:n_cmp], func=AF.Silu),
                              r=[p], w=[hTc])
                        for nt in range(NCT):
                            rows = min(128, n_cmp - nt * 128)
                            for ch in range(2):
                                I(PE, lambda ch=ch, nt=nt, rows=rows: nc.tensor.matmul(pk[0:rows, :], lhsT=hTc[:, ch, nt * 128:nt * 128 + rows],
                                                                                       rhs=w2b[:, ch, :], start=(ch == 0), stop=(ch == 1)),
                                  r=[hTc, w2b], w=[pk])
                            if kv == 0:
                                I(A, lambda rows=rows: nc.scalar.copy(out=kf[0:rows, :], in_=pk[0:rows, :]), r=[pk], w=[kf])
                                I(V, lambda rows=rows: nc.vector.tensor_tensor(out=kf2[0:rows, :], in0=kf[0:rows, :], in1=kf[0:rows, :], op=ALU.mult),
                                  r=[kf], w=[kf2])
                                I(V, lambda rows=rows: nc.vector.tensor_reduce(out=ssk[0:rows, :], in_=kf2[0:rows, :], axis=AX.X, op=ALU.add),
                                  r=[kf2], w=[ssk])
                                rsqrt_ms(ssk, ssk[0:rows, :], rsk, rsk[0:rows, :], 1.0 / 128, rows=rows)
                                I(V, lambda rows=rows: nc.vector.scalar_tensor_tensor(out=kb[0:rows, :], in0=kf[0:rows, :], scalar=rsk[0:rows, 0:1],
                                                                                      in1=gkc_t[0:rows, :], op0=ALU.mult, op1=ALU.mult),
                                  r=[kf, rsk, gkc_t], w=[kb])
                                I(PE, lambda rows=rows: nc.tensor.transpose(out=pkT[:, 0:rows], in_=kb[0:rows, :], identity=ident[0:rows, 0:rows]),
                                  r=[kb, ident], w=[pkT])
                                I(A, lambda rows=rows, nt=nt, g=g: nc.scalar.copy(out=kcT[g][:, nt * 128:nt * 128 + rows], in_=pkT[:, 0:rows]),
                                  r=[pkT], w=[kcT[g]])
                            else:
                                I(A, lambda rows=rows, nt=nt, g=g: nc.scalar.copy(out=vcb[g][0:rows, nt, :], in_=pk[0:rows, :]), r=[pk], w=[vcb[g]])
                fw.barrier()
            if upto == 2:
                return nc
            with contextlib.ExitStack() as st:
                W0 = 512 + 8 * (NT - 1)
                m0 = sb(st, "m0", [128, W0], F32)
                aext = sb(st, "aext", [128, NSEL + CO], F32)
                fext = sb(st, "fext", [128, NSEL + CO], F32)
                dma(m0[:], c_m0, w=[m0])
                dma(aext[:], c_aext, w=[aext])
                dma(fext[:], c_fext, w=[fext])
                PW = 4 * NSEL + 8

                def mkset(u):
                    B = {}
                    B["qT"] = [sb(st, f"qT{u}{i}", [128, 8, 128], BF16) for i in range(2)]
                    B["gn3"] = [sb(st, f"gn3{u}{i}", [128, 24], F32) for i in range(2)]
                    B["Ef"] = [sb(st, f"Ef{u}{i}", [128, 512], F32) for i in range(2)]
                    B["Pm"] = sb(st, f"Pm{u}", [128, 8, 512], F32)
                    B["Pb"] = [sb(st, f"Pb{u}{i}", [128, 512], BF16) for i in range(2)]
                    B["PbT"] = [sb(st, f"PbT{u}{i}", [128, 4, 128], BF16) for i in range(2)]
                    for nm, shp, dt in (("Z", [128, 8], F32), ("rz", [128, 8], F32), ("gz", [128, 8], F32), ("ppad", [128, 2, PW], F32),
                                        ("imp", [128, 2, NSEL], F32), ("score", [128, 2, NSEL], F32), ("sc2", [128, 2, NSEL], F32),
                                        ("m1", [128, 2, 8], F32), ("m2", [128, 2, 8], F32), ("Btb", [128, 2, NSEL], BF16),
                                        ("BtT", [KJ, 2, 128], BF16), ("ocm", [128, 8, 128], F32)):
                        B[nm] = sb(st, f"{nm}{u}", shp, dt)
                    B["pS"] = ps(st, f"pS{u}", [128, 512])
                    B["pTB"] = ps(st, f"pTB{u}", [128, 1024], BF16)
                    B["pO"] = [ps(st, f"pO{u}{i}", [128, 512]) for i in range(2)]
                    I(V, lambda: nc.vector.memset(B["ppad"][:], 0.0), w=[B["ppad"]])
                    return B
                sets = [mkset(0), mkset(1)]

                def ld3(t):
                    B = sets[t % 2]
                    b = (t // 2) % 2
                    tk = slice(t * 128, (t + 1) * 128)
                    dma(B["qT"][b][:], QaT[:, :, tk].rearrange("h d t -> d h t"), w=[B["qT"][b]])
                    dma(B["gn3"][b][:], Gn[tk, :], w=[B["gn3"][b]])

                def p3_tile(t):
                    B = sets[t % 2]
                    b = (t // 2) % 2
                    if t + 2 < NT:
                        ld3(t + 2)
                    tk = slice(t * 128, (t + 1) * 128)
                    NW = min(n_cmp, 8 * t + 7)
                    off = 8 * (NT - 1) - 8 * t
                    nkt = (NW + 127) // 128
                    q_, gn_, oc_ = B["qT"][b], B["gn3"][b], B["ocm"]
                    Ef, Pm, Pb, PbT, Z, rz, gz, ppad = B["Ef"], B["Pm"], B["Pb"], B["PbT"], B["Z"], B["rz"], B["gz"], B["ppad"]
                    imp, score, sc2, m1, m2, Btb, BtT = B["imp"], B["score"], B["sc2"], B["m1"], B["m2"], B["Btb"], B["BtT"]
                    pS_, pTB, pO = B["pS"], B["pTB"], B["pO"]
                    for h in range(8):
                        g = h // 4
                        hb_ = h % 2
                        I(PE, lambda h=h, g=g: nc.tensor.matmul(pS_[:, 0:NW], lhsT=q_[:, h, :], rhs=kcT[g][:, 0:NW], start=True, stop=True),
                          r=[q_, kcT[g]], w=[pS_])
                        I(A, lambda hb_=hb_: nc.scalar.activation(out=Ef[hb_][:, 0:NW], in_=pS_[:, 0:NW], func=AF.Exp, scale=SC128),
                          r=[pS_, eps_t], w=[Ef[hb_]])
                        I(V, lambda h=h, hb_=hb_: nc.vector.scalar_tensor_tensor(out=Pm[:, h, 0:NW], in0=Ef[hb_][:, 0:NW], scalar=1.0,
                                                                                  in1=m0[:, off:off + NW], op0=ALU.mult, op1=ALU.mult,
                                                                                  accum_out=Z[:, h:h + 1]), r=[Ef[hb_], m0], w=[Pm, Z])
                        I(G, lambda h=h, hb_=hb_: nc.gpsimd.tensor_copy(out=Pb[hb_][:, 0:NW], in_=Pm[:, h, 0:NW]), r=[Pm], w=[Pb[hb_]])
                        for kt in range(nkt):
                            rows = min(128, NW - kt * 128)
                            I(PE, lambda kt=kt, rows=rows, hb_=hb_: nc.tensor.transpose(out=pTB[0:rows, kt * 128:(kt + 1) * 128],
                                                                                         in_=Pb[hb_][:, kt * 128:kt * 128 + rows], identity=ident[:]),
                              r=[Pb[hb_], ident], w=[pTB])
                        for kt in range(nkt):
                            rows = min(128, NW - kt * 128)
                            I(A, lambda kt=kt, rows=rows, hb_=hb_: nc.scalar.copy(out=PbT[hb_][0:rows, kt, :], in_=pTB[0:rows, kt * 128:(kt + 1) * 128]),
                              r=[pTB], w=[PbT[hb_]])
                        for kt in range(nkt):
                            rows = min(128, NW - kt * 128)
                            I(PE, lambda kt=kt, rows=rows, h=h, g=g, hb_=hb_: nc.tensor.matmul(
                                pO[g][:, (h % 4) * 128:(h % 4 + 1) * 128], lhsT=PbT[hb_][0:rows, kt, :], rhs=vcb[g][0:rows, kt, :],
                                start=(h % 4 == 0 and kt == 0), stop=(kt == nkt - 1)), r=[PbT[hb_], vcb[g]], w=[pO[g]])
                    I(V, lambda: nc.vector.tensor_scalar(out=rz[:], in0=Z[:], scalar1=1e-30, scalar2=None, op0=ALU.max), r=[Z], w=[rz])
                    I(V, lambda: nc.vector.reciprocal(out=rz[:], in_=rz[:]), r=[rz], w=[rz])
                    I(V, lambda: nc.vector.tensor_tensor(out=gz[:], in0=rz[:], in1=gn_[:].rearrange("p (h j) -> p h j", j=3)[:, :, 0], op=ALU.mult),
                      r=[rz, gn_], w=[gz])
                    for g in range(2):
                        I(V, lambda g=g: nc.vector.tensor_tensor(out=oc_[:, 4 * g:4 * g + 4, :], in0=pO[g][:, 0:512].rearrange("p (h d) -> p h d", h=4),
                                                                 in1=gz[:, 4 * g:4 * g + 4].unsqueeze(2).broadcast_to([128, 4, 128]), op=ALU.mult),
                          r=[pO[g], gz], w=[oc_])
                    dma(Oc[tk, :], oc_[:].rearrange("p h d -> p (h d)"), r=[oc_])
                    for g in range(2):
                        for r_ in range(4):
                            h = 4 * g + r_
                            if r_ == 0:
                                I(V, lambda g=g, h=h: nc.vector.tensor_scalar(out=ppad[:, g, 4:4 + NW], in0=Pm[:, h, 0:NW], scalar1=rz[:, h:h + 1],
                                                                              scalar2=None, op0=ALU.mult), r=[Pm, rz], w=[ppad])
                            else:
                                I(V, lambda g=g, h=h: nc.vector.scalar_tensor_tensor(out=ppad[:, g, 4:4 + NW], in0=Pm[:, h, 0:NW], scalar=rz[:, h:h + 1],
                                                                                     in1=ppad[:, g, 4:4 + NW], op0=ALU.mult, op1=ALU.add),
                                  r=[Pm, rz, ppad], w=[ppad])
                    a_t = aext[:, CO - 2 * t:CO - 2 * t + NSEL]
                    f_t = fext[:, CO - 2 * t:CO - 2 * t + NSEL]
                    for g in range(2):
                        I(V, lambda g=g: nc.vector.tensor_reduce(out=imp[:, g, :], in_=ppad[:, g, 4:4 + 4 * NSEL].rearrange("p (j f) -> p j f", f=4),
                                                                 axis=AX.X, op=ALU.add), r=[ppad], w=[imp])
                        I(V, lambda g=g: nc.vector.tensor_tensor(out=imp[:, g, :], in0=imp[:, g, :],
                                                                 in1=ppad[:, g, 0:4 * NSEL].rearrange("p (j f) -> p j f", f=4)[:, :, 3], op=ALU.add),
                          r=[ppad, imp], w=[imp])
                        I(V, lambda g=g: nc.vector.tensor_tensor(out=score[:, g, :], in0=imp[:, g, :], in1=a_t, op=ALU.mult), r=[imp, aext], w=[score])
                        I(V, lambda g=g: nc.vector.tensor_tensor(out=score[:, g, :], in0=score[:, g, :], in1=f_t, op=ALU.add), r=[score, fext], w=[score])
                        I(V, lambda g=g: nc.vector.memset(score[:, g, 0:1], 1e9), r=[score], w=[score])
                        if NSEL > 16:
                            I(V, lambda g=g: nc.vector.max(out=m1[:, g, :], in_=score[:, g, :]), r=[score], w=[m1])
                            I(V, lambda g=g: nc.vector.match_replace(out=sc2[:, g, :], in_to_replace=m1[:, g, :], in_values=score[:, g, :], imm_value=-2.0),
                              r=[score, m1], w=[sc2])
                            I(V, lambda g=g: nc.vector.max(out=m2[:, g, :], in_=sc2[:, g, :]), r=[sc2], w=[m2])
                            I(V, lambda g=g: nc.vector.tensor_scalar(out=Btb[:, g, :], in0=score[:, g, :], scalar1=m2[:, g, 7:8], scalar2=NEGB,
                                                                     op0=ALU.is_lt, op1=ALU.mult), r=[score, m2], w=[Btb])
                        else:
                            I(V, lambda g=g: nc.vector.memset(Btb[:, g, :], 0.0), w=[Btb])
                        I(PE, lambda g=g: nc.tensor.transpose(out=pTB[0:KJ, 512 + g * 128:512 + (g + 1) * 128], in_=Btb[:, g, 0:KJ], identity=ident[:]),
                          r=[Btb, ident], w=[pTB])
                    I(A, lambda: nc.scalar.copy(out=BtT[:], in_=pTB[0:KJ, 512:768].rearrange("p (g t) -> p g t", g=2)), r=[pTB], w=[BtT])
                    dma(BT[:, :, tk].rearrange("g j t -> j g t"), BtT[:], r=[BtT])

                ld3(0)
                if NT > 1:
                    ld3(1)
                str0, str1 = [], []
                for t in range(0, NT, 2):
                    str0 += fw.record(lambda t=t: p3_tile(t))
                    if t + 1 < NT:
                        str1 += fw.record(lambda t=t: p3_tile(t + 1))
                skew = (len(str0) // max(1, (NT + 1) // 2)) // 2
                fw.interleave([str0, [(lambda: None)] * skew + str1])
                fw.barrier()
        if upto == 3:
            return nc

        def attn_pipeline(pairs, stageA, stageB):
            n = len(pairs)
            if n == 0:
                return
            stageA(0, pairs[0])
            for i in range(n):
                if i + 1 < n:
                    stageA(i + 1, pairs[i + 1])
                stageB(i, pairs[i])

        with contextlib.ExitStack() as st:
            esel = sb(st, "esel", [KJ, NT, 128], BF16)
            dma(esel[:], c_esel, w=[esel])
            Ks_s = sb(st, "Ks_s", [128, S], BF16)
            Kw_s = sb(st, "Kw_s", [128, S], BF16)
            Vs_s = sb(st, "Vs_s", [128, NT, 129], BF16)
            Vw_s = sb(st, "Vw_s", [128, NT, 129], BF16)
            qT4 = [sb(st, f"qT4{i}", [128, 4, 128], BF16) for i in range(2)]
            btl = [sb(st, f"btl{i}", [KJ, 128], BF16) for i in range(2)]
            BT4 = [sb(st, f"BT4{i}", [KJ, 4, 128], BF16) for i in range(2)]
            PT = [sb(st, f"PT{i}", [128, 4, 128], BF16) for i in range(3)]
            gn4 = [sb(st, f"gn4{i}", [128, 24], F32) for i in range(2)]
            ocl = [sb(st, f"ocl{i}", [128, 4, 128], F32) for i in range(2)]
            bga = [sb(st, f"bga{i}", [128, 512], BF16) for i in range(2)]
            oa = sb(st, "oa", [128, 4, 128], F32)
            yat = [sb(st, f"yat{i}", [128, 512], F32) for i in range(2)]
            rzs = sb(st, "rzs", [128, 8], F32)
            cfs = sb(st, "cfs", [128, 8], F32)
            pS = [ps(st, f"qS{i}", [128, 512]) for i in range(2)]
            pOTs = ps(st, "pOTs", [128, 512])
            pOTw = ps(st, "pOTw", [128, 512])
            pTrs = ps(st, "pTrs", [128, 512])
            pTrw = ps(st, "pTrw", [128, 512])
            pZ4 = ps(st, "pZ4", [128, 512])
            acc_s = sb(st, "acc_s", [128, 512], F32)
            acc_w = sb(st, "acc_w", [128, 512], F32)
            oT_s = sb(st, "oT_s", [128, 512], F32)
            oT_w = sb(st, "oT_w", [128, 512], F32)
            I(V, lambda: nc.vector.memset(Vs_s[:, :, 128:129], 1.0), w=[Vs_s])
            I(V, lambda: nc.vector.memset(Vw_s[:, :, 128:129], 1.0), w=[Vw_s])
            cnt4 = [0]
            for g in range(2):
                dma(Ks_s[:], KsT[g, :, :], w=[Ks_s])
                dma(Kw_s[:], KwT[g, :, :], w=[Kw_s])
                dma(Vs_s[:, :, 0:128], Vs[:, g, :].rearrange("(t p) d -> p t d", p=128), w=[Vs_s])
                dma(Vw_s[:, :, 0:128], Vw[:, g, :].rearrange("(t p) d -> p t d", p=128), w=[Vw_s])

                def ld4(t, g=g):
                    b = t % 2
                    tk = slice(t * 128, (t + 1) * 128)
                    dma(qT4[b][:], QaT[4 * g:4 * g + 4, :, tk].rearrange("h d t -> d h t"), w=[qT4[b]])
                    dma(btl[b][:], BT[g, :, tk], w=[btl[b]])
                    dma(gn4[b][:], Gn[tk, :], w=[gn4[b]])
                    dma(ocl[b][:], Oc[tk, 4 * g * 128:(4 * g + 4) * 128].rearrange("p (h d) -> p h d", h=4), w=[ocl[b]])
                    dma(bga[b][:], BG[tk, 4 * g * 128:(4 * g + 4) * 128], w=[bga[b]])
                ld4(0)
                for t in range(NT):
                    if t + 1 < NT:
                        ld4(t + 1)
                    b = t % 2
                    tk = slice(t * 128, (t + 1) * 128)
                    q_ = qT4[b]
                    I(G, lambda: nc.gpsimd.tensor_copy(out=BT4[b][:], in_=btl[b][:].unsqueeze(1).broadcast_to([KJ, 4, 128])), r=[btl[b]], w=[BT4[b]])
                    pairs = [("s", kt) for kt in range(t + 1)] + [("w", kt) for kt in range(max(0, t - 4), t + 1)]
                    base = cnt4[0]
                    cnt4[0] += len(pairs)

                    def stA(i, pr):
                        kind, kt = pr
                        k = base + i
                        p = pS[k % 2]
                        pt = PT[k % 3]
                        ksl = slice(kt * 128, (kt + 1) * 128)
                        if kind == "s":
                            I(PE, lambda: nc.tensor.matmul(p[:, :], lhsT=Ks_s[:, ksl], rhs=q_[:].rearrange("p h t -> p (h t)"), start=True, stop=False),
                              r=[Ks_s, q_], w=[p])
                            I(PE, lambda: nc.tensor.matmul(p[:, :], lhsT=esel[:, kt, :], rhs=BT4[b][:].rearrange("p h t -> p (h t)"), start=False, stop=True),
                              r=[esel, BT4[b]], w=[p])
                        else:
                            I(PE, lambda: nc.tensor.matmul(p[:, :], lhsT=Kw_s[:, ksl], rhs=q_[:].rearrange("p h t -> p (h t)"), start=True, stop=True),
                              r=[Kw_s, q_], w=[p])
                        I(A, lambda: nc.scalar.activation(out=pt[:].rearrange("p h t -> p (h t)"), in_=p[:, :], func=AF.Exp, scale=SC128),
                          r=[p, eps_t], w=[pt])
                        mk = None
                        if kt == t:
                            mk = tri
                        elif kind == "w" and kt == t - 4:
                            mk = atri
                        if mk is not None:
                            I(G, lambda: nc.gpsimd.tensor_tensor(out=pt[:], in0=pt[:], in1=mk[:].unsqueeze(1).broadcast_to([128, 4, 128]), op=ALU.mult),
                              r=[pt, mk], w=[pt])

                    def stB(i, pr):
                        kind, kt = pr
                        k = base + i
                        pt = PT[k % 3]
                        ptf = pt[:].rearrange("p h t -> p (h t)")
                        if kind == "s":
                            pO_, vv, acc_, first, last = pOTs, Vs_s, acc_s, (kt == 0), (kt == t)
                        else:
                            pO_, vv, acc_, first, last = pOTw, Vw_s, acc_w, (kt == max(0, t - 4)), (kt == t)
                        I(PE, lambda: nc.tensor.matmul(pO_[:, :], lhsT=vv[:, kt, 0:128], rhs=ptf, start=first, stop=last), r=[pt, vv], w=[pO_])
                        if first:
                            I(V, lambda: nc.vector.tensor_copy(out=acc_[:], in_=ptf), r=[pt], w=[acc_])
                        else:
                            I(V, lambda: nc.vector.tensor_tensor(out=acc_[:], in0=acc_[:], in1=ptf, op=ALU.add), r=[pt, acc_], w=[acc_])
                    attn_pipeline(pairs, stA, stB)
                    for ki, (pO_, acc_, oT_, pTr_) in enumerate(((pOTs, acc_s, oT_s, pTrs), (pOTw, acc_w, oT_w, pTrw))):
                        for hh in range(4):
                            I(PE, lambda hh=hh, ki=ki, acc_=acc_: nc.tensor.matmul(pZ4[:, 8 * ki + 2 * hh:8 * ki + 2 * hh + 2], lhsT=acc_[:, hh * 128:(hh + 1) * 128],
                                                                                    rhs=ones_f[:, 0:2], start=True, stop=True), r=[acc_, ones_f], w=[pZ4])
                        I(A, lambda pO_=pO_, oT_=oT_: nc.scalar.copy(out=oT_[:], in_=pO_[:, :]), r=[pO_], w=[oT_])
                        for hh in range(4):
                            I(PE, lambda hh=hh, oT_=oT_, pTr_=pTr_: nc.tensor.transpose(out=pTr_[:, hh * 128:(hh + 1) * 128], in_=oT_[:, hh * 128:(hh + 1) * 128],
                                                                                         identity=identf[:]), r=[oT_, identf], w=[pTr_])
                    I(V, lambda: nc.vector.reciprocal(out=rzs[:, 0:8], in_=pZ4[:, 0:16:2]), r=[pZ4], w=[rzs])
                    gv = gn4[b][:].rearrange("p (h j) -> p h j", j=3)
                    I(V, lambda: nc.vector.tensor_tensor(out=cfs[:, 0:4], in0=rzs[:, 0:4], in1=gv[:, 4 * g:4 * g + 4, 1], op=ALU.mult), r=[rzs, gn4[b]], w=[cfs])
                    I(V, lambda: nc.vector.tensor_tensor(out=cfs[:, 4:8], in0=rzs[:, 4:8], in1=gv[:, 4 * g:4 * g + 4, 2], op=ALU.mult), r=[rzs, gn4[b]], w=[cfs])
                    for hh in range(4):
                        col = (hh % 2) * 129
                        I(V, lambda hh=hh, col=col: nc.vector.scalar_tensor_tensor(out=oa[:, hh, :], in0=pTrs[:, hh * 128:(hh + 1) * 128], scalar=cfs[:, hh:hh + 1],
                                                                                   in1=ocl[b][:, hh, :], op0=ALU.mult, op1=ALU.add),
                          r=[pTrs, cfs, ocl[b]], w=[oa])
                        I(V, lambda hh=hh, col=col: nc.vector.scalar_tensor_tensor(out=oa[:, hh, :], in0=pTrw[:, hh * 128:(hh + 1) * 128], scalar=cfs[:, 4 + hh:5 + hh],
                                                                                   in1=oa[:, hh, :], op0=ALU.mult, op1=ALU.add),
                          r=[pTrw, cfs, oa], w=[oa])
                    I(G, lambda: nc.gpsimd.tensor_tensor(out=yat[b][:], in0=oa[:].rearrange("p h d -> p (h d)"), in1=bga[b][:], op=ALU.mult),
                      r=[oa, bga[b]], w=[yat[b]])
                    dma(Ya[tk, 4 * g * 128:(4 * g + 4) * 128], yat[b][:], r=[yat[b]])
            fw.barrier()
        if upto == 4:
            return nc

        NQB = S // 512
        with contextlib.ExitStack() as st:
            KN = [sb(st, f"KN{i}", [128, S], BF16) for i in range(2)]
            KR = [sb(st, f"KR{i}", [64, S], BF16) for i in range(2)]
            VB = [sb(st, f"VB{i}", [128, NT, 129], BF16) for i in range(2)]
            QN = [sb(st, f"QN{i}", [128, 512], BF16) for i in range(2)]
            QR = [sb(st, f"QR{i}", [64, 512], BF16) for i in range(2)]
            PT = [sb(st, f"PTm{i}", [128, 512], BF16) for i in range(3)]
            bgb = [sb(st, f"bgb{i}", [128, 4, 128], BF16) for i in range(2)]
            yal = [sb(st, f"yal{i}", [128, 4, 128], F32) for i in range(2)]
            yo = [sb(st, f"yo{i}", [128, 4, 128], BF16) for i in range(2)]
            obf = sb(st, "obf", [128, 4, 128], F32)
            rz5 = sb(st, "rz5", [128, 4], F32)
            pS = [ps(st, f"mS{i}", [128, 512]) for i in range(2)]
            pOT5 = [ps(st, f"mOT{j}", [128, 512]) for j in range(2)]
            pTr5 = ps(st, "mTr", [128, 512])
            pZ5 = ps(st, "mZ", [128, 512])
            acc5 = [sb(st, f"acc5{j}", [128, 512], F32) for j in range(2)]
            oT5 = sb(st, "oT5", [128, 512], F32)
            for i in range(2):
                I(V, lambda i=i: nc.vector.memset(VB[i][:, :, 128:129], 1.0), w=[VB[i]])
            cnt5 = [0]

            def ldh(h):
                b = h % 2
                dma(KN[b][:], KbN[h, :, :], w=[KN[b]])
                dma(KR[b][:], KbR[h, :, :], w=[KR[b]])
                dma(VB[b][:, :, 0:128], Vb[:, h, :].rearrange("(t p) d -> p t d", p=128), w=[VB[b]])

            def ldq(h, qb):
                bq = (h * NQB + qb) % 2
                rows = slice(qb * 512, (qb + 1) * 512)
                dma(QN[bq][:], QbN[h, :, rows], w=[QN[bq]])
                dma(QR[bq][:], QbR[h, :, rows], w=[QR[bq]])
                dma(bgb[bq][:], BG[rows, 1024 + h * 128:1024 + (h + 1) * 128].rearrange("(s p) c -> p s c", p=128), w=[bgb[bq]])
                dma(yal[bq][:], Ya[rows, h * 128:(h + 1) * 128].rearrange("(s p) c -> p s c", p=128), w=[yal[bq]])
            ldh(0)
            ldq(0, 0)
            for h in range(8):
                if h + 1 < 8:
                    ldh(h + 1)
                b = h % 2
                for qb in range(NQB):
                    nxt = h * NQB + qb + 1
                    if nxt < 8 * NQB:
                        ldq(nxt // NQB, nxt % NQB)
                    bq = (h * NQB + qb) % 2
                    rows = slice(qb * 512, (qb + 1) * 512)
                    pO_ = pOT5[bq]
                    acc_ = acc5[bq]
                    pairs = list(range(4 * qb + 4))
                    base = cnt5[0]
                    cnt5[0] += len(pairs)

                    def stA(i, kt):
                        k = base + i
                        p = pS[k % 2]
                        pt = PT[k % 3]
                        j = kt - 4 * qb
                        c0 = 128 * max(j, 0)
                        ksl = slice(kt * 128, (kt + 1) * 128)
                        I(PE, lambda: nc.tensor.matmul(p[:, c0:512], lhsT=KN[b][:, ksl], rhs=QN[bq][:, c0:512], start=True, stop=False), r=[KN[b], QN[bq]], w=[p])
                        I(PE, lambda: nc.tensor.matmul(p[:, c0:512], lhsT=KR[b][:, ksl], rhs=QR[bq][:, c0:512], start=False, stop=True), r=[KR[b], QR[bq]], w=[p])
                        I(A, lambda: nc.scalar.activation(out=pt[:, c0:512], in_=p[:, c0:512], func=AF.Exp, scale=SC192),
                          r=[p, eps_t], w=[pt])
                        if j >= 0:
                            I(G, lambda: nc.gpsimd.tensor_tensor(out=pt[:, c0:c0 + 128], in0=pt[:, c0:c0 + 128], in1=tri[:], op=ALU.mult), r=[pt, tri], w=[pt])

                    def stB(i, kt):
                        k = base + i
                        pt = PT[k % 3]
                        j = kt - 4 * qb
                        c0 = 128 * max(j, 0)
                        I(PE, lambda: nc.tensor.matmul(pO_[:, c0:512], lhsT=VB[b][:, kt, 0:128], rhs=pt[:, c0:512], start=(kt == 0), stop=(kt == 4 * qb + 3)),
                          r=[pt, VB[b]], w=[pO_])
                        if kt == 0:
                            I(V, lambda: nc.vector.tensor_copy(out=acc_[:], in_=pt[:, 0:512]), r=[pt], w=[acc_])
                        else:
                            I(V, lambda: nc.vector.tensor_tensor(out=acc_[:, c0:512], in0=acc_[:, c0:512], in1=pt[:, c0:512], op=ALU.add), r=[pt, acc_], w=[acc_])
                    attn_pipeline(pairs, stA, stB)
                    for sub in range(4):
                        I(PE, lambda sub=sub: nc.tensor.matmul(pZ5[:, 2 * sub:2 * sub + 2], lhsT=acc_[:, sub * 128:(sub + 1) * 128], rhs=ones_f[:, 0:2],
                                                               start=True, stop=True), r=[acc_, ones_f], w=[pZ5])
                    I(A, lambda: nc.scalar.copy(out=oT5[:], in_=pO_[:, :]), r=[pO_], w=[oT5])
                    for sub in range(4):
                        I(PE, lambda sub=sub: nc.tensor.transpose(out=pTr5[:, sub * 128:(sub + 1) * 128], in_=oT5[:, sub * 128:(sub + 1) * 128], identity=identf[:]),
                          r=[oT5, identf], w=[pTr5])
                    I(V, lambda: nc.vector.reciprocal(out=rz5[:, 0:4], in_=pZ5[:, 0:8:2]), r=[pZ5], w=[rz5])
                    for sub in range(4):
                        I(V, lambda sub=sub: nc.vector.tensor_scalar(out=obf[:, sub, :], in0=pTr5[:, sub * 128:(sub + 1) * 128], scalar1=rz5[:, sub:sub + 1],
                                                                     scalar2=None, op0=ALU.mult), r=[pTr5, rz5], w=[obf])
                    I(G, lambda: nc.gpsimd.tensor_tensor(out=obf[:], in0=obf[:], in1=bgb[bq][:], op=ALU.mult), r=[obf, bgb[bq]], w=[obf])
                    I(G, lambda: nc.gpsimd.tensor_tensor(out=yo[bq][:], in0=obf[:], in1=yal[bq][:], op=ALU.add), r=[obf, yal[bq]], w=[yo[bq]])
                    dma(Yb[rows, h * 128:(h + 1) * 128].rearrange("(s p) c -> p s c", p=128), yo[bq][:], r=[yo[bq]])
            fw.barrier()
        if upto == 5:
            return nc

        with contextlib.ExitStack() as st:
            wob = sb(st, "wob", [128, 8, 1024], BF16)
            with contextlib.ExitStack() as st2:
                stg = [sb(st2, f"stgo{i}", [128, 1024], F32) for i in range(2)]
                load_cast(stg, wob, lambda i: wob[:, i, :], lambda i: w_o[i * 128:(i + 1) * 128, :], 8, [128, 1024])
                fw.barrier()
            modb = TO(st.enter_context(nc.sbuf_tensor("sb_mod6a", [128, 1024], F32)), 2048)
            dma(modb.t[:], MODB[:, 2048:3072], w=[modb])
            yt = [sb(st, f"yt{i}", [128, 1024], BF16) for i in range(2)]
            xl = [sb(st, f"xla{i}", [128, 1024], F32) for i in range(2)]
            YT = sb(st, "YT", [128, 8, 128], BF16)
            tmp = sb(st, "tmpa", [128, 1024], F32)
            x1t = [sb(st, f"x1t{i}", [128, 1024], F32) for i in range(2)]
            pTa = [ps(st, f"pTa{i}", [128, 1024], BF16) for i in range(2)]
            pA = [ps(st, f"pA{i}", [128, 512]) for i in range(4)]

            def ld6(t):
                b = t % 2
                tk = slice(t * 128, (t + 1) * 128)
                dma(yt[b][:], Yb[tk, :], w=[yt[b]])
                dma(xl[b][:], x[tk, :], w=[xl[b]])
            ld6(0)
            for t in range(NT):
                if t + 1 < NT:
                    ld6(t + 1)
                b = t % 2
                tk = slice(t * 128, (t + 1) * 128)
                tr_generic(pTa[t % 2], yt[b], lambda i: yt[b][:, i * 128:(i + 1) * 128], 8, 128, YT, YT[:])
                for half in range(2):
                    p = pA[(2 * t + half) % 4]
                    hs = slice(half * 512, (half + 1) * 512)
                    for c in range(8):
                        I(PE, lambda c=c, p=p, hs=hs: nc.tensor.matmul(p[:, :], lhsT=YT[:, c, :], rhs=wob[:, c, hs], start=(c == 0), stop=(c == 7)),
                          r=[YT, wob], w=[p])
                    gsl = slice(2048 + half * 512, 2048 + (half + 1) * 512)
                    I(V, lambda p=p, hs=hs, gsl=gsl: nc.vector.tensor_tensor(out=tmp[:, hs], in0=p[:, :], in1=modb[:, gsl], op=ALU.mult), r=[p, modb], w=[tmp])
                I(G, lambda: nc.gpsimd.tensor_tensor(out=x1t[b][:], in0=tmp[:], in1=xl[b][:], op=ALU.add), r=[tmp, xl[b]], w=[x1t[b]])
                dma(out[tk, :], x1t[b][:], r=[x1t[b]])
            fw.barrier()
        if upto == 6:
            return nc

        NB6 = S // 256
        with contextlib.ExitStack() as st:
            wupb = sb(st, "wupb", [128, 8, 5632], BF16)
            wdb = sb(st, "wdb", [128, 22, 1024], BF16)
            with contextlib.ExitStack() as st2:
                stg = [sb(st2, f"stgu{i}", [128, 5632], F32) for i in range(2)]
                load_cast(stg, wupb, lambda i: wupb[:, i, :], lambda i: w_up[i * 128:(i + 1) * 128, :], 8, [128, 5632])
                load_cast(stg, wdb, lambda i: wdb[:, i, :], lambda i: w_down[i * 128:(i + 1) * 128, :], 22, [128, 1024])
                fw.barrier()
            modb = TO(st.enter_context(nc.sbuf_tensor("sb_mod6b", [128, 3072], F32)), 3072)
            dma(modb.t[:], MODB[:, 3072:6144], w=[modb])
            wc = sb(st, "wc", [128, 44, 3], F32)
            bc = sb(st, "bc", [128, 44], F32)
            dma(wc[:], wconv_l, w=[wc])
            dma(bc[:], bconv_l, w=[bc])
            xb = [sb(st, f"xb{i}", [128, 2, 1024], F32) for i in range(2)]
            h2f = sb(st, "h2f", [128, 1024], F32)
            tmpd = sb(st, "tmpd", [128, 1024], F32)
            h2b = sb(st, "h2b", [128, 1024], BF16)
            h2T = [sb(st, f"h2T{i}", [128, 8, 256], BF16) for i in range(2)]
            zb = [sb(st, f"zb{i}", [128, 258], F32) for i in range(3)]
            uv = [sb(st, f"uv{i}", [128, 256], F32) for i in range(2)]
            ug = [sb(st, f"ug{i}", [128, 256], F32) for i in range(2)]
            sgm = [sb(st, f"sgm{i}", [128, 256], F32) for i in range(2)]
            actT = sb(st, "actT", [128, 22, 256], BF16)
            halo = sb(st, "halo", [128, 44, 2], F32)
            ss6 = sb(st, "ss6", [128, 2], F32)
            rs6 = sb(st, "rs6", [128, 2], F32)
            pTb = [ps(st, f"pTb{i}", [128, 1024], BF16) for i in range(2)]
            pU = [ps(st, f"pU{i}", [128, 512]) for i in range(3)]
            pD = [ps(st, f"pD{i}", [128, 512]) for i in range(2)]
            I(V, lambda: nc.vector.memset(halo[:], 0.0), w=[halo])
            nu = [0]

            def load6(blk):
                rows = slice(blk * 256, (blk + 1) * 256)
                dma(xb[blk % 2][:], out[rows, :].rearrange("(s p) c -> p s c", p=128), w=[xb[blk % 2]])

            def prep6(blk):
                xb_ = xb[blk % 2]
                hT_ = h2T[blk % 2]
                for s_ in range(2):
                    I(A, lambda s_=s_: nc.scalar.activation(out=h2b[:], in_=xb_[:, s_, :], func=AF.Square, accum_out=ss6[:, s_:s_ + 1]), r=[xb_], w=[h2b, ss6])
                rsqrt_ms(ss6, ss6[:], rs6, rs6[:], 1.0 / 1024)
                for s_ in range(2):
                    I(V, lambda s_=s_: nc.vector.scalar_tensor_tensor(out=h2f[:], in0=xb_[:, s_, :], scalar=rs6[:, s_:s_ + 1], in1=modb[:, A_FFN],
                                                                      op0=ALU.mult, op1=ALU.mult), r=[xb_, rs6, modb], w=[h2f])
                    I(V, lambda: nc.vector.tensor_tensor(out=h2b[:], in0=h2f[:], in1=modb[:, SH_F], op=ALU.add), r=[h2f, modb], w=[h2b])
                    tr_generic(pTb[s_], h2b, lambda i: h2b[:, i * 128:(i + 1) * 128], 8, 128, hT_, hT_[:, :, s_ * 128:(s_ + 1) * 128])

            def gate6(k):
                sg_ = sgm[k % 2]
                I(A, lambda: nc.scalar.activation(out=sg_[:], in_=ug[k % 2][:], func=AF.Silu), r=[ug[k % 2]], w=[sg_])
                I(G, lambda: nc.gpsimd.tensor_tensor(out=actT[:, k, :], in0=sg_[:], in1=uv[k % 2][:], op=ALU.mult), r=[sg_, uv[k % 2]], w=[actT])

            load6(0)
            prep6(0)
            for blk in range(NB6):
                rows = slice(blk * 256, (blk + 1) * 256)
                xb_ = xb[blk % 2]
                hT_ = h2T[blk % 2]
                if blk + 1 < NB6:
                    load6(blk + 1)
                for k in range(22):
                    for which, fc in ((0, k), (1, 22 + k)):
                        i = nu[0]
                        nu[0] += 1
                        p = pU[i % 3]
                        z = zb[i % 3]
                        u = (uv if which == 0 else ug)[k % 2]
                        for c in range(8):
                            I(PE, lambda c=c, p=p, fc=fc: nc.tensor.matmul(p[:, 0:256], lhsT=wupb[:, c, fc * 128:(fc + 1) * 128], rhs=hT_[:, c, :],
                                                                           start=(c == 0), stop=(c == 7)), r=[wupb, hT_], w=[p])
                        I(G, lambda z=z, fc=fc: nc.gpsimd.tensor_copy(out=z[:, 0:2], in_=halo[:, fc, :]), r=[halo], w=[z])
                        I(A, lambda z=z, p=p: nc.scalar.copy(out=z[:, 2:258], in_=p[:, 0:256]), r=[p], w=[z])
                        I(G, lambda z=z, fc=fc: nc.gpsimd.tensor_copy(out=halo[:, fc, :], in_=z[:, 256:258]), r=[z], w=[halo])
                        I(A, lambda u=u, p=p, fc=fc: nc.scalar.activation(out=u[:], in_=p[:, 0:256], func=AF.Identity, scale=wc[:, fc, 2:3], bias=bc[:, fc:fc + 1]),
                          r=[p, wc, bc], w=[u])
                        I(V, lambda u=u, z=z, fc=fc: nc.vector.scalar_tensor_tensor(out=u[:], in0=z[:, 1:257], scalar=wc[:, fc, 1:2], in1=u[:],
                                                                                    op0=ALU.mult, op1=ALU.add), r=[z, wc, u], w=[u])
                        I(V, lambda u=u, z=z, fc=fc: nc.vector.scalar_tensor_tensor(out=u[:], in0=z[:, 0:256], scalar=wc[:, fc, 0:1], in1=u[:],
                                                                                    op0=ALU.mult, op1=ALU.add), r=[z, wc, u], w=[u])
                    if k >= 1:
                        gate6(k - 1)
                gate6(21)
                if blk + 1 < NB6:
                    prep6(blk + 1)
                for s_ in range(2):
                    for half in range(2):
                        p = pD[half]
                        hs = slice(half * 512, (half + 1) * 512)
                        for k in range(22):
                            I(PE, lambda k=k, p=p, hs=hs, s_=s_: nc.tensor.matmul(p[:, :], lhsT=actT[:, k, s_ * 128:(s_ + 1) * 128], rhs=wdb[:, k, hs],
                                                                                  start=(k == 0), stop=(k == 21)), r=[actT, wdb], w=[p])
                        gsl = slice(5120 + half * 512, 5120 + (half + 1) * 512)
                        I(V, lambda p=p, hs=hs, gsl=gsl: nc.vector.tensor_tensor(out=tmpd[:, hs], in0=p[:, :], in1=modb[:, gsl], op=ALU.mult), r=[p, modb], w=[tmpd])
                    I(G, lambda s_=s_: nc.gpsimd.tensor_tensor(out=xb_[:, s_, :], in0=tmpd[:], in1=xb_[:, s_, :], op=ALU.add), r=[tmpd, xb_], w=[xb_])
                dma(out[rows, :].rearrange("(s p) c -> p s c", p=128), xb_[:], r=[xb_])
            fw.barrier()
    return nc


_PARAM_NAMES = ["w_ada", "b_ada", "attn_norm", "ffn_norm", "w_in", "nsa_q_norm", "nsa_kc_norm", "nsa_ks_norm", "nsa_kw_norm",
                "cmp_k_w1", "cmp_k_w2", "cmp_v_w1", "cmp_v_w2", "mla_cq_norm", "mla_ckv_norm", "w_uq", "w_ukv",
                "mla_q_norm", "mla_k_norm", "w_o", "w_up", "w_down"]
_CONSTS = {}


def make_in_map(inp, b, S):
    if S not in _CONSTS:
        _CONSTS[S] = host_consts(S)
    m = {}
    m["x"] = np.ascontiguousarray(np.asarray(inp["x"])[b, :S], dtype=np.float32)
    m["ccol"] = np.ascontiguousarray(np.asarray(inp["c"])[b].reshape(8, 128).T, dtype=np.float32)
    for k in _PARAM_NAMES:
        m[k] = np.ascontiguousarray(np.asarray(inp[k])[0], dtype=np.float32)
    m["pe_kT"] = np.ascontiguousarray(np.asarray(inp["cmp_k_pe"])[0].T, dtype=np.float32)
    m["pe_vT"] = np.ascontiguousarray(np.asarray(inp["cmp_v_pe"])[0].T, dtype=np.float32)
    m["wconv_l"] = np.ascontiguousarray(np.asarray(inp["w_conv"])[0].reshape(3, 44, 128).transpose(2, 1, 0), dtype=np.float32)
    m["bconv_l"] = np.ascontiguousarray(np.asarray(inp["b_conv"])[0].reshape(44, 128).T, dtype=np.float32)
    m.update(_CONSTS[S])
    return m


_NC = {}


def kernel(**inputs):
    S = 8192
    if S not in _NC:
        _NC[S] = build(S)
    nc = _NC[S]
    in_maps = [make_in_map(inputs, b, S) for b in range(8)]
    res = run_bass_kernel_spmd(nc, in_maps, core_ids=list(range(8)))
    return np.stack([np.asarray(r["out"], dtype=np.float32) for r in res.results], axis=0)
```

```python
import contextlib
import numpy as np
import ml_dtypes
import concourse.bass as bass
import concourse.mybir as mybir
from concourse.bass_utils import run_bass_kernel_spmd

F32 = mybir.dt.float32
BF16 = mybir.dt.bfloat16
AF = mybir.ActivationFunctionType
ALU = mybir.AluOpType
AX = mybir.AxisListType

EPS = 1e-6
NEGB = -30000.0
EXPB = -4.0


class Buf:
    __slots__ = ("w", "r")

    def __init__(self):
        self.w = {}
        self.r = {}


class T:
    def __init__(self, t):
        self.t = t
        self.b = Buf()

    def __getitem__(self, k):
        return self.t[k]


class TO(T):
    def __init__(self, t, off):
        super().__init__(t)
        self.off = off

    def __getitem__(self, k):
        p, c = k
        return self.t[p, slice(c.start - self.off, c.stop - self.off)]


def _b(x):
    return x.b if isinstance(x, T) else x


class FW:
    ROT = 12000
    NQ = 24

    def __init__(self, nc, es):
        self.nc, self.es = nc, es
        self.E = {"pe": nc.tensor, "act": nc.scalar, "dve": nc.vector, "pool": nc.gpsimd, "sp": nc.sync}
        self.sem, self.cnt = {}, {}
        self.nsem = 0
        for e in self.E:
            self._newsem(e)
        self.waited = {e: {} for e in self.E}
        self.dq = {}
        self.n = 0
        self.rec = None

    def _newsem(self, e):
        s = self.es.enter_context(self.nc.semaphore(f"s{e}{self.nsem}"))
        self.nsem += 1
        self.sem[e] = s
        self.cnt[e] = 0

    def _wait(self, e, deps):
        for s, (v, pe) in deps.items():
            if self.waited[e].get(s, 0) >= v:
                continue
            self.E[e].wait_ge(s, v)
            self.waited[e][s] = v
            self.n += 1

    def _deps(self, e, r, w):
        deps = {}

        def add(d, raw):
            for s, (v, pe) in d.items():
                if pe == e and e != "dma":
                    if e == "pe" or not raw:
                        continue
                if deps.get(s, (0,))[0] < v:
                    deps[s] = (v, pe)
        for b in r:
            add(_b(b).w, True)
        for b in w:
            add(_b(b).w, False)
            add(_b(b).r, False)
        return deps

    def I(self, e, fn, r=(), w=()):
        if self.rec is not None:
            r, w = list(r), list(w)
            self.rec.append(lambda: self._I(e, fn, r, w))
            return None
        return self._I(e, fn, r, w)

    def _I(self, e, fn, r=(), w=()):
        self._wait(e, self._deps(e, r, w))
        if self.cnt[e] >= self.ROT:
            self._newsem(e)
        inst = fn()
        s = self.sem[e]
        inst.then_inc(s, 1)
        self.cnt[e] += 1
        self.n += 1
        tok = (self.cnt[e], e)
        for b in w:
            b = _b(b)
            b.w = {s: tok}
            b.r = {}
        for b in r:
            _b(b).r[s] = tok
        return inst

    def dma(self, out, in_, r=(), w=(), q="sp"):
        if self.rec is not None:
            r, w = list(r), list(w)
            self.rec.append(lambda: self._dma(out, in_, r, w, q))
            return None
        return self._dma(out, in_, r, w, q)

    def record(self, fn):
        self.rec = []
        fn()
        lst, self.rec = self.rec, None
        return lst

    @staticmethod
    def interleave_skewed(streams):
        n = len(streams)
        L = max(len(st_) for st_ in streams)
        pos = [0] * n
        start = [0] + [0] * (n - 1)
        i = 0
        while any(pos[k] < len(streams[k]) for k in range(n)):
            for k in range(n):
                if i >= start[k] and pos[k] < len(streams[k]):
                    streams[k][pos[k]]()
                    pos[k] += 1
            i += 1

    @staticmethod
    def interleave(lists):
        for i in range(max(len(l) for l in lists)):
            for l in lists:
                if i < len(l):
                    l[i]()

    def _dma(self, out, in_, r=(), w=(), q="sp"):
        self._wait(q, self._deps("dma", r, w))
        d = self.dq.setdefault(q, {"sems": [], "i": 0})
        if len(d["sems"]) < self.NQ:
            s = self.es.enter_context(self.nc.semaphore(f"d{q}{len(d['sems'])}"))
            ent = [s, 0]
            d["sems"].append(ent)
        else:
            ent = d["sems"][d["i"] % self.NQ]
            d["i"] += 1
            self._wait(q, {ent[0]: (16 * ent[1], "dma")})
        inst = self.E[q].dma_start(out=out, in_=in_)
        inst.then_inc(ent[0], 16)
        ent[1] += 1
        self.n += 1
        tok = (16 * ent[1], "dma")
        for b in w:
            b = _b(b)
            b.w = {ent[0]: tok}
            b.r = {}
        for b in r:
            _b(b).r[ent[0]] = tok

    def barrier(self):
        toks = {}
        for e in self.E:
            if self.cnt[e] > 0:
                toks[self.sem[e]] = (self.cnt[e], "x")
        for q, d in self.dq.items():
            for s, c in d["sems"]:
                if c:
                    toks[s] = (16 * c, "dma")
        for e in self.E:
            self._wait(e, toks)


O_NQ, O_NKC, O_NVC, O_NKS, O_NVS, O_NKW, O_NVW, O_NG, O_CQ, O_CKV, O_KR, O_BG = (
    0, 1024, 1280, 1536, 1792, 2048, 2304, 2560, 2584, 2968, 3224, 3288)
IN_W = 5336


def host_consts(S):
    NT = S // 128
    NSEL = S // 64
    n_cmp = S // 16 - 1
    pos = np.arange(S, dtype=np.float32)
    inv128 = (10000.0 ** (-np.arange(64, dtype=np.float32) * 2.0 / 128)).astype(np.float32)
    inv64 = (10000.0 ** (-np.arange(32, dtype=np.float32) * 2.0 / 64)).astype(np.float32)
    a128 = pos[:, None] * inv128[None, :]
    a64 = pos[:, None] * inv64[None, :]
    c = {}
    c["cs128"] = np.concatenate([np.cos(a128), np.sin(a128)], axis=1).astype(np.float32)
    c["cs64"] = np.concatenate([np.cos(a64), np.sin(a64)], axis=1).astype(np.float32)
    p = np.arange(128)[:, None]
    f = np.arange(128)[None, :]
    c["ident"] = (p == f).astype(ml_dtypes.bfloat16)
    c["identf"] = (p == f).astype(np.float32)
    c["tri"] = (p <= f).astype(ml_dtypes.bfloat16)
    c["atri"] = (p > f).astype(ml_dtypes.bfloat16)
    W0 = 512 + 8 * (NT - 1)
    cc = np.arange(W0)[None, :]
    m = cc - 8 * (NT - 1)
    c["m0ext"] = ((16 * m + 31) <= p).astype(np.float32)
    CO = 2 * (NT - 1)
    W1 = NSEL + CO
    cc = np.arange(W1)[None, :]
    d = cc - CO
    hi = (p >= 64).astype(np.int64)
    c["aext"] = (d <= hi - 2).astype(np.float32)
    forced = (d == hi) | (d == hi - 1)
    c["fext"] = np.where(forced, 1e9, np.where(d > hi, -1.0, 0.0)).astype(np.float32)
    KJ = min(128, NSEL)
    E = np.zeros((KJ, NT, 128), dtype=np.float32)
    for kt in range(NT):
        E[2 * kt, kt, :64] = 1.0
        E[2 * kt + 1, kt, 64:] = 1.0
    c["esel"] = E.astype(ml_dtypes.bfloat16)
    return c


def build(S, dbg=False, upto=99):
    NT = S // 128
    NSEL = S // 64
    KJ = min(128, NSEL)
    n_cmp = S // 16 - 1
    NCT = (n_cmp + 127) // 128
    NCP = NCT * 128
    CO = 2 * (NT - 1)
    nc = bass.Bass("TRN2", target_bir_lowering=False)
    okind = "ExternalOutput"

    def din(name, shape, dt=F32):
        return nc.dram_tensor(name, list(shape), dt, kind="ExternalInput").ap()

    def dscr(name, shape, dt):
        return nc.dram_tensor(name, list(shape), dt, kind=okind).ap()

    x = din("x", [S, 1024])
    ccol = din("ccol", [128, 8])
    w_ada = din("w_ada", [1024, 6144])
    b_ada = din("b_ada", [6144])
    attn_norm = din("attn_norm", [1024])
    ffn_norm = din("ffn_norm", [1024])
    w_in = din("w_in", [1024, IN_W])
    g_q = din("nsa_q_norm", [128])
    g_kc = din("nsa_kc_norm", [128])
    g_ks = din("nsa_ks_norm", [128])
    g_kw = din("nsa_kw_norm", [128])
    pe_kT = din("pe_kT", [128, 32])
    k_w1 = din("cmp_k_w1", [4096, 256])
    k_w2 = din("cmp_k_w2", [256, 128])
    pe_vT = din("pe_vT", [128, 32])
    v_w1 = din("cmp_v_w1", [4096, 256])
    v_w2 = din("cmp_v_w2", [256, 128])
    g_cq = din("mla_cq_norm", [384])
    g_ckv = din("mla_ckv_norm", [256])
    w_uq = din("w_uq", [384, 1536])
    w_ukv = din("w_ukv", [256, 2048])
    g_mq = din("mla_q_norm", [192])
    g_mk = din("mla_k_norm", [192])
    w_o = din("w_o", [1024, 1024])
    w_up = din("w_up", [1024, 5632])
    wconv_l = din("wconv_l", [128, 44, 3])
    bconv_l = din("bconv_l", [128, 44])
    w_down = din("w_down", [2816, 1024])
    cs128 = din("cs128", [S, 128])
    cs64 = din("cs64", [S, 64])
    c_ident = din("ident", [128, 128], BF16)
    c_identf = din("identf", [128, 128])
    c_tri = din("tri", [128, 128], BF16)
    c_atri = din("atri", [128, 128], BF16)
    c_m0 = din("m0ext", [128, 512 + 8 * (NT - 1)])
    c_aext = din("aext", [128, NSEL + CO])
    c_fext = din("fext", [128, NSEL + CO])
    c_esel = din("esel", [KJ, NT, 128], BF16)
    out = nc.dram_tensor("out", [S, 1024], F32, kind="ExternalOutput").ap()

    QaT = dscr("QaT", [8, 128, S], BF16)
    KcT = dscr("KcT", [2, 128, S], BF16)
    VcT = dscr("VcT", [2, 128, S], BF16)
    KsT = dscr("KsT", [2, 128, S], BF16)
    KwT = dscr("KwT", [2, 128, S], BF16)
    Vs = dscr("Vs", [S, 2, 128], BF16)
    Vw = dscr("Vw", [S, 2, 128], BF16)
    Gn = dscr("Gn", [S, 24], F32)
    BG = dscr("BG", [S, 2048], BF16)
    QbN = dscr("QbN", [8, 128, S], BF16)
    QbR = dscr("QbR", [8, 64, S], BF16)
    KbN = dscr("KbN", [8, 128, S], BF16)
    KbR = dscr("KbR", [8, 64, S], BF16)
    Vb = dscr("Vb", [S, 8, 128], BF16)
    Oc = dscr("Oc", [S, 1024], F32)
    BT = dscr("BT", [2, KJ, S], BF16)
    Ya = dscr("Ya", [S, 1024], F32)
    Yb = dscr("Yb", [S, 1024], BF16)
    MODB = dscr("MODB", [128, 6144], F32)

    with contextlib.ExitStack() as es:
        fw = FW(nc, es)
        global LASTFW
        LASTFW = fw
        I = fw.I
        dma = fw.dma
        V, G, A, PE = "dve", "pool", "act", "pe"

        def sb(st, name, shape, dt):
            return T(st.enter_context(nc.sbuf_tensor("sb_" + name, list(shape), dt)))

        def ps(st, name, shape, dt=F32):
            return T(st.enter_context(nc.psum_tensor("ps_" + name, list(shape), dt)))

        ident = sb(es, "ident", [128, 128], BF16)
        tri = sb(es, "tri", [128, 128], BF16)
        atri = sb(es, "atri", [128, 128], BF16)
        dma(ident[:], c_ident, w=[ident])
        dma(tri[:], c_tri, w=[tri])
        identf = sb(es, "identf", [128, 128], F32)
        dma(identf[:], c_identf, w=[identf])
        ones_f = sb(es, "ones_f", [128, 2], F32)
        I(V, lambda: nc.vector.memset(ones_f[:], 1.0), w=[ones_f])
        dma(atri[:], c_atri, w=[atri])
        SH_A, A_ATT, G_A, SH_F, A_FFN, G_F = [slice(i * 1024, (i + 1) * 1024) for i in range(6)]

        def rsqrt_ms(ssT, ss_ap, rsT, rs_ap, inv_n, rows=128):
            I(A, lambda: nc.scalar.activation(out=rs_ap, in_=ss_ap, func=AF.Sqrt, bias=eps_t[0:rows, 0:1], scale=inv_n),
              r=[ssT, eps_t], w=[rsT])
            I(V, lambda: nc.vector.reciprocal(out=rs_ap, in_=rs_ap), r=[rsT], w=[rsT])

        eps_t = sb(es, "eps_t", [128, 2], F32)
        I(V, lambda: nc.vector.memset(eps_t[:, 0:1], EPS), w=[eps_t])
        I(V, lambda: nc.vector.memset(eps_t[:, 1:2], EXPB), w=[eps_t])

        with contextlib.ExitStack() as st:
            modb = sb(st, "modb", [128, 6144], F32)
            cs_t = sb(st, "cs_t", [128, 8], F32)
            sc_t = sb(st, "sc_t", [128, 8], F32)
            scb = sb(st, "scb", [128, 8, 128], F32)
            wst = [sb(st, f"wst{i}", [128, 3072], F32) for i in range(2)]
            gtmp = sb(st, "gtmp", [128, 1024], F32)
            pm = [ps(st, f"pm{i}", [128, 512]) for i in range(6)]
            dma(cs_t[:], ccol, w=[cs_t])
            dma(modb[:], b_ada.partition_broadcast(128), w=[modb])
            I(A, lambda: nc.scalar.activation(out=sc_t[:], in_=cs_t[:], func=AF.Silu), r=[cs_t], w=[sc_t])
            I(V, lambda: nc.vector.tensor_copy(out=scb[:], in_=sc_t[:].unsqueeze(2).broadcast_to([128, 8, 128])),
              r=[sc_t], w=[scb])
            for half in range(2):
                for kc in range(8):
                    wb = wst[kc % 2]
                    dma(wb[:], w_ada[kc * 128:(kc + 1) * 128, half * 3072:(half + 1) * 3072], w=[wb])
                    for j in range(6):
                        I(PE, lambda j=j, wb=wb, kc=kc: nc.tensor.matmul(
                            pm[j][:], lhsT=scb[:, kc, :], rhs=wb[:, j * 512:(j + 1) * 512],
                            start=(kc == 0), stop=(kc == 7)), r=[scb, wb], w=[pm[j]])
                for j in range(6):
                    cs = slice(half * 3072 + j * 512, half * 3072 + (j + 1) * 512)
                    I(V, lambda j=j, cs=cs: nc.vector.tensor_tensor(out=modb[:, cs], in0=pm[j][:], in1=modb[:, cs],
                                                                    op=ALU.add), r=[pm[j], modb], w=[modb])
            for gsrc, sl in ((attn_norm, A_ATT), (ffn_norm, A_FFN)):
                dma(gtmp[:], gsrc.partition_broadcast(128), w=[gtmp])
                I(V, lambda sl=sl: nc.vector.scalar_tensor_tensor(out=modb[:, sl], in0=modb[:, sl], scalar=1.0,
                                                                  in1=gtmp[:], op0=ALU.add, op1=ALU.mult),
                  r=[modb, gtmp], w=[modb])
            dma(MODB, modb[:], r=[modb])
            fw.barrier()

        def load_cast(st_scratch, dst, dst_ap_fn, src_ap_fn, nchunks, shape, engs=(V, G, A)):
            for i in range(nchunks):
                stg = st_scratch[i % len(st_scratch)]
                dma(stg[tuple(slice(0, s_) for s_ in shape)] if False else stg_view(stg, shape), src_ap_fn(i), w=[stg])
                e = engs[i % len(engs)]
                if e == A:
                    I(A, lambda i=i, stg=stg: nc.scalar.copy(out=dst_ap_fn(i), in_=stg_view(stg, shape)), r=[stg], w=[dst])
                elif e == V:
                    I(V, lambda i=i, stg=stg: nc.vector.tensor_copy(out=dst_ap_fn(i), in_=stg_view(stg, shape)), r=[stg], w=[dst])
                else:
                    I(G, lambda i=i, stg=stg: nc.gpsimd.tensor_copy(out=dst_ap_fn(i), in_=stg_view(stg, shape)), r=[stg], w=[dst])

        def stg_view(stg, shape):
            n = 1
            for s_ in shape[1:]:
                n *= s_
            v = stg[0:shape[0], 0:n]
            if len(shape) == 3:
                v = v.rearrange("p (a b) -> p a b", a=shape[1])
            return v

        def bcast_load(st, name, src, n):
            t = sb(st, name, [128, n], F32)
            dma(t[:], src.partition_broadcast(128), w=[t])
            return t

        with contextlib.ExitStack() as st:
            winb = sb(st, "winb", [128, 8, IN_W], BF16)
            wuqb = sb(st, "wuqb", [128, 3, 1536], BF16)
            wukvb = sb(st, "wukvb", [128, 2, 2048], BF16)
            with contextlib.ExitStack() as st2:
                stg = [sb(st2, f"stg{i}", [128, IN_W], F32) for i in range(2)]
                load_cast(stg, winb, lambda i: winb[:, i, :], lambda i: w_in[i * 128:(i + 1) * 128, :], 8, [128, IN_W])
                load_cast(stg, wuqb, lambda i: wuqb[:, i, :], lambda i: w_uq[i * 128:(i + 1) * 128, :], 3, [128, 1536])
                load_cast(stg, wukvb, lambda i: wukvb[:, i, :], lambda i: w_ukv[i * 128:(i + 1) * 128, :], 2, [128, 2048])
                fw.barrier()
            modb = TO(st.enter_context(nc.sbuf_tensor("sb_mod1", [128, 2048], F32)), 0)
            dma(modb.t[:], MODB[:, 0:2048], w=[modb])
            gq_t = bcast_load(st, "gq_t", g_q, 128)
            gks_t = bcast_load(st, "gks_t", g_ks, 128)
            gkw_t = bcast_load(st, "gkw_t", g_kw, 128)
            gcq_t = bcast_load(st, "gcq_t", g_cq, 384)
            gckv_t = bcast_load(st, "gckv_t", g_ckv, 256)
            gmq_t = bcast_load(st, "gmq_t", g_mq, 192)
            gmk_t = bcast_load(st, "gmk_t", g_mk, 192)

            def mk1(u):
                B = {}
                B["cst2"] = [sb(st, f"cst_{u}{i}", [128, 192], F32) for i in range(2)]
                for nm, shp, dt in (("xt", [128, 1024], F32), ("ss1", [128, 1], F32), ("rs1", [128, 1], F32),
                                    ("nb", [128, 8, 192], BF16), ("hT", [128, 8, 128], BF16), ("f_a", [128, 1536], F32),
                                    ("f_b", [128, 1536], F32), ("f_d", [128, 1024], F32), ("ssn", [128, 8], F32), ("rsn", [128, 8], F32),
                                    ("vbt", [128, 8, 128], BF16), ("gnt", [128, 24], F32), ("cqT", [128, 3, 128], BF16),
                                    ("ckvT", [128, 2, 128], BF16), ("krf", [128, 64], F32)):
                    B[nm] = sb(st, f"{nm}_{u}", shp, dt)
                B["bgt"] = [sb(st, f"bgt_{u}{i}", [128, 512], BF16) for i in range(2)]
                B["sgA"] = [sb(st, f"sgA_{u}{i}", [128, 8, 128], BF16) for i in range(2)]
                B["sgR"] = [sb(st, f"sgR_{u}{i}", [64, 8, 128], BF16) for i in range(2)]
                B["pp"] = [ps(st, f"pp_{u}{i}", [128, 512]) for i in range(2)]
                B["pT"] = ps(st, f"pT_{u}", [128, 1024], BF16)
                B["npp"] = 0
                B["nA"] = 0
                B["nR"] = 0
                return B
            sets1 = [mk1(0), mk1(1)]

            def load_tile(t):
                B = sets1[t % 2]
                dma(B["xt"][:], x[t * 128:(t + 1) * 128, :], w=[B["xt"]])
                c_ = B["cst2"][(t // 2) % 2]
                dma(c_[:, 0:128], cs128[t * 128:(t + 1) * 128, :], w=[c_])
                dma(c_[:, 128:192], cs64[t * 128:(t + 1) * 128, :], w=[c_])

            def p1_tile(t):
                B = sets1[t % 2]
                xtt, ss1, rs1, nb, hT, f_a, f_b, f_d = (B[k] for k in ("xt", "ss1", "rs1", "nb", "hT", "f_a", "f_b", "f_d"))
                cs_ = B["cst2"][(t // 2) % 2]
                ssn, rsn, vbt, gnt, cqT, ckvT, krf, pT_ = (B[k] for k in ("ssn", "rsn", "vbt", "gnt", "cqT", "ckvT", "krf", "pT"))
                tok = slice(t * 128, (t + 1) * 128)
                nbf = nb[:].rearrange("p h d -> p (h d)")

                def proj(lhsT_t, lhs_fn, nk, w_t, c0, c1):
                    p = B["pp"][B["npp"] % 2]
                    B["npp"] += 1
                    for kc in range(nk):
                        I(PE, lambda kc=kc: nc.tensor.matmul(p[:, 0:c1 - c0], lhsT=lhs_fn(kc), rhs=w_t[:, kc, c0:c1],
                                                             start=(kc == 0), stop=(kc == nk - 1)), r=[lhsT_t, w_t], w=[p])
                    return p

                def evac(p, c, dstT, dst_ap, eng=A):
                    if eng == A:
                        I(A, lambda: nc.scalar.copy(out=dst_ap, in_=p[:, 0:c]), r=[p], w=[dstT])
                    else:
                        I(V, lambda: nc.vector.tensor_copy(out=dst_ap, in_=p[:, 0:c]), r=[p], w=[dstT])

                def norm_rope(src, src_ap, nh, hd, gain_t, do_norm, rope_off, rope_half, cs_off, dst_ap, dstT):
                    if do_norm:
                        sq = f_b[:, 0:nh * hd].rearrange("p (h d) -> p h d", h=nh)
                        I(A, lambda: nc.scalar.activation(out=sq, in_=src_ap, func=AF.Square), r=[src], w=[f_b])
                        I(V, lambda: nc.vector.tensor_reduce(out=ssn[:, 0:nh], in_=sq, axis=AX.X, op=ALU.add), r=[f_b], w=[ssn])
                        rsqrt_ms(ssn, ssn[:, 0:nh], rsn, rsn[:, 0:nh], 1.0 / hd)
                        I(V, lambda: nc.vector.tensor_tensor(out=src_ap, in0=src_ap, in1=rsn[:, 0:nh].unsqueeze(2).broadcast_to([128, nh, hd]),
                                                             op=ALU.mult), r=[src, rsn], w=[src])
                        I(G, lambda: nc.gpsimd.tensor_tensor(out=src_ap, in0=src_ap, in1=gain_t[:, 0:hd].unsqueeze(1).broadcast_to([128, nh, hd]),
                                                             op=ALU.mult), r=[src, gain_t], w=[src])
                    if rope_half == 0:
                        I(V, lambda: nc.vector.tensor_copy(out=dst_ap, in_=src_ap), r=[src], w=[dstT])
                        return
                    if rope_off > 0:
                        I(G, lambda: nc.gpsimd.tensor_copy(out=dst_ap[:, :, 0:rope_off], in_=src_ap[:, :, 0:rope_off]), r=[src], w=[dstT])
                    hh_ = rope_half
                    x1 = src_ap[:, :, rope_off:rope_off + hh_]
                    x2 = src_ap[:, :, rope_off + hh_:rope_off + 2 * hh_]
                    cb = cs_[:, cs_off:cs_off + hh_].unsqueeze(1).broadcast_to([128, nh, hh_])
                    sbb = cs_[:, cs_off + hh_:cs_off + 2 * hh_].unsqueeze(1).broadcast_to([128, nh, hh_])
                    t1 = f_d[:, 0:nh * hh_].rearrange("p (h d) -> p h d", h=nh)
                    t2 = f_d[:, 512:512 + nh * hh_].rearrange("p (h d) -> p h d", h=nh)
                    t3 = f_b[:, 0:nh * hh_].rearrange("p (h d) -> p h d", h=nh)
                    t4 = f_b[:, 512:512 + nh * hh_].rearrange("p (h d) -> p h d", h=nh)
                    I(V, lambda: nc.vector.tensor_tensor(out=t1, in0=x1, in1=cb, op=ALU.mult), r=[src, cs_], w=[f_d])
                    I(V, lambda: nc.vector.tensor_tensor(out=t2, in0=x2, in1=sbb, op=ALU.mult), r=[src, cs_], w=[f_d])
                    I(G, lambda: nc.gpsimd.tensor_tensor(out=t3, in0=x1, in1=sbb, op=ALU.mult), r=[src, cs_], w=[f_b])
                    I(G, lambda: nc.gpsimd.tensor_tensor(out=t4, in0=x2, in1=cb, op=ALU.mult), r=[src, cs_], w=[f_b])
                    I(V, lambda: nc.vector.tensor_tensor(out=dst_ap[:, :, rope_off:rope_off + hh_], in0=t1, in1=t2, op=ALU.subtract),
                      r=[f_d], w=[dstT])
                    I(G, lambda: nc.gpsimd.tensor_tensor(out=dst_ap[:, :, rope_off + hh_:rope_off + 2 * hh_], in0=t3, in1=t4, op=ALU.add),
                      r=[f_b], w=[dstT])

                def transposes(src_t, src_ap_fn, n, rows, dstT, dst_ap):
                    for i in range(n):
                        I(PE, lambda i=i: nc.tensor.transpose(out=pT_[0:rows, i * 128:(i + 1) * 128], in_=src_ap_fn(i), identity=ident[:]),
                          r=[src_t, ident], w=[pT_])
                    I(A, lambda: nc.scalar.copy(out=dst_ap, in_=pT_[0:rows, 0:n * 128].rearrange("p (a b) -> p a b", a=n)),
                      r=[pT_], w=[dstT])

                def slotA():
                    sg = B["sgA"][B["nA"] % 2]
                    B["nA"] += 1
                    return sg

                def slotR():
                    sg = B["sgR"][B["nR"] % 2]
                    B["nR"] += 1
                    return sg

                def outT(dst, sg, b0, nblk):
                    dma(dst[:, :, tok].rearrange("h d t -> d h t"), sg[:, b0:b0 + nblk, :], r=[sg])

                if t + 2 < NT:
                    pass
                I(A, lambda: nc.scalar.activation(out=nbf[:, 0:1024], in_=xtt[:], func=AF.Square, accum_out=ss1[:, 0:1]), r=[xtt], w=[nb, ss1])
                rsqrt_ms(ss1, ss1[:, 0:1], rs1, rs1[:, 0:1], 1.0 / 1024)
                I(V, lambda: nc.vector.scalar_tensor_tensor(out=f_b[:, 0:1024], in0=xtt[:], scalar=rs1[:, 0:1], in1=modb[:, A_ATT],
                                                            op0=ALU.mult, op1=ALU.mult), r=[xtt, rs1, modb], w=[f_b])
                I(G, lambda: nc.gpsimd.tensor_tensor(out=nbf[:, 0:1024], in0=f_b[:, 0:1024], in1=modb[:, SH_A], op=ALU.add), r=[f_b, modb], w=[nb])
                transposes(nb, lambda i: nbf[:, i * 128:(i + 1) * 128], 8, 128, hT, hT[:])
                if t + 2 < NT:
                    load_tile(t + 2)
                lh = lambda kc: hT[:, kc, :]
                for gq in range(2):
                    p = proj(hT, lh, 8, winb, O_NQ + gq * 512, O_NQ + (gq + 1) * 512)
                    evac(p, 512, f_a, f_a[:, gq * 512:(gq + 1) * 512], eng=(A if gq else V))
                norm_rope(f_a, f_a[:, 0:1024].rearrange("p (h d) -> p h d", h=8), 8, 128, gq_t, True, 0, 64, 0, nb[:, :, 0:128], nb)
                sg = slotA()
                transposes(nb, lambda i: nb[:, i, 0:128], 8, 128, sg, sg[:, 0:8, :])
                outT(QaT, sg, 0, 8)
                p = proj(hT, lh, 8, winb, O_NKC, O_NKC + 512)
                evac(p, 512, f_a, f_a[:, 0:512])
                norm_rope(f_a, f_a[:, 0:256].rearrange("p (h d) -> p h d", h=2), 2, 128, None, False, 0, 64, 0, nb[:, 0:2, 0:128], nb)
                I(V, lambda: nc.vector.tensor_copy(out=nb[:, 2:4, 0:128], in_=f_a[:, 256:512].rearrange("p (h d) -> p h d", h=2)),
                  r=[f_a], w=[nb])
                sg = slotA()
                transposes(nb, lambda i: nb[:, i, 0:128], 4, 128, sg, sg[:, 0:4, :])
                outT(KcT, sg, 0, 2)
                outT(VcT, sg, 2, 2)
                sg = slotA()
                for wi, (off, gt_) in enumerate(((O_NKS, gks_t), (O_NKW, gkw_t))):
                    p = proj(hT, lh, 8, winb, off, off + 512)
                    evac(p, 512, f_a, f_a[:, 0:512])
                    norm_rope(f_a, f_a[:, 0:256].rearrange("p (h d) -> p h d", h=2), 2, 128, gt_, True, 0, 64, 0, nb[:, 0:2, 0:128], nb)
                    I(V, lambda wi=wi: nc.vector.tensor_copy(out=vbt[:, 2 * wi:2 * wi + 2, :], in_=f_a[:, 256:512].rearrange("p (h d) -> p h d", h=2)),
                      r=[f_a], w=[vbt])
                    transposes(nb, lambda i: nb[:, i, 0:128], 2, 128, sg, sg[:, 2 * wi:2 * wi + 2, :])
                outT(KsT, sg, 0, 2)
                outT(KwT, sg, 2, 2)
                dma(Vs[tok, :, :], vbt[:, 0:2, :], r=[vbt])
                dma(Vw[tok, :, :], vbt[:, 2:4, :], r=[vbt])
                p = proj(hT, lh, 8, winb, O_NG, O_CKV)
                I(A, lambda: nc.scalar.activation(out=gnt[:], in_=p[:, 0:24], func=AF.Sigmoid), r=[p], w=[gnt])
                evac(p, 408, f_a, f_a[:, 0:408], eng=V)
                dma(Gn[tok, :], gnt[:], r=[gnt])
                norm_rope(f_a, f_a[:, 24:408].rearrange("p (h d) -> p h d", h=1), 1, 384, gcq_t, True, 0, 0, 0,
                          nbf[:, 0:384].rearrange("p (h d) -> p h d", h=1), nb)
                transposes(nb, lambda i: nbf[:, i * 128:(i + 1) * 128], 3, 128, cqT, cqT[:])
                p = proj(hT, lh, 8, winb, O_CKV, O_BG)
                evac(p, 320, f_a, f_a[:, 0:320], eng=V)
                I(G, lambda: nc.gpsimd.tensor_copy(out=krf[:], in_=f_a[:, 256:320]), r=[f_a], w=[krf])
                norm_rope(f_a, f_a[:, 0:256].rearrange("p (h d) -> p h d", h=1), 1, 256, gckv_t, True, 0, 0, 0,
                          nbf[:, 512:768].rearrange("p (h d) -> p h d", h=1), nb)
                transposes(nb, lambda i: nbf[:, 512 + i * 128:512 + (i + 1) * 128], 2, 128, ckvT, ckvT[:])
                for j in range(4):
                    p = proj(hT, lh, 8, winb, O_BG + j * 512, O_BG + (j + 1) * 512)
                    bg_ = B["bgt"][j % 2]
                    I(A, lambda bg_=bg_, p=p: nc.scalar.activation(out=bg_[:], in_=p[:, 0:512], func=AF.Sigmoid), r=[p], w=[bg_])
                    dma(BG[tok, j * 512:(j + 1) * 512], bg_[:], r=[bg_])
                for j in range(3):
                    p = proj(cqT, lambda kc: cqT[:, kc, :], 3, wuqb, j * 512, (j + 1) * 512)
                    evac(p, 512, f_a, f_a[:, j * 512:(j + 1) * 512], eng=(A if j % 2 else V))
                norm_rope(f_a, f_a[:, 0:1536].rearrange("p (h d) -> p h d", h=8), 8, 192, gmq_t, True, 128, 32, 128, nb[:, :, :], nb)
                sg = slotA()
                transposes(nb, lambda i: nb[:, i, 0:128], 8, 128, sg, sg[:, 0:8, :])
                outT(QbN, sg, 0, 8)
                sgr = slotR()
                transposes(nb, lambda i: nb[:, i, 128:192], 8, 64, sgr, sgr[:, 0:8, :])
                outT(QbR, sgr, 0, 8)
                for half in range(2):
                    for j in range(2):
                        c0 = half * 1024 + j * 512
                        p = proj(ckvT, lambda kc: ckvT[:, kc, :], 2, wukvb, c0, c0 + 512)
                        evac(p, 512, f_a, f_a[:, j * 512:(j + 1) * 512], eng=(A if j % 2 else V))
                    kvv = f_a[:, 0:1024].rearrange("p (h d) -> p h d", h=4)
                    I(G, lambda half=half, kvv=kvv: nc.gpsimd.tensor_copy(out=vbt[:, 4 * half:4 * half + 4, :], in_=kvv[:, :, 128:256]), r=[f_a], w=[vbt])
                    I(V, lambda kvv=kvv: nc.vector.tensor_copy(out=kvv[:, :, 128:192], in_=krf[:].unsqueeze(1).broadcast_to([128, 4, 64])),
                      r=[krf, f_a], w=[f_a])
                    norm_rope(f_a, kvv[:, :, 0:192], 4, 192, gmk_t, True, 128, 32, 128, nb[:, 4 * half:4 * half + 4, :], nb)
                dma(Vb[tok, :, :], vbt[:], r=[vbt])
                sg = slotA()
                transposes(nb, lambda i: nb[:, i, 0:128], 8, 128, sg, sg[:, 0:8, :])
                outT(KbN, sg, 0, 8)
                sgr = slotR()
                transposes(nb, lambda i: nb[:, i, 128:192], 8, 64, sgr, sgr[:, 0:8, :])
                outT(KbR, sgr, 0, 8)

            load_tile(0)
            if NT > 1:
                load_tile(1)
            str0, str1 = [], []
            for t in range(0, NT, 2):
                str0 += fw.record(lambda t=t: p1_tile(t))
                if t + 1 < NT:
                    str1 += fw.record(lambda t=t: p1_tile(t + 1))
            skew = (len(str0) // max(1, (NT + 1) // 2)) // 2
            fw.interleave([str0, [(lambda: None)] * skew + str1])
            fw.barrier()
            if upto == 1:
                return nc

        SC128 = 128.0 ** -0.5
        SC192 = 192.0 ** -0.5

        def tr_generic(pbuf, src_t, src_ap_fn, n, rows, dstT, dst_ap, eng=A):
            for i in range(n):
                I(PE, lambda i=i: nc.tensor.transpose(out=pbuf[0:rows, i * 128:(i + 1) * 128], in_=src_ap_fn(i), identity=ident[:]),
                  r=[src_t, ident], w=[pbuf])
            if eng == A:
                I(A, lambda: nc.scalar.copy(out=dst_ap, in_=pbuf[0:rows, 0:n * 128].rearrange("p (a b) -> p a b", a=n)), r=[pbuf], w=[dstT])
            else:
                I(V, lambda: nc.vector.tensor_copy(out=dst_ap, in_=pbuf[0:rows, 0:n * 128].rearrange("p (a b) -> p a b", a=n)), r=[pbuf], w=[dstT])

        with contextlib.ExitStack() as st23:
            kcT = [sb(st23, f"kcT{g}", [128, NCP], BF16) for g in range(2)]
            vcb = [sb(st23, f"vcb{g}", [128, NCT, 128], BF16) for g in range(2)]
            for g in range(2):
                I(V, lambda g=g: nc.vector.memset(kcT[g][:], 0.0), w=[kcT[g]])
                I(G, lambda g=g: nc.gpsimd.memset(vcb[g][:], 0.0), w=[vcb[g]])
            with contextlib.ExitStack() as st:
                XT = sb(st, "XT", [128, S], BF16)
                Xl = sb(st, "Xl", [128, 32, 512], BF16)
                w1s = sb(st, "w1s", [128, 32 * 256], F32)
                w1b = sb(st, "w1b", [128, 32, 256], BF16)
                w2s = sb(st, "w2s", [128, 256], F32)
                w2b = sb(st, "w2b", [128, 2, 128], BF16)
                peT = sb(st, "peT", [128, 32], F32)
                hTc = sb(st, "hTc", [128, 2, 512], BF16)
                gkc_t = bcast_load(st, "gkc_t", g_kc, 128)
                kf = sb(st, "kf", [128, 128], F32)
                kf2 = sb(st, "kf2", [128, 128], F32)
                kb = sb(st, "kb", [128, 128], BF16)
                ssk = sb(st, "ssk", [128, 1], F32)
                rsk = sb(st, "rsk", [128, 1], F32)
                pc = [ps(st, f"pc{i}", [128, 512]) for i in range(2)]
                pk = ps(st, "pk", [128, 128])
                pkT = ps(st, "pkT", [128, 128], BF16)
                for kv in range(2):
                    w1, w2, pe_ = (k_w1, k_w2, pe_kT) if kv == 0 else (v_w1, v_w2, pe_vT)
                    dma(w1s[:].rearrange("p (l c) -> p l c", l=32), w1.rearrange("(l d) c -> d l c", d=128), w=[w1s])
                    I(V, lambda: nc.vector.tensor_copy(out=w1b[:], in_=w1s[:].rearrange("p (l c) -> p l c", l=32)), r=[w1s], w=[w1b])
                    dma(w2s[:].rearrange("p (k d) -> p k d", k=2), w2.rearrange("(k c) d -> c k d", c=128), w=[w2s])
                    I(G, lambda: nc.gpsimd.tensor_copy(out=w2b[:], in_=w2s[:].rearrange("p (k d) -> p k d", k=2)), r=[w2s], w=[w2b])
                    dma(peT[:], pe_, w=[peT])
                    for g in range(2):
                        src = KcT if kv == 0 else VcT
                        dma(XT[:], src[g, :, :], w=[XT])
                        for l in range(32):
                            xin = XT[:, l:l + 16 * (n_cmp - 1) + 1:16]
                            if l % 2 == 0:
                                I(V, lambda l=l, xin=xin: nc.vector.tensor_scalar(out=Xl[:, l, 0:n_cmp], in0=xin, scalar1=peT[:, l:l + 1],
                                                                                  scalar2=None, op0=ALU.add), r=[XT, peT], w=[Xl])
                            else:
                                I(A, lambda l=l, xin=xin: nc.scalar.activation(out=Xl[:, l, 0:n_cmp], in_=xin, func=AF.Identity,
                                                                               bias=peT[:, l:l + 1], scale=1.0), r=[XT, peT], w=[Xl])
                        for ch in range(2):
                            p = pc[ch]
                            for l in range(32):
                                I(PE, lambda l=l, p=p, ch=ch: nc.tensor.matmul(p[:, 0:n_cmp], lhsT=w1b[:, l, ch * 128:(ch + 1) * 128],
                                                                               rhs=Xl[:, l, 0:n_cmp], start=(l == 0), stop=(l == 31)),
                                  r=[w1b, Xl], w=[p])
                            I(A, lambda p=p, ch=ch: nc.scalar.activation(out=hTc[:, ch, 0:n_cmp], in_=p[:, 0:n_cmp], func=AF.Silu),
                              r=[p], w=[hTc])
                        for nt in range(NCT):
                            rows = min(128, n_cmp - nt * 128)
                            for ch in range(2):
                                I(PE, lambda ch=ch, nt=nt, rows=rows: nc.tensor.matmul(pk[0:rows, :], lhsT=hTc[:, ch, nt * 128:nt * 128 + rows],
                                                                                       rhs=w2b[:, ch, :], start=(ch == 0), stop=(ch == 1)),
                                  r=[hTc, w2b], w=[pk])
                            if kv == 0:
                                I(A, lambda rows=rows: nc.scalar.copy(out=kf[0:rows, :], in_=pk[0:rows, :]), r=[pk], w=[kf])
                                I(V, lambda rows=rows: nc.vector.tensor_tensor(out=kf2[0:rows, :], in0=kf[0:rows, :], in1=kf[0:rows, :], op=ALU.mult),
                                  r=[kf], w=[kf2])
                                I(V, lambda rows=rows: nc.vector.tensor_reduce(out=ssk[0:rows, :], in_=kf2[0:rows, :], axis=AX.X, op=ALU.add),
                                  r=[kf2], w=[ssk])
                                rsqrt_ms(ssk, ssk[0:rows, :], rsk, rsk[0:rows, :], 1.0 / 128, rows=rows)
                                I(V, lambda rows=rows: nc.vector.scalar_tensor_tensor(out=kb[0:rows, :], in0=kf[0:rows, :], scalar=rsk[0:rows, 0:1],
                                                                                      in1=gkc_t[0:rows, :], op0=ALU.mult, op1=ALU.mult),
                                  r=[kf, rsk, gkc_t], w=[kb])
                                I(PE, lambda rows=rows: nc.tensor.transpose(out=pkT[:, 0:rows], in_=kb[0:rows, :], identity=ident[0:rows, 0:rows]),
                                  r=[kb, ident], w=[pkT])
                                I(A, lambda rows=rows, nt=nt, g=g: nc.scalar.copy(out=kcT[g][:, nt * 128:nt * 128 + rows], in_=pkT[:, 0:rows]),
                                  r=[pkT], w=[kcT[g]])
                            else:
                                I(A, lambda rows=rows, nt=nt, g=g: nc.scalar.copy(out=vcb[g][0:rows, nt, :], in_=pk[0:rows, :]), r=[pk], w=[vcb[g]])
                fw.barrier()
            if upto == 2:
                return nc
            with contextlib.ExitStack() as st:
                W0 = 512 + 8 * (NT - 1)
                m0 = sb(st, "m0", [128, W0], F32)
                aext = sb(st, "aext", [128, NSEL + CO], F32)
                fext = sb(st, "fext", [128, NSEL + CO], F32)
                dma(m0[:], c_m0, w=[m0])
                dma(aext[:], c_aext, w=[aext])
                dma(fext[:], c_fext, w=[fext])
                PW = 4 * NSEL + 8

                def mkset(u):
                    B = {}
                    B["qT"] = [sb(st, f"qT{u}{i}", [128, 8, 128], BF16) for i in range(2)]
                    B["gn3"] = [sb(st, f"gn3{u}{i}", [128, 24], F32) for i in range(2)]
                    B["Ef"] = [sb(st, f"Ef{u}{i}", [128, 512], F32) for i in range(2)]
                    B["Pm"] = sb(st, f"Pm{u}", [128, 8, 512], F32)
                    B["Pb"] = [sb(st, f"Pb{u}{i}", [128, 512], BF16) for i in range(2)]
                    B["PbT"] = [sb(st, f"PbT{u}{i}", [128, 4, 128], BF16) for i in range(2)]
                    for nm, shp, dt in (("Z", [128, 8], F32), ("rz", [128, 8], F32), ("gz", [128, 8], F32), ("ppad", [128, 2, PW], F32),
                                        ("imp", [128, 2, NSEL], F32), ("score", [128, 2, NSEL], F32), ("sc2", [128, 2, NSEL], F32),
                                        ("m1", [128, 2, 8], F32), ("m2", [128, 2, 8], F32), ("Btb", [128, 2, NSEL], BF16),
                                        ("BtT", [KJ, 2, 128], BF16), ("ocm", [128, 8, 128], F32)):
                        B[nm] = sb(st, f"{nm}{u}", shp, dt)
                    B["pS"] = ps(st, f"pS{u}", [128, 512])
                    B["pTB"] = ps(st, f"pTB{u}", [128, 1024], BF16)
                    B["pO"] = [ps(st, f"pO{u}{i}", [128, 512]) for i in range(2)]
                    I(V, lambda: nc.vector.memset(B["ppad"][:], 0.0), w=[B["ppad"]])
                    return B
                sets = [mkset(0), mkset(1)]

                def ld3(t):
                    B = sets[t % 2]
                    b = (t // 2) % 2
                    tk = slice(t * 128, (t + 1) * 128)
                    dma(B["qT"][b][:], QaT[:, :, tk].rearrange("h d t -> d h t"), w=[B["qT"][b]])
                    dma(B["gn3"][b][:], Gn[tk, :], w=[B["gn3"][b]])

                def p3_tile(t):
                    B = sets[t % 2]
                    b = (t // 2) % 2
                    if t + 2 < NT:
                        ld3(t + 2)
                    tk = slice(t * 128, (t + 1) * 128)
                    NW = min(n_cmp, 8 * t + 7)
                    off = 8 * (NT - 1) - 8 * t
                    nkt = (NW + 127) // 128
                    q_, gn_, oc_ = B["qT"][b], B["gn3"][b], B["ocm"]
                    Ef, Pm, Pb, PbT, Z, rz, gz, ppad = B["Ef"], B["Pm"], B["Pb"], B["PbT"], B["Z"], B["rz"], B["gz"], B["ppad"]
                    imp, score, sc2, m1, m2, Btb, BtT = B["imp"], B["score"], B["sc2"], B["m1"], B["m2"], B["Btb"], B["BtT"]
                    pS_, pTB, pO = B["pS"], B["pTB"], B["pO"]
                    for h in range(8):
                        g = h // 4
                        hb_ = h % 2
                        I(PE, lambda h=h, g=g: nc.tensor.matmul(pS_[:, 0:NW], lhsT=q_[:, h, :], rhs=kcT[g][:, 0:NW], start=True, stop=True),
                          r=[q_, kcT[g]], w=[pS_])
                        I(A, lambda hb_=hb_: nc.scalar.activation(out=Ef[hb_][:, 0:NW], in_=pS_[:, 0:NW], func=AF.Exp, scale=SC128),
                          r=[pS_, eps_t], w=[Ef[hb_]])
                        I(V, lambda h=h, hb_=hb_: nc.vector.scalar_tensor_tensor(out=Pm[:, h, 0:NW], in0=Ef[hb_][:, 0:NW], scalar=1.0,
                                                                                  in1=m0[:, off:off + NW], op0=ALU.mult, op1=ALU.mult,
                                                                                  accum_out=Z[:, h:h + 1]), r=[Ef[hb_], m0], w=[Pm, Z])
                        I(G, lambda h=h, hb_=hb_: nc.gpsimd.tensor_copy(out=Pb[hb_][:, 0:NW], in_=Pm[:, h, 0:NW]), r=[Pm], w=[Pb[hb_]])
                        for kt in range(nkt):
                            rows = min(128, NW - kt * 128)
                            I(PE, lambda kt=kt, rows=rows, hb_=hb_: nc.tensor.transpose(out=pTB[0:rows, kt * 128:(kt + 1) * 128],
                                                                                         in_=Pb[hb_][:, kt * 128:kt * 128 + rows], identity=ident[:]),
                              r=[Pb[hb_], ident], w=[pTB])
                        for kt in range(nkt):
                            rows = min(128, NW - kt * 128)
                            I(A, lambda kt=kt, rows=rows, hb_=hb_: nc.scalar.copy(out=PbT[hb_][0:rows, kt, :], in_=pTB[0:rows, kt * 128:(kt + 1) * 128]),
                              r=[pTB], w=[PbT[hb_]])
                        for kt in range(nkt):
                            rows = min(128, NW - kt * 128)
                            I(PE, lambda kt=kt, rows=rows, h=h, g=g, hb_=hb_: nc.tensor.matmul(
                                pO[g][:, (h % 4) * 128:(h % 4 + 1) * 128], lhsT=PbT[hb_][0:rows, kt, :], rhs=vcb[g][0:rows, kt, :],
                                start=(h % 4 == 0 and kt == 0), stop=(kt == nkt - 1)), r=[PbT[hb_], vcb[g]], w=[pO[g]])
                    I(V, lambda: nc.vector.tensor_scalar(out=rz[:], in0=Z[:], scalar1=1e-30, scalar2=None, op0=ALU.max), r=[Z], w=[rz])
                    I(V, lambda: nc.vector.reciprocal(out=rz[:], in_=rz[:]), r=[rz], w=[rz])
                    I(V, lambda: nc.vector.tensor_tensor(out=gz[:], in0=rz[:], in1=gn_[:].rearrange("p (h j) -> p h j", j=3)[:, :, 0], op=ALU.mult),
                      r=[rz, gn_], w=[gz])
                    for g in range(2):
                        I(V, lambda g=g: nc.vector.tensor_tensor(out=oc_[:, 4 * g:4 * g + 4, :], in0=pO[g][:, 0:512].rearrange("p (h d) -> p h d", h=4),
                                                                 in1=gz[:, 4 * g:4 * g + 4].unsqueeze(2).broadcast_to([128, 4, 128]), op=ALU.mult),
                          r=[pO[g], gz], w=[oc_])
                    dma(Oc[tk, :], oc_[:].rearrange("p h d -> p (h d)"), r=[oc_])
                    for g in range(2):
                        for r_ in range(4):
                            h = 4 * g + r_
                            if r_ == 0:
                                I(V, lambda g=g, h=h: nc.vector.tensor_scalar(out=ppad[:, g, 4:4 + NW], in0=Pm[:, h, 0:NW], scalar1=rz[:, h:h + 1],
                                                                              scalar2=None, op0=ALU.mult), r=[Pm, rz], w=[ppad])
                            else:
                                I(V, lambda g=g, h=h: nc.vector.scalar_tensor_tensor(out=ppad[:, g, 4:4 + NW], in0=Pm[:, h, 0:NW], scalar=rz[:, h:h + 1],
                                                                                     in1=ppad[:, g, 4:4 + NW], op0=ALU.mult, op1=ALU.add),
                                  r=[Pm, rz, ppad], w=[ppad])
                    a_t = aext[:, CO - 2 * t:CO - 2 * t + NSEL]
                    f_t = fext[:, CO - 2 * t:CO - 2 * t + NSEL]
                    for g in range(2):
                        I(V, lambda g=g: nc.vector.tensor_reduce(out=imp[:, g, :], in_=ppad[:, g, 4:4 + 4 * NSEL].rearrange("p (j f) -> p j f", f=4),
                                                                 axis=AX.X, op=ALU.add), r=[ppad], w=[imp])
                        I(V, lambda g=g: nc.vector.tensor_tensor(out=imp[:, g, :], in0=imp[:, g, :],
                                                                 in1=ppad[:, g, 0:4 * NSEL].rearrange("p (j f) -> p j f", f=4)[:, :, 3], op=ALU.add),
                          r=[ppad, imp], w=[imp])
                        I(V, lambda g=g: nc.vector.tensor_tensor(out=score[:, g, :], in0=imp[:, g, :], in1=a_t, op=ALU.mult), r=[imp, aext], w=[score])
                        I(V, lambda g=g: nc.vector.tensor_tensor(out=score[:, g, :], in0=score[:, g, :], in1=f_t, op=ALU.add), r=[score, fext], w=[score])
                        I(V, lambda g=g: nc.vector.memset(score[:, g, 0:1], 1e9), r=[score], w=[score])
                        if NSEL > 16:
                            I(V, lambda g=g: nc.vector.max(out=m1[:, g, :], in_=score[:, g, :]), r=[score], w=[m1])
                            I(V, lambda g=g: nc.vector.match_replace(out=sc2[:, g, :], in_to_replace=m1[:, g, :], in_values=score[:, g, :], imm_value=-2.0),
                              r=[score, m1], w=[sc2])
                            I(V, lambda g=g: nc.vector.max(out=m2[:, g, :], in_=sc2[:, g, :]), r=[sc2], w=[m2])
                            I(V, lambda g=g: nc.vector.tensor_scalar(out=Btb[:, g, :], in0=score[:, g, :], scalar1=m2[:, g, 7:8], scalar2=NEGB,
                                                                     op0=ALU.is_lt, op1=ALU.mult), r=[score, m2], w=[Btb])
                        else:
                            I(V, lambda g=g: nc.vector.memset(Btb[:, g, :], 0.0), w=[Btb])
                        I(PE, lambda g=g: nc.tensor.transpose(out=pTB[0:KJ, 512 + g * 128:512 + (g + 1) * 128], in_=Btb[:, g, 0:KJ], identity=ident[:]),
                          r=[Btb, ident], w=[pTB])
                    I(A, lambda: nc.scalar.copy(out=BtT[:], in_=pTB[0:KJ, 512:768].rearrange("p (g t) -> p g t", g=2)), r=[pTB], w=[BtT])
                    dma(BT[:, :, tk].rearrange("g j t -> j g t"), BtT[:], r=[BtT])

                ld3(0)
                if NT > 1:
                    ld3(1)
                str0, str1 = [], []
                for t in range(0, NT, 2):
                    str0 += fw.record(lambda t=t: p3_tile(t))
                    if t + 1 < NT:
                        str1 += fw.record(lambda t=t: p3_tile(t + 1))
                skew = (len(str0) // max(1, (NT + 1) // 2)) // 2
                fw.interleave([str0, [(lambda: None)] * skew + str1])
                fw.barrier()
        if upto == 3:
            return nc

        def attn_pipeline(pairs, stageA, stageB):
            n = len(pairs)
            if n == 0:
                return
            stageA(0, pairs[0])
            for i in range(n):
                if i + 1 < n:
                    stageA(i + 1, pairs[i + 1])
                stageB(i, pairs[i])

        with contextlib.ExitStack() as st:
            esel = sb(st, "esel", [KJ, NT, 128], BF16)
            dma(esel[:], c_esel, w=[esel])
            Ks_s = sb(st, "Ks_s", [128, S], BF16)
            Kw_s = sb(st, "Kw_s", [128, S], BF16)
            Vs_s = sb(st, "Vs_s", [128, NT, 129], BF16)
            Vw_s = sb(st, "Vw_s", [128, NT, 129], BF16)
            qT4 = [sb(st, f"qT4{i}", [128, 4, 128], BF16) for i in range(2)]
            btl = [sb(st, f"btl{i}", [KJ, 128], BF16) for i in range(2)]
            BT4 = [sb(st, f"BT4{i}", [KJ, 4, 128], BF16) for i in range(2)]
            PT = [sb(st, f"PT{i}", [128, 4, 128], BF16) for i in range(3)]
            gn4 = [sb(st, f"gn4{i}", [128, 24], F32) for i in range(2)]
            ocl = [sb(st, f"ocl{i}", [128, 4, 128], F32) for i in range(2)]
            bga = [sb(st, f"bga{i}", [128, 512], BF16) for i in range(2)]
            oa = sb(st, "oa", [128, 4, 128], F32)
            yat = [sb(st, f"yat{i}", [128, 512], F32) for i in range(2)]
            rzs = sb(st, "rzs", [128, 8], F32)
            cfs = sb(st, "cfs", [128, 8], F32)
            pS = [ps(st, f"qS{i}", [128, 512]) for i in range(2)]
            pOTs = ps(st, "pOTs", [128, 512])
            pOTw = ps(st, "pOTw", [128, 512])
            pTrs = ps(st, "pTrs", [128, 512])
            pTrw = ps(st, "pTrw", [128, 512])
            pZ4 = ps(st, "pZ4", [128, 512])
            acc_s = sb(st, "acc_s", [128, 512], F32)
            acc_w = sb(st, "acc_w", [128, 512], F32)
            oT_s = sb(st, "oT_s", [128, 512], F32)
            oT_w = sb(st, "oT_w", [128, 512], F32)
            I(V, lambda: nc.vector.memset(Vs_s[:, :, 128:129], 1.0), w=[Vs_s])
            I(V, lambda: nc.vector.memset(Vw_s[:, :, 128:129], 1.0), w=[Vw_s])
            cnt4 = [0]
            for g in range(2):
                dma(Ks_s[:], KsT[g, :, :], w=[Ks_s])
                dma(Kw_s[:], KwT[g, :, :], w=[Kw_s])
                dma(Vs_s[:, :, 0:128], Vs[:, g, :].rearrange("(t p) d -> p t d", p=128), w=[Vs_s])
                dma(Vw_s[:, :, 0:128], Vw[:, g, :].rearrange("(t p) d -> p t d", p=128), w=[Vw_s])

                def ld4(t, g=g):
                    b = t % 2
                    tk = slice(t * 128, (t + 1) * 128)
                    dma(qT4[b][:], QaT[4 * g:4 * g + 4, :, tk].rearrange("h d t -> d h t"), w=[qT4[b]])
                    dma(btl[b][:], BT[g, :, tk], w=[btl[b]])
                    dma(gn4[b][:], Gn[tk, :], w=[gn4[b]])
                    dma(ocl[b][:], Oc[tk, 4 * g * 128:(4 * g + 4) * 128].rearrange("p (h d) -> p h d", h=4), w=[ocl[b]])
                    dma(bga[b][:], BG[tk, 4 * g * 128:(4 * g + 4) * 128], w=[bga[b]])
                ld4(0)
                for t in range(NT):
                    if t + 1 < NT:
                        ld4(t + 1)
                    b = t % 2
                    tk = slice(t * 128, (t + 1) * 128)
                    q_ = qT4[b]
                    I(G, lambda: nc.gpsimd.tensor_copy(out=BT4[b][:], in_=btl[b][:].unsqueeze(1).broadcast_to([KJ, 4, 128])), r=[btl[b]], w=[BT4[b]])
                    pairs = [("s", kt) for kt in range(t + 1)] + [("w", kt) for kt in range(max(0, t - 4), t + 1)]
                    base = cnt4[0]
                    cnt4[0] += len(pairs)

                    def stA(i, pr):
                        kind, kt = pr
                        k = base + i
                        p = pS[k % 2]
                        pt = PT[k % 3]
                        ksl = slice(kt * 128, (kt + 1) * 128)
                        if kind == "s":
                            I(PE, lambda: nc.tensor.matmul(p[:, :], lhsT=Ks_s[:, ksl], rhs=q_[:].rearrange("p h t -> p (h t)"), start=True, stop=False),
                              r=[Ks_s, q_], w=[p])
                            I(PE, lambda: nc.tensor.matmul(p[:, :], lhsT=esel[:, kt, :], rhs=BT4[b][:].rearrange("p h t -> p (h t)"), start=False, stop=True),
                              r=[esel, BT4[b]], w=[p])
                        else:
                            I(PE, lambda: nc.tensor.matmul(p[:, :], lhsT=Kw_s[:, ksl], rhs=q_[:].rearrange("p h t -> p (h t)"), start=True, stop=True),
                              r=[Kw_s, q_], w=[p])
                        I(A, lambda: nc.scalar.activation(out=pt[:].rearrange("p h t -> p (h t)"), in_=p[:, :], func=AF.Exp, scale=SC128),
                          r=[p, eps_t], w=[pt])
                        mk = None
                        if kt == t:
                            mk = tri
                        elif kind == "w" and kt == t - 4:
                            mk = atri
                        if mk is not None:
                            I(G, lambda: nc.gpsimd.tensor_tensor(out=pt[:], in0=pt[:], in1=mk[:].unsqueeze(1).broadcast_to([128, 4, 128]), op=ALU.mult),
                              r=[pt, mk], w=[pt])

                    def stB(i, pr):
                        kind, kt = pr
                        k = base + i
                        pt = PT[k % 3]
                        ptf = pt[:].rearrange("p h t -> p (h t)")
                        if kind == "s":
                            pO_, vv, acc_, first, last = pOTs, Vs_s, acc_s, (kt == 0), (kt == t)
                        else:
                            pO_, vv, acc_, first, last = pOTw, Vw_s, acc_w, (kt == max(0, t - 4)), (kt == t)
                        I(PE, lambda: nc.tensor.matmul(pO_[:, :], lhsT=vv[:, kt, 0:128], rhs=ptf, start=first, stop=last), r=[pt, vv], w=[pO_])
                        if first:
                            I(V, lambda: nc.vector.tensor_copy(out=acc_[:], in_=ptf), r=[pt], w=[acc_])
                        else:
                            I(V, lambda: nc.vector.tensor_tensor(out=acc_[:], in0=acc_[:], in1=ptf, op=ALU.add), r=[pt, acc_], w=[acc_])
                    attn_pipeline(pairs, stA, stB)
                    for ki, (pO_, acc_, oT_, pTr_) in enumerate(((pOTs, acc_s, oT_s, pTrs), (pOTw, acc_w, oT_w, pTrw))):
                        for hh in range(4):
                            I(PE, lambda hh=hh, ki=ki, acc_=acc_: nc.tensor.matmul(pZ4[:, 8 * ki + 2 * hh:8 * ki + 2 * hh + 2], lhsT=acc_[:, hh * 128:(hh + 1) * 128],
                                                                                    rhs=ones_f[:, 0:2], start=True, stop=True), r=[acc_, ones_f], w=[pZ4])
                        I(A, lambda pO_=pO_, oT_=oT_: nc.scalar.copy(out=oT_[:], in_=pO_[:, :]), r=[pO_], w=[oT_])
                        for hh in range(4):
                            I(PE, lambda hh=hh, oT_=oT_, pTr_=pTr_: nc.tensor.transpose(out=pTr_[:, hh * 128:(hh + 1) * 128], in_=oT_[:, hh * 128:(hh + 1) * 128],
                                                                                         identity=identf[:]), r=[oT_, identf], w=[pTr_])
                    I(V, lambda: nc.vector.reciprocal(out=rzs[:, 0:8], in_=pZ4[:, 0:16:2]), r=[pZ4], w=[rzs])
                    gv = gn4[b][:].rearrange("p (h j) -> p h j", j=3)
                    I(V, lambda: nc.vector.tensor_tensor(out=cfs[:, 0:4], in0=rzs[:, 0:4], in1=gv[:, 4 * g:4 * g + 4, 1], op=ALU.mult), r=[rzs, gn4[b]], w=[cfs])
                    I(V, lambda: nc.vector.tensor_tensor(out=cfs[:, 4:8], in0=rzs[:, 4:8], in1=gv[:, 4 * g:4 * g + 4, 2], op=ALU.mult), r=[rzs, gn4[b]], w=[cfs])
                    for hh in range(4):
                        col = (hh % 2) * 129
                        I(V, lambda hh=hh, col=col: nc.vector.scalar_tensor_tensor(out=oa[:, hh, :], in0=pTrs[:, hh * 128:(hh + 1) * 128], scalar=cfs[:, hh:hh + 1],
                                                                                   in1=ocl[b][:, hh, :], op0=ALU.mult, op1=ALU.add),
                          r=[pTrs, cfs, ocl[b]], w=[oa])
                        I(V, lambda hh=hh, col=col: nc.vector.scalar_tensor_tensor(out=oa[:, hh, :], in0=pTrw[:, hh * 128:(hh + 1) * 128], scalar=cfs[:, 4 + hh:5 + hh],
                                                                                   in1=oa[:, hh, :], op0=ALU.mult, op1=ALU.add),
                          r=[pTrw, cfs, oa], w=[oa])
                    I(G, lambda: nc.gpsimd.tensor_tensor(out=yat[b][:], in0=oa[:].rearrange("p h d -> p (h d)"), in1=bga[b][:], op=ALU.mult),
                      r=[oa, bga[b]], w=[yat[b]])
                    dma(Ya[tk, 4 * g * 128:(4 * g + 4) * 128], yat[b][:], r=[yat[b]])
            fw.barrier()
        if upto == 4:
            return nc

        NQB = S // 512
        with contextlib.ExitStack() as st:
            KN = [sb(st, f"KN{i}", [128, S], BF16) for i in range(2)]
            KR = [sb(st, f"KR{i}", [128, S], BF16) for i in range(2)]
            VB = [sb(st, f"VB{i}", [128, NT, 129], BF16) for i in range(2)]
            QN = [sb(st, f"QN{i}", [128, 512], BF16) for i in range(2)]
            QR = [sb(st, f"QR{i}", [128, 512], BF16) for i in range(2)]
            PT = [sb(st, f"PTm{i}", [128, 512], BF16) for i in range(3)]
            bgb = [sb(st, f"bgb{i}", [128, 4, 128], BF16) for i in range(2)]
            yal = [sb(st, f"yal{i}", [128, 4, 128], F32) for i in range(2)]
            yo = [sb(st, f"yo{i}", [128, 4, 128], BF16) for i in range(2)]
            obf = sb(st, "obf", [128, 4, 128], F32)
            rz5 = sb(st, "rz5", [128, 4], F32)
            pS = [ps(st, f"mS{i}", [128, 512]) for i in range(2)]
            pOT5 = [ps(st, f"mOT{j}", [128, 512]) for j in range(2)]
            pTr5 = ps(st, "mTr", [128, 512])
            pZ5 = ps(st, "mZ", [128, 512])
            acc5 = [sb(st, f"acc5{j}", [128, 512], F32) for j in range(2)]
            oT5 = sb(st, "oT5", [128, 512], F32)
            for i in range(2):
                I(V, lambda i=i: nc.vector.memset(VB[i][:, :, 128:129], 1.0), w=[VB[i]])
                I(G, lambda i=i: nc.gpsimd.memset(KR[i][64:128, :], 0.0), w=[KR[i]])
                I(G, lambda i=i: nc.gpsimd.memset(QR[i][64:128, :], 0.0), w=[QR[i]])
            cnt5 = [0]

            def ldh(h):
                b = h % 2
                dma(KN[b][:], KbN[h, :, :], w=[KN[b]])
                dma(KR[b][0:64, :], KbR[h, :, :], w=[KR[b]])
                dma(VB[b][:, :, 0:128], Vb[:, h, :].rearrange("(t p) d -> p t d", p=128), w=[VB[b]])

            def ldq(h, qb):
                bq = (h * NQB + qb) % 2
                rows = slice(qb * 512, (qb + 1) * 512)
                dma(QN[bq][:], QbN[h, :, rows], w=[QN[bq]])
                dma(QR[bq][0:64, :], QbR[h, :, rows], w=[QR[bq]])
                dma(bgb[bq][:], BG[rows, 1024 + h * 128:1024 + (h + 1) * 128].rearrange("(s p) c -> p s c", p=128), w=[bgb[bq]])
                dma(yal[bq][:], Ya[rows, h * 128:(h + 1) * 128].rearrange("(s p) c -> p s c", p=128), w=[yal[bq]])
            ldh(0)
            ldq(0, 0)
            for h in range(8):
                if h + 1 < 8:
                    ldh(h + 1)
                b = h % 2
                for qb in range(NQB):
                    nxt = h * NQB + qb + 1
                    if nxt < 8 * NQB:
                        ldq(nxt // NQB, nxt % NQB)
                    bq = (h * NQB + qb) % 2
                    rows = slice(qb * 512, (qb + 1) * 512)
                    pO_ = pOT5[bq]
                    acc_ = acc5[bq]
                    pairs = list(range(4 * qb + 4))
                    base = cnt5[0]
                    cnt5[0] += len(pairs)

                    def stA(i, kt):
                        k = base + i
                        p = pS[k % 2]
                        pt = PT[k % 3]
                        j = kt - 4 * qb
                        c0 = 128 * max(j, 0)
                        ksl = slice(kt * 128, (kt + 1) * 128)
                        I(PE, lambda: nc.tensor.matmul(p[:, c0:512], lhsT=KN[b][:, ksl], rhs=QN[bq][:, c0:512], start=True, stop=False), r=[KN[b], QN[bq]], w=[p])
                        I(PE, lambda: nc.tensor.matmul(p[:, c0:512], lhsT=KR[b][:, ksl], rhs=QR[bq][:, c0:512], start=False, stop=True), r=[KR[b], QR[bq]], w=[p])
                        I(A, lambda: nc.scalar.activation(out=pt[:, c0:512], in_=p[:, c0:512], func=AF.Exp, scale=SC192),
                          r=[p, eps_t], w=[pt])
                        if j >= 0:
                            I(G, lambda: nc.gpsimd.tensor_tensor(out=pt[:, c0:c0 + 128], in0=pt[:, c0:c0 + 128], in1=tri[:], op=ALU.mult), r=[pt, tri], w=[pt])

                    def stB(i, kt):
                        k = base + i
                        pt = PT[k % 3]
                        j = kt - 4 * qb
                        c0 = 128 * max(j, 0)
                        I(PE, lambda: nc.tensor.matmul(pO_[:, c0:512], lhsT=VB[b][:, kt, 0:128], rhs=pt[:, c0:512], start=(kt == 0), stop=(kt == 4 * qb + 3)),
                          r=[pt, VB[b]], w=[pO_])
                        if kt == 0:
                            I(V, lambda: nc.vector.tensor_copy(out=acc_[:], in_=pt[:, 0:512]), r=[pt], w=[acc_])
                        else:
                            I(V, lambda: nc.vector.tensor_tensor(out=acc_[:, c0:512], in0=acc_[:, c0:512], in1=pt[:, c0:512], op=ALU.add), r=[pt, acc_], w=[acc_])
                    attn_pipeline(pairs, stA, stB)
                    for sub in range(4):
                        I(PE, lambda sub=sub: nc.tensor.matmul(pZ5[:, 2 * sub:2 * sub + 2], lhsT=acc_[:, sub * 128:(sub + 1) * 128], rhs=ones_f[:, 0:2],
                                                               start=True, stop=True), r=[acc_, ones_f], w=[pZ5])
                    I(A, lambda: nc.scalar.copy(out=oT5[:], in_=pO_[:, :]), r=[pO_], w=[oT5])
                    for sub in range(4):
                        I(PE, lambda sub=sub: nc.tensor.transpose(out=pTr5[:, sub * 128:(sub + 1) * 128], in_=oT5[:, sub * 128:(sub + 1) * 128], identity=identf[:]),
                          r=[oT5, identf], w=[pTr5])
                    I(V, lambda: nc.vector.reciprocal(out=rz5[:, 0:4], in_=pZ5[:, 0:8:2]), r=[pZ5], w=[rz5])
                    for sub in range(4):
                        I(V, lambda sub=sub: nc.vector.tensor_scalar(out=obf[:, sub, :], in0=pTr5[:, sub * 128:(sub + 1) * 128], scalar1=rz5[:, sub:sub + 1],
                                                                     scalar2=None, op0=ALU.mult), r=[pTr5, rz5], w=[obf])
                    I(G, lambda: nc.gpsimd.tensor_tensor(out=obf[:], in0=obf[:], in1=bgb[bq][:], op=ALU.mult), r=[obf, bgb[bq]], w=[obf])
                    I(G, lambda: nc.gpsimd.tensor_tensor(out=yo[bq][:], in0=obf[:], in1=yal[bq][:], op=ALU.add), r=[obf, yal[bq]], w=[yo[bq]])
                    dma(Yb[rows, h * 128:(h + 1) * 128].rearrange("(s p) c -> p s c", p=128), yo[bq][:], r=[yo[bq]])
            fw.barrier()
        if upto == 5:
            return nc

        with contextlib.ExitStack() as st:
            wob = sb(st, "wob", [128, 8, 1024], BF16)
            with contextlib.ExitStack() as st2:
                stg = [sb(st2, f"stgo{i}", [128, 1024], F32) for i in range(2)]
                load_cast(stg, wob, lambda i: wob[:, i, :], lambda i: w_o[i * 128:(i + 1) * 128, :], 8, [128, 1024])
                fw.barrier()
            modb = TO(st.enter_context(nc.sbuf_tensor("sb_mod6a", [128, 1024], F32)), 2048)
            dma(modb.t[:], MODB[:, 2048:3072], w=[modb])
            yt = [sb(st, f"yt{i}", [128, 1024], BF16) for i in range(2)]
            xl = [sb(st, f"xla{i}", [128, 1024], F32) for i in range(2)]
            YT = sb(st, "YT", [128, 8, 128], BF16)
            tmp = sb(st, "tmpa", [128, 1024], F32)
            x1t = [sb(st, f"x1t{i}", [128, 1024], F32) for i in range(2)]
            pTa = [ps(st, f"pTa{i}", [128, 1024], BF16) for i in range(2)]
            pA = [ps(st, f"pA{i}", [128, 512]) for i in range(4)]

            def ld6(t):
                b = t % 2
                tk = slice(t * 128, (t + 1) * 128)
                dma(yt[b][:], Yb[tk, :], w=[yt[b]])
                dma(xl[b][:], x[tk, :], w=[xl[b]])
            ld6(0)
            for t in range(NT):
                if t + 1 < NT:
                    ld6(t + 1)
                b = t % 2
                tk = slice(t * 128, (t + 1) * 128)
                tr_generic(pTa[t % 2], yt[b], lambda i: yt[b][:, i * 128:(i + 1) * 128], 8, 128, YT, YT[:])
                for half in range(2):
                    p = pA[(2 * t + half) % 4]
                    hs = slice(half * 512, (half + 1) * 512)
                    for c in range(8):
                        I(PE, lambda c=c, p=p, hs=hs: nc.tensor.matmul(p[:, :], lhsT=YT[:, c, :], rhs=wob[:, c, hs], start=(c == 0), stop=(c == 7)),
                          r=[YT, wob], w=[p])
                    gsl = slice(2048 + half * 512, 2048 + (half + 1) * 512)
                    I(V, lambda p=p, hs=hs, gsl=gsl: nc.vector.tensor_tensor(out=tmp[:, hs], in0=p[:, :], in1=modb[:, gsl], op=ALU.mult), r=[p, modb], w=[tmp])
                I(G, lambda: nc.gpsimd.tensor_tensor(out=x1t[b][:], in0=tmp[:], in1=xl[b][:], op=ALU.add), r=[tmp, xl[b]], w=[x1t[b]])
                dma(out[tk, :], x1t[b][:], r=[x1t[b]])
            fw.barrier()
        if upto == 6:
            return nc

        NB6 = S // 256
        with contextlib.ExitStack() as st:
            wupb = sb(st, "wupb", [128, 8, 5632], BF16)
            wdb = sb(st, "wdb", [128, 22, 1024], BF16)
            with contextlib.ExitStack() as st2:
                stg = [sb(st2, f"stgu{i}", [128, 5632], F32) for i in range(2)]
                load_cast(stg, wupb, lambda i: wupb[:, i, :], lambda i: w_up[i * 128:(i + 1) * 128, :], 8, [128, 5632])
                load_cast(stg, wdb, lambda i: wdb[:, i, :], lambda i: w_down[i * 128:(i + 1) * 128, :], 22, [128, 1024])
                fw.barrier()
            modb = TO(st.enter_context(nc.sbuf_tensor("sb_mod6b", [128, 3072], F32)), 3072)
            dma(modb.t[:], MODB[:, 3072:6144], w=[modb])
            wc = sb(st, "wc", [128, 44, 3], F32)
            bc = sb(st, "bc", [128, 44], F32)
            dma(wc[:], wconv_l, w=[wc])
            dma(bc[:], bconv_l, w=[bc])
            xb = [sb(st, f"xb{i}", [128, 2, 1024], F32) for i in range(2)]
            h2f = sb(st, "h2f", [128, 1024], F32)
            tmpd = sb(st, "tmpd", [128, 1024], F32)
            h2b = sb(st, "h2b", [128, 1024], BF16)
            h2T = [sb(st, f"h2T{i}", [128, 8, 256], BF16) for i in range(2)]
            zb = [sb(st, f"zb{i}", [128, 258], F32) for i in range(3)]
            uv = [sb(st, f"uv{i}", [128, 256], F32) for i in range(2)]
            ug = [sb(st, f"ug{i}", [128, 256], F32) for i in range(2)]
            sgm = [sb(st, f"sgm{i}", [128, 256], F32) for i in range(2)]
            actT = sb(st, "actT", [128, 22, 256], BF16)
            halo = sb(st, "halo", [128, 44, 2], F32)
            ss6 = sb(st, "ss6", [128, 2], F32)
            rs6 = sb(st, "rs6", [128, 2], F32)
            pTb = [ps(st, f"pTb{i}", [128, 1024], BF16) for i in range(2)]
            pU = [ps(st, f"pU{i}", [128, 512]) for i in range(3)]
            pD = [ps(st, f"pD{i}", [128, 512]) for i in range(2)]
            I(V, lambda: nc.vector.memset(halo[:], 0.0), w=[halo])
            nu = [0]

            def load6(blk):
                rows = slice(blk * 256, (blk + 1) * 256)
                dma(xb[blk % 2][:], out[rows, :].rearrange("(s p) c -> p s c", p=128), w=[xb[blk % 2]])

            def prep6(blk):
                xb_ = xb[blk % 2]
                hT_ = h2T[blk % 2]
                for s_ in range(2):
                    I(A, lambda s_=s_: nc.scalar.activation(out=h2b[:], in_=xb_[:, s_, :], func=AF.Square, accum_out=ss6[:, s_:s_ + 1]), r=[xb_], w=[h2b, ss6])
                rsqrt_ms(ss6, ss6[:], rs6, rs6[:], 1.0 / 1024)
                for s_ in range(2):
                    I(V, lambda s_=s_: nc.vector.scalar_tensor_tensor(out=h2f[:], in0=xb_[:, s_, :], scalar=rs6[:, s_:s_ + 1], in1=modb[:, A_FFN],
                                                                      op0=ALU.mult, op1=ALU.mult), r=[xb_, rs6, modb], w=[h2f])
                    I(V, lambda: nc.vector.tensor_tensor(out=h2b[:], in0=h2f[:], in1=modb[:, SH_F], op=ALU.add), r=[h2f, modb], w=[h2b])
                    tr_generic(pTb[s_], h2b, lambda i: h2b[:, i * 128:(i + 1) * 128], 8, 128, hT_, hT_[:, :, s_ * 128:(s_ + 1) * 128])

            def gate6(k):
                sg_ = sgm[k % 2]
                I(A, lambda: nc.scalar.activation(out=sg_[:], in_=ug[k % 2][:], func=AF.Silu), r=[ug[k % 2]], w=[sg_])
                I(G, lambda: nc.gpsimd.tensor_tensor(out=actT[:, k, :], in0=sg_[:], in1=uv[k % 2][:], op=ALU.mult), r=[sg_, uv[k % 2]], w=[actT])

            load6(0)
            prep6(0)
            for blk in range(NB6):
                rows = slice(blk * 256, (blk + 1) * 256)
                xb_ = xb[blk % 2]
                hT_ = h2T[blk % 2]
                if blk + 1 < NB6:
                    load6(blk + 1)
                for k in range(22):
                    for which, fc in ((0, k), (1, 22 + k)):
                        i = nu[0]
                        nu[0] += 1
                        p = pU[i % 3]
                        z = zb[i % 3]
                        u = (uv if which == 0 else ug)[k % 2]
                        for c in range(8):
                            I(PE, lambda c=c, p=p, fc=fc: nc.tensor.matmul(p[:, 0:256], lhsT=wupb[:, c, fc * 128:(fc + 1) * 128], rhs=hT_[:, c, :],
                                                                           start=(c == 0), stop=(c == 7)), r=[wupb, hT_], w=[p])
                        I(G, lambda z=z, fc=fc: nc.gpsimd.tensor_copy(out=z[:, 0:2], in_=halo[:, fc, :]), r=[halo], w=[z])
                        I(A, lambda z=z, p=p: nc.scalar.copy(out=z[:, 2:258], in_=p[:, 0:256]), r=[p], w=[z])
                        I(G, lambda z=z, fc=fc: nc.gpsimd.tensor_copy(out=halo[:, fc, :], in_=z[:, 256:258]), r=[z], w=[halo])
                        I(A, lambda u=u, p=p, fc=fc: nc.scalar.activation(out=u[:], in_=p[:, 0:256], func=AF.Identity, scale=wc[:, fc, 2:3], bias=bc[:, fc:fc + 1]),
                          r=[p, wc, bc], w=[u])
                        I(V, lambda u=u, z=z, fc=fc: nc.vector.scalar_tensor_tensor(out=u[:], in0=z[:, 1:257], scalar=wc[:, fc, 1:2], in1=u[:],
                                                                                    op0=ALU.mult, op1=ALU.add), r=[z, wc, u], w=[u])
                        I(V, lambda u=u, z=z, fc=fc: nc.vector.scalar_tensor_tensor(out=u[:], in0=z[:, 0:256], scalar=wc[:, fc, 0:1], in1=u[:],
                                                                                    op0=ALU.mult, op1=ALU.add), r=[z, wc, u], w=[u])
                    if k >= 1:
                        gate6(k - 1)
                gate6(21)
                if blk + 1 < NB6:
                    prep6(blk + 1)
                for s_ in range(2):
                    for half in range(2):
                        p = pD[half]
                        hs = slice(half * 512, (half + 1) * 512)
                        for k in range(22):
                            I(PE, lambda k=k, p=p, hs=hs, s_=s_: nc.tensor.matmul(p[:, :], lhsT=actT[:, k, s_ * 128:(s_ + 1) * 128], rhs=wdb[:, k, hs],
                                                                                  start=(k == 0), stop=(k == 21)), r=[actT, wdb], w=[p])
                        gsl = slice(5120 + half * 512, 5120 + (half + 1) * 512)
                        I(V, lambda p=p, hs=hs, gsl=gsl: nc.vector.tensor_tensor(out=tmpd[:, hs], in0=p[:, :], in1=modb[:, gsl], op=ALU.mult), r=[p, modb], w=[tmpd])
                    I(G, lambda s_=s_: nc.gpsimd.tensor_tensor(out=xb_[:, s_, :], in0=tmpd[:], in1=xb_[:, s_, :], op=ALU.add), r=[tmpd, xb_], w=[xb_])
                dma(out[rows, :].rearrange("(s p) c -> p s c", p=128), xb_[:], r=[xb_])
            fw.barrier()
    return nc


_PARAM_NAMES = ["w_ada", "b_ada", "attn_norm", "ffn_norm", "w_in", "nsa_q_norm", "nsa_kc_norm", "nsa_ks_norm", "nsa_kw_norm",
                "cmp_k_w1", "cmp_k_w2", "cmp_v_w1", "cmp_v_w2", "mla_cq_norm", "mla_ckv_norm", "w_uq", "w_ukv",
                "mla_q_norm", "mla_k_norm", "w_o", "w_up", "w_down"]
_CONSTS = {}


def make_in_map(inp, b, S):
    if S not in _CONSTS:
        _CONSTS[S] = host_consts(S)
    m = {}
    m["x"] = np.ascontiguousarray(np.asarray(inp["x"])[b, :S], dtype=np.float32)
    m["ccol"] = np.ascontiguousarray(np.asarray(inp["c"])[b].reshape(8, 128).T, dtype=np.float32)
    for k in _PARAM_NAMES:
        m[k] = np.ascontiguousarray(np.asarray(inp[k])[0], dtype=np.float32)
    m["pe_kT"] = np.ascontiguousarray(np.asarray(inp["cmp_k_pe"])[0].T, dtype=np.float32)
    m["pe_vT"] = np.ascontiguousarray(np.asarray(inp["cmp_v_pe"])[0].T, dtype=np.float32)
    m["wconv_l"] = np.ascontiguousarray(np.asarray(inp["w_conv"])[0].reshape(3, 44, 128).transpose(2, 1, 0), dtype=np.float32)
    m["bconv_l"] = np.ascontiguousarray(np.asarray(inp["b_conv"])[0].reshape(44, 128).T, dtype=np.float32)
    m.update(_CONSTS[S])
    return m


_NC = {}


def kernel(**inputs):
    S = 8192
    if S not in _NC:
        _NC[S] = build(S)
    nc = _NC[S]
    in_maps = [make_in_map(inputs, b, S) for b in range(8)]
    res = run_bass_kernel_spmd(nc, in_maps, core_ids=list(range(8)))
    return np.stack([np.asarray(r["out"], dtype=np.float32) for r in res.results], axis=0)
```

```python
import contextlib
import numpy as np
import ml_dtypes
import concourse.bass as bass
import concourse.mybir as mybir
from concourse.bass_utils import run_bass_kernel_spmd

F32 = mybir.dt.float32
BF16 = mybir.dt.bfloat16
AF = mybir.ActivationFunctionType
ALU = mybir.AluOpType
AX = mybir.AxisListType

EPS = 1e-6
NEGB = -30000.0
EXPB = -4.0


class Buf:
    __slots__ = ("w", "r")

    def __init__(self):
        self.w = {}
        self.r = {}


class T:
    def __init__(self, t):
        self.t = t
        self.b = Buf()

    def __getitem__(self, k):
        return self.t[k]


class TO(T):
    def __init__(self, t, off):
        super().__init__(t)
        self.off = off

    def __getitem__(self, k):
        p, c = k
        return self.t[p, slice(c.start - self.off, c.stop - self.off)]


def _b(x):
    return x.b if isinstance(x, T) else x


class FW:
    ROT = 12000
    NQ = 24

    def __init__(self, nc, es):
        self.nc, self.es = nc, es
        self.E = {"pe": nc.tensor, "act": nc.scalar, "dve": nc.vector, "pool": nc.gpsimd, "sp": nc.sync}
        self.sem, self.cnt = {}, {}
        self.nsem = 0
        for e in self.E:
            self._newsem(e)
        self.waited = {e: {} for e in self.E}
        self.dq = {}
        self.n = 0
        self.rec = None

    def _newsem(self, e):
        s = self.es.enter_context(self.nc.semaphore(f"s{e}{self.nsem}"))
        self.nsem += 1
        self.sem[e] = s
        self.cnt[e] = 0

    def _wait(self, e, deps):
        for s, (v, pe) in deps.items():
            if self.waited[e].get(s, 0) >= v:
                continue
            self.E[e].wait_ge(s, v)
            self.waited[e][s] = v
            self.n += 1

    def _deps(self, e, r, w):
        deps = {}

        def add(d, raw):
            for s, (v, pe) in d.items():
                if pe == e and e != "dma":
                    if e == "pe" or not raw:
                        continue
                if deps.get(s, (0,))[0] < v:
                    deps[s] = (v, pe)
        for b in r:
            add(_b(b).w, True)
        for b in w:
            add(_b(b).w, False)
            add(_b(b).r, False)
        return deps

    def I(self, e, fn, r=(), w=()):
        if self.rec is not None:
            r, w = list(r), list(w)
            self.rec.append(lambda: self._I(e, fn, r, w))
            return None
        return self._I(e, fn, r, w)

    def _I(self, e, fn, r=(), w=()):
        self._wait(e, self._deps(e, r, w))
        if self.cnt[e] >= self.ROT:
            self._newsem(e)
        inst = fn()
        s = self.sem[e]
        inst.then_inc(s, 1)
        self.cnt[e] += 1
        self.n += 1
        tok = (self.cnt[e], e)
        for b in w:
            b = _b(b)
            b.w = {s: tok}
            b.r = {}
        for b in r:
            _b(b).r[s] = tok
        return inst

    def dma(self, out, in_, r=(), w=(), q="sp"):
        if self.rec is not None:
            r, w = list(r), list(w)
            self.rec.append(lambda: self._dma(out, in_, r, w, q))
            return None
        return self._dma(out, in_, r, w, q)

    def record(self, fn):
        self.rec = []
        fn()
        lst, self.rec = self.rec, None
        return lst

    @staticmethod
    def interleave_skewed(streams):
        n = len(streams)
        L = max(len(st_) for st_ in streams)
        pos = [0] * n
        start = [0] + [0] * (n - 1)
        i = 0
        while any(pos[k] < len(streams[k]) for k in range(n)):
            for k in range(n):
                if i >= start[k] and pos[k] < len(streams[k]):
                    streams[k][pos[k]]()
                    pos[k] += 1
            i += 1

    @staticmethod
    def interleave(lists):
        for i in range(max(len(l) for l in lists)):
            for l in lists:
                if i < len(l):
                    l[i]()

    def _dma(self, out, in_, r=(), w=(), q="sp"):
        self._wait(q, self._deps("dma", r, w))
        d = self.dq.setdefault(q, {"sems": [], "i": 0})
        if len(d["sems"]) < self.NQ:
            s = self.es.enter_context(self.nc.semaphore(f"d{q}{len(d['sems'])}"))
            ent = [s, 0]
            d["sems"].append(ent)
        else:
            ent = d["sems"][d["i"] % self.NQ]
            d["i"] += 1
            self._wait(q, {ent[0]: (16 * ent[1], "dma")})
        inst = self.E[q].dma_start(out=out, in_=in_)
        inst.then_inc(ent[0], 16)
        ent[1] += 1
        self.n += 1
        tok = (16 * ent[1], "dma")
        for b in w:
            b = _b(b)
            b.w = {ent[0]: tok}
            b.r = {}
        for b in r:
            _b(b).r[ent[0]] = tok

    def barrier(self):
        toks = {}
        for e in self.E:
            if self.cnt[e] > 0:
                toks[self.sem[e]] = (self.cnt[e], "x")
        for q, d in self.dq.items():
            for s, c in d["sems"]:
                if c:
                    toks[s] = (16 * c, "dma")
        for e in self.E:
            self._wait(e, toks)


O_NQ, O_NKC, O_NVC, O_NKS, O_NVS, O_NKW, O_NVW, O_NG, O_CQ, O_CKV, O_KR, O_BG = (
    0, 1024, 1280, 1536, 1792, 2048, 2304, 2560, 2584, 2968, 3224, 3288)
IN_W = 5336


def host_consts(S):
    NT = S // 128
    NSEL = S // 64
    n_cmp = S // 16 - 1
    pos = np.arange(S, dtype=np.float32)
    inv128 = (10000.0 ** (-np.arange(64, dtype=np.float32) * 2.0 / 128)).astype(np.float32)
    inv64 = (10000.0 ** (-np.arange(32, dtype=np.float32) * 2.0 / 64)).astype(np.float32)
    a128 = pos[:, None] * inv128[None, :]
    a64 = pos[:, None] * inv64[None, :]
    c = {}
    c["cs128"] = np.concatenate([np.cos(a128), np.sin(a128)], axis=1).astype(np.float32)
    c["cs64"] = np.concatenate([np.cos(a64), np.sin(a64)], axis=1).astype(np.float32)
    p = np.arange(128)[:, None]
    f = np.arange(128)[None, :]
    c["ident"] = (p == f).astype(ml_dtypes.bfloat16)
    c["identf"] = (p == f).astype(np.float32)
    c["tri"] = (p <= f).astype(ml_dtypes.bfloat16)
    c["atri"] = (p > f).astype(ml_dtypes.bfloat16)
    W0 = 512 + 8 * (NT - 1)
    cc = np.arange(W0)[None, :]
    m = cc - 8 * (NT - 1)
    c["m0ext"] = ((16 * m + 31) <= p).astype(np.float32)
    CO = 2 * (NT - 1)
    W1 = NSEL + CO
    cc = np.arange(W1)[None, :]
    d = cc - CO
    hi = (p >= 64).astype(np.int64)
    c["aext"] = (d <= hi - 2).astype(np.float32)
    forced = (d == hi) | (d == hi - 1)
    c["fext"] = np.where(forced, 1e9, np.where(d > hi, -1.0, 0.0)).astype(np.float32)
    KJ = min(128, NSEL)
    E = np.zeros((KJ, NT, 128), dtype=np.float32)
    for kt in range(NT):
        E[2 * kt, kt, :64] = 1.0
        E[2 * kt + 1, kt, 64:] = 1.0
    c["esel"] = E.astype(ml_dtypes.bfloat16)
    return c


def build(S, dbg=False, upto=99):
    NT = S // 128
    NSEL = S // 64
    KJ = min(128, NSEL)
    n_cmp = S // 16 - 1
    NCT = (n_cmp + 127) // 128
    NCP = NCT * 128
    CO = 2 * (NT - 1)
    nc = bass.Bass("TRN2", target_bir_lowering=False)
    okind = "ExternalOutput"

    def din(name, shape, dt=F32):
        return nc.dram_tensor(name, list(shape), dt, kind="ExternalInput").ap()

    def dscr(name, shape, dt):
        return nc.dram_tensor(name, list(shape), dt, kind=okind).ap()

    x = din("x", [S, 1024])
    ccol = din("ccol", [128, 8])
    w_ada = din("w_ada", [1024, 6144])
    b_ada = din("b_ada", [6144])
    attn_norm = din("attn_norm", [1024])
    ffn_norm = din("ffn_norm", [1024])
    w_in = din("w_in", [1024, IN_W])
    g_q = din("nsa_q_norm", [128])
    g_kc = din("nsa_kc_norm", [128])
    g_ks = din("nsa_ks_norm", [128])
    g_kw = din("nsa_kw_norm", [128])
    pe_kT = din("pe_kT", [128, 32])
    k_w1 = din("cmp_k_w1", [4096, 256])
    k_w2 = din("cmp_k_w2", [256, 128])
    pe_vT = din("pe_vT", [128, 32])
    v_w1 = din("cmp_v_w1", [4096, 256])
    v_w2 = din("cmp_v_w2", [256, 128])
    g_cq = din("mla_cq_norm", [384])
    g_ckv = din("mla_ckv_norm", [256])
    w_uq = din("w_uq", [384, 1536])
    w_ukv = din("w_ukv", [256, 2048])
    g_mq = din("mla_q_norm", [192])
    g_mk = din("mla_k_norm", [192])
    w_o = din("w_o", [1024, 1024])
    w_up = din("w_up", [1024, 5632])
    wconv_l = din("wconv_l", [128, 44, 3])
    bconv_l = din("bconv_l", [128, 44])
    w_down = din("w_down", [2816, 1024])
    cs128 = din("cs128", [S, 128])
    cs64 = din("cs64", [S, 64])
    c_ident = din("ident", [128, 128], BF16)
    c_identf = din("identf", [128, 128])
    c_tri = din("tri", [128, 128], BF16)
    c_atri = din("atri", [128, 128], BF16)
    c_m0 = din("m0ext", [128, 512 + 8 * (NT - 1)])
    c_aext = din("aext", [128, NSEL + CO])
    c_fext = din("fext", [128, NSEL + CO])
    c_esel = din("esel", [KJ, NT, 128], BF16)
    out = nc.dram_tensor("out", [S, 1024], F32, kind="ExternalOutput").ap()

    QaT = dscr("QaT", [8, 128, S], BF16)
    KcT = dscr("KcT", [2, 128, S], BF16)
    VcT = dscr("VcT", [2, 128, S], BF16)
    KsT = dscr("KsT", [2, 128, S], BF16)
    KwT = dscr("KwT", [2, 128, S], BF16)
    Vs = dscr("Vs", [S, 2, 128], BF16)
    Vw = dscr("Vw", [S, 2, 128], BF16)
    Gn = dscr("Gn", [S, 24], F32)
    BG = dscr("BG", [S, 2048], BF16)
    QbN = dscr("QbN", [8, 128, S], BF16)
    QbR = dscr("QbR", [8, 64, S], BF16)
    KbN = dscr("KbN", [8, 128, S], BF16)
    KbR = dscr("KbR", [8, 64, S], BF16)
    Vb = dscr("Vb", [S, 8, 128], BF16)
    Oc = dscr("Oc", [S, 1024], F32)
    BT = dscr("BT", [2, KJ, S], BF16)
    Ya = dscr("Ya", [S, 1024], F32)
    Yb = dscr("Yb", [S, 1024], BF16)
    MODB = dscr("MODB", [128, 6144], F32)

    with contextlib.ExitStack() as es:
        fw = FW(nc, es)
        global LASTFW
        LASTFW = fw
        I = fw.I
        dma = fw.dma
        V, G, A, PE = "dve", "pool", "act", "pe"

        def sb(st, name, shape, dt):
            return T(st.enter_context(nc.sbuf_tensor("sb_" + name, list(shape), dt)))

        def ps(st, name, shape, dt=F32):
            return T(st.enter_context(nc.psum_tensor("ps_" + name, list(shape), dt)))

        ident = sb(es, "ident", [128, 128], BF16)
        tri = sb(es, "tri", [128, 128], BF16)
        atri = sb(es, "atri", [128, 128], BF16)
        dma(ident[:], c_ident, w=[ident])
        dma(tri[:], c_tri, w=[tri])
        identf = sb(es, "identf", [128, 128], F32)
        dma(identf[:], c_identf, w=[identf])
        ones_f = sb(es, "ones_f", [128, 2], F32)
        I(V, lambda: nc.vector.memset(ones_f[:], 1.0), w=[ones_f])
        dma(atri[:], c_atri, w=[atri])
        SH_A, A_ATT, G_A, SH_F, A_FFN, G_F = [slice(i * 1024, (i + 1) * 1024) for i in range(6)]

        def rsqrt_ms(ssT, ss_ap, rsT, rs_ap, inv_n, rows=128):
            I(A, lambda: nc.scalar.activation(out=rs_ap, in_=ss_ap, func=AF.Sqrt, bias=eps_t[0:rows, 0:1], scale=inv_n),
              r=[ssT, eps_t], w=[rsT])
            I(V, lambda: nc.vector.reciprocal(out=rs_ap, in_=rs_ap), r=[rsT], w=[rsT])

        eps_t = sb(es, "eps_t", [128, 2], F32)
        I(V, lambda: nc.vector.memset(eps_t[:, 0:1], EPS), w=[eps_t])
        I(V, lambda: nc.vector.memset(eps_t[:, 1:2], EXPB), w=[eps_t])

        with contextlib.ExitStack() as st:
            modb = sb(st, "modb", [128, 6144], F32)
            cs_t = sb(st, "cs_t", [128, 8], F32)
            sc_t = sb(st, "sc_t", [128, 8], F32)
            scb = sb(st, "scb", [128, 8, 128], F32)
            wst = [sb(st, f"wst{i}", [128, 3072], F32) for i in range(2)]
            gtmp = sb(st, "gtmp", [128, 1024], F32)
            pm = [ps(st, f"pm{i}", [128, 512]) for i in range(6)]
            dma(cs_t[:], ccol, w=[cs_t])
            dma(modb[:], b_ada.partition_broadcast(128), w=[modb])
            I(A, lambda: nc.scalar.activation(out=sc_t[:], in_=cs_t[:], func=AF.Silu), r=[cs_t], w=[sc_t])
            I(V, lambda: nc.vector.tensor_copy(out=scb[:], in_=sc_t[:].unsqueeze(2).broadcast_to([128, 8, 128])),
              r=[sc_t], w=[scb])
            for half in range(2):
                for kc in range(8):
                    wb = wst[kc % 2]
                    dma(wb[:], w_ada[kc * 128:(kc + 1) * 128, half * 3072:(half + 1) * 3072], w=[wb])
                    for j in range(6):
                        I(PE, lambda j=j, wb=wb, kc=kc: nc.tensor.matmul(
                            pm[j][:], lhsT=scb[:, kc, :], rhs=wb[:, j * 512:(j + 1) * 512],
                            start=(kc == 0), stop=(kc == 7)), r=[scb, wb], w=[pm[j]])
                for j in range(6):
                    cs = slice(half * 3072 + j * 512, half * 3072 + (j + 1) * 512)
                    I(V, lambda j=j, cs=cs: nc.vector.tensor_tensor(out=modb[:, cs], in0=pm[j][:], in1=modb[:, cs],
                                                                    op=ALU.add), r=[pm[j], modb], w=[modb])
            for gsrc, sl in ((attn_norm, A_ATT), (ffn_norm, A_FFN)):
                dma(gtmp[:], gsrc.partition_broadcast(128), w=[gtmp])
                I(V, lambda sl=sl: nc.vector.scalar_tensor_tensor(out=modb[:, sl], in0=modb[:, sl], scalar=1.0,
                                                                  in1=gtmp[:], op0=ALU.add, op1=ALU.mult),
                  r=[modb, gtmp], w=[modb])
            dma(MODB, modb[:], r=[modb])
            fw.barrier()

        def load_cast(st_scratch, dst, dst_ap_fn, src_ap_fn, nchunks, shape, engs=(V, G, A)):
            for i in range(nchunks):
                stg = st_scratch[i % len(st_scratch)]
                dma(stg[tuple(slice(0, s_) for s_ in shape)] if False else stg_view(stg, shape), src_ap_fn(i), w=[stg])
                e = engs[i % len(engs)]
                if e == A:
                    I(A, lambda i=i, stg=stg: nc.scalar.copy(out=dst_ap_fn(i), in_=stg_view(stg, shape)), r=[stg], w=[dst])
                elif e == V:
                    I(V, lambda i=i, stg=stg: nc.vector.tensor_copy(out=dst_ap_fn(i), in_=stg_view(stg, shape)), r=[stg], w=[dst])
                else:
                    I(G, lambda i=i, stg=stg: nc.gpsimd.tensor_copy(out=dst_ap_fn(i), in_=stg_view(stg, shape)), r=[stg], w=[dst])

        def stg_view(stg, shape):
            n = 1
            for s_ in shape[1:]:
                n *= s_
            v = stg[0:shape[0], 0:n]
            if len(shape) == 3:
                v = v.rearrange("p (a b) -> p a b", a=shape[1])
            return v

        def bcast_load(st, name, src, n):
            t = sb(st, name, [128, n], F32)
            dma(t[:], src.partition_broadcast(128), w=[t])
            return t

        with contextlib.ExitStack() as st:
            winb = sb(st, "winb", [128, 8, IN_W], BF16)
            wuqb = sb(st, "wuqb", [128, 3, 1536], BF16)
            wukvb = sb(st, "wukvb", [128, 2, 2048], BF16)
            with contextlib.ExitStack() as st2:
                stg = [sb(st2, f"stg{i}", [128, IN_W], F32) for i in range(2)]
                load_cast(stg, winb, lambda i: winb[:, i, :], lambda i: w_in[i * 128:(i + 1) * 128, :], 8, [128, IN_W])
                load_cast(stg, wuqb, lambda i: wuqb[:, i, :], lambda i: w_uq[i * 128:(i + 1) * 128, :], 3, [128, 1536])
                load_cast(stg, wukvb, lambda i: wukvb[:, i, :], lambda i: w_ukv[i * 128:(i + 1) * 128, :], 2, [128, 2048])
                fw.barrier()
            modb = TO(st.enter_context(nc.sbuf_tensor("sb_mod1", [128, 2048], F32)), 0)
            dma(modb.t[:], MODB[:, 0:2048], w=[modb])
            gq_t = bcast_load(st, "gq_t", g_q, 128)
            gks_t = bcast_load(st, "gks_t", g_ks, 128)
            gkw_t = bcast_load(st, "gkw_t", g_kw, 128)
            gcq_t = bcast_load(st, "gcq_t", g_cq, 384)
            gckv_t = bcast_load(st, "gckv_t", g_ckv, 256)
            gmq_t = bcast_load(st, "gmq_t", g_mq, 192)
            gmk_t = bcast_load(st, "gmk_t", g_mk, 192)

            def mk1(u):
                B = {}
                B["cst2"] = [sb(st, f"cst_{u}{i}", [128, 192], F32) for i in range(2)]
                for nm, shp, dt in (("xt", [128, 1024], F32), ("ss1", [128, 1], F32), ("rs1", [128, 1], F32),
                                    ("nb", [128, 8, 192], BF16), ("hT", [128, 8, 128], BF16), ("f_a", [128, 1536], F32),
                                    ("f_b", [128, 1536], F32), ("f_d", [128, 1024], F32), ("ssn", [128, 8], F32), ("rsn", [128, 8], F32),
                                    ("vbt", [128, 8, 128], BF16), ("gnt", [128, 24], F32), ("cqT", [128, 3, 128], BF16),
                                    ("ckvT", [128, 2, 128], BF16), ("krf", [128, 64], F32)):
                    B[nm] = sb(st, f"{nm}_{u}", shp, dt)
                B["bgt"] = [sb(st, f"bgt_{u}{i}", [128, 512], BF16) for i in range(2)]
                B["sgA"] = [sb(st, f"sgA_{u}{i}", [128, 8, 128], BF16) for i in range(2)]
                B["sgR"] = [sb(st, f"sgR_{u}{i}", [64, 8, 128], BF16) for i in range(2)]
                B["pp"] = [ps(st, f"pp_{u}{i}", [128, 512]) for i in range(2)]
                B["pT"] = ps(st, f"pT_{u}", [128, 1024], BF16)
                B["npp"] = 0
                B["nA"] = 0
                B["nR"] = 0
                return B
            sets1 = [mk1(0), mk1(1)]

            def load_tile(t):
                B = sets1[t % 2]
                dma(B["xt"][:], x[t * 128:(t + 1) * 128, :], w=[B["xt"]])
                c_ = B["cst2"][(t // 2) % 2]
                dma(c_[:, 0:128], cs128[t * 128:(t + 1) * 128, :], w=[c_])
                dma(c_[:, 128:192], cs64[t * 128:(t + 1) * 128, :], w=[c_])

            def p1_tile(t):
                B = sets1[t % 2]
                xtt, ss1, rs1, nb, hT, f_a, f_b, f_d = (B[k] for k in ("xt", "ss1", "rs1", "nb", "hT", "f_a", "f_b", "f_d"))
                cs_ = B["cst2"][(t // 2) % 2]
                ssn, rsn, vbt, gnt, cqT, ckvT, krf, pT_ = (B[k] for k in ("ssn", "rsn", "vbt", "gnt", "cqT", "ckvT", "krf", "pT"))
                tok = slice(t * 128, (t + 1) * 128)
                nbf = nb[:].rearrange("p h d -> p (h d)")

                def proj(lhsT_t, lhs_fn, nk, w_t, c0, c1):
                    p = B["pp"][B["npp"] % 2]
                    B["npp"] += 1
                    for kc in range(nk):
                        I(PE, lambda kc=kc: nc.tensor.matmul(p[:, 0:c1 - c0], lhsT=lhs_fn(kc), rhs=w_t[:, kc, c0:c1],
                                                             start=(kc == 0), stop=(kc == nk - 1)), r=[lhsT_t, w_t], w=[p])
                    return p

                def evac(p, c, dstT, dst_ap, eng=A):
                    if eng == A:
                        I(A, lambda: nc.scalar.copy(out=dst_ap, in_=p[:, 0:c]), r=[p], w=[dstT])
                    else:
                        I(V, lambda: nc.vector.tensor_copy(out=dst_ap, in_=p[:, 0:c]), r=[p], w=[dstT])

                def norm_rope(src, src_ap, nh, hd, gain_t, do_norm, rope_off, rope_half, cs_off, dst_ap, dstT):
                    if do_norm:
                        sq = f_b[:, 0:nh * hd].rearrange("p (h d) -> p h d", h=nh)
                        I(A, lambda: nc.scalar.activation(out=sq, in_=src_ap, func=AF.Square), r=[src], w=[f_b])
                        I(V, lambda: nc.vector.tensor_reduce(out=ssn[:, 0:nh], in_=sq, axis=AX.X, op=ALU.add), r=[f_b], w=[ssn])
                        rsqrt_ms(ssn, ssn[:, 0:nh], rsn, rsn[:, 0:nh], 1.0 / hd)
                        I(V, lambda: nc.vector.tensor_tensor(out=src_ap, in0=src_ap, in1=rsn[:, 0:nh].unsqueeze(2).broadcast_to([128, nh, hd]),
                                                             op=ALU.mult), r=[src, rsn], w=[src])
                        I(G, lambda: nc.gpsimd.tensor_tensor(out=src_ap, in0=src_ap, in1=gain_t[:, 0:hd].unsqueeze(1).broadcast_to([128, nh, hd]),
                                                             op=ALU.mult), r=[src, gain_t], w=[src])
                    if rope_half == 0:
                        I(V, lambda: nc.vector.tensor_copy(out=dst_ap, in_=src_ap), r=[src], w=[dstT])
                        return
                    if rope_off > 0:
                        I(G, lambda: nc.gpsimd.tensor_copy(out=dst_ap[:, :, 0:rope_off], in_=src_ap[:, :, 0:rope_off]), r=[src], w=[dstT])
                    hh_ = rope_half
                    x1 = src_ap[:, :, rope_off:rope_off + hh_]
                    x2 = src_ap[:, :, rope_off + hh_:rope_off + 2 * hh_]
                    cb = cs_[:, cs_off:cs_off + hh_].unsqueeze(1).broadcast_to([128, nh, hh_])
                    sbb = cs_[:, cs_off + hh_:cs_off + 2 * hh_].unsqueeze(1).broadcast_to([128, nh, hh_])
                    t1 = f_d[:, 0:nh * hh_].rearrange("p (h d) -> p h d", h=nh)
                    t2 = f_d[:, 512:512 + nh * hh_].rearrange("p (h d) -> p h d", h=nh)
                    t3 = f_b[:, 0:nh * hh_].rearrange("p (h d) -> p h d", h=nh)
                    t4 = f_b[:, 512:512 + nh * hh_].rearrange("p (h d) -> p h d", h=nh)
                    I(V, lambda: nc.vector.tensor_tensor(out=t1, in0=x1, in1=cb, op=ALU.mult), r=[src, cs_], w=[f_d])
                    I(V, lambda: nc.vector.tensor_tensor(out=t2, in0=x2, in1=sbb, op=ALU.mult), r=[src, cs_], w=[f_d])
                    I(G, lambda: nc.gpsimd.tensor_tensor(out=t3, in0=x1, in1=sbb, op=ALU.mult), r=[src, cs_], w=[f_b])
                    I(G, lambda: nc.gpsimd.tensor_tensor(out=t4, in0=x2, in1=cb, op=ALU.mult), r=[src, cs_], w=[f_b])
                    I(V, lambda: nc.vector.tensor_tensor(out=dst_ap[:, :, rope_off:rope_off + hh_], in0=t1, in1=t2, op=ALU.subtract),
                      r=[f_d], w=[dstT])
                    I(G, lambda: nc.gpsimd.tensor_tensor(out=dst_ap[:, :, rope_off + hh_:rope_off + 2 * hh_], in0=t3, in1=t4, op=ALU.add),
                      r=[f_b], w=[dstT])

                def transposes(src_t, src_ap_fn, n, rows, dstT, dst_ap):
                    for i in range(n):
                        I(PE, lambda i=i: nc.tensor.transpose(out=pT_[0:rows, i * 128:(i + 1) * 128], in_=src_ap_fn(i), identity=ident[:]),
                          r=[src_t, ident], w=[pT_])
                    I(A, lambda: nc.scalar.copy(out=dst_ap, in_=pT_[0:rows, 0:n * 128].rearrange("p (a b) -> p a b", a=n)),
                      r=[pT_], w=[dstT])

                def slotA():
                    sg = B["sgA"][B["nA"] % 2]
                    B["nA"] += 1
                    return sg

                def slotR():
                    sg = B["sgR"][B["nR"] % 2]
                    B["nR"] += 1
                    return sg

                def outT(dst, sg, b0, nblk):
                    dma(dst[:, :, tok].rearrange("h d t -> d h t"), sg[:, b0:b0 + nblk, :], r=[sg])

                if t + 2 < NT:
                    pass
                I(A, lambda: nc.scalar.activation(out=nbf[:, 0:1024], in_=xtt[:], func=AF.Square, accum_out=ss1[:, 0:1]), r=[xtt], w=[nb, ss1])
                rsqrt_ms(ss1, ss1[:, 0:1], rs1, rs1[:, 0:1], 1.0 / 1024)
                I(V, lambda: nc.vector.scalar_tensor_tensor(out=f_b[:, 0:1024], in0=xtt[:], scalar=rs1[:, 0:1], in1=modb[:, A_ATT],
                                                            op0=ALU.mult, op1=ALU.mult), r=[xtt, rs1, modb], w=[f_b])
                I(G, lambda: nc.gpsimd.tensor_tensor(out=nbf[:, 0:1024], in0=f_b[:, 0:1024], in1=modb[:, SH_A], op=ALU.add), r=[f_b, modb], w=[nb])
                transposes(nb, lambda i: nbf[:, i * 128:(i + 1) * 128], 8, 128, hT, hT[:])
                if t + 2 < NT:
                    load_tile(t + 2)
                lh = lambda kc: hT[:, kc, :]
                for gq in range(2):
                    p = proj(hT, lh, 8, winb, O_NQ + gq * 512, O_NQ + (gq + 1) * 512)
                    evac(p, 512, f_a, f_a[:, gq * 512:(gq + 1) * 512], eng=(A if gq else V))
                norm_rope(f_a, f_a[:, 0:1024].rearrange("p (h d) -> p h d", h=8), 8, 128, gq_t, True, 0, 64, 0, nb[:, :, 0:128], nb)
                sg = slotA()
                transposes(nb, lambda i: nb[:, i, 0:128], 8, 128, sg, sg[:, 0:8, :])
                outT(QaT, sg, 0, 8)
                p = proj(hT, lh, 8, winb, O_NKC, O_NKC + 512)
                evac(p, 512, f_a, f_a[:, 0:512])
                norm_rope(f_a, f_a[:, 0:256].rearrange("p (h d) -> p h d", h=2), 2, 128, None, False, 0, 64, 0, nb[:, 0:2, 0:128], nb)
                I(V, lambda: nc.vector.tensor_copy(out=nb[:, 2:4, 0:128], in_=f_a[:, 256:512].rearrange("p (h d) -> p h d", h=2)),
                  r=[f_a], w=[nb])
                sg = slotA()
                transposes(nb, lambda i: nb[:, i, 0:128], 4, 128, sg, sg[:, 0:4, :])
                outT(KcT, sg, 0, 2)
                outT(VcT, sg, 2, 2)
                sg = slotA()
                for wi, (off, gt_) in enumerate(((O_NKS, gks_t), (O_NKW, gkw_t))):
                    p = proj(hT, lh, 8, winb, off, off + 512)
                    evac(p, 512, f_a, f_a[:, 0:512])
                    norm_rope(f_a, f_a[:, 0:256].rearrange("p (h d) -> p h d", h=2), 2, 128, gt_, True, 0, 64, 0, nb[:, 0:2, 0:128], nb)
                    I(V, lambda wi=wi: nc.vector.tensor_copy(out=vbt[:, 2 * wi:2 * wi + 2, :], in_=f_a[:, 256:512].rearrange("p (h d) -> p h d", h=2)),
                      r=[f_a], w=[vbt])
                    transposes(nb, lambda i: nb[:, i, 0:128], 2, 128, sg, sg[:, 2 * wi:2 * wi + 2, :])
                outT(KsT, sg, 0, 2)
                outT(KwT, sg, 2, 2)
                dma(Vs[tok, :, :], vbt[:, 0:2, :], r=[vbt])
                dma(Vw[tok, :, :], vbt[:, 2:4, :], r=[vbt])
                p = proj(hT, lh, 8, winb, O_NG, O_CKV)
                I(A, lambda: nc.scalar.activation(out=gnt[:], in_=p[:, 0:24], func=AF.Sigmoid), r=[p], w=[gnt])
                evac(p, 408, f_a, f_a[:, 0:408], eng=V)
                dma(Gn[tok, :], gnt[:], r=[gnt])
                norm_rope(f_a, f_a[:, 24:408].rearrange("p (h d) -> p h d", h=1), 1, 384, gcq_t, True, 0, 0, 0,
                          nbf[:, 0:384].rearrange("p (h d) -> p h d", h=1), nb)
                transposes(nb, lambda i: nbf[:, i * 128:(i + 1) * 128], 3, 128, cqT, cqT[:])
                p = proj(hT, lh, 8, winb, O_CKV, O_BG)
                evac(p, 320, f_a, f_a[:, 0:320], eng=V)
                I(G, lambda: nc.gpsimd.tensor_copy(out=krf[:], in_=f_a[:, 256:320]), r=[f_a], w=[krf])
                norm_rope(f_a, f_a[:, 0:256].rearrange("p (h d) -> p h d", h=1), 1, 256, gckv_t, True, 0, 0, 0,
                          nbf[:, 512:768].rearrange("p (h d) -> p h d", h=1), nb)
                transposes(nb, lambda i: nbf[:, 512 + i * 128:512 + (i + 1) * 128], 2, 128, ckvT, ckvT[:])
                for j in range(4):
                    p = proj(hT, lh, 8, winb, O_BG + j * 512, O_BG + (j + 1) * 512)
                    bg_ = B["bgt"][j % 2]
                    I(A, lambda bg_=bg_, p=p: nc.scalar.activation(out=bg_[:], in_=p[:, 0:512], func=AF.Sigmoid), r=[p], w=[bg_])
                    dma(BG[tok, j * 512:(j + 1) * 512], bg_[:], r=[bg_])
                for j in range(3):
                    p = proj(cqT, lambda kc: cqT[:, kc, :], 3, wuqb, j * 512, (j + 1) * 512)
                    evac(p, 512, f_a, f_a[:, j * 512:(j + 1) * 512], eng=(A if j % 2 else V))
                norm_rope(f_a, f_a[:, 0:1536].rearrange("p (h d) -> p h d", h=8), 8, 192, gmq_t, True, 128, 32, 128, nb[:, :, :], nb)
                sg = slotA()
                transposes(nb, lambda i: nb[:, i, 0:128], 8, 128, sg, sg[:, 0:8, :])
                outT(QbN, sg, 0, 8)
                sgr = slotR()
                transposes(nb, lambda i: nb[:, i, 128:192], 8, 64, sgr, sgr[:, 0:8, :])
                outT(QbR, sgr, 0, 8)
                for half in range(2):
                    for j in range(2):
                        c0 = half * 1024 + j * 512
                        p = proj(ckvT, lambda kc: ckvT[:, kc, :], 2, wukvb, c0, c0 + 512)
                        evac(p, 512, f_a, f_a[:, j * 512:(j + 1) * 512], eng=(A if j % 2 else V))
                    kvv = f_a[:, 0:1024].rearrange("p (h d) -> p h d", h=4)
                    I(G, lambda half=half, kvv=kvv: nc.gpsimd.tensor_copy(out=vbt[:, 4 * half:4 * half + 4, :], in_=kvv[:, :, 128:256]), r=[f_a], w=[vbt])
                    I(V, lambda kvv=kvv: nc.vector.tensor_copy(out=kvv[:, :, 128:192], in_=krf[:].unsqueeze(1).broadcast_to([128, 4, 64])),
                      r=[krf, f_a], w=[f_a])
                    norm_rope(f_a, kvv[:, :, 0:192], 4, 192, gmk_t, True, 128, 32, 128, nb[:, 4 * half:4 * half + 4, :], nb)
                dma(Vb[tok, :, :], vbt[:], r=[vbt])
                sg = slotA()
                transposes(nb, lambda i: nb[:, i, 0:128], 8, 128, sg, sg[:, 0:8, :])
                outT(KbN, sg, 0, 8)
                sgr = slotR()
                transposes(nb, lambda i: nb[:, i, 128:192], 8, 64, sgr, sgr[:, 0:8, :])
                outT(KbR, sgr, 0, 8)

            load_tile(0)
            if NT > 1:
                load_tile(1)
            str0, str1 = [], []
            for t in range(0, NT, 2):
                str0 += fw.record(lambda t=t: p1_tile(t))
                if t + 1 < NT:
                    str1 += fw.record(lambda t=t: p1_tile(t + 1))
            skew = (len(str0) // max(1, (NT + 1) // 2)) // 2
            fw.interleave([str0, [(lambda: None)] * skew + str1])
            fw.barrier()
            if upto == 1:
                return nc

        SC128 = 128.0 ** -0.5
        SC192 = 192.0 ** -0.5

        def tr_generic(pbuf, src_t, src_ap_fn, n, rows, dstT, dst_ap, eng=A):
            for i in range(n):
                I(PE, lambda i=i: nc.tensor.transpose(out=pbuf[0:rows, i * 128:(i + 1) * 128], in_=src_ap_fn(i), identity=ident[:]),
                  r=[src_t, ident], w=[pbuf])
            if eng == A:
                I(A, lambda: nc.scalar.copy(out=dst_ap, in_=pbuf[0:rows, 0:n * 128].rearrange("p (a b) -> p a b", a=n)), r=[pbuf], w=[dstT])
            else:
                I(V, lambda: nc.vector.tensor_copy(out=dst_ap, in_=pbuf[0:rows, 0:n * 128].rearrange("p (a b) -> p a b", a=n)), r=[pbuf], w=[dstT])

        with contextlib.ExitStack() as st23:
            kcT = [sb(st23, f"kcT{g}", [128, NCP], BF16) for g in range(2)]
            vcb = [sb(st23, f"vcb{g}", [128, NCT, 128], BF16) for g in range(2)]
            for g in range(2):
                I(V, lambda g=g: nc.vector.memset(kcT[g][:], 0.0), w=[kcT[g]])
                I(G, lambda g=g: nc.gpsimd.memset(vcb[g][:], 0.0), w=[vcb[g]])
            with contextlib.ExitStack() as st:
                XT = sb(st, "XT", [128, S], BF16)
                Xl = sb(st, "Xl", [128, 32, 512], BF16)
                w1s = sb(st, "w1s", [128, 32 * 256], F32)
                w1b = sb(st, "w1b", [128, 32, 256], BF16)
                w2s = sb(st, "w2s", [128, 256], F32)
                w2b = sb(st, "w2b", [128, 2, 128], BF16)
                peT = sb(st, "peT", [128, 32], F32)
                hTc = sb(st, "hTc", [128, 2, 512], BF16)
                gkc_t = bcast_load(st, "gkc_t", g_kc, 128)
                kf = sb(st, "kf", [128, 128], F32)
                kf2 = sb(st, "kf2", [128, 128], F32)
                kb = sb(st, "kb", [128, 128], BF16)
                ssk = sb(st, "ssk", [128, 1], F32)
                rsk = sb(st, "rsk", [128, 1], F32)
                pc = [ps(st, f"pc{i}", [128, 512]) for i in range(2)]
                pk = ps(st, "pk", [128, 128])
                pkT = ps(st, "pkT", [128, 128], BF16)
                for kv in range(2):
                    w1, w2, pe_ = (k_w1, k_w2, pe_kT) if kv == 0 else (v_w1, v_w2, pe_vT)
                    dma(w1s[:].rearrange("p (l c) -> p l c", l=32), w1.rearrange("(l d) c -> d l c", d=128), w=[w1s])
                    I(V, lambda: nc.vector.tensor_copy(out=w1b[:], in_=w1s[:].rearrange("p (l c) -> p l c", l=32)), r=[w1s], w=[w1b])
                    dma(w2s[:].rearrange("p (k d) -> p k d", k=2), w2.rearrange("(k c) d -> c k d", c=128), w=[w2s])
                    I(G, lambda: nc.gpsimd.tensor_copy(out=w2b[:], in_=w2s[:].rearrange("p (k d) -> p k d", k=2)), r=[w2s], w=[w2b])
                    dma(peT[:], pe_, w=[peT])
                    for g in range(2):
                        src = KcT if kv == 0 else VcT
                        dma(XT[:], src[g, :, :], w=[XT])
                        for l in range(32):
                            xin = XT[:, l:l + 16 * (n_cmp - 1) + 1:16]
                            if l % 2 == 0:
                                I(V, lambda l=l, xin=xin: nc.vector.tensor_scalar(out=Xl[:, l, 0:n_cmp], in0=xin, scalar1=peT[:, l:l + 1],
                                                                                  scalar2=None, op0=ALU.add), r=[XT, peT], w=[Xl])
                            else:
                                I(A, lambda l=l, xin=xin: nc.scalar.activation(out=Xl[:, l, 0:n_cmp], in_=xin, func=AF.Identity,
                                                                               bias=peT[:, l:l + 1], scale=1.0), r=[XT, peT], w=[Xl])
                        for ch in range(2):
                            p = pc[ch]
                            for l in range(32):
                                I(PE, lambda l=l, p=p, ch=ch: nc.tensor.matmul(p[:, 0:n_cmp], lhsT=w1b[:, l, ch * 128:(ch + 1) * 128],
                                                                               rhs=Xl[:, l, 0:n_cmp], start=(l == 0), stop=(l == 31)),
                                  r=[w1b, Xl], w=[p])
                            I(A, lambda p=p, ch=ch: nc.scalar.activation(out=hTc[:, ch, 0:n_cmp], in_=p[:, 0:n_cmp], func=AF.Silu),
                              r=[p], w=[hTc])
                        for nt in range(NCT):
                            rows = min(128, n_cmp - nt * 128)
                            for ch in range(2):
                                I(PE, lambda ch=ch, nt=nt, rows=rows: nc.tensor.matmul(pk[0:rows, :], lhsT=hTc[:, ch, nt * 128:nt * 128 + rows],
                                                                                       rhs=w2b[:, ch, :], start=(ch == 0), stop=(ch == 1)),
                                  r=[hTc, w2b], w=[pk])
                            if kv == 0:
                                I(A, lambda rows=rows: nc.scalar.copy(out=kf[0:rows, :], in_=pk[0:rows, :]), r=[pk], w=[kf])
                                I(V, lambda rows=rows: nc.vector.tensor_tensor(out=kf2[0:rows, :], in0=kf[0:rows, :], in1=kf[0:rows, :], op=ALU.mult),
                                  r=[kf], w=[kf2])
                                I(V, lambda rows=rows: nc.vector.tensor_reduce(out=ssk[0:rows, :], in_=kf2[0:rows, :], axis=AX.X, op=ALU.add),
                                  r=[kf2], w=[ssk])
                                rsqrt_ms(ssk, ssk[0:rows, :], rsk, rsk[0:rows, :], 1.0 / 128, rows=rows)
                                I(V, lambda rows=rows: nc.vector.scalar_tensor_tensor(out=kb[0:rows, :], in0=kf[0:rows, :], scalar=rsk[0:rows, 0:1],
                                                                                      in1=gkc_t[0:rows, :], op0=ALU.mult, op1=ALU.mult),
                                  r=[kf, rsk, gkc_t], w=[kb])
                                I(PE, lambda rows=rows: nc.tensor.transpose(out=pkT[:, 0:rows], in_=kb[0:rows, :], identity=ident[0:rows, 0:rows]),
                                  r=[kb, ident], w=[pkT])
                                I(A, lambda rows=rows, nt=nt, g=g: nc.scalar.copy(out=kcT[g][:, nt * 128:nt * 128 + rows], in_=pkT[:, 0:rows]),
                                  r=[pkT], w=[kcT[g]])
                            else:
                                I(A, lambda rows=rows, nt=nt, g=g: nc.scalar.copy(out=vcb[g][0:rows, nt, :], in_=pk[0:rows, :]), r=[pk], w=[vcb[g]])
                fw.barrier()
            if upto == 2:
                return nc
            with contextlib.ExitStack() as st:
                W0 = 512 + 8 * (NT - 1)
                m0 = sb(st, "m0", [128, W0], F32)
                aext = sb(st, "aext", [128, NSEL + CO], F32)
                fext = sb(st, "fext", [128, NSEL + CO], F32)
                dma(m0[:], c_m0, w=[m0])
                dma(aext[:], c_aext, w=[aext])
                dma(fext[:], c_fext, w=[fext])
                PW = 4 * NSEL + 8

                def mkset(u):
                    B = {}
                    B["qT"] = [sb(st, f"qT{u}{i}", [128, 8, 128], BF16) for i in range(2)]
                    B["gn3"] = [sb(st, f"gn3{u}{i}", [128, 24], F32) for i in range(2)]
                    B["Ef"] = [sb(st, f"Ef{u}{i}", [128, 512], F32) for i in range(2)]
                    B["Pm"] = sb(st, f"Pm{u}", [128, 8, 512], F32)
                    B["Pb"] = [sb(st, f"Pb{u}{i}", [128, 512], BF16) for i in range(2)]
                    B["PbT"] = [sb(st, f"PbT{u}{i}", [128, 4, 128], BF16) for i in range(2)]
                    for nm, shp, dt in (("Z", [128, 8], F32), ("rz", [128, 8], F32), ("gz", [128, 8], F32), ("ppad", [128, 2, PW], F32),
                                        ("imp", [128, 2, NSEL], F32), ("score", [128, 2, NSEL], F32), ("sc2", [128, 2, NSEL], F32),
                                        ("m1", [128, 2, 8], F32), ("m2", [128, 2, 8], F32), ("Btb", [128, 2, NSEL], BF16),
                                        ("BtT", [KJ, 2, 128], BF16), ("ocm", [128, 8, 128], F32)):
                        B[nm] = sb(st, f"{nm}{u}", shp, dt)
                    B["pS"] = ps(st, f"pS{u}", [128, 512])
                    B["pTB"] = ps(st, f"pTB{u}", [128, 1024], BF16)
                    B["pO"] = [ps(st, f"pO{u}{i}", [128, 512]) for i in range(2)]
                    I(V, lambda: nc.vector.memset(B["ppad"][:], 0.0), w=[B["ppad"]])
                    return B
                sets = [mkset(0), mkset(1)]

                def ld3(t):
                    B = sets[t % 2]
                    b = (t // 2) % 2
                    tk = slice(t * 128, (t + 1) * 128)
                    dma(B["qT"][b][:], QaT[:, :, tk].rearrange("h d t -> d h t"), w=[B["qT"][b]])
                    dma(B["gn3"][b][:], Gn[tk, :], w=[B["gn3"][b]])

                def p3_tile(t):
                    B = sets[t % 2]
                    b = (t // 2) % 2
                    if t + 2 < NT:
                        ld3(t + 2)
                    tk = slice(t * 128, (t + 1) * 128)
                    NW = min(n_cmp, 8 * t + 7)
                    off = 8 * (NT - 1) - 8 * t
                    nkt = (NW + 127) // 128
                    q_, gn_, oc_ = B["qT"][b], B["gn3"][b], B["ocm"]
                    Ef, Pm, Pb, PbT, Z, rz, gz, ppad = B["Ef"], B["Pm"], B["Pb"], B["PbT"], B["Z"], B["rz"], B["gz"], B["ppad"]
                    imp, score, sc2, m1, m2, Btb, BtT = B["imp"], B["score"], B["sc2"], B["m1"], B["m2"], B["Btb"], B["BtT"]
                    pS_, pTB, pO = B["pS"], B["pTB"], B["pO"]
                    for h in range(8):
                        g = h // 4
                        hb_ = h % 2
                        I(PE, lambda h=h, g=g: nc.tensor.matmul(pS_[:, 0:NW], lhsT=q_[:, h, :], rhs=kcT[g][:, 0:NW], start=True, stop=True),
                          r=[q_, kcT[g]], w=[pS_])
                        I(A, lambda hb_=hb_: nc.scalar.activation(out=Ef[hb_][:, 0:NW], in_=pS_[:, 0:NW], func=AF.Exp, scale=SC128),
                          r=[pS_, eps_t], w=[Ef[hb_]])
                        I(V, lambda h=h, hb_=hb_: nc.vector.scalar_tensor_tensor(out=Pm[:, h, 0:NW], in0=Ef[hb_][:, 0:NW], scalar=1.0,
                                                                                  in1=m0[:, off:off + NW], op0=ALU.mult, op1=ALU.mult,
                                                                                  accum_out=Z[:, h:h + 1]), r=[Ef[hb_], m0], w=[Pm, Z])
                        I(G, lambda h=h, hb_=hb_: nc.gpsimd.tensor_copy(out=Pb[hb_][:, 0:NW], in_=Pm[:, h, 0:NW]), r=[Pm], w=[Pb[hb_]])
                        for kt in range(nkt):
                            rows = min(128, NW - kt * 128)
                            I(PE, lambda kt=kt, rows=rows, hb_=hb_: nc.tensor.transpose(out=pTB[0:rows, kt * 128:(kt + 1) * 128],
                                                                                         in_=Pb[hb_][:, kt * 128:kt * 128 + rows], identity=ident[:]),
                              r=[Pb[hb_], ident], w=[pTB])
                        for kt in range(nkt):
                            rows = min(128, NW - kt * 128)
                            I(A, lambda kt=kt, rows=rows, hb_=hb_: nc.scalar.copy(out=PbT[hb_][0:rows, kt, :], in_=pTB[0:rows, kt * 128:(kt + 1) * 128]),
                              r=[pTB], w=[PbT[hb_]])
                        for kt in range(nkt):
                            rows = min(128, NW - kt * 128)
                            I(PE, lambda kt=kt, rows=rows, h=h, g=g, hb_=hb_: nc.tensor.matmul(
                                pO[g][:, (h % 4) * 128:(h % 4 + 1) * 128], lhsT=PbT[hb_][0:rows, kt, :], rhs=vcb[g][0:rows, kt, :],
                                start=(h % 4 == 0 and kt == 0), stop=(kt == nkt - 1)), r=[PbT[hb_], vcb[g]], w=[pO[g]])
                    I(V, lambda: nc.vector.tensor_scalar(out=rz[:], in0=Z[:], scalar1=1e-30, scalar2=None, op0=ALU.max), r=[Z], w=[rz])
                    I(V, lambda: nc.vector.reciprocal(out=rz[:], in_=rz[:]), r=[rz], w=[rz])
                    I(V, lambda: nc.vector.tensor_tensor(out=gz[:], in0=rz[:], in1=gn_[:].rearrange("p (h j) -> p h j", j=3)[:, :, 0], op=ALU.mult),
                      r=[rz, gn_], w=[gz])
                    for g in range(2):
                        I(V, lambda g=g: nc.vector.tensor_tensor(out=oc_[:, 4 * g:4 * g + 4, :], in0=pO[g][:, 0:512].rearrange("p (h d) -> p h d", h=4),
                                                                 in1=gz[:, 4 * g:4 * g + 4].unsqueeze(2).broadcast_to([128, 4, 128]), op=ALU.mult),
                          r=[pO[g], gz], w=[oc_])
                    dma(Oc[tk, :], oc_[:].rearrange("p h d -> p (h d)"), r=[oc_])
                    for g in range(2):
                        for r_ in range(4):
                            h = 4 * g + r_
                            if r_ == 0:
                                I(V, lambda g=g, h=h: nc.vector.tensor_scalar(out=ppad[:, g, 4:4 + NW], in0=Pm[:, h, 0:NW], scalar1=rz[:, h:h + 1],
                                                                              scalar2=None, op0=ALU.mult), r=[Pm, rz], w=[ppad])
                            else:
                                I(V, lambda g=g, h=h: nc.vector.scalar_tensor_tensor(out=ppad[:, g, 4:4 + NW], in0=Pm[:, h, 0:NW], scalar=rz[:, h:h + 1],
                                                                                     in1=ppad[:, g, 4:4 + NW], op0=ALU.mult, op1=ALU.add),
                                  r=[Pm, rz, ppad], w=[ppad])
                    a_t = aext[:, CO - 2 * t:CO - 2 * t + NSEL]
                    f_t = fext[:, CO - 2 * t:CO - 2 * t + NSEL]
                    for g in range(2):
                        I(V, lambda g=g: nc.vector.tensor_reduce(out=imp[:, g, :], in_=ppad[:, g, 4:4 + 4 * NSEL].rearrange("p (j f) -> p j f", f=4),
                                                                 axis=AX.X, op=ALU.add), r=[ppad], w=[imp])
                        I(V, lambda g=g: nc.vector.tensor_tensor(out=imp[:, g, :], in0=imp[:, g, :],
                                                                 in1=ppad[:, g, 0:4 * NSEL].rearrange("p (j f) -> p j f", f=4)[:, :, 3], op=ALU.add),
                          r=[ppad, imp], w=[imp])
                        I(V, lambda g=g: nc.vector.tensor_tensor(out=score[:, g, :], in0=imp[:, g, :], in1=a_t, op=ALU.mult), r=[imp, aext], w=[score])
                        I(V, lambda g=g: nc.vector.tensor_tensor(out=score[:, g, :], in0=score[:, g, :], in1=f_t, op=ALU.add), r=[score, fext], w=[score])
                        I(V, lambda g=g: nc.vector.memset(score[:, g, 0:1], 1e9), r=[score], w=[score])
                        if NSEL > 16:
                            I(V, lambda g=g: nc.vector.max(out=m1[:, g, :], in_=score[:, g, :]), r=[score], w=[m1])
                            I(V, lambda g=g: nc.vector.match_replace(out=sc2[:, g, :], in_to_replace=m1[:, g, :], in_values=score[:, g, :], imm_value=-2.0),
                              r=[score, m1], w=[sc2])
                            I(V, lambda g=g: nc.vector.max(out=m2[:, g, :], in_=sc2[:, g, :]), r=[sc2], w=[m2])
                            I(V, lambda g=g: nc.vector.tensor_scalar(out=Btb[:, g, :], in0=score[:, g, :], scalar1=m2[:, g, 7:8], scalar2=NEGB,
                                                                     op0=ALU.is_lt, op1=ALU.mult), r=[score, m2], w=[Btb])
                        else:
                            I(V, lambda g=g: nc.vector.memset(Btb[:, g, :], 0.0), w=[Btb])
                        I(PE, lambda g=g: nc.tensor.transpose(out=pTB[0:KJ, 512 + g * 128:512 + (g + 1) * 128], in_=Btb[:, g, 0:KJ], identity=ident[:]),
                          r=[Btb, ident], w=[pTB])
                    I(A, lambda: nc.scalar.copy(out=BtT[:], in_=pTB[0:KJ, 512:768].rearrange("p (g t) -> p g t", g=2)), r=[pTB], w=[BtT])
                    dma(BT[:, :, tk].rearrange("g j t -> j g t"), BtT[:], r=[BtT])

                ld3(0)
                if NT > 1:
                    ld3(1)
                str0, str1 = [], []
                for t in range(0, NT, 2):
                    str0 += fw.record(lambda t=t: p3_tile(t))
                    if t + 1 < NT:
                        str1 += fw.record(lambda t=t: p3_tile(t + 1))
                skew = (len(str0) // max(1, (NT + 1) // 2)) // 2
                fw.interleave([str0, [(lambda: None)] * skew + str1])
                fw.barrier()
        if upto == 3:
            return nc

        def attn_pipeline(pairs, stageA, stageB):
            n = len(pairs)
            if n == 0:
                return
            stageA(0, pairs[0])
            for i in range(n):
                if i + 1 < n:
                    stageA(i + 1, pairs[i + 1])
                stageB(i, pairs[i])

        with contextlib.ExitStack() as st:
            esel = sb(st, "esel", [KJ, NT, 128], BF16)
            dma(esel[:], c_esel, w=[esel])
            Ks_s = sb(st, "Ks_s", [128, S], BF16)
            Kw_s = sb(st, "Kw_s", [128, S], BF16)
            Vs_s = sb(st, "Vs_s", [128, NT, 129], BF16)
            Vw_s = sb(st, "Vw_s", [128, NT, 129], BF16)
            qT4 = [sb(st, f"qT4{i}", [128, 4, 128], BF16) for i in range(4)]
            btl = [sb(st, f"btl{i}", [KJ, 128], BF16) for i in range(4)]
            BT4 = [sb(st, f"BT4{i}", [KJ, 4, 128], BF16) for i in range(4)]
            PT = [sb(st, f"PT{i}", [128, 4, 128], BF16) for i in range(3)]
            gn4 = [sb(st, f"gn4{i}", [128, 24], F32) for i in range(4)]
            ocl = [sb(st, f"ocl{i}", [128, 4, 128], F32) for i in range(4)]
            bga = [sb(st, f"bga{i}", [128, 512], BF16) for i in range(4)]
            oa = sb(st, "oa", [128, 4, 128], F32)
            yat = [sb(st, f"yat{i}", [128, 512], F32) for i in range(2)]
            rzs = sb(st, "rzs", [128, 8], F32)
            cfs = sb(st, "cfs", [128, 8], F32)
            pS = [ps(st, f"qS{i}", [128, 512]) for i in range(2)]
            pOTs = [ps(st, f"pOTs{i}", [128, 512]) for i in range(2)]
            pOTw = [ps(st, f"pOTw{i}", [128, 512]) for i in range(2)]
            pTr = ps(st, "pTr4", [128, 512])
            pZ4 = ps(st, "pZ4", [128, 512])
            acc_s = [sb(st, f"acc_s{i}", [128, 512], F32) for i in range(2)]
            acc_w = [sb(st, f"acc_w{i}", [128, 512], F32) for i in range(2)]
            oT_s = sb(st, "oT_s", [128, 512], F32)
            oT_w = sb(st, "oT_w", [128, 512], F32)
            I(V, lambda: nc.vector.memset(Vs_s[:, :, 128:129], 1.0), w=[Vs_s])
            I(V, lambda: nc.vector.memset(Vw_s[:, :, 128:129], 1.0), w=[Vw_s])
            cnt4 = [0]
            for g in range(2):
                dma(Ks_s[:], KsT[g, :, :], w=[Ks_s])
                dma(Kw_s[:], KwT[g, :, :], w=[Kw_s])
                dma(Vs_s[:, :, 0:128], Vs[:, g, :].rearrange("(t p) d -> p t d", p=128), w=[Vs_s])
                dma(Vw_s[:, :, 0:128], Vw[:, g, :].rearrange("(t p) d -> p t d", p=128), w=[Vw_s])

                def ld4(t, g=g):
                    b = t % 4
                    tk = slice(t * 128, (t + 1) * 128)
                    dma(qT4[b][:], QaT[4 * g:4 * g + 4, :, tk].rearrange("h d t -> d h t"), w=[qT4[b]])
                    dma(btl[b][:], BT[g, :, tk], w=[btl[b]])
                    dma(gn4[b][:], Gn[tk, :], w=[gn4[b]])
                    dma(ocl[b][:], Oc[tk, 4 * g * 128:(4 * g + 4) * 128].rearrange("p (h d) -> p h d", h=4), w=[ocl[b]])
                    dma(bga[b][:], BG[tk, 4 * g * 128:(4 * g + 4) * 128], w=[bga[b]])

                def bt4(t):
                    b = t % 4
                    I(G, lambda: nc.gpsimd.tensor_copy(out=BT4[b][:], in_=btl[b][:].unsqueeze(1).broadcast_to([KJ, 4, 128])), r=[btl[b]], w=[BT4[b]])

                entries = []
                for t in range(NT):
                    prs = [("s", kt) for kt in range(t + 1)] + [("w", kt) for kt in range(max(0, t - 4), t + 1)]
                    for idx, (kind, kt) in enumerate(prs):
                        entries.append((t, kind, kt, idx == 0, idx == len(prs) - 1))
                base = cnt4[0]
                cnt4[0] += len(entries)
                ld4(0)
                if NT > 1:
                    ld4(1)
                bt4(0)

                def stA(i, e):
                    t, kind, kt, first_of_tile, last_of_tile = e
                    if first_of_tile:
                        if t + 2 < NT:
                            ld4(t + 2)
                        if t + 1 < NT:
                            bt4(t + 1)
                    b = t % 4
                    q_ = qT4[b]
                    k = base + i
                    p = pS[k % 2]
                    pt = PT[k % 3]
                    ksl = slice(kt * 128, (kt + 1) * 128)
                    if kind == "s":
                        I(PE, lambda: nc.tensor.matmul(p[:, :], lhsT=Ks_s[:, ksl], rhs=q_[:].rearrange("p h t -> p (h t)"), start=True, stop=False),
                          r=[Ks_s, q_], w=[p])
                        I(PE, lambda: nc.tensor.matmul(p[:, :], lhsT=esel[:, kt, :], rhs=BT4[b][:].rearrange("p h t -> p (h t)"), start=False, stop=True),
                          r=[esel, BT4[b]], w=[p])
                    else:
                        I(PE, lambda: nc.tensor.matmul(p[:, :], lhsT=Kw_s[:, ksl], rhs=q_[:].rearrange("p h t -> p (h t)"), start=True, stop=True),
                          r=[Kw_s, q_], w=[p])
                    I(A, lambda: nc.scalar.activation(out=pt[:].rearrange("p h t -> p (h t)"), in_=p[:, :], func=AF.Exp, scale=SC128),
                      r=[p], w=[pt])
                    mk = None
                    if kt == t:
                        mk = tri
                    elif kind == "w" and kt == t - 4:
                        mk = atri
                    if mk is not None:
                        I(G, lambda: nc.gpsimd.tensor_tensor(out=pt[:], in0=pt[:], in1=mk[:].unsqueeze(1).broadcast_to([128, 4, 128]), op=ALU.mult),
                          r=[pt, mk], w=[pt])

                def stB(i, e):
                    t, kind, kt, first_of_tile, last_of_tile = e
                    k = base + i
                    pt = PT[k % 3]
                    ptf = pt[:].rearrange("p h t -> p (h t)")
                    if kind == "s":
                        pO_, vv, acc_, first, last = pOTs[t % 2], Vs_s, acc_s[t % 2], (kt == 0), (kt == t)
                    else:
                        pO_, vv, acc_, first, last = pOTw[t % 2], Vw_s, acc_w[t % 2], (kt == max(0, t - 4)), (kt == t)
                    I(PE, lambda: nc.tensor.matmul(pO_[:, :], lhsT=vv[:, kt, 0:128], rhs=ptf, start=first, stop=last), r=[pt, vv], w=[pO_])
                    if kind == "s":
                        if first:
                            I(V, lambda: nc.vector.tensor_copy(out=acc_[:], in_=ptf), r=[pt], w=[acc_])
                        else:
                            I(V, lambda: nc.vector.tensor_tensor(out=acc_[:], in0=acc_[:], in1=ptf, op=ALU.add), r=[pt, acc_], w=[acc_])
                    else:
                        if first:
                            I(G, lambda: nc.gpsimd.tensor_copy(out=acc_[:], in_=ptf), r=[pt], w=[acc_])
                        else:
                            I(G, lambda: nc.gpsimd.tensor_tensor(out=acc_[:], in0=acc_[:], in1=ptf, op=ALU.add), r=[pt, acc_], w=[acc_])

                def finA(t, g=g):
                    b = t % 4
                    for ki, (pO_, acc_, oT_) in enumerate(((pOTs[t % 2], acc_s[t % 2], oT_s), (pOTw[t % 2], acc_w[t % 2], oT_w))):
                        for hh in range(4):
                            I(PE, lambda hh=hh, ki=ki, acc_=acc_: nc.tensor.matmul(pZ4[:, 8 * ki + 2 * hh:8 * ki + 2 * hh + 2], lhsT=acc_[:, hh * 128:(hh + 1) * 128],
                                                                                    rhs=ones_f[:, 0:2], start=True, stop=True), r=[acc_, ones_f], w=[pZ4])
                        I(A, lambda pO_=pO_, oT_=oT_: nc.scalar.copy(out=oT_[:], in_=pO_[:, :]), r=[pO_], w=[oT_])
                    for hh in range(4):
                        I(PE, lambda hh=hh: nc.tensor.transpose(out=pTr[:, hh * 128:(hh + 1) * 128], in_=oT_s[:, hh * 128:(hh + 1) * 128], identity=identf[:]),
                          r=[oT_s, identf], w=[pTr])
                    I(V, lambda: nc.vector.reciprocal(out=rzs[:, 0:8], in_=pZ4[:, 0:16:2]), r=[pZ4], w=[rzs])
                    gv = gn4[b][:].rearrange("p (h j) -> p h j", j=3)
                    I(V, lambda: nc.vector.tensor_tensor(out=cfs[:, 0:4], in0=rzs[:, 0:4], in1=gv[:, 4 * g:4 * g + 4, 1], op=ALU.mult), r=[rzs, gn4[b]], w=[cfs])
                    I(V, lambda: nc.vector.tensor_tensor(out=cfs[:, 4:8], in0=rzs[:, 4:8], in1=gv[:, 4 * g:4 * g + 4, 2], op=ALU.mult), r=[rzs, gn4[b]], w=[cfs])
                    for hh in range(4):
                        I(V, lambda hh=hh: nc.vector.scalar_tensor_tensor(out=oa[:, hh, :], in0=pTr[:, hh * 128:(hh + 1) * 128], scalar=cfs[:, hh:hh + 1],
                                                                          in1=ocl[b][:, hh, :], op0=ALU.mult, op1=ALU.add),
                          r=[pTr, cfs, ocl[b]], w=[oa])

                def finB(t, g=g):
                    b = t % 4
                    tk = slice(t * 128, (t + 1) * 128)
                    for hh in range(4):
                        I(PE, lambda hh=hh: nc.tensor.transpose(out=pTr[:, hh * 128:(hh + 1) * 128], in_=oT_w[:, hh * 128:(hh + 1) * 128], identity=identf[:]),
                          r=[oT_w, identf], w=[pTr])
                    for hh in range(4):
                        I(V, lambda hh=hh: nc.vector.scalar_tensor_tensor(out=oa[:, hh, :], in0=pTr[:, hh * 128:(hh + 1) * 128], scalar=cfs[:, 4 + hh:5 + hh],
                                                                          in1=oa[:, hh, :], op0=ALU.mult, op1=ALU.add),
                          r=[pTr, cfs, oa], w=[oa])
                    yb_ = yat[t % 2]
                    I(G, lambda: nc.gpsimd.tensor_tensor(out=yb_[:], in0=oa[:].rearrange("p h d -> p (h d)"), in1=bga[b][:], op=ALU.mult),
                      r=[oa, bga[b]], w=[yb_])
                    dma(Ya[tk, 4 * g * 128:(4 * g + 4) * 128], yb_[:], r=[yb_])

                n_e = len(entries)
                pend = []
                stA(0, entries[0])
                for i in range(n_e):
                    while pend and pend[0][0] <= i:
                        _, fn_, t_ = pend.pop(0)
                        fn_(t_)
                    if i + 1 < n_e:
                        stA(i + 1, entries[i + 1])
                    stB(i, entries[i])
                    if entries[i][4]:
                        pend.append((i + 2, finA, entries[i][0]))
                        pend.append((i + 4, finB, entries[i][0]))
                while pend:
                    _, fn_, t_ = pend.pop(0)
                    fn_(t_)
            fw.barrier()
        if upto == 4:
            return nc

        NQB = S // 512
        with contextlib.ExitStack() as st:
            KN = [sb(st, f"KN{i}", [128, S], BF16) for i in range(2)]
            KR = [sb(st, f"KR{i}", [128, S], BF16) for i in range(2)]
            VB = [sb(st, f"VB{i}", [128, NT, 129], BF16) for i in range(2)]
            QN = [sb(st, f"QN{i}", [128, 512], BF16) for i in range(2)]
            QR = [sb(st, f"QR{i}", [128, 512], BF16) for i in range(2)]
            PT = [sb(st, f"PTm{i}", [128, 512], BF16) for i in range(3)]
            bgb = [sb(st, f"bgb{i}", [128, 4, 128], BF16) for i in range(2)]
            yal = [sb(st, f"yal{i}", [128, 4, 128], F32) for i in range(2)]
            yo = [sb(st, f"yo{i}", [128, 4, 128], BF16) for i in range(2)]
            obf = sb(st, "obf", [128, 4, 128], F32)
            rz5 = sb(st, "rz5", [128, 4], F32)
            pS = [ps(st, f"mS{i}", [128, 512]) for i in range(2)]
            pOT5 = [ps(st, f"mOT{j}", [128, 512]) for j in range(2)]
            pTr5 = ps(st, "mTr", [128, 512])
            pZ5 = ps(st, "mZ", [128, 512])
            acc5 = [sb(st, f"acc5{j}", [128, 512], F32) for j in range(2)]
            oT5 = sb(st, "oT5", [128, 512], F32)
            for i in range(2):
                I(V, lambda i=i: nc.vector.memset(VB[i][:, :, 128:129], 1.0), w=[VB[i]])
                I(G, lambda i=i: nc.gpsimd.memset(KR[i][64:128, :], 0.0), w=[KR[i]])
                I(G, lambda i=i: nc.gpsimd.memset(QR[i][64:128, :], 0.0), w=[QR[i]])
            cnt5 = [0]

            def ldh(h):
                b = h % 2
                dma(KN[b][:], KbN[h, :, :], w=[KN[b]])
                dma(KR[b][0:64, :], KbR[h, :, :], w=[KR[b]])
                dma(VB[b][:, :, 0:128], Vb[:, h, :].rearrange("(t p) d -> p t d", p=128), w=[VB[b]])

            def ldq(h, qb):
                bq = (h * NQB + qb) % 2
                rows = slice(qb * 512, (qb + 1) * 512)
                dma(QN[bq][:], QbN[h, :, rows], w=[QN[bq]])
                dma(QR[bq][0:64, :], QbR[h, :, rows], w=[QR[bq]])
                dma(bgb[bq][:], BG[rows, 1024 + h * 128:1024 + (h + 1) * 128].rearrange("(s p) c -> p s c", p=128), w=[bgb[bq]])
                dma(yal[bq][:], Ya[rows, h * 128:(h + 1) * 128].rearrange("(s p) c -> p s c", p=128), w=[yal[bq]])
            ldh(0)
            ldq(0, 0)
            for h in range(8):
                if h + 1 < 8:
                    ldh(h + 1)
                b = h % 2
                for qb in range(NQB):
                    nxt = h * NQB + qb + 1
                    if nxt < 8 * NQB:
                        ldq(nxt // NQB, nxt % NQB)
                    bq = (h * NQB + qb) % 2
                    rows = slice(qb * 512, (qb + 1) * 512)
                    pO_ = pOT5[bq]
                    acc_ = acc5[bq]
                    pairs = list(range(4 * qb + 4))
                    base = cnt5[0]
                    cnt5[0] += len(pairs)

                    def stA(i, kt):
                        k = base + i
                        p = pS[k % 2]
                        pt = PT[k % 3]
                        j = kt - 4 * qb
                        c0 = 128 * max(j, 0)
                        ksl = slice(kt * 128, (kt + 1) * 128)
                        I(PE, lambda: nc.tensor.matmul(p[:, c0:512], lhsT=KN[b][:, ksl], rhs=QN[bq][:, c0:512], start=True, stop=False), r=[KN[b], QN[bq]], w=[p])
                        I(PE, lambda: nc.tensor.matmul(p[:, c0:512], lhsT=KR[b][:, ksl], rhs=QR[bq][:, c0:512], start=False, stop=True), r=[KR[b], QR[bq]], w=[p])
                        I(A, lambda: nc.scalar.activation(out=pt[:, c0:512], in_=p[:, c0:512], func=AF.Exp, scale=SC192),
                          r=[p, eps_t], w=[pt])
                        if j >= 0:
                            I(G, lambda: nc.gpsimd.tensor_tensor(out=pt[:, c0:c0 + 128], in0=pt[:, c0:c0 + 128], in1=tri[:], op=ALU.mult), r=[pt, tri], w=[pt])

                    def stB(i, kt):
                        k = base + i
                        pt = PT[k % 3]
                        j = kt - 4 * qb
                        c0 = 128 * max(j, 0)
                        I(PE, lambda: nc.tensor.matmul(pO_[:, c0:512], lhsT=VB[b][:, kt, 0:128], rhs=pt[:, c0:512], start=(kt == 0), stop=(kt == 4 * qb + 3)),
                          r=[pt, VB[b]], w=[pO_])
                        if kt == 0:
                            I(V, lambda: nc.vector.tensor_copy(out=acc_[:], in_=pt[:, 0:512]), r=[pt], w=[acc_])
                        else:
                            I(V, lambda: nc.vector.tensor_tensor(out=acc_[:, c0:512], in0=acc_[:, c0:512], in1=pt[:, c0:512], op=ALU.add), r=[pt, acc_], w=[acc_])
                    attn_pipeline(pairs, stA, stB)
                    for sub in range(4):
                        I(PE, lambda sub=sub: nc.tensor.matmul(pZ5[:, 2 * sub:2 * sub + 2], lhsT=acc_[:, sub * 128:(sub + 1) * 128], rhs=ones_f[:, 0:2],
                                                               start=True, stop=True), r=[acc_, ones_f], w=[pZ5])
                    I(A, lambda: nc.scalar.copy(out=oT5[:], in_=pO_[:, :]), r=[pO_], w=[oT5])
                    for sub in range(4):
                        I(PE, lambda sub=sub: nc.tensor.transpose(out=pTr5[:, sub * 128:(sub + 1) * 128], in_=oT5[:, sub * 128:(sub + 1) * 128], identity=identf[:]),
                          r=[oT5, identf], w=[pTr5])
                    I(V, lambda: nc.vector.reciprocal(out=rz5[:, 0:4], in_=pZ5[:, 0:8:2]), r=[pZ5], w=[rz5])
                    for sub in range(4):
                        I(V, lambda sub=sub: nc.vector.tensor_scalar(out=obf[:, sub, :], in0=pTr5[:, sub * 128:(sub + 1) * 128], scalar1=rz5[:, sub:sub + 1],
                                                                     scalar2=None, op0=ALU.mult), r=[pTr5, rz5], w=[obf])
                    I(G, lambda: nc.gpsimd.tensor_tensor(out=obf[:], in0=obf[:], in1=bgb[bq][:], op=ALU.mult), r=[obf, bgb[bq]], w=[obf])
                    I(G, lambda: nc.gpsimd.tensor_tensor(out=yo[bq][:], in0=obf[:], in1=yal[bq][:], op=ALU.add), r=[obf, yal[bq]], w=[yo[bq]])
                    dma(Yb[rows, h * 128:(h + 1) * 128].rearrange("(s p) c -> p s c", p=128), yo[bq][:], r=[yo[bq]])
            fw.barrier()
        if upto == 5:
            return nc

        with contextlib.ExitStack() as st:
            wob = sb(st, "wob", [128, 8, 1024], BF16)
            with contextlib.ExitStack() as st2:
                stg = [sb(st2, f"stgo{i}", [128, 1024], F32) for i in range(2)]
                load_cast(stg, wob, lambda i: wob[:, i, :], lambda i: w_o[i * 128:(i + 1) * 128, :], 8, [128, 1024])
                fw.barrier()
            modb = TO(st.enter_context(nc.sbuf_tensor("sb_mod6a", [128, 1024], F32)), 2048)
            dma(modb.t[:], MODB[:, 2048:3072], w=[modb])
            yt = [sb(st, f"yt{i}", [128, 1024], BF16) for i in range(2)]
            xl = [sb(st, f"xla{i}", [128, 1024], F32) for i in range(2)]
            YT = sb(st, "YT", [128, 8, 128], BF16)
            tmp = sb(st, "tmpa", [128, 1024], F32)
            x1t = [sb(st, f"x1t{i}", [128, 1024], F32) for i in range(2)]
            pTa = [ps(st, f"pTa{i}", [128, 1024], BF16) for i in range(2)]
            pA = [ps(st, f"pA{i}", [128, 512]) for i in range(4)]

            def ld6(t):
                b = t % 2
                tk = slice(t * 128, (t + 1) * 128)
                dma(yt[b][:], Yb[tk, :], w=[yt[b]])
                dma(xl[b][:], x[tk, :], w=[xl[b]])
            ld6(0)
            for t in range(NT):
                if t + 1 < NT:
                    ld6(t + 1)
                b = t % 2
                tk = slice(t * 128, (t + 1) * 128)
                tr_generic(pTa[t % 2], yt[b], lambda i: yt[b][:, i * 128:(i + 1) * 128], 8, 128, YT, YT[:])
                for half in range(2):
                    p = pA[(2 * t + half) % 4]
                    hs = slice(half * 512, (half + 1) * 512)
                    for c in range(8):
                        I(PE, lambda c=c, p=p, hs=hs: nc.tensor.matmul(p[:, :], lhsT=YT[:, c, :], rhs=wob[:, c, hs], start=(c == 0), stop=(c == 7)),
                          r=[YT, wob], w=[p])
                    gsl = slice(2048 + half * 512, 2048 + (half + 1) * 512)
                    I(V, lambda p=p, hs=hs, gsl=gsl: nc.vector.tensor_tensor(out=tmp[:, hs], in0=p[:, :], in1=modb[:, gsl], op=ALU.mult), r=[p, modb], w=[tmp])
                I(G, lambda: nc.gpsimd.tensor_tensor(out=x1t[b][:], in0=tmp[:], in1=xl[b][:], op=ALU.add), r=[tmp, xl[b]], w=[x1t[b]])
                dma(out[tk, :], x1t[b][:], r=[x1t[b]])
            fw.barrier()
        if upto == 6:
            return nc

        NB6 = S // 256
        with contextlib.ExitStack() as st:
            wupb = sb(st, "wupb", [128, 8, 5632], BF16)
            wdb = sb(st, "wdb", [128, 22, 1024], BF16)
            with contextlib.ExitStack() as st2:
                stg = [sb(st2, f"stgu{i}", [128, 5632], F32) for i in range(2)]
                load_cast(stg, wupb, lambda i: wupb[:, i, :], lambda i: w_up[i * 128:(i + 1) * 128, :], 8, [128, 5632])
                load_cast(stg, wdb, lambda i: wdb[:, i, :], lambda i: w_down[i * 128:(i + 1) * 128, :], 22, [128, 1024])
                fw.barrier()
            modb = TO(st.enter_context(nc.sbuf_tensor("sb_mod6b", [128, 3072], F32)), 3072)
            dma(modb.t[:], MODB[:, 3072:6144], w=[modb])
            wc = sb(st, "wc", [128, 44, 3], F32)
            bc = sb(st, "bc", [128, 44], F32)
            dma(wc[:], wconv_l, w=[wc])
            dma(bc[:], bconv_l, w=[bc])
            xb = [sb(st, f"xb{i}", [128, 2, 1024], F32) for i in range(2)]
            h2f = sb(st, "h2f", [128, 1024], F32)
            tmpd = sb(st, "tmpd", [128, 1024], F32)
            h2b = sb(st, "h2b", [128, 1024], BF16)
            h2T = [sb(st, f"h2T{i}", [128, 8, 256], BF16) for i in range(2)]
            zb = [sb(st, f"zb{i}", [128, 258], F32) for i in range(3)]
            uv = [sb(st, f"uv{i}", [128, 256], F32) for i in range(2)]
            ug = [sb(st, f"ug{i}", [128, 256], F32) for i in range(2)]
            sgm = [sb(st, f"sgm{i}", [128, 256], F32) for i in range(2)]
            actT = sb(st, "actT", [128, 22, 256], BF16)
            halo = sb(st, "halo", [128, 44, 2], F32)
            ss6 = sb(st, "ss6", [128, 2], F32)
            rs6 = sb(st, "rs6", [128, 2], F32)
            pTb = [ps(st, f"pTb{i}", [128, 1024], BF16) for i in range(2)]
            pU = [ps(st, f"pU{i}", [128, 512]) for i in range(3)]
            pD = [ps(st, f"pD{i}", [128, 512]) for i in range(2)]
            I(V, lambda: nc.vector.memset(halo[:], 0.0), w=[halo])
            nu = [0]

            def load6(blk):
                rows = slice(blk * 256, (blk + 1) * 256)
                dma(xb[blk % 2][:], out[rows, :].rearrange("(s p) c -> p s c", p=128), w=[xb[blk % 2]])

            def prep6(blk):
                xb_ = xb[blk % 2]
                hT_ = h2T[blk % 2]
                for s_ in range(2):
                    I(A, lambda s_=s_: nc.scalar.activation(out=h2b[:], in_=xb_[:, s_, :], func=AF.Square, accum_out=ss6[:, s_:s_ + 1]), r=[xb_], w=[h2b, ss6])
                rsqrt_ms(ss6, ss6[:], rs6, rs6[:], 1.0 / 1024)
                for s_ in range(2):
                    I(V, lambda s_=s_: nc.vector.scalar_tensor_tensor(out=h2f[:], in0=xb_[:, s_, :], scalar=rs6[:, s_:s_ + 1], in1=modb[:, A_FFN],
                                                                      op0=ALU.mult, op1=ALU.mult), r=[xb_, rs6, modb], w=[h2f])
                    I(V, lambda: nc.vector.tensor_tensor(out=h2b[:], in0=h2f[:], in1=modb[:, SH_F], op=ALU.add), r=[h2f, modb], w=[h2b])
                    tr_generic(pTb[s_], h2b, lambda i: h2b[:, i * 128:(i + 1) * 128], 8, 128, hT_, hT_[:, :, s_ * 128:(s_ + 1) * 128])

            def gate6(k):
                sg_ = sgm[k % 2]
                I(A, lambda: nc.scalar.activation(out=sg_[:], in_=ug[k % 2][:], func=AF.Silu), r=[ug[k % 2]], w=[sg_])
                I(G, lambda: nc.gpsimd.tensor_tensor(out=actT[:, k, :], in0=sg_[:], in1=uv[k % 2][:], op=ALU.mult), r=[sg_, uv[k % 2]], w=[actT])

            load6(0)
            prep6(0)
            for blk in range(NB6):
                rows = slice(blk * 256, (blk + 1) * 256)
                xb_ = xb[blk % 2]
                hT_ = h2T[blk % 2]
                if blk + 1 < NB6:
                    load6(blk + 1)
                for k in range(22):
                    for which, fc in ((0, k), (1, 22 + k)):
                        i = nu[0]
                        nu[0] += 1
                        p = pU[i % 3]
                        z = zb[i % 3]
                        u = (uv if which == 0 else ug)[k % 2]
                        for c in range(8):
                            I(PE, lambda c=c, p=p, fc=fc: nc.tensor.matmul(p[:, 0:256], lhsT=wupb[:, c, fc * 128:(fc + 1) * 128], rhs=hT_[:, c, :],
                                                                           start=(c == 0), stop=(c == 7)), r=[wupb, hT_], w=[p])
                        I(G, lambda z=z, fc=fc: nc.gpsimd.tensor_copy(out=z[:, 0:2], in_=halo[:, fc, :]), r=[halo], w=[z])
                        I(A, lambda z=z, p=p: nc.scalar.copy(out=z[:, 2:258], in_=p[:, 0:256]), r=[p], w=[z])
                        I(G, lambda z=z, fc=fc: nc.gpsimd.tensor_copy(out=halo[:, fc, :], in_=z[:, 256:258]), r=[z], w=[halo])
                        I(A, lambda u=u, p=p, fc=fc: nc.scalar.activation(out=u[:], in_=p[:, 0:256], func=AF.Identity, scale=wc[:, fc, 2:3], bias=bc[:, fc:fc + 1]),
                          r=[p, wc, bc], w=[u])
                        I(V, lambda u=u, z=z, fc=fc: nc.vector.scalar_tensor_tensor(out=u[:], in0=z[:, 1:257], scalar=wc[:, fc, 1:2], in1=u[:],
                                                                                    op0=ALU.mult, op1=ALU.add), r=[z, wc, u], w=[u])
                        I(V, lambda u=u, z=z, fc=fc: nc.vector.scalar_tensor_tensor(out=u[:], in0=z[:, 0:256], scalar=wc[:, fc, 0:1], in1=u[:],
                                                                                    op0=ALU.mult, op1=ALU.add), r=[z, wc, u], w=[u])
                    if k >= 1:
                        gate6(k - 1)
                gate6(21)
                if blk + 1 < NB6:
                    prep6(blk + 1)
                for s_ in range(2):
                    for half in range(2):
                        p = pD[half]
                        hs = slice(half * 512, (half + 1) * 512)
                        for k in range(22):
                            I(PE, lambda k=k, p=p, hs=hs, s_=s_: nc.tensor.matmul(p[:, :], lhsT=actT[:, k, s_ * 128:(s_ + 1) * 128], rhs=wdb[:, k, hs],
                                                                                  start=(k == 0), stop=(k == 21)), r=[actT, wdb], w=[p])
                        gsl = slice(5120 + half * 512, 5120 + (half + 1) * 512)
                        I(V, lambda p=p, hs=hs, gsl=gsl: nc.vector.tensor_tensor(out=tmpd[:, hs], in0=p[:, :], in1=modb[:, gsl], op=ALU.mult), r=[p, modb], w=[tmpd])
                    I(G, lambda s_=s_: nc.gpsimd.tensor_tensor(out=xb_[:, s_, :], in0=tmpd[:], in1=xb_[:, s_, :], op=ALU.add), r=[tmpd, xb_], w=[xb_])
                dma(out[rows, :].rearrange("(s p) c -> p s c", p=128), xb_[:], r=[xb_])
            fw.barrier()
    return nc


_PARAM_NAMES = ["w_ada", "b_ada", "attn_norm", "ffn_norm", "w_in", "nsa_q_norm", "nsa_kc_norm", "nsa_ks_norm", "nsa_kw_norm",
                "cmp_k_w1", "cmp_k_w2", "cmp_v_w1", "cmp_v_w2", "mla_cq_norm", "mla_ckv_norm", "w_uq", "w_ukv",
                "mla_q_norm", "mla_k_norm", "w_o", "w_up", "w_down"]
_CONSTS = {}


def make_in_map(inp, b, S):
    if S not in _CONSTS:
        _CONSTS[S] = host_consts(S)
    m = {}
    m["x"] = np.ascontiguousarray(np.asarray(inp["x"])[b, :S], dtype=np.float32)
    m["ccol"] = np.ascontiguousarray(np.asarray(inp["c"])[b].reshape(8, 128).T, dtype=np.float32)
    for k in _PARAM_NAMES:
        m[k] = np.ascontiguousarray(np.asarray(inp[k])[0], dtype=np.float32)
    m["pe_kT"] = np.ascontiguousarray(np.asarray(inp["cmp_k_pe"])[0].T, dtype=np.float32)
    m["pe_vT"] = np.ascontiguousarray(np.asarray(inp["cmp_v_pe"])[0].T, dtype=np.float32)
    m["wconv_l"] = np.ascontiguousarray(np.asarray(inp["w_conv"])[0].reshape(3, 44, 128).transpose(2, 1, 0), dtype=np.float32)
    m["bconv_l"] = np.ascontiguousarray(np.asarray(inp["b_conv"])[0].reshape(44, 128).T, dtype=np.float32)
    m.update(_CONSTS[S])
    return m


_NC = {}


def kernel(**inputs):
    S = 8192
    if S not in _NC:
        _NC[S] = build(S)
    nc = _NC[S]
    in_maps = [make_in_map(inputs, b, S) for b in range(8)]
    res = run_bass_kernel_spmd(nc, in_maps, core_ids=list(range(8)))
    return np.stack([np.asarray(r["out"], dtype=np.float32) for r in res.results], axis=0)
```

```python
import contextlib
import numpy as np
import ml_dtypes
import concourse.bass as bass
import concourse.mybir as mybir
from concourse.bass_utils import run_bass_kernel_spmd

F32 = mybir.dt.float32
BF16 = mybir.dt.bfloat16
AF = mybir.ActivationFunctionType
ALU = mybir.AluOpType
AX = mybir.AxisListType

EPS = 1e-6
NEGB = -30000.0
EXPB = -4.0


class Buf:
    __slots__ = ("w", "r")

    def __init__(self):
        self.w = {}
        self.r = {}


class T:
    def __init__(self, t):
        self.t = t
        self.b = Buf()

    def __getitem__(self, k):
        return self.t[k]


class TO(T):
    def __init__(self, t, off):
        super().__init__(t)
        self.off = off

    def __getitem__(self, k):
        p, c = k
        return self.t[p, slice(c.start - self.off, c.stop - self.off)]


def _b(x):
    return x.b if isinstance(x, T) else x


class FW:
    ROT = 12000
    NQ = 24

    def __init__(self, nc, es):
        self.nc, self.es = nc, es
        self.E = {"pe": nc.tensor, "act": nc.scalar, "dve": nc.vector, "pool": nc.gpsimd, "sp": nc.sync}
        self.sem, self.cnt = {}, {}
        self.nsem = 0
        for e in self.E:
            self._newsem(e)
        self.waited = {e: {} for e in self.E}
        self.dq = {}
        self.n = 0
        self.rec = None

    def _newsem(self, e):
        s = self.es.enter_context(self.nc.semaphore(f"s{e}{self.nsem}"))
        self.nsem += 1
        self.sem[e] = s
        self.cnt[e] = 0

    def _wait(self, e, deps):
        for s, (v, pe) in deps.items():
            if self.waited[e].get(s, 0) >= v:
                continue
            self.E[e].wait_ge(s, v)
            self.waited[e][s] = v
            self.n += 1

    def _deps(self, e, r, w):
        deps = {}

        def add(d, raw):
            for s, (v, pe) in d.items():
                if pe == e and e != "dma":
                    if e == "pe" or not raw:
                        continue
                if deps.get(s, (0,))[0] < v:
                    deps[s] = (v, pe)
        for b in r:
            add(_b(b).w, True)
        for b in w:
            add(_b(b).w, False)
            add(_b(b).r, False)
        return deps

    def I(self, e, fn, r=(), w=()):
        if self.rec is not None:
            r, w = list(r), list(w)
            self.rec.append(lambda: self._I(e, fn, r, w))
            return None
        return self._I(e, fn, r, w)

    def _I(self, e, fn, r=(), w=()):
        self._wait(e, self._deps(e, r, w))
        if self.cnt[e] >= self.ROT:
            self._newsem(e)
        inst = fn()
        s = self.sem[e]
        inst.then_inc(s, 1)
        self.cnt[e] += 1
        self.n += 1
        tok = (self.cnt[e], e)
        for b in w:
            b = _b(b)
            b.w = {s: tok}
            b.r = {}
        for b in r:
            _b(b).r[s] = tok
        return inst

    def dma(self, out, in_, r=(), w=(), q="sp"):
        if self.rec is not None:
            r, w = list(r), list(w)
            self.rec.append(lambda: self._dma(out, in_, r, w, q))
            return None
        return self._dma(out, in_, r, w, q)

    def record(self, fn):
        self.rec = []
        fn()
        lst, self.rec = self.rec, None
        return lst

    @staticmethod
    def interleave_skewed(streams):
        n = len(streams)
        L = max(len(st_) for st_ in streams)
        pos = [0] * n
        start = [0] + [0] * (n - 1)
        i = 0
        while any(pos[k] < len(streams[k]) for k in range(n)):
            for k in range(n):
                if i >= start[k] and pos[k] < len(streams[k]):
                    streams[k][pos[k]]()
                    pos[k] += 1
            i += 1

    @staticmethod
    def interleave(lists):
        for i in range(max(len(l) for l in lists)):
            for l in lists:
                if i < len(l):
                    l[i]()

    def _dma(self, out, in_, r=(), w=(), q="sp"):
        self._wait(q, self._deps("dma", r, w))
        d = self.dq.setdefault(q, {"sems": [], "i": 0})
        if len(d["sems"]) < self.NQ:
            s = self.es.enter_context(self.nc.semaphore(f"d{q}{len(d['sems'])}"))
            ent = [s, 0]
            d["sems"].append(ent)
        else:
            ent = d["sems"][d["i"] % self.NQ]
            d["i"] += 1
            self._wait(q, {ent[0]: (16 * ent[1], "dma")})
        inst = self.E[q].dma_start(out=out, in_=in_)
        inst.then_inc(ent[0], 16)
        ent[1] += 1
        self.n += 1
        tok = (16 * ent[1], "dma")
        for b in w:
            b = _b(b)
            b.w = {ent[0]: tok}
            b.r = {}
        for b in r:
            _b(b).r[ent[0]] = tok

    def barrier(self):
        toks = {}
        for e in self.E:
            if self.cnt[e] > 0:
                toks[self.sem[e]] = (self.cnt[e], "x")
        for q, d in self.dq.items():
            for s, c in d["sems"]:
                if c:
                    toks[s] = (16 * c, "dma")
        for e in self.E:
            self._wait(e, toks)


O_NQ, O_NKC, O_NVC, O_NKS, O_NVS, O_NKW, O_NVW, O_NG, O_CQ, O_CKV, O_KR, O_BG = (
    0, 1024, 1280, 1536, 1792, 2048, 2304, 2560, 2584, 2968, 3224, 3288)
IN_W = 5336


def host_consts(S):
    NT = S // 128
    NSEL = S // 64
    n_cmp = S // 16 - 1
    pos = np.arange(S, dtype=np.float32)
    inv128 = (10000.0 ** (-np.arange(64, dtype=np.float32) * 2.0 / 128)).astype(np.float32)
    inv64 = (10000.0 ** (-np.arange(32, dtype=np.float32) * 2.0 / 64)).astype(np.float32)
    a128 = pos[:, None] * inv128[None, :]
    a64 = pos[:, None] * inv64[None, :]
    c = {}
    c["cs128"] = np.concatenate([np.cos(a128), np.sin(a128)], axis=1).astype(np.float32)
    c["cs64"] = np.concatenate([np.cos(a64), np.sin(a64)], axis=1).astype(np.float32)
    p = np.arange(128)[:, None]
    f = np.arange(128)[None, :]
    c["ident"] = (p == f).astype(ml_dtypes.bfloat16)
    c["identf"] = (p == f).astype(np.float32)
    c["tri"] = (p <= f).astype(ml_dtypes.bfloat16)
    c["atri"] = (p > f).astype(ml_dtypes.bfloat16)
    W0 = 512 + 8 * (NT - 1)
    cc = np.arange(W0)[None, :]
    m = cc - 8 * (NT - 1)
    c["m0ext"] = ((16 * m + 31) <= p).astype(np.float32)
    CO = 2 * (NT - 1)
    W1 = NSEL + CO
    cc = np.arange(W1)[None, :]
    d = cc - CO
    hi = (p >= 64).astype(np.int64)
    c["aext"] = (d <= hi - 2).astype(np.float32)
    forced = (d == hi) | (d == hi - 1)
    c["fext"] = np.where(forced, 1e9, np.where(d > hi, -1.0, 0.0)).astype(np.float32)
    KJ = min(128, NSEL)
    E = np.zeros((KJ, NT, 128), dtype=np.float32)
    for kt in range(NT):
        E[2 * kt, kt, :64] = 1.0
        E[2 * kt + 1, kt, 64:] = 1.0
    c["esel"] = E.astype(ml_dtypes.bfloat16)
    return c


def build(S, dbg=False, upto=99):
    NT = S // 128
    NSEL = S // 64
    KJ = min(128, NSEL)
    n_cmp = S // 16 - 1
    NCT = (n_cmp + 127) // 128
    NCP = NCT * 128
    CO = 2 * (NT - 1)
    nc = bass.Bass("TRN2", target_bir_lowering=False)
    okind = "ExternalOutput"

    def din(name, shape, dt=F32):
        return nc.dram_tensor(name, list(shape), dt, kind="ExternalInput").ap()

    def dscr(name, shape, dt):
        return nc.dram_tensor(name, list(shape), dt, kind=okind).ap()

    x = din("x", [S, 1024])
    ccol = din("ccol", [128, 8])
    w_ada = din("w_ada", [1024, 6144])
    b_ada = din("b_ada", [6144])
    attn_norm = din("attn_norm", [1024])
    ffn_norm = din("ffn_norm", [1024])
    w_in = din("w_in", [1024, IN_W])
    g_q = din("nsa_q_norm", [128])
    g_kc = din("nsa_kc_norm", [128])
    g_ks = din("nsa_ks_norm", [128])
    g_kw = din("nsa_kw_norm", [128])
    pe_kT = din("pe_kT", [128, 32])
    k_w1 = din("cmp_k_w1", [4096, 256])
    k_w2 = din("cmp_k_w2", [256, 128])
    pe_vT = din("pe_vT", [128, 32])
    v_w1 = din("cmp_v_w1", [4096, 256])
    v_w2 = din("cmp_v_w2", [256, 128])
    g_cq = din("mla_cq_norm", [384])
    g_ckv = din("mla_ckv_norm", [256])
    w_uq = din("w_uq", [384, 1536])
    w_ukv = din("w_ukv", [256, 2048])
    g_mq = din("mla_q_norm", [192])
    g_mk = din("mla_k_norm", [192])
    w_o = din("w_o", [1024, 1024])
    w_up = din("w_up", [1024, 5632])
    wconv_l = din("wconv_l", [128, 44, 3])
    bconv_l = din("bconv_l", [128, 44])
    w_down = din("w_down", [2816, 1024])
    cs128 = din("cs128", [S, 128])
    cs64 = din("cs64", [S, 64])
    c_ident = din("ident", [128, 128], BF16)
    c_identf = din("identf", [128, 128])
    c_tri = din("tri", [128, 128], BF16)
    c_atri = din("atri", [128, 128], BF16)
    c_m0 = din("m0ext", [128, 512 + 8 * (NT - 1)])
    c_aext = din("aext", [128, NSEL + CO])
    c_fext = din("fext", [128, NSEL + CO])
    c_esel = din("esel", [KJ, NT, 128], BF16)
    out = nc.dram_tensor("out", [S, 1024], F32, kind="ExternalOutput").ap()

    QaT = dscr("QaT", [8, 128, S], BF16)
    KcT = dscr("KcT", [2, 128, S], BF16)
    VcT = dscr("VcT", [2, 128, S], BF16)
    KsT = dscr("KsT", [2, 128, S], BF16)
    KwT = dscr("KwT", [2, 128, S], BF16)
    Vs = dscr("Vs", [S, 2, 128], BF16)
    Vw = dscr("Vw", [S, 2, 128], BF16)
    Gn = dscr("Gn", [S, 24], F32)
    BG = dscr("BG", [S, 2048], BF16)
    QbN = dscr("QbN", [8, 128, S], BF16)
    QbR = dscr("QbR", [8, 64, S], BF16)
    KbN = dscr("KbN", [8, 128, S], BF16)
    KbR = dscr("KbR", [8, 64, S], BF16)
    Vb = dscr("Vb", [S, 8, 128], BF16)
    Oc = dscr("Oc", [S, 1024], F32)
    BT = dscr("BT", [2, KJ, S], BF16)
    Ya = dscr("Ya", [S, 1024], F32)
    Yb = dscr("Yb", [S, 1024], BF16)
    MODB = dscr("MODB", [128, 6144], F32)

    with contextlib.ExitStack() as es:
        fw = FW(nc, es)
        global LASTFW
        LASTFW = fw
        I = fw.I
        dma = fw.dma
        V, G, A, PE = "dve", "pool", "act", "pe"

        def sb(st, name, shape, dt):
            return T(st.enter_context(nc.sbuf_tensor("sb_" + name, list(shape), dt)))

        def ps(st, name, shape, dt=F32):
            return T(st.enter_context(nc.psum_tensor("ps_" + name, list(shape), dt)))

        ident = sb(es, "ident", [128, 128], BF16)
        tri = sb(es, "tri", [128, 128], BF16)
        atri = sb(es, "atri", [128, 128], BF16)
        dma(ident[:], c_ident, w=[ident])
        dma(tri[:], c_tri, w=[tri])
        identf = sb(es, "identf", [128, 128], F32)
        dma(identf[:], c_identf, w=[identf])
        ones_f = sb(es, "ones_f", [128, 2], F32)
        I(V, lambda: nc.vector.memset(ones_f[:], 1.0), w=[ones_f])
        dma(atri[:], c_atri, w=[atri])
        SH_A, A_ATT, G_A, SH_F, A_FFN, G_F = [slice(i * 1024, (i + 1) * 1024) for i in range(6)]

        def rsqrt_ms(ssT, ss_ap, rsT, rs_ap, inv_n, rows=128):
            I(A, lambda: nc.scalar.activation(out=rs_ap, in_=ss_ap, func=AF.Sqrt, bias=eps_t[0:rows, 0:1], scale=inv_n),
              r=[ssT, eps_t], w=[rsT])
            I(V, lambda: nc.vector.reciprocal(out=rs_ap, in_=rs_ap), r=[rsT], w=[rsT])

        eps_t = sb(es, "eps_t", [128, 2], F32)
        I(V, lambda: nc.vector.memset(eps_t[:, 0:1], EPS), w=[eps_t])
        I(V, lambda: nc.vector.memset(eps_t[:, 1:2], EXPB), w=[eps_t])

        with contextlib.ExitStack() as st:
            modb = sb(st, "modb", [128, 6144], F32)
            cs_t = sb(st, "cs_t", [128, 8], F32)
            sc_t = sb(st, "sc_t", [128, 8], F32)
            scb = sb(st, "scb", [128, 8, 128], F32)
            wst = [sb(st, f"wst{i}", [128, 3072], F32) for i in range(2)]
            gtmp = sb(st, "gtmp", [128, 1024], F32)
            pm = [ps(st, f"pm{i}", [128, 512]) for i in range(6)]
            dma(cs_t[:], ccol, w=[cs_t])
            dma(modb[:], b_ada.partition_broadcast(128), w=[modb])
            I(A, lambda: nc.scalar.activation(out=sc_t[:], in_=cs_t[:], func=AF.Silu), r=[cs_t], w=[sc_t])
            I(V, lambda: nc.vector.tensor_copy(out=scb[:], in_=sc_t[:].unsqueeze(2).broadcast_to([128, 8, 128])),
              r=[sc_t], w=[scb])
            for half in range(2):
                for kc in range(8):
                    wb = wst[kc % 2]
                    dma(wb[:], w_ada[kc * 128:(kc + 1) * 128, half * 3072:(half + 1) * 3072], w=[wb])
                    for j in range(6):
                        I(PE, lambda j=j, wb=wb, kc=kc: nc.tensor.matmul(
                            pm[j][:], lhsT=scb[:, kc, :], rhs=wb[:, j * 512:(j + 1) * 512],
                            start=(kc == 0), stop=(kc == 7)), r=[scb, wb], w=[pm[j]])
                for j in range(6):
                    cs = slice(half * 3072 + j * 512, half * 3072 + (j + 1) * 512)
                    I(V, lambda j=j, cs=cs: nc.vector.tensor_tensor(out=modb[:, cs], in0=pm[j][:], in1=modb[:, cs],
                                                                    op=ALU.add), r=[pm[j], modb], w=[modb])
            for gsrc, sl in ((attn_norm, A_ATT), (ffn_norm, A_FFN)):
                dma(gtmp[:], gsrc.partition_broadcast(128), w=[gtmp])
                I(V, lambda sl=sl: nc.vector.scalar_tensor_tensor(out=modb[:, sl], in0=modb[:, sl], scalar=1.0,
                                                                  in1=gtmp[:], op0=ALU.add, op1=ALU.mult),
                  r=[modb, gtmp], w=[modb])
            dma(MODB, modb[:], r=[modb])
            fw.barrier()

        def load_cast(st_scratch, dst, dst_ap_fn, src_ap_fn, nchunks, shape, engs=(V, G, A)):
            for i in range(nchunks):
                stg = st_scratch[i % len(st_scratch)]
                dma(stg[tuple(slice(0, s_) for s_ in shape)] if False else stg_view(stg, shape), src_ap_fn(i), w=[stg])
                e = engs[i % len(engs)]
                if e == A:
                    I(A, lambda i=i, stg=stg: nc.scalar.copy(out=dst_ap_fn(i), in_=stg_view(stg, shape)), r=[stg], w=[dst])
                elif e == V:
                    I(V, lambda i=i, stg=stg: nc.vector.tensor_copy(out=dst_ap_fn(i), in_=stg_view(stg, shape)), r=[stg], w=[dst])
                else:
                    I(G, lambda i=i, stg=stg: nc.gpsimd.tensor_copy(out=dst_ap_fn(i), in_=stg_view(stg, shape)), r=[stg], w=[dst])

        def stg_view(stg, shape):
            n = 1
            for s_ in shape[1:]:
                n *= s_
            v = stg[0:shape[0], 0:n]
            if len(shape) == 3:
                v = v.rearrange("p (a b) -> p a b", a=shape[1])
            return v

        def bcast_load(st, name, src, n):
            t = sb(st, name, [128, n], F32)
            dma(t[:], src.partition_broadcast(128), w=[t])
            return t

        with contextlib.ExitStack() as st:
            winb = sb(st, "winb", [128, 8, IN_W], BF16)
            wuqb = sb(st, "wuqb", [128, 3, 1536], BF16)
            wukvb = sb(st, "wukvb", [128, 2, 2048], BF16)
            with contextlib.ExitStack() as st2:
                stg = [sb(st2, f"stg{i}", [128, IN_W], F32) for i in range(2)]
                load_cast(stg, winb, lambda i: winb[:, i, :], lambda i: w_in[i * 128:(i + 1) * 128, :], 8, [128, IN_W])
                load_cast(stg, wuqb, lambda i: wuqb[:, i, :], lambda i: w_uq[i * 128:(i + 1) * 128, :], 3, [128, 1536])
                load_cast(stg, wukvb, lambda i: wukvb[:, i, :], lambda i: w_ukv[i * 128:(i + 1) * 128, :], 2, [128, 2048])
                fw.barrier()
            modb = TO(st.enter_context(nc.sbuf_tensor("sb_mod1", [128, 2048], F32)), 0)
            dma(modb.t[:], MODB[:, 0:2048], w=[modb])
            gq_t = bcast_load(st, "gq_t", g_q, 128)
            gks_t = bcast_load(st, "gks_t", g_ks, 128)
            gkw_t = bcast_load(st, "gkw_t", g_kw, 128)
            gcq_t = bcast_load(st, "gcq_t", g_cq, 384)
            gckv_t = bcast_load(st, "gckv_t", g_ckv, 256)
            gmq_t = bcast_load(st, "gmq_t", g_mq, 192)
            gmk_t = bcast_load(st, "gmk_t", g_mk, 192)

            def mk1(u):
                B = {}
                B["cst2"] = [sb(st, f"cst_{u}{i}", [128, 192], F32) for i in range(2)]
                for nm, shp, dt in (("xt", [128, 1024], F32), ("ss1", [128, 1], F32), ("rs1", [128, 1], F32),
                                    ("nb", [128, 8, 192], BF16), ("hT", [128, 8, 128], BF16), ("f_a", [128, 1536], F32),
                                    ("f_b", [128, 1536], F32), ("f_d", [128, 1024], F32), ("ssn", [128, 8], F32), ("rsn", [128, 8], F32),
                                    ("vbt", [128, 8, 128], BF16), ("gnt", [128, 24], F32), ("cqT", [128, 3, 128], BF16),
                                    ("ckvT", [128, 2, 128], BF16), ("krf", [128, 64], F32)):
                    B[nm] = sb(st, f"{nm}_{u}", shp, dt)
                B["bgt"] = [sb(st, f"bgt_{u}{i}", [128, 512], BF16) for i in range(2)]
                B["sgA"] = [sb(st, f"sgA_{u}{i}", [128, 8, 128], BF16) for i in range(2)]
                B["sgR"] = [sb(st, f"sgR_{u}{i}", [64, 8, 128], BF16) for i in range(2)]
                B["pp"] = [ps(st, f"pp_{u}{i}", [128, 512]) for i in range(2)]
                B["pT"] = ps(st, f"pT_{u}", [128, 1024], BF16)
                B["npp"] = 0
                B["nA"] = 0
                B["nR"] = 0
                return B
            sets1 = [mk1(0), mk1(1)]

            def load_tile(t):
                B = sets1[t % 2]
                dma(B["xt"][:], x[t * 128:(t + 1) * 128, :], w=[B["xt"]])
                c_ = B["cst2"][(t // 2) % 2]
                dma(c_[:, 0:128], cs128[t * 128:(t + 1) * 128, :], w=[c_])
                dma(c_[:, 128:192], cs64[t * 128:(t + 1) * 128, :], w=[c_])

            def p1_tile(t):
                B = sets1[t % 2]
                xtt, ss1, rs1, nb, hT, f_a, f_b, f_d = (B[k] for k in ("xt", "ss1", "rs1", "nb", "hT", "f_a", "f_b", "f_d"))
                cs_ = B["cst2"][(t // 2) % 2]
                ssn, rsn, vbt, gnt, cqT, ckvT, krf, pT_ = (B[k] for k in ("ssn", "rsn", "vbt", "gnt", "cqT", "ckvT", "krf", "pT"))
                tok = slice(t * 128, (t + 1) * 128)
                nbf = nb[:].rearrange("p h d -> p (h d)")

                def proj(lhsT_t, lhs_fn, nk, w_t, c0, c1):
                    p = B["pp"][B["npp"] % 2]
                    B["npp"] += 1
                    for kc in range(nk):
                        I(PE, lambda kc=kc: nc.tensor.matmul(p[:, 0:c1 - c0], lhsT=lhs_fn(kc), rhs=w_t[:, kc, c0:c1],
                                                             start=(kc == 0), stop=(kc == nk - 1)), r=[lhsT_t, w_t], w=[p])
                    return p

                def evac(p, c, dstT, dst_ap, eng=A):
                    if eng == A:
                        I(A, lambda: nc.scalar.copy(out=dst_ap, in_=p[:, 0:c]), r=[p], w=[dstT])
                    else:
                        I(V, lambda: nc.vector.tensor_copy(out=dst_ap, in_=p[:, 0:c]), r=[p], w=[dstT])

                def norm_rope(src, src_ap, nh, hd, gain_t, do_norm, rope_off, rope_half, cs_off, dst_ap, dstT):
                    if do_norm:
                        sq = f_b[:, 0:nh * hd].rearrange("p (h d) -> p h d", h=nh)
                        I(A, lambda: nc.scalar.activation(out=sq, in_=src_ap, func=AF.Square), r=[src], w=[f_b])
                        I(V, lambda: nc.vector.tensor_reduce(out=ssn[:, 0:nh], in_=sq, axis=AX.X, op=ALU.add), r=[f_b], w=[ssn])
                        rsqrt_ms(ssn, ssn[:, 0:nh], rsn, rsn[:, 0:nh], 1.0 / hd)
                        I(V, lambda: nc.vector.tensor_tensor(out=src_ap, in0=src_ap, in1=rsn[:, 0:nh].unsqueeze(2).broadcast_to([128, nh, hd]),
                                                             op=ALU.mult), r=[src, rsn], w=[src])
                        I(G, lambda: nc.gpsimd.tensor_tensor(out=src_ap, in0=src_ap, in1=gain_t[:, 0:hd].unsqueeze(1).broadcast_to([128, nh, hd]),
                                                             op=ALU.mult), r=[src, gain_t], w=[src])
                    if rope_half == 0:
                        I(V, lambda: nc.vector.tensor_copy(out=dst_ap, in_=src_ap), r=[src], w=[dstT])
                        return
                    if rope_off > 0:
                        I(G, lambda: nc.gpsimd.tensor_copy(out=dst_ap[:, :, 0:rope_off], in_=src_ap[:, :, 0:rope_off]), r=[src], w=[dstT])
                    hh_ = rope_half
                    x1 = src_ap[:, :, rope_off:rope_off + hh_]
                    x2 = src_ap[:, :, rope_off + hh_:rope_off + 2 * hh_]
                    cb = cs_[:, cs_off:cs_off + hh_].unsqueeze(1).broadcast_to([128, nh, hh_])
                    sbb = cs_[:, cs_off + hh_:cs_off + 2 * hh_].unsqueeze(1).broadcast_to([128, nh, hh_])
                    t1 = f_d[:, 0:nh * hh_].rearrange("p (h d) -> p h d", h=nh)
                    t2 = f_d[:, 512:512 + nh * hh_].rearrange("p (h d) -> p h d", h=nh)
                    t3 = f_b[:, 0:nh * hh_].rearrange("p (h d) -> p h d", h=nh)
                    t4 = f_b[:, 512:512 + nh * hh_].rearrange("p (h d) -> p h d", h=nh)
                    I(V, lambda: nc.vector.tensor_tensor(out=t1, in0=x1, in1=cb, op=ALU.mult), r=[src, cs_], w=[f_d])
                    I(V, lambda: nc.vector.tensor_tensor(out=t2, in0=x2, in1=sbb, op=ALU.mult), r=[src, cs_], w=[f_d])
                    I(G, lambda: nc.gpsimd.tensor_tensor(out=t3, in0=x1, in1=sbb, op=ALU.mult), r=[src, cs_], w=[f_b])
                    I(G, lambda: nc.gpsimd.tensor_tensor(out=t4, in0=x2, in1=cb, op=ALU.mult), r=[src, cs_], w=[f_b])
                    I(V, lambda: nc.vector.tensor_tensor(out=dst_ap[:, :, rope_off:rope_off + hh_], in0=t1, in1=t2, op=ALU.subtract),
                      r=[f_d], w=[dstT])
                    I(G, lambda: nc.gpsimd.tensor_tensor(out=dst_ap[:, :, rope_off + hh_:rope_off + 2 * hh_], in0=t3, in1=t4, op=ALU.add),
                      r=[f_b], w=[dstT])

                def transposes(src_t, src_ap_fn, n, rows, dstT, dst_ap):
                    for i in range(n):
                        I(PE, lambda i=i: nc.tensor.transpose(out=pT_[0:rows, i * 128:(i + 1) * 128], in_=src_ap_fn(i), identity=ident[:]),
                          r=[src_t, ident], w=[pT_])
                    I(A, lambda: nc.scalar.copy(out=dst_ap, in_=pT_[0:rows, 0:n * 128].rearrange("p (a b) -> p a b", a=n)),
                      r=[pT_], w=[dstT])

                def slotA():
                    sg = B["sgA"][B["nA"] % 2]
                    B["nA"] += 1
                    return sg

                def slotR():
                    sg = B["sgR"][B["nR"] % 2]
                    B["nR"] += 1
                    return sg

                def outT(dst, sg, b0, nblk):
                    dma(dst[:, :, tok].rearrange("h d t -> d h t"), sg[:, b0:b0 + nblk, :], r=[sg])

                if t + 2 < NT:
                    pass
                I(A, lambda: nc.scalar.activation(out=nbf[:, 0:1024], in_=xtt[:], func=AF.Square, accum_out=ss1[:, 0:1]), r=[xtt], w=[nb, ss1])
                rsqrt_ms(ss1, ss1[:, 0:1], rs1, rs1[:, 0:1], 1.0 / 1024)
                I(V, lambda: nc.vector.scalar_tensor_tensor(out=f_b[:, 0:1024], in0=xtt[:], scalar=rs1[:, 0:1], in1=modb[:, A_ATT],
                                                            op0=ALU.mult, op1=ALU.mult), r=[xtt, rs1, modb], w=[f_b])
                I(G, lambda: nc.gpsimd.tensor_tensor(out=nbf[:, 0:1024], in0=f_b[:, 0:1024], in1=modb[:, SH_A], op=ALU.add), r=[f_b, modb], w=[nb])
                transposes(nb, lambda i: nbf[:, i * 128:(i + 1) * 128], 8, 128, hT, hT[:])
                if t + 2 < NT:
                    load_tile(t + 2)
                lh = lambda kc: hT[:, kc, :]
                for gq in range(2):
                    p = proj(hT, lh, 8, winb, O_NQ + gq * 512, O_NQ + (gq + 1) * 512)
                    evac(p, 512, f_a, f_a[:, gq * 512:(gq + 1) * 512], eng=(A if gq else V))
                norm_rope(f_a, f_a[:, 0:1024].rearrange("p (h d) -> p h d", h=8), 8, 128, gq_t, True, 0, 64, 0, nb[:, :, 0:128], nb)
                sg = slotA()
                transposes(nb, lambda i: nb[:, i, 0:128], 8, 128, sg, sg[:, 0:8, :])
                outT(QaT, sg, 0, 8)
                p = proj(hT, lh, 8, winb, O_NKC, O_NKC + 512)
                evac(p, 512, f_a, f_a[:, 0:512])
                norm_rope(f_a, f_a[:, 0:256].rearrange("p (h d) -> p h d", h=2), 2, 128, None, False, 0, 64, 0, nb[:, 0:2, 0:128], nb)
                I(V, lambda: nc.vector.tensor_copy(out=nb[:, 2:4, 0:128], in_=f_a[:, 256:512].rearrange("p (h d) -> p h d", h=2)),
                  r=[f_a], w=[nb])
                sg = slotA()
                transposes(nb, lambda i: nb[:, i, 0:128], 4, 128, sg, sg[:, 0:4, :])
                outT(KcT, sg, 0, 2)
                outT(VcT, sg, 2, 2)
                sg = slotA()
                for wi, (off, gt_) in enumerate(((O_NKS, gks_t), (O_NKW, gkw_t))):
                    p = proj(hT, lh, 8, winb, off, off + 512)
                    evac(p, 512, f_a, f_a[:, 0:512])
                    norm_rope(f_a, f_a[:, 0:256].rearrange("p (h d) -> p h d", h=2), 2, 128, gt_, True, 0, 64, 0, nb[:, 0:2, 0:128], nb)
                    I(V, lambda wi=wi: nc.vector.tensor_copy(out=vbt[:, 2 * wi:2 * wi + 2, :], in_=f_a[:, 256:512].rearrange("p (h d) -> p h d", h=2)),
                      r=[f_a], w=[vbt])
                    transposes(nb, lambda i: nb[:, i, 0:128], 2, 128, sg, sg[:, 2 * wi:2 * wi + 2, :])
                outT(KsT, sg, 0, 2)
                outT(KwT, sg, 2, 2)
                dma(Vs[tok, :, :], vbt[:, 0:2, :], r=[vbt])
                dma(Vw[tok, :, :], vbt[:, 2:4, :], r=[vbt])
                p = proj(hT, lh, 8, winb, O_NG, O_CKV)
                I(A, lambda: nc.scalar.activation(out=gnt[:], in_=p[:, 0:24], func=AF.Sigmoid), r=[p], w=[gnt])
                evac(p, 408, f_a, f_a[:, 0:408], eng=V)
                dma(Gn[tok, :], gnt[:], r=[gnt])
                norm_rope(f_a, f_a[:, 24:408].rearrange("p (h d) -> p h d", h=1), 1, 384, gcq_t, True, 0, 0, 0,
                          nbf[:, 0:384].rearrange("p (h d) -> p h d", h=1), nb)
                transposes(nb, lambda i: nbf[:, i * 128:(i + 1) * 128], 3, 128, cqT, cqT[:])
                p = proj(hT, lh, 8, winb, O_CKV, O_BG)
                evac(p, 320, f_a, f_a[:, 0:320], eng=V)
                I(G, lambda: nc.gpsimd.tensor_copy(out=krf[:], in_=f_a[:, 256:320]), r=[f_a], w=[krf])
                norm_rope(f_a, f_a[:, 0:256].rearrange("p (h d) -> p h d", h=1), 1, 256, gckv_t, True, 0, 0, 0,
                          nbf[:, 512:768].rearrange("p (h d) -> p h d", h=1), nb)
                transposes(nb, lambda i: nbf[:, 512 + i * 128:512 + (i + 1) * 128], 2, 128, ckvT, ckvT[:])
                for j in range(4):
                    p = proj(hT, lh, 8, winb, O_BG + j * 512, O_BG + (j + 1) * 512)
                    bg_ = B["bgt"][j % 2]
                    I(A, lambda bg_=bg_, p=p: nc.scalar.activation(out=bg_[:], in_=p[:, 0:512], func=AF.Sigmoid), r=[p], w=[bg_])
                    dma(BG[tok, j * 512:(j + 1) * 512], bg_[:], r=[bg_])
                for j in range(3):
                    p = proj(cqT, lambda kc: cqT[:, kc, :], 3, wuqb, j * 512, (j + 1) * 512)
                    evac(p, 512, f_a, f_a[:, j * 512:(j + 1) * 512], eng=(A if j % 2 else V))
                norm_rope(f_a, f_a[:, 0:1536].rearrange("p (h d) -> p h d", h=8), 8, 192, gmq_t, True, 128, 32, 128, nb[:, :, :], nb)
                sg = slotA()
                transposes(nb, lambda i: nb[:, i, 0:128], 8, 128, sg, sg[:, 0:8, :])
                outT(QbN, sg, 0, 8)
                sgr = slotR()
                transposes(nb, lambda i: nb[:, i, 128:192], 8, 64, sgr, sgr[:, 0:8, :])
                outT(QbR, sgr, 0, 8)
                for half in range(2):
                    for j in range(2):
                        c0 = half * 1024 + j * 512
                        p = proj(ckvT, lambda kc: ckvT[:, kc, :], 2, wukvb, c0, c0 + 512)
                        evac(p, 512, f_a, f_a[:, j * 512:(j + 1) * 512], eng=(A if j % 2 else V))
                    kvv = f_a[:, 0:1024].rearrange("p (h d) -> p h d", h=4)
                    I(G, lambda half=half, kvv=kvv: nc.gpsimd.tensor_copy(out=vbt[:, 4 * half:4 * half + 4, :], in_=kvv[:, :, 128:256]), r=[f_a], w=[vbt])
                    I(V, lambda kvv=kvv: nc.vector.tensor_copy(out=kvv[:, :, 128:192], in_=krf[:].unsqueeze(1).broadcast_to([128, 4, 64])),
                      r=[krf, f_a], w=[f_a])
                    norm_rope(f_a, kvv[:, :, 0:192], 4, 192, gmk_t, True, 128, 32, 128, nb[:, 4 * half:4 * half + 4, :], nb)
                dma(Vb[tok, :, :], vbt[:], r=[vbt])
                sg = slotA()
                transposes(nb, lambda i: nb[:, i, 0:128], 8, 128, sg, sg[:, 0:8, :])
                outT(KbN, sg, 0, 8)
                sgr = slotR()
                transposes(nb, lambda i: nb[:, i, 128:192], 8, 64, sgr, sgr[:, 0:8, :])
                outT(KbR, sgr, 0, 8)

            load_tile(0)
            if NT > 1:
                load_tile(1)
            str0, str1 = [], []
            for t in range(0, NT, 2):
                str0 += fw.record(lambda t=t: p1_tile(t))
                if t + 1 < NT:
                    str1 += fw.record(lambda t=t: p1_tile(t + 1))
            skew = (len(str0) // max(1, (NT + 1) // 2)) // 2
            fw.interleave([str0, [(lambda: None)] * skew + str1])
            fw.barrier()
            if upto == 1:
                return nc

        SC128 = 128.0 ** -0.5
        SC192 = 192.0 ** -0.5

        def tr_generic(pbuf, src_t, src_ap_fn, n, rows, dstT, dst_ap, eng=A):
            for i in range(n):
                I(PE, lambda i=i: nc.tensor.transpose(out=pbuf[0:rows, i * 128:(i + 1) * 128], in_=src_ap_fn(i), identity=ident[:]),
                  r=[src_t, ident], w=[pbuf])
            if eng == A:
                I(A, lambda: nc.scalar.copy(out=dst_ap, in_=pbuf[0:rows, 0:n * 128].rearrange("p (a b) -> p a b", a=n)), r=[pbuf], w=[dstT])
            else:
                I(V, lambda: nc.vector.tensor_copy(out=dst_ap, in_=pbuf[0:rows, 0:n * 128].rearrange("p (a b) -> p a b", a=n)), r=[pbuf], w=[dstT])

        with contextlib.ExitStack() as st23:
            kcT = [sb(st23, f"kcT{g}", [128, NCP], BF16) for g in range(2)]
            vcb = [sb(st23, f"vcb{g}", [128, NCT, 128], BF16) for g in range(2)]
            for g in range(2):
                I(V, lambda g=g: nc.vector.memset(kcT[g][:], 0.0), w=[kcT[g]])
                I(G, lambda g=g: nc.gpsimd.memset(vcb[g][:], 0.0), w=[vcb[g]])
            with contextlib.ExitStack() as st:
                XT = sb(st, "XT", [128, S], BF16)
                Xl = sb(st, "Xl", [128, 32, 512], BF16)
                w1s = sb(st, "w1s", [128, 32 * 256], F32)
                w1b = sb(st, "w1b", [128, 32, 256], BF16)
                w2s = sb(st, "w2s", [128, 256], F32)
                w2b = sb(st, "w2b", [128, 2, 128], BF16)
                peT = sb(st, "peT", [128, 32], F32)
                hTc = sb(st, "hTc", [128, 2, 512], BF16)
                gkc_t = bcast_load(st, "gkc_t", g_kc, 128)
                kf = sb(st, "kf", [128, 128], F32)
                kf2 = sb(st, "kf2", [128, 128], F32)
                kb = sb(st, "kb", [128, 128], BF16)
                ssk = sb(st, "ssk", [128, 1], F32)
                rsk = sb(st, "rsk", [128, 1], F32)
                pc = [ps(st, f"pc{i}", [128, 512]) for i in range(2)]
                pk = ps(st, "pk", [128, 128])
                pkT = ps(st, "pkT", [128, 128], BF16)
                for kv in range(2):
                    w1, w2, pe_ = (k_w1, k_w2, pe_kT) if kv == 0 else (v_w1, v_w2, pe_vT)
                    dma(w1s[:].rearrange("p (l c) -> p l c", l=32), w1.rearrange("(l d) c -> d l c", d=128), w=[w1s])
                    I(V, lambda: nc.vector.tensor_copy(out=w1b[:], in_=w1s[:].rearrange("p (l c) -> p l c", l=32)), r=[w1s], w=[w1b])
                    dma(w2s[:].rearrange("p (k d) -> p k d", k=2), w2.rearrange("(k c) d -> c k d", c=128), w=[w2s])
                    I(G, lambda: nc.gpsimd.tensor_copy(out=w2b[:], in_=w2s[:].rearrange("p (k d) -> p k d", k=2)), r=[w2s], w=[w2b])
                    dma(peT[:], pe_, w=[peT])
                    for g in range(2):
                        src = KcT if kv == 0 else VcT
                        dma(XT[:], src[g, :, :], w=[XT])
                        for l in range(32):
                            xin = XT[:, l:l + 16 * (n_cmp - 1) + 1:16]
                            if l % 2 == 0:
                                I(V, lambda l=l, xin=xin: nc.vector.tensor_scalar(out=Xl[:, l, 0:n_cmp], in0=xin, scalar1=peT[:, l:l + 1],
                                                                                  scalar2=None, op0=ALU.add), r=[XT, peT], w=[Xl])
                            else:
                                I(A, lambda l=l, xin=xin: nc.scalar.activation(out=Xl[:, l, 0:n_cmp], in_=xin, func=AF.Identity,
                                                                               bias=peT[:, l:l + 1], scale=1.0), r=[XT, peT], w=[Xl])
                        for ch in range(2):
                            p = pc[ch]
                            for l in range(32):
                                I(PE, lambda l=l, p=p, ch=ch: nc.tensor.matmul(p[:, 0:n_cmp], lhsT=w1b[:, l, ch * 128:(ch + 1) * 128],
                                                                               rhs=Xl[:, l, 0:n_cmp], start=(l == 0), stop=(l == 31)),
                                  r=[w1b, Xl], w=[p])
                            I(A, lambda p=p, ch=ch: nc.scalar.activation(out=hTc[:, ch, 0:n_cmp], in_=p[:, 0:n_cmp], func=AF.Silu),
                              r=[p], w=[hTc])
                        for nt in range(NCT):
                            rows = min(128, n_cmp - nt * 128)
                            for ch in range(2):
                                I(PE, lambda ch=ch, nt=nt, rows=rows: nc.tensor.matmul(pk[0:rows, :], lhsT=hTc[:, ch, nt * 128:nt * 128 + rows],
                                                                                       rhs=w2b[:, ch, :], start=(ch == 0), stop=(ch == 1)),
                                  r=[hTc, w2b], w=[pk])
                            if kv == 0:
                                I(A, lambda rows=rows: nc.scalar.copy(out=kf[0:rows, :], in_=pk[0:rows, :]), r=[pk], w=[kf])
                                I(V, lambda rows=rows: nc.vector.tensor_tensor(out=kf2[0:rows, :], in0=kf[0:rows, :], in1=kf[0:rows, :], op=ALU.mult),
                                  r=[kf], w=[kf2])
                                I(V, lambda rows=rows: nc.vector.tensor_reduce(out=ssk[0:rows, :], in_=kf2[0:rows, :], axis=AX.X, op=ALU.add),
                                  r=[kf2], w=[ssk])
                                rsqrt_ms(ssk, ssk[0:rows, :], rsk, rsk[0:rows, :], 1.0 / 128, rows=rows)
                                I(V, lambda rows=rows: nc.vector.scalar_tensor_tensor(out=kb[0:rows, :], in0=kf[0:rows, :], scalar=rsk[0:rows, 0:1],
                                                                                      in1=gkc_t[0:rows, :], op0=ALU.mult, op1=ALU.mult),
                                  r=[kf, rsk, gkc_t], w=[kb])
                                I(PE, lambda rows=rows: nc.tensor.transpose(out=pkT[:, 0:rows], in_=kb[0:rows, :], identity=ident[0:rows, 0:rows]),
                                  r=[kb, ident], w=[pkT])
                                I(A, lambda rows=rows, nt=nt, g=g: nc.scalar.copy(out=kcT[g][:, nt * 128:nt * 128 + rows], in_=pkT[:, 0:rows]),
                                  r=[pkT], w=[kcT[g]])
                            else:
                                I(A, lambda rows=rows, nt=nt, g=g: nc.scalar.copy(out=vcb[g][0:rows, nt, :], in_=pk[0:rows, :]), r=[pk], w=[vcb[g]])
                fw.barrier()
            if upto == 2:
                return nc
            with contextlib.ExitStack() as st:
                W0 = 512 + 8 * (NT - 1)
                m0 = sb(st, "m0", [128, W0], F32)
                aext = sb(st, "aext", [128, NSEL + CO], F32)
                fext = sb(st, "fext", [128, NSEL + CO], F32)
                dma(m0[:], c_m0, w=[m0])
                dma(aext[:], c_aext, w=[aext])
                dma(fext[:], c_fext, w=[fext])
                PW = 4 * NSEL + 8

                def mkset(u):
                    B = {}
                    B["qT"] = [sb(st, f"qT{u}{i}", [128, 8, 128], BF16) for i in range(2)]
                    B["gn3"] = [sb(st, f"gn3{u}{i}", [128, 24], F32) for i in range(2)]
                    B["Ef"] = [sb(st, f"Ef{u}{i}", [128, 512], F32) for i in range(2)]
                    B["Pm"] = sb(st, f"Pm{u}", [128, 8, 512], F32)
                    B["Pb"] = [sb(st, f"Pb{u}{i}", [128, 512], BF16) for i in range(2)]
                    B["PbT"] = [sb(st, f"PbT{u}{i}", [128, 4, 128], BF16) for i in range(2)]
                    for nm, shp, dt in (("Z", [128, 8], F32), ("rz", [128, 8], F32), ("gz", [128, 8], F32), ("ppad", [128, 2, PW], F32),
                                        ("imp", [128, 2, NSEL], F32), ("score", [128, 2, NSEL], F32), ("sc2", [128, 2, NSEL], F32),
                                        ("m1", [128, 2, 8], F32), ("m2", [128, 2, 8], F32), ("Btb", [128, 2, NSEL], BF16),
                                        ("BtT", [KJ, 2, 128], BF16), ("ocm", [128, 8, 128], F32)):
                        B[nm] = sb(st, f"{nm}{u}", shp, dt)
                    B["pS"] = ps(st, f"pS{u}", [128, 512])
                    B["pTB"] = ps(st, f"pTB{u}", [128, 1024], BF16)
                    B["pO"] = [ps(st, f"pO{u}{i}", [128, 512]) for i in range(2)]
                    I(V, lambda: nc.vector.memset(B["ppad"][:], 0.0), w=[B["ppad"]])
                    return B
                sets = [mkset(0), mkset(1)]

                def ld3(t):
                    B = sets[t % 2]
                    b = (t // 2) % 2
                    tk = slice(t * 128, (t + 1) * 128)
                    dma(B["qT"][b][:], QaT[:, :, tk].rearrange("h d t -> d h t"), w=[B["qT"][b]])
                    dma(B["gn3"][b][:], Gn[tk, :], w=[B["gn3"][b]])

                def p3_tile(t):
                    B = sets[t % 2]
                    b = (t // 2) % 2
                    if t + 2 < NT:
                        ld3(t + 2)
                    tk = slice(t * 128, (t + 1) * 128)
                    NW = min(n_cmp, 8 * t + 7)
                    off = 8 * (NT - 1) - 8 * t
                    nkt = (NW + 127) // 128
                    q_, gn_, oc_ = B["qT"][b], B["gn3"][b], B["ocm"]
                    Ef, Pm, Pb, PbT, Z, rz, gz, ppad = B["Ef"], B["Pm"], B["Pb"], B["PbT"], B["Z"], B["rz"], B["gz"], B["ppad"]
                    imp, score, sc2, m1, m2, Btb, BtT = B["imp"], B["score"], B["sc2"], B["m1"], B["m2"], B["Btb"], B["BtT"]
                    pS_, pTB, pO = B["pS"], B["pTB"], B["pO"]
                    for h in range(8):
                        g = h // 4
                        hb_ = h % 2
                        I(PE, lambda h=h, g=g: nc.tensor.matmul(pS_[:, 0:NW], lhsT=q_[:, h, :], rhs=kcT[g][:, 0:NW], start=True, stop=True),
                          r=[q_, kcT[g]], w=[pS_])
                        I(A, lambda hb_=hb_: nc.scalar.activation(out=Ef[hb_][:, 0:NW], in_=pS_[:, 0:NW], func=AF.Exp, scale=SC128),
                          r=[pS_, eps_t], w=[Ef[hb_]])
                        I(V, lambda h=h, hb_=hb_: nc.vector.scalar_tensor_tensor(out=Pm[:, h, 0:NW], in0=Ef[hb_][:, 0:NW], scalar=1.0,
                                                                                  in1=m0[:, off:off + NW], op0=ALU.mult, op1=ALU.mult,
                                                                                  accum_out=Z[:, h:h + 1]), r=[Ef[hb_], m0], w=[Pm, Z])
                        I(G, lambda h=h, hb_=hb_: nc.gpsimd.tensor_copy(out=Pb[hb_][:, 0:NW], in_=Pm[:, h, 0:NW]), r=[Pm], w=[Pb[hb_]])
                        for kt in range(nkt):
                            rows = min(128, NW - kt * 128)
                            I(PE, lambda kt=kt, rows=rows, hb_=hb_: nc.tensor.transpose(out=pTB[0:rows, kt * 128:(kt + 1) * 128],
                                                                                         in_=Pb[hb_][:, kt * 128:kt * 128 + rows], identity=ident[:]),
                              r=[Pb[hb_], ident], w=[pTB])
                        for kt in range(nkt):
                            rows = min(128, NW - kt * 128)
                            I(A, lambda kt=kt, rows=rows, hb_=hb_: nc.scalar.copy(out=PbT[hb_][0:rows, kt, :], in_=pTB[0:rows, kt * 128:(kt + 1) * 128]),
                              r=[pTB], w=[PbT[hb_]])
                        for kt in range(nkt):
                            rows = min(128, NW - kt * 128)
                            I(PE, lambda kt=kt, rows=rows, h=h, g=g, hb_=hb_: nc.tensor.matmul(
                                pO[g][:, (h % 4) * 128:(h % 4 + 1) * 128], lhsT=PbT[hb_][0:rows, kt, :], rhs=vcb[g][0:rows, kt, :],
                                start=(h % 4 == 0 and kt == 0), stop=(kt == nkt - 1)), r=[PbT[hb_], vcb[g]], w=[pO[g]])
                    I(V, lambda: nc.vector.tensor_scalar(out=rz[:], in0=Z[:], scalar1=1e-30, scalar2=None, op0=ALU.max), r=[Z], w=[rz])
                    I(V, lambda: nc.vector.reciprocal(out=rz[:], in_=rz[:]), r=[rz], w=[rz])
                    I(V, lambda: nc.vector.tensor_tensor(out=gz[:], in0=rz[:], in1=gn_[:].rearrange("p (h j) -> p h j", j=3)[:, :, 0], op=ALU.mult),
                      r=[rz, gn_], w=[gz])
                    for g in range(2):
                        I(V, lambda g=g: nc.vector.tensor_tensor(out=oc_[:, 4 * g:4 * g + 4, :], in0=pO[g][:, 0:512].rearrange("p (h d) -> p h d", h=4),
                                                                 in1=gz[:, 4 * g:4 * g + 4].unsqueeze(2).broadcast_to([128, 4, 128]), op=ALU.mult),
                          r=[pO[g], gz], w=[oc_])
                    dma(Oc[tk, :], oc_[:].rearrange("p h d -> p (h d)"), r=[oc_])
                    for g in range(2):
                        for r_ in range(4):
                            h = 4 * g + r_
                            if r_ == 0:
                                I(V, lambda g=g, h=h: nc.vector.tensor_scalar(out=ppad[:, g, 4:4 + NW], in0=Pm[:, h, 0:NW], scalar1=rz[:, h:h + 1],
                                                                              scalar2=None, op0=ALU.mult), r=[Pm, rz], w=[ppad])
                            else:
                                I(V, lambda g=g, h=h: nc.vector.scalar_tensor_tensor(out=ppad[:, g, 4:4 + NW], in0=Pm[:, h, 0:NW], scalar=rz[:, h:h + 1],
                                                                                     in1=ppad[:, g, 4:4 + NW], op0=ALU.mult, op1=ALU.add),
                                  r=[Pm, rz, ppad], w=[ppad])
                    a_t = aext[:, CO - 2 * t:CO - 2 * t + NSEL]
                    f_t = fext[:, CO - 2 * t:CO - 2 * t + NSEL]
                    for g in range(2):
                        I(V, lambda g=g: nc.vector.tensor_reduce(out=imp[:, g, :], in_=ppad[:, g, 4:4 + 4 * NSEL].rearrange("p (j f) -> p j f", f=4),
                                                                 axis=AX.X, op=ALU.add), r=[ppad], w=[imp])
                        I(V, lambda g=g: nc.vector.tensor_tensor(out=imp[:, g, :], in0=imp[:, g, :],
                                                                 in1=ppad[:, g, 0:4 * NSEL].rearrange("p (j f) -> p j f", f=4)[:, :, 3], op=ALU.add),
                          r=[ppad, imp], w=[imp])
                        I(V, lambda g=g: nc.vector.tensor_tensor(out=score[:, g, :], in0=imp[:, g, :], in1=a_t, op=ALU.mult), r=[imp, aext], w=[score])
                        I(V, lambda g=g: nc.vector.tensor_tensor(out=score[:, g, :], in0=score[:, g, :], in1=f_t, op=ALU.add), r=[score, fext], w=[score])
                        I(V, lambda g=g: nc.vector.memset(score[:, g, 0:1], 1e9), r=[score], w=[score])
                        if NSEL > 16:
                            I(V, lambda g=g: nc.vector.max(out=m1[:, g, :], in_=score[:, g, :]), r=[score], w=[m1])
                            I(V, lambda g=g: nc.vector.match_replace(out=sc2[:, g, :], in_to_replace=m1[:, g, :], in_values=score[:, g, :], imm_value=-2.0),
                              r=[score, m1], w=[sc2])
                            I(V, lambda g=g: nc.vector.max(out=m2[:, g, :], in_=sc2[:, g, :]), r=[sc2], w=[m2])
                            I(V, lambda g=g: nc.vector.tensor_scalar(out=Btb[:, g, :], in0=score[:, g, :], scalar1=m2[:, g, 7:8], scalar2=NEGB,
                                                                     op0=ALU.is_lt, op1=ALU.mult), r=[score, m2], w=[Btb])
                        else:
                            I(V, lambda g=g: nc.vector.memset(Btb[:, g, :], 0.0), w=[Btb])
                        I(PE, lambda g=g: nc.tensor.transpose(out=pTB[0:KJ, 512 + g * 128:512 + (g + 1) * 128], in_=Btb[:, g, 0:KJ], identity=ident[:]),
                          r=[Btb, ident], w=[pTB])
                    I(A, lambda: nc.scalar.copy(out=BtT[:], in_=pTB[0:KJ, 512:768].rearrange("p (g t) -> p g t", g=2)), r=[pTB], w=[BtT])
                    dma(BT[:, :, tk].rearrange("g j t -> j g t"), BtT[:], r=[BtT])

                ld3(0)
                if NT > 1:
                    ld3(1)
                str0, str1 = [], []
                for t in range(0, NT, 2):
                    str0 += fw.record(lambda t=t: p3_tile(t))
                    if t + 1 < NT:
                        str1 += fw.record(lambda t=t: p3_tile(t + 1))
                skew = (len(str0) // max(1, (NT + 1) // 2)) // 2
                fw.interleave([str0, [(lambda: None)] * skew + str1])
                fw.barrier()
        if upto == 3:
            return nc

        def attn_pipeline(pairs, stageA, stageB):
            n = len(pairs)
            if n == 0:
                return
            stageA(0, pairs[0])
            for i in range(n):
                if i + 1 < n:
                    stageA(i + 1, pairs[i + 1])
                stageB(i, pairs[i])

        with contextlib.ExitStack() as st:
            esel = sb(st, "esel", [KJ, NT, 128], BF16)
            dma(esel[:], c_esel, w=[esel])
            Ks_s = sb(st, "Ks_s", [128, S], BF16)
            Kw_s = sb(st, "Kw_s", [128, S], BF16)
            Vs_s = sb(st, "Vs_s", [128, NT, 129], BF16)
            Vw_s = sb(st, "Vw_s", [128, NT, 129], BF16)
            qT4 = [sb(st, f"qT4{i}", [128, 4, 128], BF16) for i in range(4)]
            btl = [sb(st, f"btl{i}", [KJ, 128], BF16) for i in range(4)]
            BT4 = [sb(st, f"BT4{i}", [KJ, 4, 128], BF16) for i in range(4)]
            PT = [sb(st, f"PT{i}", [128, 4, 128], BF16) for i in range(3)]
            gn4 = [sb(st, f"gn4{i}", [128, 24], F32) for i in range(4)]
            ocl = [sb(st, f"ocl{i}", [128, 4, 128], F32) for i in range(4)]
            bga = [sb(st, f"bga{i}", [128, 512], BF16) for i in range(4)]
            oa = sb(st, "oa", [128, 4, 128], F32)
            yat = [sb(st, f"yat{i}", [128, 512], F32) for i in range(2)]
            rzs = sb(st, "rzs", [128, 8], F32)
            cfs = sb(st, "cfs", [128, 8], F32)
            pS = [ps(st, f"qS{i}", [128, 512]) for i in range(2)]
            pOTs = [ps(st, f"pOTs{i}", [128, 512]) for i in range(2)]
            pOTw = [ps(st, f"pOTw{i}", [128, 512]) for i in range(2)]
            pTr = ps(st, "pTr4", [128, 512])
            pZ4 = ps(st, "pZ4", [128, 512])
            acc_s = [sb(st, f"acc_s{i}", [128, 512], F32) for i in range(2)]
            acc_w = [sb(st, f"acc_w{i}", [128, 512], F32) for i in range(2)]
            oT_s = sb(st, "oT_s", [128, 512], F32)
            oT_w = sb(st, "oT_w", [128, 512], F32)
            I(V, lambda: nc.vector.memset(Vs_s[:, :, 128:129], 1.0), w=[Vs_s])
            I(V, lambda: nc.vector.memset(Vw_s[:, :, 128:129], 1.0), w=[Vw_s])
            cnt4 = [0]
            for g in range(2):
                dma(Ks_s[:], KsT[g, :, :], w=[Ks_s])
                dma(Kw_s[:], KwT[g, :, :], w=[Kw_s])
                dma(Vs_s[:, :, 0:128], Vs[:, g, :].rearrange("(t p) d -> p t d", p=128), w=[Vs_s])
                dma(Vw_s[:, :, 0:128], Vw[:, g, :].rearrange("(t p) d -> p t d", p=128), w=[Vw_s])

                def ld4(t, g=g):
                    b = t % 4
                    tk = slice(t * 128, (t + 1) * 128)
                    dma(qT4[b][:], QaT[4 * g:4 * g + 4, :, tk].rearrange("h d t -> d h t"), w=[qT4[b]])
                    dma(btl[b][:], BT[g, :, tk], w=[btl[b]])
                    dma(gn4[b][:], Gn[tk, :], w=[gn4[b]])
                    dma(ocl[b][:], Oc[tk, 4 * g * 128:(4 * g + 4) * 128].rearrange("p (h d) -> p h d", h=4), w=[ocl[b]])
                    dma(bga[b][:], BG[tk, 4 * g * 128:(4 * g + 4) * 128], w=[bga[b]])

                def bt4(t):
                    b = t % 4
                    I(G, lambda: nc.gpsimd.tensor_copy(out=BT4[b][:], in_=btl[b][:].unsqueeze(1).broadcast_to([KJ, 4, 128])), r=[btl[b]], w=[BT4[b]])

                entries = []
                for t in range(NT):
                    prs = [("s", kt) for kt in range(t + 1)] + [("w", kt) for kt in range(max(0, t - 4), t + 1)]
                    for idx, (kind, kt) in enumerate(prs):
                        entries.append((t, kind, kt, idx == 0, idx == len(prs) - 1))
                base = cnt4[0]
                cnt4[0] += len(entries)
                ld4(0)
                if NT > 1:
                    ld4(1)
                bt4(0)

                def stA(i, e):
                    t, kind, kt, first_of_tile, last_of_tile = e
                    if first_of_tile:
                        if t + 2 < NT:
                            ld4(t + 2)
                        if t + 1 < NT:
                            bt4(t + 1)
                    b = t % 4
                    q_ = qT4[b]
                    k = base + i
                    p = pS[k % 2]
                    pt = PT[k % 3]
                    ksl = slice(kt * 128, (kt + 1) * 128)
                    if kind == "s":
                        I(PE, lambda: nc.tensor.matmul(p[:, :], lhsT=Ks_s[:, ksl], rhs=q_[:].rearrange("p h t -> p (h t)"), start=True, stop=False),
                          r=[Ks_s, q_], w=[p])
                        I(PE, lambda: nc.tensor.matmul(p[:, :], lhsT=esel[:, kt, :], rhs=BT4[b][:].rearrange("p h t -> p (h t)"), start=False, stop=True),
                          r=[esel, BT4[b]], w=[p])
                    else:
                        I(PE, lambda: nc.tensor.matmul(p[:, :], lhsT=Kw_s[:, ksl], rhs=q_[:].rearrange("p h t -> p (h t)"), start=True, stop=True),
                          r=[Kw_s, q_], w=[p])
                    I(A, lambda: nc.scalar.activation(out=pt[:].rearrange("p h t -> p (h t)"), in_=p[:, :], func=AF.Exp, scale=SC128),
                      r=[p], w=[pt])
                    mk = None
                    if kt == t:
                        mk = tri
                    elif kind == "w" and kt == t - 4:
                        mk = atri
                    if mk is not None:
                        I(G, lambda: nc.gpsimd.tensor_tensor(out=pt[:], in0=pt[:], in1=mk[:].unsqueeze(1).broadcast_to([128, 4, 128]), op=ALU.mult),
                          r=[pt, mk], w=[pt])

                def stB(i, e):
                    t, kind, kt, first_of_tile, last_of_tile = e
                    k = base + i
                    pt = PT[k % 3]
                    ptf = pt[:].rearrange("p h t -> p (h t)")
                    if kind == "s":
                        pO_, vv, acc_, first, last = pOTs[t % 2], Vs_s, acc_s[t % 2], (kt == 0), (kt == t)
                    else:
                        pO_, vv, acc_, first, last = pOTw[t % 2], Vw_s, acc_w[t % 2], (kt == max(0, t - 4)), (kt == t)
                    I(PE, lambda: nc.tensor.matmul(pO_[:, :], lhsT=vv[:, kt, 0:128], rhs=ptf, start=first, stop=last), r=[pt, vv], w=[pO_])
                    if kind == "s":
                        if first:
                            I(V, lambda: nc.vector.tensor_copy(out=acc_[:], in_=ptf), r=[pt], w=[acc_])
                        else:
                            I(V, lambda: nc.vector.tensor_tensor(out=acc_[:], in0=acc_[:], in1=ptf, op=ALU.add), r=[pt, acc_], w=[acc_])
                    else:
                        if first:
                            I(G, lambda: nc.gpsimd.tensor_copy(out=acc_[:], in_=ptf), r=[pt], w=[acc_])
                        else:
                            I(G, lambda: nc.gpsimd.tensor_tensor(out=acc_[:], in0=acc_[:], in1=ptf, op=ALU.add), r=[pt, acc_], w=[acc_])

                def finA(t, g=g):
                    b = t % 4
                    for ki, (pO_, acc_, oT_) in enumerate(((pOTs[t % 2], acc_s[t % 2], oT_s), (pOTw[t % 2], acc_w[t % 2], oT_w))):
                        for hh in range(4):
                            I(PE, lambda hh=hh, ki=ki, acc_=acc_: nc.tensor.matmul(pZ4[:, 8 * ki + 2 * hh:8 * ki + 2 * hh + 2], lhsT=acc_[:, hh * 128:(hh + 1) * 128],
                                                                                    rhs=ones_f[:, 0:2], start=True, stop=True), r=[acc_, ones_f], w=[pZ4])
                        I(A, lambda pO_=pO_, oT_=oT_: nc.scalar.copy(out=oT_[:], in_=pO_[:, :]), r=[pO_], w=[oT_])
                    for hh in range(4):
                        I(PE, lambda hh=hh: nc.tensor.transpose(out=pTr[:, hh * 128:(hh + 1) * 128], in_=oT_s[:, hh * 128:(hh + 1) * 128], identity=identf[:]),
                          r=[oT_s, identf], w=[pTr])
                    I(V, lambda: nc.vector.reciprocal(out=rzs[:, 0:8], in_=pZ4[:, 0:16:2]), r=[pZ4], w=[rzs])
                    gv = gn4[b][:].rearrange("p (h j) -> p h j", j=3)
                    I(V, lambda: nc.vector.tensor_tensor(out=cfs[:, 0:4], in0=rzs[:, 0:4], in1=gv[:, 4 * g:4 * g + 4, 1], op=ALU.mult), r=[rzs, gn4[b]], w=[cfs])
                    I(V, lambda: nc.vector.tensor_tensor(out=cfs[:, 4:8], in0=rzs[:, 4:8], in1=gv[:, 4 * g:4 * g + 4, 2], op=ALU.mult), r=[rzs, gn4[b]], w=[cfs])
                    for hh in range(4):
                        I(V, lambda hh=hh: nc.vector.scalar_tensor_tensor(out=oa[:, hh, :], in0=pTr[:, hh * 128:(hh + 1) * 128], scalar=cfs[:, hh:hh + 1],
                                                                          in1=ocl[b][:, hh, :], op0=ALU.mult, op1=ALU.add),
                          r=[pTr, cfs, ocl[b]], w=[oa])

                def finB(t, g=g):
                    b = t % 4
                    tk = slice(t * 128, (t + 1) * 128)
                    for hh in range(4):
                        I(PE, lambda hh=hh: nc.tensor.transpose(out=pTr[:, hh * 128:(hh + 1) * 128], in_=oT_w[:, hh * 128:(hh + 1) * 128], identity=identf[:]),
                          r=[oT_w, identf], w=[pTr])
                    for hh in range(4):
                        I(V, lambda hh=hh: nc.vector.scalar_tensor_tensor(out=oa[:, hh, :], in0=pTr[:, hh * 128:(hh + 1) * 128], scalar=cfs[:, 4 + hh:5 + hh],
                                                                          in1=oa[:, hh, :], op0=ALU.mult, op1=ALU.add),
                          r=[pTr, cfs, oa], w=[oa])
                    yb_ = yat[t % 2]
                    I(G, lambda: nc.gpsimd.tensor_tensor(out=yb_[:], in0=oa[:].rearrange("p h d -> p (h d)"), in1=bga[b][:], op=ALU.mult),
                      r=[oa, bga[b]], w=[yb_])
                    dma(Ya[tk, 4 * g * 128:(4 * g + 4) * 128], yb_[:], r=[yb_])

                n_e = len(entries)
                pend = []
                stA(0, entries[0])
                for i in range(n_e):
                    while pend and pend[0][0] <= i:
                        _, fn_, t_ = pend.pop(0)
                        fn_(t_)
                    if i + 1 < n_e:
                        stA(i + 1, entries[i + 1])
                    stB(i, entries[i])
                    if entries[i][4]:
                        pend.append((i + 2, finA, entries[i][0]))
                        pend.append((i + 4, finB, entries[i][0]))
                while pend:
                    _, fn_, t_ = pend.pop(0)
                    fn_(t_)
            fw.barrier()
        if upto == 4:
            return nc

        NQB = S // 512
        NBLK = 8 * NQB
        with contextlib.ExitStack() as st:
            KN = [sb(st, f"KN{i}", [128, S], BF16) for i in range(2)]
            KR = [sb(st, f"KR{i}", [128, S], BF16) for i in range(2)]
            VB = [sb(st, f"VB{i}", [128, NT, 129], BF16) for i in range(2)]
            QN = [sb(st, f"QN{i}", [128, 512], BF16) for i in range(4)]
            QR = [sb(st, f"QR{i}", [128, 512], BF16) for i in range(4)]
            PT = [sb(st, f"PTm{i}", [128, 512], BF16) for i in range(3)]
            bgb = [sb(st, f"bgb{i}", [128, 4, 128], BF16) for i in range(4)]
            yal = [sb(st, f"yal{i}", [128, 4, 128], F32) for i in range(4)]
            yo = [sb(st, f"yo{i}", [128, 4, 128], BF16) for i in range(4)]
            obf = sb(st, "obf", [128, 4, 128], F32)
            rz5 = sb(st, "rz5", [128, 4], F32)
            pS = [ps(st, f"mS{i}", [128, 512]) for i in range(2)]
            pOT5 = [ps(st, f"mOT{j}", [128, 512]) for j in range(2)]
            pTr5 = ps(st, "mTr", [128, 512])
            pZ5 = ps(st, "mZ", [128, 512])
            acc5 = [sb(st, f"acc5{j}", [128, 512], F32) for j in range(2)]
            oT5 = sb(st, "oT5", [128, 512], F32)
            for i in range(2):
                I(V, lambda i=i: nc.vector.memset(VB[i][:, :, 128:129], 1.0), w=[VB[i]])
                I(G, lambda i=i: nc.gpsimd.memset(KR[i][64:128, :], 0.0), w=[KR[i]])
            for i in range(4):
                I(G, lambda i=i: nc.gpsimd.memset(QR[i][64:128, :], 0.0), w=[QR[i]])

            def ldh(h):
                b = h % 2
                dma(KN[b][:], KbN[h, :, :], w=[KN[b]])
                dma(KR[b][0:64, :], KbR[h, :, :], w=[KR[b]])
                dma(VB[b][:, :, 0:128], Vb[:, h, :].rearrange("(t p) d -> p t d", p=128), w=[VB[b]])

            def ldq(n):
                h, qb = n // NQB, n % NQB
                bq = n % 4
                rows = slice(qb * 512, (qb + 1) * 512)
                dma(QN[bq][:], QbN[h, :, rows], w=[QN[bq]])
                dma(QR[bq][0:64, :], QbR[h, :, rows], w=[QR[bq]])
                dma(bgb[bq][:], BG[rows, 1024 + h * 128:1024 + (h + 1) * 128].rearrange("(s p) c -> p s c", p=128), w=[bgb[bq]])
                dma(yal[bq][:], Ya[rows, h * 128:(h + 1) * 128].rearrange("(s p) c -> p s c", p=128), w=[yal[bq]])

            entries = []
            for n in range(NBLK):
                qb = n % NQB
                nk = 4 * qb + 4
                for kt in range(nk):
                    entries.append((n, kt, kt == 0, kt == nk - 1))
            ldh(0)
            ldq(0)
            if NBLK > 1:
                ldq(1)

            def stA(i, e):
                n, kt, first, last = e
                h, qb = n // NQB, n % NQB
                if first and n + 2 < NBLK:
                    ldq(n + 2)
                if qb == 0 and kt == 3 and h + 1 < 8:
                    ldh(h + 1)
                b, bq = h % 2, n % 4
                p = pS[i % 2]
                pt = PT[i % 3]
                j = kt - 4 * qb
                c0 = 128 * max(j, 0)
                ksl = slice(kt * 128, (kt + 1) * 128)
                I(PE, lambda: nc.tensor.matmul(p[:, c0:512], lhsT=KN[b][:, ksl], rhs=QN[bq][:, c0:512], start=True, stop=False), r=[KN[b], QN[bq]], w=[p])
                I(PE, lambda: nc.tensor.matmul(p[:, c0:512], lhsT=KR[b][:, ksl], rhs=QR[bq][:, c0:512], start=False, stop=True), r=[KR[b], QR[bq]], w=[p])
                I(A, lambda: nc.scalar.activation(out=pt[:, c0:512], in_=p[:, c0:512], func=AF.Exp, scale=SC192), r=[p], w=[pt])
                if j >= 0:
                    I(G, lambda: nc.gpsimd.tensor_tensor(out=pt[:, c0:c0 + 128], in0=pt[:, c0:c0 + 128], in1=tri[:], op=ALU.mult), r=[pt, tri], w=[pt])

            def stB(i, e):
                n, kt, first, last = e
                h, qb = n // NQB, n % NQB
                b = h % 2
                pt = PT[i % 3]
                pO_, acc_ = pOT5[n % 2], acc5[n % 2]
                j = kt - 4 * qb
                c0 = 128 * max(j, 0)
                I(PE, lambda: nc.tensor.matmul(pO_[:, c0:512], lhsT=VB[b][:, kt, 0:128], rhs=pt[:, c0:512], start=first, stop=last),
                  r=[pt, VB[b]], w=[pO_])
                if first:
                    I(V, lambda: nc.vector.tensor_copy(out=acc_[:], in_=pt[:, 0:512]), r=[pt], w=[acc_])
                else:
                    I(V, lambda: nc.vector.tensor_tensor(out=acc_[:, c0:512], in0=acc_[:, c0:512], in1=pt[:, c0:512], op=ALU.add), r=[pt, acc_], w=[acc_])

            def fin5(n):
                h, qb = n // NQB, n % NQB
                bq = n % 4
                rows = slice(qb * 512, (qb + 1) * 512)
                pO_, acc_ = pOT5[n % 2], acc5[n % 2]
                for sub in range(4):
                    I(PE, lambda sub=sub: nc.tensor.matmul(pZ5[:, 2 * sub:2 * sub + 2], lhsT=acc_[:, sub * 128:(sub + 1) * 128], rhs=ones_f[:, 0:2],
                                                           start=True, stop=True), r=[acc_, ones_f], w=[pZ5])
                I(A, lambda: nc.scalar.copy(out=oT5[:], in_=pO_[:, :]), r=[pO_], w=[oT5])
                for sub in range(4):
                    I(PE, lambda sub=sub: nc.tensor.transpose(out=pTr5[:, sub * 128:(sub + 1) * 128], in_=oT5[:, sub * 128:(sub + 1) * 128], identity=identf[:]),
                      r=[oT5, identf], w=[pTr5])
                I(V, lambda: nc.vector.reciprocal(out=rz5[:, 0:4], in_=pZ5[:, 0:8:2]), r=[pZ5], w=[rz5])
                for sub in range(4):
                    I(V, lambda sub=sub: nc.vector.tensor_scalar(out=obf[:, sub, :], in0=pTr5[:, sub * 128:(sub + 1) * 128], scalar1=rz5[:, sub:sub + 1],
                                                                 scalar2=None, op0=ALU.mult), r=[pTr5, rz5], w=[obf])
                I(G, lambda: nc.gpsimd.tensor_tensor(out=obf[:], in0=obf[:], in1=bgb[bq][:], op=ALU.mult), r=[obf, bgb[bq]], w=[obf])
                I(G, lambda: nc.gpsimd.tensor_tensor(out=yo[bq][:], in0=obf[:], in1=yal[bq][:], op=ALU.add), r=[obf, yal[bq]], w=[yo[bq]])
                dma(Yb[rows, h * 128:(h + 1) * 128].rearrange("(s p) c -> p s c", p=128), yo[bq][:], r=[yo[bq]])

            n_e = len(entries)
            pend = []
            stA(0, entries[0])
            for i in range(n_e):
                while pend and pend[0][0] <= i:
                    _, n_ = pend.pop(0)
                    fin5(n_)
                if i + 1 < n_e:
                    stA(i + 1, entries[i + 1])
                stB(i, entries[i])
                if entries[i][3]:
                    pend.append((i + 2, entries[i][0]))
            while pend:
                _, n_ = pend.pop(0)
                fin5(n_)
            fw.barrier()
        if upto == 5:
            return nc

        with contextlib.ExitStack() as st:
            wob = sb(st, "wob", [128, 8, 1024], BF16)
            with contextlib.ExitStack() as st2:
                stg = [sb(st2, f"stgo{i}", [128, 1024], F32) for i in range(2)]
                load_cast(stg, wob, lambda i: wob[:, i, :], lambda i: w_o[i * 128:(i + 1) * 128, :], 8, [128, 1024])
                fw.barrier()
            modb = TO(st.enter_context(nc.sbuf_tensor("sb_mod6a", [128, 1024], F32)), 2048)
            dma(modb.t[:], MODB[:, 2048:3072], w=[modb])
            yt = [sb(st, f"yt{i}", [128, 1024], BF16) for i in range(2)]
            xl = [sb(st, f"xla{i}", [128, 1024], F32) for i in range(2)]
            YT = sb(st, "YT", [128, 8, 128], BF16)
            tmp = sb(st, "tmpa", [128, 1024], F32)
            x1t = [sb(st, f"x1t{i}", [128, 1024], F32) for i in range(2)]
            pTa = [ps(st, f"pTa{i}", [128, 1024], BF16) for i in range(2)]
            pA = [ps(st, f"pA{i}", [128, 512]) for i in range(4)]

            def ld6(t):
                b = t % 2
                tk = slice(t * 128, (t + 1) * 128)
                dma(yt[b][:], Yb[tk, :], w=[yt[b]])
                dma(xl[b][:], x[tk, :], w=[xl[b]])
            ld6(0)
            for t in range(NT):
                if t + 1 < NT:
                    ld6(t + 1)
                b = t % 2
                tk = slice(t * 128, (t + 1) * 128)
                tr_generic(pTa[t % 2], yt[b], lambda i: yt[b][:, i * 128:(i + 1) * 128], 8, 128, YT, YT[:])
                for half in range(2):
                    p = pA[(2 * t + half) % 4]
                    hs = slice(half * 512, (half + 1) * 512)
                    for c in range(8):
                        I(PE, lambda c=c, p=p, hs=hs: nc.tensor.matmul(p[:, :], lhsT=YT[:, c, :], rhs=wob[:, c, hs], start=(c == 0), stop=(c == 7)),
                          r=[YT, wob], w=[p])
                    gsl = slice(2048 + half * 512, 2048 + (half + 1) * 512)
                    I(V, lambda p=p, hs=hs, gsl=gsl: nc.vector.tensor_tensor(out=tmp[:, hs], in0=p[:, :], in1=modb[:, gsl], op=ALU.mult), r=[p, modb], w=[tmp])
                I(G, lambda: nc.gpsimd.tensor_tensor(out=x1t[b][:], in0=tmp[:], in1=xl[b][:], op=ALU.add), r=[tmp, xl[b]], w=[x1t[b]])
                dma(out[tk, :], x1t[b][:], r=[x1t[b]])
            fw.barrier()
        if upto == 6:
            return nc

        NB6 = S // 256
        with contextlib.ExitStack() as st:
            wupb = sb(st, "wupb", [128, 8, 5632], BF16)
            wdb = sb(st, "wdb", [128, 22, 1024], BF16)
            with contextlib.ExitStack() as st2:
                stg = [sb(st2, f"stgu{i}", [128, 5632], F32) for i in range(2)]
                load_cast(stg, wupb, lambda i: wupb[:, i, :], lambda i: w_up[i * 128:(i + 1) * 128, :], 8, [128, 5632])
                load_cast(stg, wdb, lambda i: wdb[:, i, :], lambda i: w_down[i * 128:(i + 1) * 128, :], 22, [128, 1024])
                fw.barrier()
            modb = TO(st.enter_context(nc.sbuf_tensor("sb_mod6b", [128, 3072], F32)), 3072)
            dma(modb.t[:], MODB[:, 3072:6144], w=[modb])
            wc = sb(st, "wc", [128, 44, 3], F32)
            bc = sb(st, "bc", [128, 44], F32)
            dma(wc[:], wconv_l, w=[wc])
            dma(bc[:], bconv_l, w=[bc])
            xb = [sb(st, f"xb{i}", [128, 2, 1024], F32) for i in range(2)]
            h2f = sb(st, "h2f", [128, 1024], F32)
            tmpd = sb(st, "tmpd", [128, 1024], F32)
            h2b = sb(st, "h2b", [128, 1024], BF16)
            h2T = [sb(st, f"h2T{i}", [128, 8, 256], BF16) for i in range(2)]
            zb = [sb(st, f"zb{i}", [128, 258], F32) for i in range(3)]
            uv = [sb(st, f"uv{i}", [128, 256], F32) for i in range(2)]
            ug = [sb(st, f"ug{i}", [128, 256], F32) for i in range(2)]
            sgm = [sb(st, f"sgm{i}", [128, 256], F32) for i in range(2)]
            actT = sb(st, "actT", [128, 22, 256], BF16)
            halo = sb(st, "halo", [128, 44, 2], F32)
            ss6 = sb(st, "ss6", [128, 2], F32)
            rs6 = sb(st, "rs6", [128, 2], F32)
            pTb = [ps(st, f"pTb{i}", [128, 1024], BF16) for i in range(2)]
            pU = [ps(st, f"pU{i}", [128, 512]) for i in range(3)]
            pD = [ps(st, f"pD{i}", [128, 512]) for i in range(2)]
            I(V, lambda: nc.vector.memset(halo[:], 0.0), w=[halo])
            nu = [0]

            def load6(blk):
                rows = slice(blk * 256, (blk + 1) * 256)
                dma(xb[blk % 2][:], out[rows, :].rearrange("(s p) c -> p s c", p=128), w=[xb[blk % 2]])

            def prep6(blk):
                xb_ = xb[blk % 2]
                hT_ = h2T[blk % 2]
                for s_ in range(2):
                    I(A, lambda s_=s_: nc.scalar.activation(out=h2b[:], in_=xb_[:, s_, :], func=AF.Square, accum_out=ss6[:, s_:s_ + 1]), r=[xb_], w=[h2b, ss6])
                rsqrt_ms(ss6, ss6[:], rs6, rs6[:], 1.0 / 1024)
                for s_ in range(2):
                    I(V, lambda s_=s_: nc.vector.scalar_tensor_tensor(out=h2f[:], in0=xb_[:, s_, :], scalar=rs6[:, s_:s_ + 1], in1=modb[:, A_FFN],
                                                                      op0=ALU.mult, op1=ALU.mult), r=[xb_, rs6, modb], w=[h2f])
                    I(V, lambda: nc.vector.tensor_tensor(out=h2b[:], in0=h2f[:], in1=modb[:, SH_F], op=ALU.add), r=[h2f, modb], w=[h2b])
                    tr_generic(pTb[s_], h2b, lambda i: h2b[:, i * 128:(i + 1) * 128], 8, 128, hT_, hT_[:, :, s_ * 128:(s_ + 1) * 128])

            def gate6(k):
                sg_ = sgm[k % 2]
                I(A, lambda: nc.scalar.activation(out=sg_[:], in_=ug[k % 2][:], func=AF.Silu), r=[ug[k % 2]], w=[sg_])
                I(G, lambda: nc.gpsimd.tensor_tensor(out=actT[:, k, :], in0=sg_[:], in1=uv[k % 2][:], op=ALU.mult), r=[sg_, uv[k % 2]], w=[actT])

            load6(0)
            prep6(0)
            for blk in range(NB6):
                rows = slice(blk * 256, (blk + 1) * 256)
                xb_ = xb[blk % 2]
                hT_ = h2T[blk % 2]
                if blk + 1 < NB6:
                    load6(blk + 1)
                for k in range(22):
                    for which, fc in ((0, k), (1, 22 + k)):
                        i = nu[0]
                        nu[0] += 1
                        p = pU[i % 3]
                        z = zb[i % 3]
                        u = (uv if which == 0 else ug)[k % 2]
                        for c in range(8):
                            I(PE, lambda c=c, p=p, fc=fc: nc.tensor.matmul(p[:, 0:256], lhsT=wupb[:, c, fc * 128:(fc + 1) * 128], rhs=hT_[:, c, :],
                                                                           start=(c == 0), stop=(c == 7)), r=[wupb, hT_], w=[p])
                        I(G, lambda z=z, fc=fc: nc.gpsimd.tensor_copy(out=z[:, 0:2], in_=halo[:, fc, :]), r=[halo], w=[z])
                        I(A, lambda z=z, p=p: nc.scalar.copy(out=z[:, 2:258], in_=p[:, 0:256]), r=[p], w=[z])
                        I(G, lambda z=z, fc=fc: nc.gpsimd.tensor_copy(out=halo[:, fc, :], in_=z[:, 256:258]), r=[z], w=[halo])
                        I(A, lambda u=u, p=p, fc=fc: nc.scalar.activation(out=u[:], in_=p[:, 0:256], func=AF.Identity, scale=wc[:, fc, 2:3], bias=bc[:, fc:fc + 1]),
                          r=[p, wc, bc], w=[u])
                        I(V, lambda u=u, z=z, fc=fc: nc.vector.scalar_tensor_tensor(out=u[:], in0=z[:, 1:257], scalar=wc[:, fc, 1:2], in1=u[:],
                                                                                    op0=ALU.mult, op1=ALU.add), r=[z, wc, u], w=[u])
                        I(V, lambda u=u, z=z, fc=fc: nc.vector.scalar_tensor_tensor(out=u[:], in0=z[:, 0:256], scalar=wc[:, fc, 0:1], in1=u[:],
                                                                                    op0=ALU.mult, op1=ALU.add), r=[z, wc, u], w=[u])
                    if k >= 1:
                        gate6(k - 1)
                gate6(21)
                if blk + 1 < NB6:
                    prep6(blk + 1)
                for s_ in range(2):
                    for half in range(2):
                        p = pD[half]
                        hs = slice(half * 512, (half + 1) * 512)
                        for k in range(22):
                            I(PE, lambda k=k, p=p, hs=hs, s_=s_: nc.tensor.matmul(p[:, :], lhsT=actT[:, k, s_ * 128:(s_ + 1) * 128], rhs=wdb[:, k, hs],
                                                                                  start=(k == 0), stop=(k == 21)), r=[actT, wdb], w=[p])
                        gsl = slice(5120 + half * 512, 5120 + (half + 1) * 512)
                        I(V, lambda p=p, hs=hs, gsl=gsl: nc.vector.tensor_tensor(out=tmpd[:, hs], in0=p[:, :], in1=modb[:, gsl], op=ALU.mult), r=[p, modb], w=[tmpd])
                    I(G, lambda s_=s_: nc.gpsimd.tensor_tensor(out=xb_[:, s_, :], in0=tmpd[:], in1=xb_[:, s_, :], op=ALU.add), r=[tmpd, xb_], w=[xb_])
                dma(out[rows, :].rearrange("(s p) c -> p s c", p=128), xb_[:], r=[xb_])
            fw.barrier()
    return nc


_PARAM_NAMES = ["w_ada", "b_ada", "attn_norm", "ffn_norm", "w_in", "nsa_q_norm", "nsa_kc_norm", "nsa_ks_norm", "nsa_kw_norm",
                "cmp_k_w1", "cmp_k_w2", "cmp_v_w1", "cmp_v_w2", "mla_cq_norm", "mla_ckv_norm", "w_uq", "w_ukv",
                "mla_q_norm", "mla_k_norm", "w_o", "w_up", "w_down"]
_CONSTS = {}


def make_in_map(inp, b, S):
    if S not in _CONSTS:
        _CONSTS[S] = host_consts(S)
    m = {}
    m["x"] = np.ascontiguousarray(np.asarray(inp["x"])[b, :S], dtype=np.float32)
    m["ccol"] = np.ascontiguousarray(np.asarray(inp["c"])[b].reshape(8, 128).T, dtype=np.float32)
    for k in _PARAM_NAMES:
        m[k] = np.ascontiguousarray(np.asarray(inp[k])[0], dtype=np.float32)
    m["pe_kT"] = np.ascontiguousarray(np.asarray(inp["cmp_k_pe"])[0].T, dtype=np.float32)
    m["pe_vT"] = np.ascontiguousarray(np.asarray(inp["cmp_v_pe"])[0].T, dtype=np.float32)
    m["wconv_l"] = np.ascontiguousarray(np.asarray(inp["w_conv"])[0].reshape(3, 44, 128).transpose(2, 1, 0), dtype=np.float32)
    m["bconv_l"] = np.ascontiguousarray(np.asarray(inp["b_conv"])[0].reshape(44, 128).T, dtype=np.float32)
    m.update(_CONSTS[S])
    return m


_NC = {}


def kernel(**inputs):
    S = 8192
    if S not in _NC:
        _NC[S] = build(S)
    nc = _NC[S]
    in_maps = [make_in_map(inputs, b, S) for b in range(8)]
    res = run_bass_kernel_spmd(nc, in_maps, core_ids=list(range(8)))
    return np.stack([np.asarray(r["out"], dtype=np.float32) for r in res.results], axis=0)
```

```python
import contextlib
import numpy as np
import ml_dtypes
import concourse.bass as bass
import concourse.mybir as mybir
from concourse.bass_utils import run_bass_kernel_spmd

F32 = mybir.dt.float32
BF16 = mybir.dt.bfloat16
AF = mybir.ActivationFunctionType
ALU = mybir.AluOpType
AX = mybir.AxisListType

EPS = 1e-6
NEGB = -30000.0
EXPB = -4.0


class Buf:
    __slots__ = ("w", "r")

    def __init__(self):
        self.w = {}
        self.r = {}


class T:
    def __init__(self, t):
        self.t = t
        self.b = Buf()

    def __getitem__(self, k):
        return self.t[k]


class TO(T):
    def __init__(self, t, off):
        super().__init__(t)
        self.off = off

    def __getitem__(self, k):
        p, c = k
        return self.t[p, slice(c.start - self.off, c.stop - self.off)]


def _b(x):
    return x.b if isinstance(x, T) else x


class FW:
    ROT = 12000
    NQ = 24

    def __init__(self, nc, es):
        self.nc, self.es = nc, es
        self.E = {"pe": nc.tensor, "act": nc.scalar, "dve": nc.vector, "pool": nc.gpsimd, "sp": nc.sync}
        self.sem, self.cnt = {}, {}
        self.nsem = 0
        for e in self.E:
            self._newsem(e)
        self.waited = {e: {} for e in self.E}
        self.dq = {}
        self.n = 0
        self.rec = None

    def _newsem(self, e):
        s = self.es.enter_context(self.nc.semaphore(f"s{e}{self.nsem}"))
        self.nsem += 1
        self.sem[e] = s
        self.cnt[e] = 0

    def _wait(self, e, deps):
        for s, (v, pe) in deps.items():
            if self.waited[e].get(s, 0) >= v:
                continue
            self.E[e].wait_ge(s, v)
            self.waited[e][s] = v
            self.n += 1

    def _deps(self, e, r, w):
        deps = {}

        def add(d, raw):
            for s, (v, pe) in d.items():
                if pe == e and e != "dma":
                    if e == "pe" or not raw:
                        continue
                if deps.get(s, (0,))[0] < v:
                    deps[s] = (v, pe)
        for b in r:
            add(_b(b).w, True)
        for b in w:
            add(_b(b).w, False)
            add(_b(b).r, False)
        return deps

    def I(self, e, fn, r=(), w=()):
        if self.rec is not None:
            r, w = list(r), list(w)
            self.rec.append(lambda: self._I(e, fn, r, w))
            return None
        return self._I(e, fn, r, w)

    def _I(self, e, fn, r=(), w=()):
        deps = self._deps(e, r, w)
        pend = [(s_, v_) for s_, (v_, pe_) in deps.items() if self.waited[e].get(s_, 0) < v_]
        if len(pend) > 1:
            self._wait(e, {s_: (v_, "x") for s_, v_ in pend[:-1]})
        if self.cnt[e] >= self.ROT:
            self._newsem(e)
        inst = fn()
        if pend:
            s_, v_ = pend[-1]
            inst._wait_ge(s_, v_)
            self.waited[e][s_] = v_
        s = self.sem[e]
        inst.then_inc(s, 1)
        self.cnt[e] += 1
        self.n += 1
        tok = (self.cnt[e], e)
        for b in w:
            b = _b(b)
            b.w = {s: tok}
            b.r = {}
        for b in r:
            _b(b).r[s] = tok
        return inst

    def dma(self, out, in_, r=(), w=(), q="sp"):
        if self.rec is not None:
            r, w = list(r), list(w)
            self.rec.append(lambda: self._dma(out, in_, r, w, q))
            return None
        return self._dma(out, in_, r, w, q)

    def record(self, fn):
        self.rec = []
        fn()
        lst, self.rec = self.rec, None
        return lst

    @staticmethod
    def interleave_skewed(streams):
        n = len(streams)
        L = max(len(st_) for st_ in streams)
        pos = [0] * n
        start = [0] + [0] * (n - 1)
        i = 0
        while any(pos[k] < len(streams[k]) for k in range(n)):
            for k in range(n):
                if i >= start[k] and pos[k] < len(streams[k]):
                    streams[k][pos[k]]()
                    pos[k] += 1
            i += 1

    @staticmethod
    def interleave(lists):
        for i in range(max(len(l) for l in lists)):
            for l in lists:
                if i < len(l):
                    l[i]()

    def _dma(self, out, in_, r=(), w=(), q="sp"):
        self._wait(q, self._deps("dma", r, w))
        d = self.dq.setdefault(q, {"sems": [], "i": 0})
        if len(d["sems"]) < self.NQ:
            s = self.es.enter_context(self.nc.semaphore(f"d{q}{len(d['sems'])}"))
            ent = [s, 0]
            d["sems"].append(ent)
        else:
            ent = d["sems"][d["i"] % self.NQ]
            d["i"] += 1
            self._wait(q, {ent[0]: (16 * ent[1], "dma")})
        inst = self.E[q].dma_start(out=out, in_=in_)
        inst.then_inc(ent[0], 16)
        ent[1] += 1
        self.n += 1
        tok = (16 * ent[1], "dma")
        for b in w:
            b = _b(b)
            b.w = {ent[0]: tok}
            b.r = {}
        for b in r:
            _b(b).r[ent[0]] = tok

    def barrier(self):
        toks = {}
        for e in self.E:
            if self.cnt[e] > 0:
                toks[self.sem[e]] = (self.cnt[e], "x")
        for q, d in self.dq.items():
            for s, c in d["sems"]:
                if c:
                    toks[s] = (16 * c, "dma")
        for e in self.E:
            self._wait(e, toks)


O_NQ, O_NKC, O_NVC, O_NKS, O_NVS, O_NKW, O_NVW, O_NG, O_CQ, O_CKV, O_KR, O_BG = (
    0, 1024, 1280, 1536, 1792, 2048, 2304, 2560, 2584, 2968, 3224, 3288)
IN_W = 5336


def host_consts(S):
    NT = S // 128
    NSEL = S // 64
    n_cmp = S // 16 - 1
    pos = np.arange(S, dtype=np.float32)
    inv128 = (10000.0 ** (-np.arange(64, dtype=np.float32) * 2.0 / 128)).astype(np.float32)
    inv64 = (10000.0 ** (-np.arange(32, dtype=np.float32) * 2.0 / 64)).astype(np.float32)
    a128 = pos[:, None] * inv128[None, :]
    a64 = pos[:, None] * inv64[None, :]
    c = {}
    c["cs128"] = np.concatenate([np.cos(a128), np.sin(a128)], axis=1).astype(np.float32)
    c["cs64"] = np.concatenate([np.cos(a64), np.sin(a64)], axis=1).astype(np.float32)
    p = np.arange(128)[:, None]
    f = np.arange(128)[None, :]
    c["ident"] = (p == f).astype(ml_dtypes.bfloat16)
    c["identf"] = (p == f).astype(np.float32)
    c["tri"] = (p <= f).astype(ml_dtypes.bfloat16)
    c["atri"] = (p > f).astype(ml_dtypes.bfloat16)
    W0 = 512 + 8 * (NT - 1)
    cc = np.arange(W0)[None, :]
    m = cc - 8 * (NT - 1)
    c["m0ext"] = ((16 * m + 31) <= p).astype(np.float32)
    CO = 2 * (NT - 1)
    W1 = NSEL + CO
    cc = np.arange(W1)[None, :]
    d = cc - CO
    hi = (p >= 64).astype(np.int64)
    c["aext"] = (d <= hi - 2).astype(np.float32)
    forced = (d == hi) | (d == hi - 1)
    c["fext"] = np.where(forced, 1e9, np.where(d > hi, -1.0, 0.0)).astype(np.float32)
    KJ = min(128, NSEL)
    E = np.zeros((KJ, NT, 128), dtype=np.float32)
    for kt in range(NT):
        E[2 * kt, kt, :64] = 1.0
        E[2 * kt + 1, kt, 64:] = 1.0
    c["esel"] = E.astype(ml_dtypes.bfloat16)
    return c


def build(S, dbg=False, upto=99):
    NT = S // 128
    NSEL = S // 64
    KJ = min(128, NSEL)
    n_cmp = S // 16 - 1
    NCT = (n_cmp + 127) // 128
    NCP = NCT * 128
    CO = 2 * (NT - 1)
    nc = bass.Bass("TRN2", target_bir_lowering=False)
    okind = "ExternalOutput"

    def din(name, shape, dt=F32):
        return nc.dram_tensor(name, list(shape), dt, kind="ExternalInput").ap()

    def dscr(name, shape, dt):
        return nc.dram_tensor(name, list(shape), dt, kind=okind).ap()

    x = din("x", [S, 1024])
    ccol = din("ccol", [128, 8])
    w_ada = din("w_ada", [1024, 6144])
    b_ada = din("b_ada", [6144])
    attn_norm = din("attn_norm", [1024])
    ffn_norm = din("ffn_norm", [1024])
    w_in = din("w_in", [1024, IN_W])
    g_q = din("nsa_q_norm", [128])
    g_kc = din("nsa_kc_norm", [128])
    g_ks = din("nsa_ks_norm", [128])
    g_kw = din("nsa_kw_norm", [128])
    pe_kT = din("pe_kT", [128, 32])
    k_w1 = din("cmp_k_w1", [4096, 256])
    k_w2 = din("cmp_k_w2", [256, 128])
    pe_vT = din("pe_vT", [128, 32])
    v_w1 = din("cmp_v_w1", [4096, 256])
    v_w2 = din("cmp_v_w2", [256, 128])
    g_cq = din("mla_cq_norm", [384])
    g_ckv = din("mla_ckv_norm", [256])
    w_uq = din("w_uq", [384, 1536])
    w_ukv = din("w_ukv", [256, 2048])
    g_mq = din("mla_q_norm", [192])
    g_mk = din("mla_k_norm", [192])
    w_o = din("w_o", [1024, 1024])
    w_up = din("w_up", [1024, 5632])
    wconv_l = din("wconv_l", [128, 44, 3])
    bconv_l = din("bconv_l", [128, 44])
    w_down = din("w_down", [2816, 1024])
    cs128 = din("cs128", [S, 128])
    cs64 = din("cs64", [S, 64])
    c_ident = din("ident", [128, 128], BF16)
    c_identf = din("identf", [128, 128])
    c_tri = din("tri", [128, 128], BF16)
    c_atri = din("atri", [128, 128], BF16)
    c_m0 = din("m0ext", [128, 512 + 8 * (NT - 1)])
    c_aext = din("aext", [128, NSEL + CO])
    c_fext = din("fext", [128, NSEL + CO])
    c_esel = din("esel", [KJ, NT, 128], BF16)
    out = nc.dram_tensor("out", [S, 1024], F32, kind="ExternalOutput").ap()

    QaT = dscr("QaT", [8, 128, S], BF16)
    KcT = dscr("KcT", [2, 128, S], BF16)
    VcT = dscr("VcT", [2, 128, S], BF16)
    KsT = dscr("KsT", [2, 128, S], BF16)
    KwT = dscr("KwT", [2, 128, S], BF16)
    Vs = dscr("Vs", [S, 2, 128], BF16)
    Vw = dscr("Vw", [S, 2, 128], BF16)
    Gn = dscr("Gn", [S, 24], F32)
    BG = dscr("BG", [S, 2048], BF16)
    QbN = dscr("QbN", [8, 128, S], BF16)
    QbR = dscr("QbR", [8, 64, S], BF16)
    KbN = dscr("KbN", [8, 128, S], BF16)
    KbR = dscr("KbR", [8, 64, S], BF16)
    Vb = dscr("Vb", [S, 8, 128], BF16)
    Oc = dscr("Oc", [S, 1024], F32)
    BT = dscr("BT", [2, KJ, S], BF16)
    Ya = dscr("Ya", [S, 1024], F32)
    Yb = dscr("Yb", [S, 1024], BF16)
    MODB = dscr("MODB", [128, 6144], F32)

    with contextlib.ExitStack() as es:
        fw = FW(nc, es)
        global LASTFW
        LASTFW = fw
        I = fw.I
        dma = fw.dma
        V, G, A, PE = "dve", "pool", "act", "pe"

        def sb(st, name, shape, dt):
            return T(st.enter_context(nc.sbuf_tensor("sb_" + name, list(shape), dt)))

        def ps(st, name, shape, dt=F32):
            return T(st.enter_context(nc.psum_tensor("ps_" + name, list(shape), dt)))

        ident = sb(es, "ident", [128, 128], BF16)
        tri = sb(es, "tri", [128, 128], BF16)
        atri = sb(es, "atri", [128, 128], BF16)
        dma(ident[:], c_ident, w=[ident])
        dma(tri[:], c_tri, w=[tri])
        identf = sb(es, "identf", [128, 128], F32)
        dma(identf[:], c_identf, w=[identf])
        ones_f = sb(es, "ones_f", [128, 2], F32)
        I(V, lambda: nc.vector.memset(ones_f[:], 1.0), w=[ones_f])
        dma(atri[:], c_atri, w=[atri])
        SH_A, A_ATT, G_A, SH_F, A_FFN, G_F = [slice(i * 1024, (i + 1) * 1024) for i in range(6)]

        def rsqrt_ms(ssT, ss_ap, rsT, rs_ap, inv_n, rows=128):
            I(A, lambda: nc.scalar.activation(out=rs_ap, in_=ss_ap, func=AF.Sqrt, bias=eps_t[0:rows, 0:1], scale=inv_n),
              r=[ssT, eps_t], w=[rsT])
            I(V, lambda: nc.vector.reciprocal(out=rs_ap, in_=rs_ap), r=[rsT], w=[rsT])

        eps_t = sb(es, "eps_t", [128, 2], F32)
        I(V, lambda: nc.vector.memset(eps_t[:, 0:1], EPS), w=[eps_t])
        I(V, lambda: nc.vector.memset(eps_t[:, 1:2], EXPB), w=[eps_t])

        with contextlib.ExitStack() as st:
            modb = sb(st, "modb", [128, 6144], F32)
            cs_t = sb(st, "cs_t", [128, 8], F32)
            sc_t = sb(st, "sc_t", [128, 8], F32)
            scb = sb(st, "scb", [128, 8, 128], F32)
            wst = [sb(st, f"wst{i}", [128, 3072], F32) for i in range(2)]
            gtmp = sb(st, "gtmp", [128, 1024], F32)
            pm = [ps(st, f"pm{i}", [128, 512]) for i in range(6)]
            dma(cs_t[:], ccol, w=[cs_t])
            dma(modb[:], b_ada.partition_broadcast(128), w=[modb])
            I(A, lambda: nc.scalar.activation(out=sc_t[:], in_=cs_t[:], func=AF.Silu), r=[cs_t], w=[sc_t])
            I(V, lambda: nc.vector.tensor_copy(out=scb[:], in_=sc_t[:].unsqueeze(2).broadcast_to([128, 8, 128])),
              r=[sc_t], w=[scb])
            for half in range(2):
                for kc in range(8):
                    wb = wst[kc % 2]
                    dma(wb[:], w_ada[kc * 128:(kc + 1) * 128, half * 3072:(half + 1) * 3072], w=[wb])
                    for j in range(6):
                        I(PE, lambda j=j, wb=wb, kc=kc: nc.tensor.matmul(
                            pm[j][:], lhsT=scb[:, kc, :], rhs=wb[:, j * 512:(j + 1) * 512],
                            start=(kc == 0), stop=(kc == 7)), r=[scb, wb], w=[pm[j]])
                for j in range(6):
                    cs = slice(half * 3072 + j * 512, half * 3072 + (j + 1) * 512)
                    I(V, lambda j=j, cs=cs: nc.vector.tensor_tensor(out=modb[:, cs], in0=pm[j][:], in1=modb[:, cs],
                                                                    op=ALU.add), r=[pm[j], modb], w=[modb])
            for gsrc, sl in ((attn_norm, A_ATT), (ffn_norm, A_FFN)):
                dma(gtmp[:], gsrc.partition_broadcast(128), w=[gtmp])
                I(V, lambda sl=sl: nc.vector.scalar_tensor_tensor(out=modb[:, sl], in0=modb[:, sl], scalar=1.0,
                                                                  in1=gtmp[:], op0=ALU.add, op1=ALU.mult),
                  r=[modb, gtmp], w=[modb])
            dma(MODB, modb[:], r=[modb])
            fw.barrier()

        def load_cast(st_scratch, dst, dst_ap_fn, src_ap_fn, nchunks, shape, engs=(V, G, A)):
            for i in range(nchunks):
                stg = st_scratch[i % len(st_scratch)]
                dma(stg[tuple(slice(0, s_) for s_ in shape)] if False else stg_view(stg, shape), src_ap_fn(i), w=[stg])
                e = engs[i % len(engs)]
                if e == A:
                    I(A, lambda i=i, stg=stg: nc.scalar.copy(out=dst_ap_fn(i), in_=stg_view(stg, shape)), r=[stg], w=[dst])
                elif e == V:
                    I(V, lambda i=i, stg=stg: nc.vector.tensor_copy(out=dst_ap_fn(i), in_=stg_view(stg, shape)), r=[stg], w=[dst])
                else:
                    I(G, lambda i=i, stg=stg: nc.gpsimd.tensor_copy(out=dst_ap_fn(i), in_=stg_view(stg, shape)), r=[stg], w=[dst])

        def stg_view(stg, shape):
            n = 1
            for s_ in shape[1:]:
                n *= s_
            v = stg[0:shape[0], 0:n]
            if len(shape) == 3:
                v = v.rearrange("p (a b) -> p a b", a=shape[1])
            return v

        def bcast_load(st, name, src, n):
            t = sb(st, name, [128, n], F32)
            dma(t[:], src.partition_broadcast(128), w=[t])
            return t

        with contextlib.ExitStack() as st:
            winb = sb(st, "winb", [128, 8, IN_W], BF16)
            wuqb = sb(st, "wuqb", [128, 3, 1536], BF16)
            wukvb = sb(st, "wukvb", [128, 2, 2048], BF16)
            with contextlib.ExitStack() as st2:
                stg = [sb(st2, f"stg{i}", [128, IN_W], F32) for i in range(2)]
                load_cast(stg, winb, lambda i: winb[:, i, :], lambda i: w_in[i * 128:(i + 1) * 128, :], 8, [128, IN_W])
                load_cast(stg, wuqb, lambda i: wuqb[:, i, :], lambda i: w_uq[i * 128:(i + 1) * 128, :], 3, [128, 1536])
                load_cast(stg, wukvb, lambda i: wukvb[:, i, :], lambda i: w_ukv[i * 128:(i + 1) * 128, :], 2, [128, 2048])
                fw.barrier()
            modb = TO(st.enter_context(nc.sbuf_tensor("sb_mod1", [128, 2048], F32)), 0)
            dma(modb.t[:], MODB[:, 0:2048], w=[modb])
            gq_t = bcast_load(st, "gq_t", g_q, 128)
            gks_t = bcast_load(st, "gks_t", g_ks, 128)
            gkw_t = bcast_load(st, "gkw_t", g_kw, 128)
            gcq_t = bcast_load(st, "gcq_t", g_cq, 384)
            gckv_t = bcast_load(st, "gckv_t", g_ckv, 256)
            gmq_t = bcast_load(st, "gmq_t", g_mq, 192)
            gmk_t = bcast_load(st, "gmk_t", g_mk, 192)

            def mk1(u):
                B = {}
                B["cst2"] = [sb(st, f"cst_{u}{i}", [128, 192], F32) for i in range(2)]
                for nm, shp, dt in (("xt", [128, 1024], F32), ("ss1", [128, 1], F32), ("rs1", [128, 1], F32),
                                    ("nb", [128, 8, 192], BF16), ("hT", [128, 8, 128], BF16), ("f_a", [128, 1536], F32),
                                    ("f_b", [128, 1536], F32), ("f_d", [128, 1024], F32), ("ssn", [128, 8], F32), ("rsn", [128, 8], F32),
                                    ("vbt", [128, 8, 128], BF16), ("gnt", [128, 24], F32), ("cqT", [128, 3, 128], BF16),
                                    ("ckvT", [128, 2, 128], BF16), ("krf", [128, 64], F32)):
                    B[nm] = sb(st, f"{nm}_{u}", shp, dt)
                B["bgt"] = [sb(st, f"bgt_{u}{i}", [128, 512], BF16) for i in range(2)]
                B["sgA"] = [sb(st, f"sgA_{u}{i}", [128, 8, 128], BF16) for i in range(2)]
                B["sgR"] = [sb(st, f"sgR_{u}{i}", [64, 8, 128], BF16) for i in range(2)]
                B["pp"] = [ps(st, f"pp_{u}{i}", [128, 512]) for i in range(2)]
                B["pT"] = ps(st, f"pT_{u}", [128, 1024], BF16)
                B["npp"] = 0
                B["nA"] = 0
                B["nR"] = 0
                return B
            sets1 = [mk1(0), mk1(1)]

            def load_tile(t):
                B = sets1[t % 2]
                dma(B["xt"][:], x[t * 128:(t + 1) * 128, :], w=[B["xt"]])
                c_ = B["cst2"][(t // 2) % 2]
                dma(c_[:, 0:128], cs128[t * 128:(t + 1) * 128, :], w=[c_])
                dma(c_[:, 128:192], cs64[t * 128:(t + 1) * 128, :], w=[c_])

            def p1_tile(t):
                B = sets1[t % 2]
                xtt, ss1, rs1, nb, hT, f_a, f_b, f_d = (B[k] for k in ("xt", "ss1", "rs1", "nb", "hT", "f_a", "f_b", "f_d"))
                cs_ = B["cst2"][(t // 2) % 2]
                ssn, rsn, vbt, gnt, cqT, ckvT, krf, pT_ = (B[k] for k in ("ssn", "rsn", "vbt", "gnt", "cqT", "ckvT", "krf", "pT"))
                tok = slice(t * 128, (t + 1) * 128)
                nbf = nb[:].rearrange("p h d -> p (h d)")

                def proj(lhsT_t, lhs_fn, nk, w_t, c0, c1):
                    p = B["pp"][B["npp"] % 2]
                    B["npp"] += 1
                    for kc in range(nk):
                        I(PE, lambda kc=kc: nc.tensor.matmul(p[:, 0:c1 - c0], lhsT=lhs_fn(kc), rhs=w_t[:, kc, c0:c1],
                                                             start=(kc == 0), stop=(kc == nk - 1)), r=[lhsT_t, w_t], w=[p])
                    return p

                def evac(p, c, dstT, dst_ap, eng=A):
                    if eng == A:
                        I(A, lambda: nc.scalar.copy(out=dst_ap, in_=p[:, 0:c]), r=[p], w=[dstT])
                    else:
                        I(V, lambda: nc.vector.tensor_copy(out=dst_ap, in_=p[:, 0:c]), r=[p], w=[dstT])

                def norm_rope(src, src_ap, nh, hd, gain_t, do_norm, rope_off, rope_half, cs_off, dst_ap, dstT):
                    if do_norm:
                        sq = f_b[:, 0:nh * hd].rearrange("p (h d) -> p h d", h=nh)
                        I(A, lambda: nc.scalar.activation(out=sq, in_=src_ap, func=AF.Square), r=[src], w=[f_b])
                        I(V, lambda: nc.vector.tensor_reduce(out=ssn[:, 0:nh], in_=sq, axis=AX.X, op=ALU.add), r=[f_b], w=[ssn])
                        rsqrt_ms(ssn, ssn[:, 0:nh], rsn, rsn[:, 0:nh], 1.0 / hd)
                        I(V, lambda: nc.vector.tensor_tensor(out=src_ap, in0=src_ap, in1=rsn[:, 0:nh].unsqueeze(2).broadcast_to([128, nh, hd]),
                                                             op=ALU.mult), r=[src, rsn], w=[src])
                        I(G, lambda: nc.gpsimd.tensor_tensor(out=src_ap, in0=src_ap, in1=gain_t[:, 0:hd].unsqueeze(1).broadcast_to([128, nh, hd]),
                                                             op=ALU.mult), r=[src, gain_t], w=[src])
                    if rope_half == 0:
                        I(V, lambda: nc.vector.tensor_copy(out=dst_ap, in_=src_ap), r=[src], w=[dstT])
                        return
                    if rope_off > 0:
                        I(G, lambda: nc.gpsimd.tensor_copy(out=dst_ap[:, :, 0:rope_off], in_=src_ap[:, :, 0:rope_off]), r=[src], w=[dstT])
                    hh_ = rope_half
                    x1 = src_ap[:, :, rope_off:rope_off + hh_]
                    x2 = src_ap[:, :, rope_off + hh_:rope_off + 2 * hh_]
                    cb = cs_[:, cs_off:cs_off + hh_].unsqueeze(1).broadcast_to([128, nh, hh_])
                    sbb = cs_[:, cs_off + hh_:cs_off + 2 * hh_].unsqueeze(1).broadcast_to([128, nh, hh_])
                    t1 = f_d[:, 0:nh * hh_].rearrange("p (h d) -> p h d", h=nh)
                    t2 = f_d[:, 512:512 + nh * hh_].rearrange("p (h d) -> p h d", h=nh)
                    t3 = f_b[:, 0:nh * hh_].rearrange("p (h d) -> p h d", h=nh)
                    t4 = f_b[:, 512:512 + nh * hh_].rearrange("p (h d) -> p h d", h=nh)
                    I(V, lambda: nc.vector.tensor_tensor(out=t1, in0=x1, in1=cb, op=ALU.mult), r=[src, cs_], w=[f_d])
                    I(V, lambda: nc.vector.tensor_tensor(out=t2, in0=x2, in1=sbb, op=ALU.mult), r=[src, cs_], w=[f_d])
                    I(G, lambda: nc.gpsimd.tensor_tensor(out=t3, in0=x1, in1=sbb, op=ALU.mult), r=[src, cs_], w=[f_b])
                    I(G, lambda: nc.gpsimd.tensor_tensor(out=t4, in0=x2, in1=cb, op=ALU.mult), r=[src, cs_], w=[f_b])
                    I(V, lambda: nc.vector.tensor_tensor(out=dst_ap[:, :, rope_off:rope_off + hh_], in0=t1, in1=t2, op=ALU.subtract),
                      r=[f_d], w=[dstT])
                    I(G, lambda: nc.gpsimd.tensor_tensor(out=dst_ap[:, :, rope_off + hh_:rope_off + 2 * hh_], in0=t3, in1=t4, op=ALU.add),
                      r=[f_b], w=[dstT])

                def transposes(src_t, src_ap_fn, n, rows, dstT, dst_ap):
                    for i in range(n):
                        I(PE, lambda i=i: nc.tensor.transpose(out=pT_[0:rows, i * 128:(i + 1) * 128], in_=src_ap_fn(i), identity=ident[:]),
                          r=[src_t, ident], w=[pT_])
                    I(A, lambda: nc.scalar.copy(out=dst_ap, in_=pT_[0:rows, 0:n * 128].rearrange("p (a b) -> p a b", a=n)),
                      r=[pT_], w=[dstT])

                def slotA():
                    sg = B["sgA"][B["nA"] % 2]
                    B["nA"] += 1
                    return sg

                def slotR():
                    sg = B["sgR"][B["nR"] % 2]
                    B["nR"] += 1
                    return sg

                def outT(dst, sg, b0, nblk):
                    dma(dst[:, :, tok].rearrange("h d t -> d h t"), sg[:, b0:b0 + nblk, :], r=[sg])

                if t + 2 < NT:
                    pass
                I(A, lambda: nc.scalar.activation(out=nbf[:, 0:1024], in_=xtt[:], func=AF.Square, accum_out=ss1[:, 0:1]), r=[xtt], w=[nb, ss1])
                rsqrt_ms(ss1, ss1[:, 0:1], rs1, rs1[:, 0:1], 1.0 / 1024)
                I(V, lambda: nc.vector.scalar_tensor_tensor(out=f_b[:, 0:1024], in0=xtt[:], scalar=rs1[:, 0:1], in1=modb[:, A_ATT],
                                                            op0=ALU.mult, op1=ALU.mult), r=[xtt, rs1, modb], w=[f_b])
                I(G, lambda: nc.gpsimd.tensor_tensor(out=nbf[:, 0:1024], in0=f_b[:, 0:1024], in1=modb[:, SH_A], op=ALU.add), r=[f_b, modb], w=[nb])
                transposes(nb, lambda i: nbf[:, i * 128:(i + 1) * 128], 8, 128, hT, hT[:])
                if t + 2 < NT:
                    load_tile(t + 2)
                lh = lambda kc: hT[:, kc, :]
                for gq in range(2):
                    p = proj(hT, lh, 8, winb, O_NQ + gq * 512, O_NQ + (gq + 1) * 512)
                    evac(p, 512, f_a, f_a[:, gq * 512:(gq + 1) * 512], eng=(A if gq else V))
                norm_rope(f_a, f_a[:, 0:1024].rearrange("p (h d) -> p h d", h=8), 8, 128, gq_t, True, 0, 64, 0, nb[:, :, 0:128], nb)
                sg = slotA()
                transposes(nb, lambda i: nb[:, i, 0:128], 8, 128, sg, sg[:, 0:8, :])
                outT(QaT, sg, 0, 8)
                p = proj(hT, lh, 8, winb, O_NKC, O_NKC + 512)
                evac(p, 512, f_a, f_a[:, 0:512])
                norm_rope(f_a, f_a[:, 0:256].rearrange("p (h d) -> p h d", h=2), 2, 128, None, False, 0, 64, 0, nb[:, 0:2, 0:128], nb)
                I(V, lambda: nc.vector.tensor_copy(out=nb[:, 2:4, 0:128], in_=f_a[:, 256:512].rearrange("p (h d) -> p h d", h=2)),
                  r=[f_a], w=[nb])
                sg = slotA()
                transposes(nb, lambda i: nb[:, i, 0:128], 4, 128, sg, sg[:, 0:4, :])
                outT(KcT, sg, 0, 2)
                outT(VcT, sg, 2, 2)
                sg = slotA()
                for wi, (off, gt_) in enumerate(((O_NKS, gks_t), (O_NKW, gkw_t))):
                    p = proj(hT, lh, 8, winb, off, off + 512)
                    evac(p, 512, f_a, f_a[:, 0:512])
                    norm_rope(f_a, f_a[:, 0:256].rearrange("p (h d) -> p h d", h=2), 2, 128, gt_, True, 0, 64, 0, nb[:, 0:2, 0:128], nb)
                    I(V, lambda wi=wi: nc.vector.tensor_copy(out=vbt[:, 2 * wi:2 * wi + 2, :], in_=f_a[:, 256:512].rearrange("p (h d) -> p h d", h=2)),
                      r=[f_a], w=[vbt])
                    transposes(nb, lambda i: nb[:, i, 0:128], 2, 128, sg, sg[:, 2 * wi:2 * wi + 2, :])
                outT(KsT, sg, 0, 2)
                outT(KwT, sg, 2, 2)
                dma(Vs[tok, :, :], vbt[:, 0:2, :], r=[vbt])
                dma(Vw[tok, :, :], vbt[:, 2:4, :], r=[vbt])
                p = proj(hT, lh, 8, winb, O_NG, O_CKV)
                I(A, lambda: nc.scalar.activation(out=gnt[:], in_=p[:, 0:24], func=AF.Sigmoid), r=[p], w=[gnt])
                evac(p, 408, f_a, f_a[:, 0:408], eng=V)
                dma(Gn[tok, :], gnt[:], r=[gnt])
                norm_rope(f_a, f_a[:, 24:408].rearrange("p (h d) -> p h d", h=1), 1, 384, gcq_t, True, 0, 0, 0,
                          nbf[:, 0:384].rearrange("p (h d) -> p h d", h=1), nb)
                transposes(nb, lambda i: nbf[:, i * 128:(i + 1) * 128], 3, 128, cqT, cqT[:])
                p = proj(hT, lh, 8, winb, O_CKV, O_BG)
                evac(p, 320, f_a, f_a[:, 0:320], eng=V)
                I(G, lambda: nc.gpsimd.tensor_copy(out=krf[:], in_=f_a[:, 256:320]), r=[f_a], w=[krf])
                norm_rope(f_a, f_a[:, 0:256].rearrange("p (h d) -> p h d", h=1), 1, 256, gckv_t, True, 0, 0, 0,
                          nbf[:, 512:768].rearrange("p (h d) -> p h d", h=1), nb)
                transposes(nb, lambda i: nbf[:, 512 + i * 128:512 + (i + 1) * 128], 2, 128, ckvT, ckvT[:])
                for j in range(4):
                    p = proj(hT, lh, 8, winb, O_BG + j * 512, O_BG + (j + 1) * 512)
                    bg_ = B["bgt"][j % 2]
                    I(A, lambda bg_=bg_, p=p: nc.scalar.activation(out=bg_[:], in_=p[:, 0:512], func=AF.Sigmoid), r=[p], w=[bg_])
                    dma(BG[tok, j * 512:(j + 1) * 512], bg_[:], r=[bg_])
                for j in range(3):
                    p = proj(cqT, lambda kc: cqT[:, kc, :], 3, wuqb, j * 512, (j + 1) * 512)
                    evac(p, 512, f_a, f_a[:, j * 512:(j + 1) * 512], eng=(A if j % 2 else V))
                norm_rope(f_a, f_a[:, 0:1536].rearrange("p (h d) -> p h d", h=8), 8, 192, gmq_t, True, 128, 32, 128, nb[:, :, :], nb)
                sg = slotA()
                transposes(nb, lambda i: nb[:, i, 0:128], 8, 128, sg, sg[:, 0:8, :])
                outT(QbN, sg, 0, 8)
                sg = slotA()
                transposes(nb, lambda i: nb[:, i, 64:192], 8, 128, sg, sg[:, 0:8, :])
                dma(QbR[:, :, tok].rearrange("h d t -> d h t"), sg[64:128, 0:8, :], r=[sg])
                for half in range(2):
                    for j in range(2):
                        c0 = half * 1024 + j * 512
                        p = proj(ckvT, lambda kc: ckvT[:, kc, :], 2, wukvb, c0, c0 + 512)
                        evac(p, 512, f_a, f_a[:, j * 512:(j + 1) * 512], eng=(A if j % 2 else V))
                    kvv = f_a[:, 0:1024].rearrange("p (h d) -> p h d", h=4)
                    I(G, lambda half=half, kvv=kvv: nc.gpsimd.tensor_copy(out=vbt[:, 4 * half:4 * half + 4, :], in_=kvv[:, :, 128:256]), r=[f_a], w=[vbt])
                    I(V, lambda kvv=kvv: nc.vector.tensor_copy(out=kvv[:, :, 128:192], in_=krf[:].unsqueeze(1).broadcast_to([128, 4, 64])),
                      r=[krf, f_a], w=[f_a])
                    norm_rope(f_a, kvv[:, :, 0:192], 4, 192, gmk_t, True, 128, 32, 128, nb[:, 4 * half:4 * half + 4, :], nb)
                dma(Vb[tok, :, :], vbt[:], r=[vbt])
                sg = slotA()
                transposes(nb, lambda i: nb[:, i, 0:128], 8, 128, sg, sg[:, 0:8, :])
                outT(KbN, sg, 0, 8)
                sg = slotA()
                transposes(nb, lambda i: nb[:, i, 64:192], 8, 128, sg, sg[:, 0:8, :])
                dma(KbR[:, :, tok].rearrange("h d t -> d h t"), sg[64:128, 0:8, :], r=[sg])

            load_tile(0)
            if NT > 1:
                load_tile(1)
            str0, str1 = [], []
            for t in range(0, NT, 2):
                str0 += fw.record(lambda t=t: p1_tile(t))
                if t + 1 < NT:
                    str1 += fw.record(lambda t=t: p1_tile(t + 1))
            skew = (len(str0) // max(1, (NT + 1) // 2)) // 2
            fw.interleave([str0, [(lambda: None)] * skew + str1])
            fw.barrier()
            if upto == 1:
                return nc

        SC128 = 128.0 ** -0.5
        SC192 = 192.0 ** -0.5

        def tr_generic(pbuf, src_t, src_ap_fn, n, rows, dstT, dst_ap, eng=A):
            for i in range(n):
                I(PE, lambda i=i: nc.tensor.transpose(out=pbuf[0:rows, i * 128:(i + 1) * 128], in_=src_ap_fn(i), identity=ident[:]),
                  r=[src_t, ident], w=[pbuf])
            if eng == A:
                I(A, lambda: nc.scalar.copy(out=dst_ap, in_=pbuf[0:rows, 0:n * 128].rearrange("p (a b) -> p a b", a=n)), r=[pbuf], w=[dstT])
            else:
                I(V, lambda: nc.vector.tensor_copy(out=dst_ap, in_=pbuf[0:rows, 0:n * 128].rearrange("p (a b) -> p a b", a=n)), r=[pbuf], w=[dstT])

        with contextlib.ExitStack() as st23:
            kcT = [sb(st23, f"kcT{g}", [128, NCP], BF16) for g in range(2)]
            vcb = [sb(st23, f"vcb{g}", [128, NCT, 128], BF16) for g in range(2)]
            for g in range(2):
                I(V, lambda g=g: nc.vector.memset(kcT[g][:], 0.0), w=[kcT[g]])
                I(G, lambda g=g: nc.gpsimd.memset(vcb[g][:], 0.0), w=[vcb[g]])
            with contextlib.ExitStack() as st:
                XT = sb(st, "XT", [128, S], BF16)
                Xl = sb(st, "Xl", [128, 32, 512], BF16)
                w1s = sb(st, "w1s", [128, 32 * 256], F32)
                w1b = sb(st, "w1b", [128, 32, 256], BF16)
                w2s = sb(st, "w2s", [128, 256], F32)
                w2b = sb(st, "w2b", [128, 2, 128], BF16)
                peT = sb(st, "peT", [128, 32], F32)
                hTc = sb(st, "hTc", [128, 2, 512], BF16)
                gkc_t = bcast_load(st, "gkc_t", g_kc, 128)
                kf = sb(st, "kf", [128, 128], F32)
                kf2 = sb(st, "kf2", [128, 128], F32)
                kb = sb(st, "kb", [128, 128], BF16)
                ssk = sb(st, "ssk", [128, 1], F32)
                rsk = sb(st, "rsk", [128, 1], F32)
                pc = [ps(st, f"pc{i}", [128, 512]) for i in range(2)]
                pk = ps(st, "pk", [128, 128])
                pkT = ps(st, "pkT", [128, 128], BF16)
                for kv in range(2):
                    w1, w2, pe_ = (k_w1, k_w2, pe_kT) if kv == 0 else (v_w1, v_w2, pe_vT)
                    dma(w1s[:].rearrange("p (l c) -> p l c", l=32), w1.rearrange("(l d) c -> d l c", d=128), w=[w1s])
                    I(V, lambda: nc.vector.tensor_copy(out=w1b[:], in_=w1s[:].rearrange("p (l c) -> p l c", l=32)), r=[w1s], w=[w1b])
                    dma(w2s[:].rearrange("p (k d) -> p k d", k=2), w2.rearrange("(k c) d -> c k d", c=128), w=[w2s])
                    I(G, lambda: nc.gpsimd.tensor_copy(out=w2b[:], in_=w2s[:].rearrange("p (k d) -> p k d", k=2)), r=[w2s], w=[w2b])
                    dma(peT[:], pe_, w=[peT])
                    for g in range(2):
                        src = KcT if kv == 0 else VcT
                        dma(XT[:], src[g, :, :], w=[XT])
                        for l in range(32):
                            xin = XT[:, l:l + 16 * (n_cmp - 1) + 1:16]
                            if l % 2 == 0:
                                I(V, lambda l=l, xin=xin: nc.vector.tensor_scalar(out=Xl[:, l, 0:n_cmp], in0=xin, scalar1=peT[:, l:l + 1],
                                                                                  scalar2=None, op0=ALU.add), r=[XT, peT], w=[Xl])
                            else:
                                I(A, lambda l=l, xin=xin: nc.scalar.activation(out=Xl[:, l, 0:n_cmp], in_=xin, func=AF.Identity,
                                                                               bias=peT[:, l:l + 1], scale=1.0), r=[XT, peT], w=[Xl])
                        for ch in range(2):
                            p = pc[ch]
                            for l in range(32):
                                I(PE, lambda l=l, p=p, ch=ch: nc.tensor.matmul(p[:, 0:n_cmp], lhsT=w1b[:, l, ch * 128:(ch + 1) * 128],
                                                                               rhs=Xl[:, l, 0:n_cmp], start=(l == 0), stop=(l == 31)),
                                  r=[w1b, Xl], w=[p])
                            I(A, lambda p=p, ch=ch: nc.scalar.activation(out=hTc[:, ch, 0:n_cmp], in_=p[:, 0:n_cmp], func=AF.Silu),
                              r=[p], w=[hTc])
                        for nt in range(NCT):
                            rows = min(128, n_cmp - nt * 128)
                            for ch in range(2):
                                I(PE, lambda ch=ch, nt=nt, rows=rows: nc.tensor.matmul(pk[0:rows, :], lhsT=hTc[:, ch, nt * 128:nt * 128 + rows],
                                                                                       rhs=w2b[:, ch, :], start=(ch == 0), stop=(ch == 1)),
                                  r=[hTc, w2b], w=[pk])
                            if kv == 0:
                                I(A, lambda rows=rows: nc.scalar.copy(out=kf[0:rows, :], in_=pk[0:rows, :]), r=[pk], w=[kf])
                                I(V, lambda rows=rows: nc.vector.tensor_tensor(out=kf2[0:rows, :], in0=kf[0:rows, :], in1=kf[0:rows, :], op=ALU.mult),
                                  r=[kf], w=[kf2])
                                I(V, lambda rows=rows: nc.vector.tensor_reduce(out=ssk[0:rows, :], in_=kf2[0:rows, :], axis=AX.X, op=ALU.add),
                                  r=[kf2], w=[ssk])
                                rsqrt_ms(ssk, ssk[0:rows, :], rsk, rsk[0:rows, :], 1.0 / 128, rows=rows)
                                I(V, lambda rows=rows: nc.vector.scalar_tensor_tensor(out=kb[0:rows, :], in0=kf[0:rows, :], scalar=rsk[0:rows, 0:1],
                                                                                      in1=gkc_t[0:rows, :], op0=ALU.mult, op1=ALU.mult),
                                  r=[kf, rsk, gkc_t], w=[kb])
                                I(PE, lambda rows=rows: nc.tensor.transpose(out=pkT[:, 0:rows], in_=kb[0:rows, :], identity=ident[0:rows, 0:rows]),
                                  r=[kb, ident], w=[pkT])
                                I(A, lambda rows=rows, nt=nt, g=g: nc.scalar.copy(out=kcT[g][:, nt * 128:nt * 128 + rows], in_=pkT[:, 0:rows]),
                                  r=[pkT], w=[kcT[g]])
                            else:
                                I(A, lambda rows=rows, nt=nt, g=g: nc.scalar.copy(out=vcb[g][0:rows, nt, :], in_=pk[0:rows, :]), r=[pk], w=[vcb[g]])
                fw.barrier()
            if upto == 2:
                return nc
            with contextlib.ExitStack() as st:
                W0 = 512 + 8 * (NT - 1)
                m0 = sb(st, "m0", [128, W0], F32)
                aext = sb(st, "aext", [128, NSEL + CO], F32)
                fext = sb(st, "fext", [128, NSEL + CO], F32)
                dma(m0[:], c_m0, w=[m0])
                dma(aext[:], c_aext, w=[aext])
                dma(fext[:], c_fext, w=[fext])
                PW = 4 * NSEL + 8

                def mkset(u):
                    B = {}
                    B["qT"] = [sb(st, f"qT{u}{i}", [128, 8, 128], BF16) for i in range(2)]
                    B["gn3"] = [sb(st, f"gn3{u}{i}", [128, 24], F32) for i in range(2)]
                    B["Ef"] = [sb(st, f"Ef{u}{i}", [128, 512], F32) for i in range(2)]
                    B["Pm"] = sb(st, f"Pm{u}", [128, 8, 512], F32)
                    B["Pb"] = [sb(st, f"Pb{u}{i}", [128, 512], BF16) for i in range(2)]
                    B["PbT"] = [sb(st, f"PbT{u}{i}", [128, 4, 128], BF16) for i in range(2)]
                    for nm, shp, dt in (("Z", [128, 8], F32), ("rz", [128, 8], F32), ("gz", [128, 8], F32), ("ppad", [128, 2, PW], F32),
                                        ("imp", [128, 2, NSEL], F32), ("score", [128, 2, NSEL], F32), ("sc2", [128, 2, NSEL], F32),
                                        ("m1", [128, 2, 8], F32), ("m2", [128, 2, 8], F32), ("Btb", [128, 2, NSEL], BF16),
                                        ("BtT", [KJ, 2, 128], BF16), ("ocm", [128, 8, 128], F32)):
                        B[nm] = sb(st, f"{nm}{u}", shp, dt)
                    B["pS"] = ps(st, f"pS{u}", [128, 512])
                    B["pTB"] = ps(st, f"pTB{u}", [128, 1024], BF16)
                    B["pO"] = [ps(st, f"pO{u}{i}", [128, 512]) for i in range(2)]
                    I(V, lambda: nc.vector.memset(B["ppad"][:], 0.0), w=[B["ppad"]])
                    return B
                sets = [mkset(0), mkset(1)]

                def ld3(t):
                    B = sets[t % 2]
                    b = (t // 2) % 2
                    tk = slice(t * 128, (t + 1) * 128)
                    dma(B["qT"][b][:], QaT[:, :, tk].rearrange("h d t -> d h t"), w=[B["qT"][b]])
                    dma(B["gn3"][b][:], Gn[tk, :], w=[B["gn3"][b]])

                def p3_tile(t):
                    B = sets[t % 2]
                    b = (t // 2) % 2
                    if t + 2 < NT:
                        ld3(t + 2)
                    tk = slice(t * 128, (t + 1) * 128)
                    NW = min(n_cmp, 8 * t + 7)
                    off = 8 * (NT - 1) - 8 * t
                    nkt = (NW + 127) // 128
                    q_, gn_, oc_ = B["qT"][b], B["gn3"][b], B["ocm"]
                    Ef, Pm, Pb, PbT, Z, rz, gz, ppad = B["Ef"], B["Pm"], B["Pb"], B["PbT"], B["Z"], B["rz"], B["gz"], B["ppad"]
                    imp, score, sc2, m1, m2, Btb, BtT = B["imp"], B["score"], B["sc2"], B["m1"], B["m2"], B["Btb"], B["BtT"]
                    pS_, pTB, pO = B["pS"], B["pTB"], B["pO"]
                    for h in range(8):
                        g = h // 4
                        hb_ = h % 2
                        I(PE, lambda h=h, g=g: nc.tensor.matmul(pS_[:, 0:NW], lhsT=q_[:, h, :], rhs=kcT[g][:, 0:NW], start=True, stop=True),
                          r=[q_, kcT[g]], w=[pS_])
                        I(A, lambda hb_=hb_: nc.scalar.activation(out=Ef[hb_][:, 0:NW], in_=pS_[:, 0:NW], func=AF.Exp, scale=SC128),
                          r=[pS_, eps_t], w=[Ef[hb_]])
                        I(V, lambda h=h, hb_=hb_: nc.vector.scalar_tensor_tensor(out=Pm[:, h, 0:NW], in0=Ef[hb_][:, 0:NW], scalar=1.0,
                                                                                  in1=m0[:, off:off + NW], op0=ALU.mult, op1=ALU.mult,
                                                                                  accum_out=Z[:, h:h + 1]), r=[Ef[hb_], m0], w=[Pm, Z])
                        I(G, lambda h=h, hb_=hb_: nc.gpsimd.tensor_copy(out=Pb[hb_][:, 0:NW], in_=Pm[:, h, 0:NW]), r=[Pm], w=[Pb[hb_]])
                        for kt in range(nkt):
                            rows = min(128, NW - kt * 128)
                            I(PE, lambda kt=kt, rows=rows, hb_=hb_: nc.tensor.transpose(out=pTB[0:rows, kt * 128:(kt + 1) * 128],
                                                                                         in_=Pb[hb_][:, kt * 128:kt * 128 + rows], identity=ident[:]),
                              r=[Pb[hb_], ident], w=[pTB])
                        for kt in range(nkt):
                            rows = min(128, NW - kt * 128)
                            I(A, lambda kt=kt, rows=rows, hb_=hb_: nc.scalar.copy(out=PbT[hb_][0:rows, kt, :], in_=pTB[0:rows, kt * 128:(kt + 1) * 128]),
                              r=[pTB], w=[PbT[hb_]])
                        for kt in range(nkt):
                            rows = min(128, NW - kt * 128)
                            I(PE, lambda kt=kt, rows=rows, h=h, g=g, hb_=hb_: nc.tensor.matmul(
                                pO[g][:, (h % 4) * 128:(h % 4 + 1) * 128], lhsT=PbT[hb_][0:rows, kt, :], rhs=vcb[g][0:rows, kt, :],
                                start=(h % 4 == 0 and kt == 0), stop=(kt == nkt - 1)), r=[PbT[hb_], vcb[g]], w=[pO[g]])
                    I(V, lambda: nc.vector.tensor_scalar(out=rz[:], in0=Z[:], scalar1=1e-30, scalar2=None, op0=ALU.max), r=[Z], w=[rz])
                    I(V, lambda: nc.vector.reciprocal(out=rz[:], in_=rz[:]), r=[rz], w=[rz])
                    I(V, lambda: nc.vector.tensor_tensor(out=gz[:], in0=rz[:], in1=gn_[:].rearrange("p (h j) -> p h j", j=3)[:, :, 0], op=ALU.mult),
                      r=[rz, gn_], w=[gz])
                    for g in range(2):
                        I(V, lambda g=g: nc.vector.tensor_tensor(out=oc_[:, 4 * g:4 * g + 4, :], in0=pO[g][:, 0:512].rearrange("p (h d) -> p h d", h=4),
                                                                 in1=gz[:, 4 * g:4 * g + 4].unsqueeze(2).broadcast_to([128, 4, 128]), op=ALU.mult),
                          r=[pO[g], gz], w=[oc_])
                    dma(Oc[tk, :], oc_[:].rearrange("p h d -> p (h d)"), r=[oc_])
                    for g in range(2):
                        for r_ in range(4):
                            h = 4 * g + r_
                            if r_ == 0:
                                I(V, lambda g=g, h=h: nc.vector.tensor_scalar(out=ppad[:, g, 4:4 + NW], in0=Pm[:, h, 0:NW], scalar1=rz[:, h:h + 1],
                                                                              scalar2=None, op0=ALU.mult), r=[Pm, rz], w=[ppad])
                            else:
                                I(V, lambda g=g, h=h: nc.vector.scalar_tensor_tensor(out=ppad[:, g, 4:4 + NW], in0=Pm[:, h, 0:NW], scalar=rz[:, h:h + 1],
                                                                                     in1=ppad[:, g, 4:4 + NW], op0=ALU.mult, op1=ALU.add),
                                  r=[Pm, rz, ppad], w=[ppad])
                    a_t = aext[:, CO - 2 * t:CO - 2 * t + NSEL]
                    f_t = fext[:, CO - 2 * t:CO - 2 * t + NSEL]
                    for g in range(2):
                        I(V, lambda g=g: nc.vector.tensor_reduce(out=imp[:, g, :], in_=ppad[:, g, 4:4 + 4 * NSEL].rearrange("p (j f) -> p j f", f=4),
                                                                 axis=AX.X, op=ALU.add), r=[ppad], w=[imp])
                        I(V, lambda g=g: nc.vector.tensor_tensor(out=imp[:, g, :], in0=imp[:, g, :],
                                                                 in1=ppad[:, g, 0:4 * NSEL].rearrange("p (j f) -> p j f", f=4)[:, :, 3], op=ALU.add),
                          r=[ppad, imp], w=[imp])
                        I(V, lambda g=g: nc.vector.tensor_tensor(out=score[:, g, :], in0=imp[:, g, :], in1=a_t, op=ALU.mult), r=[imp, aext], w=[score])
                        I(V, lambda g=g: nc.vector.tensor_tensor(out=score[:, g, :], in0=score[:, g, :], in1=f_t, op=ALU.add), r=[score, fext], w=[score])
                        I(V, lambda g=g: nc.vector.memset(score[:, g, 0:1], 1e9), r=[score], w=[score])
                        if NSEL > 16:
                            I(V, lambda g=g: nc.vector.max(out=m1[:, g, :], in_=score[:, g, :]), r=[score], w=[m1])
                            I(V, lambda g=g: nc.vector.match_replace(out=sc2[:, g, :], in_to_replace=m1[:, g, :], in_values=score[:, g, :], imm_value=-2.0),
                              r=[score, m1], w=[sc2])
                            I(V, lambda g=g: nc.vector.max(out=m2[:, g, :], in_=sc2[:, g, :]), r=[sc2], w=[m2])
                            I(V, lambda g=g: nc.vector.tensor_scalar(out=Btb[:, g, :], in0=score[:, g, :], scalar1=m2[:, g, 7:8], scalar2=NEGB,
                                                                     op0=ALU.is_lt, op1=ALU.mult), r=[score, m2], w=[Btb])
                        else:
                            I(V, lambda g=g: nc.vector.memset(Btb[:, g, :], 0.0), w=[Btb])
                        I(PE, lambda g=g: nc.tensor.transpose(out=pTB[0:KJ, 512 + g * 128:512 + (g + 1) * 128], in_=Btb[:, g, 0:KJ], identity=ident[:]),
                          r=[Btb, ident], w=[pTB])
                    I(A, lambda: nc.scalar.copy(out=BtT[:], in_=pTB[0:KJ, 512:768].rearrange("p (g t) -> p g t", g=2)), r=[pTB], w=[BtT])
                    dma(BT[:, :, tk].rearrange("g j t -> j g t"), BtT[:], r=[BtT])

                ld3(0)
                if NT > 1:
                    ld3(1)
                str0, str1 = [], []
                for t in range(0, NT, 2):
                    str0 += fw.record(lambda t=t: p3_tile(t))
                    if t + 1 < NT:
                        str1 += fw.record(lambda t=t: p3_tile(t + 1))
                skew = (len(str0) // max(1, (NT + 1) // 2)) // 2
                fw.interleave([str0, [(lambda: None)] * skew + str1])
                fw.barrier()
        if upto == 3:
            return nc

        def attn_pipeline(pairs, stageA, stageB):
            n = len(pairs)
            if n == 0:
                return
            stageA(0, pairs[0])
            for i in range(n):
                if i + 1 < n:
                    stageA(i + 1, pairs[i + 1])
                stageB(i, pairs[i])

        with contextlib.ExitStack() as st:
            esel = sb(st, "esel", [KJ, NT, 128], BF16)
            dma(esel[:], c_esel, w=[esel])
            Ks_s = sb(st, "Ks_s", [128, S], BF16)
            Kw_s = sb(st, "Kw_s", [128, S], BF16)
            Vs_s = sb(st, "Vs_s", [128, NT, 129], BF16)
            Vw_s = sb(st, "Vw_s", [128, NT, 129], BF16)
            qT4 = [sb(st, f"qT4{i}", [128, 4, 128], BF16) for i in range(4)]
            btl = [sb(st, f"btl{i}", [KJ, 128], BF16) for i in range(4)]
            BT4 = [sb(st, f"BT4{i}", [KJ, 4, 128], BF16) for i in range(4)]
            PT = [sb(st, f"PT{i}", [128, 4, 128], BF16) for i in range(3)]
            gn4 = [sb(st, f"gn4{i}", [128, 24], F32) for i in range(4)]
            ocl = [sb(st, f"ocl{i}", [128, 4, 128], F32) for i in range(4)]
            bga = [sb(st, f"bga{i}", [128, 512], BF16) for i in range(4)]
            oa = sb(st, "oa", [128, 4, 128], F32)
            yat = [sb(st, f"yat{i}", [128, 512], F32) for i in range(2)]
            rzs = sb(st, "rzs", [128, 8], F32)
            cfs = sb(st, "cfs", [128, 8], F32)
            pS = [ps(st, f"qS{i}", [128, 512]) for i in range(2)]
            pOTs = [ps(st, f"pOTs{i}", [128, 512]) for i in range(2)]
            pOTw = [ps(st, f"pOTw{i}", [128, 512]) for i in range(2)]
            pTr = ps(st, "pTr4", [128, 512])
            pZ4 = ps(st, "pZ4", [128, 512])
            acc_s = [sb(st, f"acc_s{i}", [128, 512], F32) for i in range(2)]
            acc_w = [sb(st, f"acc_w{i}", [128, 512], F32) for i in range(2)]
            oT_s = sb(st, "oT_s", [128, 512], F32)
            oT_w = sb(st, "oT_w", [128, 512], F32)
            I(V, lambda: nc.vector.memset(Vs_s[:, :, 128:129], 1.0), w=[Vs_s])
            I(V, lambda: nc.vector.memset(Vw_s[:, :, 128:129], 1.0), w=[Vw_s])
            cnt4 = [0]
            for g in range(2):
                dma(Ks_s[:], KsT[g, :, :], w=[Ks_s])
                dma(Kw_s[:], KwT[g, :, :], w=[Kw_s])
                dma(Vs_s[:, :, 0:128], Vs[:, g, :].rearrange("(t p) d -> p t d", p=128), w=[Vs_s])
                dma(Vw_s[:, :, 0:128], Vw[:, g, :].rearrange("(t p) d -> p t d", p=128), w=[Vw_s])

                def ld4(t, g=g):
                    b = t % 4
                    tk = slice(t * 128, (t + 1) * 128)
                    dma(qT4[b][:], QaT[4 * g:4 * g + 4, :, tk].rearrange("h d t -> d h t"), w=[qT4[b]])
                    dma(btl[b][:], BT[g, :, tk], w=[btl[b]])
                    dma(gn4[b][:], Gn[tk, :], w=[gn4[b]])
                    dma(ocl[b][:], Oc[tk, 4 * g * 128:(4 * g + 4) * 128].rearrange("p (h d) -> p h d", h=4), w=[ocl[b]])
                    dma(bga[b][:], BG[tk, 4 * g * 128:(4 * g + 4) * 128], w=[bga[b]])

                def bt4(t):
                    b = t % 4
                    I(G, lambda: nc.gpsimd.tensor_copy(out=BT4[b][:], in_=btl[b][:].unsqueeze(1).broadcast_to([KJ, 4, 128])), r=[btl[b]], w=[BT4[b]])

                entries = []
                for t in range(NT):
                    prs = [("s", kt) for kt in range(t + 1)] + [("w", kt) for kt in range(max(0, t - 4), t + 1)]
                    for idx, (kind, kt) in enumerate(prs):
                        entries.append((t, kind, kt, idx == 0, idx == len(prs) - 1))
                base = cnt4[0]
                cnt4[0] += len(entries)
                ld4(0)
                if NT > 1:
                    ld4(1)
                bt4(0)

                def stA(i, e):
                    t, kind, kt, first_of_tile, last_of_tile = e
                    if first_of_tile:
                        if t + 2 < NT:
                            ld4(t + 2)
                        if t + 1 < NT:
                            bt4(t + 1)
                    b = t % 4
                    q_ = qT4[b]
                    k = base + i
                    p = pS[k % 2]
                    pt = PT[k % 3]
                    ksl = slice(kt * 128, (kt + 1) * 128)
                    if kind == "s":
                        I(PE, lambda: nc.tensor.matmul(p[:, :], lhsT=Ks_s[:, ksl], rhs=q_[:].rearrange("p h t -> p (h t)"), start=True, stop=False),
                          r=[Ks_s, q_], w=[p])
                        I(PE, lambda: nc.tensor.matmul(p[:, :], lhsT=esel[:, kt, :], rhs=BT4[b][:].rearrange("p h t -> p (h t)"), start=False, stop=True),
                          r=[esel, BT4[b]], w=[p])
                    else:
                        I(PE, lambda: nc.tensor.matmul(p[:, :], lhsT=Kw_s[:, ksl], rhs=q_[:].rearrange("p h t -> p (h t)"), start=True, stop=True),
                          r=[Kw_s, q_], w=[p])
                    I(A, lambda: nc.scalar.activation(out=pt[:].rearrange("p h t -> p (h t)"), in_=p[:, :], func=AF.Exp, scale=SC128),
                      r=[p], w=[pt])
                    mk = None
                    if kt == t:
                        mk = tri
                    elif kind == "w" and kt == t - 4:
                        mk = atri
                    if mk is not None:
                        I(G, lambda: nc.gpsimd.tensor_tensor(out=pt[:], in0=pt[:], in1=mk[:].unsqueeze(1).broadcast_to([128, 4, 128]), op=ALU.mult),
                          r=[pt, mk], w=[pt])

                def stB(i, e):
                    t, kind, kt, first_of_tile, last_of_tile = e
                    k = base + i
                    pt = PT[k % 3]
                    ptf = pt[:].rearrange("p h t -> p (h t)")
                    if kind == "s":
                        pO_, vv, acc_, first, last = pOTs[t % 2], Vs_s, acc_s[t % 2], (kt == 0), (kt == t)
                    else:
                        pO_, vv, acc_, first, last = pOTw[t % 2], Vw_s, acc_w[t % 2], (kt == max(0, t - 4)), (kt == t)
                    I(PE, lambda: nc.tensor.matmul(pO_[:, :], lhsT=vv[:, kt, 0:128], rhs=ptf, start=first, stop=last), r=[pt, vv], w=[pO_])
                    if kind == "s":
                        if first:
                            I(V, lambda: nc.vector.tensor_copy(out=acc_[:], in_=ptf), r=[pt], w=[acc_])
                        else:
                            I(V, lambda: nc.vector.tensor_tensor(out=acc_[:], in0=acc_[:], in1=ptf, op=ALU.add), r=[pt, acc_], w=[acc_])
                    else:
                        if first:
                            I(G, lambda: nc.gpsimd.tensor_copy(out=acc_[:], in_=ptf), r=[pt], w=[acc_])
                        else:
                            I(G, lambda: nc.gpsimd.tensor_tensor(out=acc_[:], in0=acc_[:], in1=ptf, op=ALU.add), r=[pt, acc_], w=[acc_])

                def finA(t, g=g):
                    b = t % 4
                    for ki, (pO_, acc_, oT_) in enumerate(((pOTs[t % 2], acc_s[t % 2], oT_s), (pOTw[t % 2], acc_w[t % 2], oT_w))):
                        for hh in range(4):
                            I(PE, lambda hh=hh, ki=ki, acc_=acc_: nc.tensor.matmul(pZ4[:, 8 * ki + 2 * hh:8 * ki + 2 * hh + 2], lhsT=acc_[:, hh * 128:(hh + 1) * 128],
                                                                                    rhs=ones_f[:, 0:2], start=True, stop=True), r=[acc_, ones_f], w=[pZ4])
                        I(A, lambda pO_=pO_, oT_=oT_: nc.scalar.copy(out=oT_[:], in_=pO_[:, :]), r=[pO_], w=[oT_])
                    for hh in range(4):
                        I(PE, lambda hh=hh: nc.tensor.transpose(out=pTr[:, hh * 128:(hh + 1) * 128], in_=oT_s[:, hh * 128:(hh + 1) * 128], identity=identf[:]),
                          r=[oT_s, identf], w=[pTr])
                    I(V, lambda: nc.vector.reciprocal(out=rzs[:, 0:8], in_=pZ4[:, 0:16:2]), r=[pZ4], w=[rzs])
                    gv = gn4[b][:].rearrange("p (h j) -> p h j", j=3)
                    I(V, lambda: nc.vector.tensor_tensor(out=cfs[:, 0:4], in0=rzs[:, 0:4], in1=gv[:, 4 * g:4 * g + 4, 1], op=ALU.mult), r=[rzs, gn4[b]], w=[cfs])
                    I(V, lambda: nc.vector.tensor_tensor(out=cfs[:, 4:8], in0=rzs[:, 4:8], in1=gv[:, 4 * g:4 * g + 4, 2], op=ALU.mult), r=[rzs, gn4[b]], w=[cfs])
                    for hh in range(4):
                        I(V, lambda hh=hh: nc.vector.scalar_tensor_tensor(out=oa[:, hh, :], in0=pTr[:, hh * 128:(hh + 1) * 128], scalar=cfs[:, hh:hh + 1],
                                                                          in1=ocl[b][:, hh, :], op0=ALU.mult, op1=ALU.add),
                          r=[pTr, cfs, ocl[b]], w=[oa])

                def finB(t, g=g):
                    b = t % 4
                    tk = slice(t * 128, (t + 1) * 128)
                    for hh in range(4):
                        I(PE, lambda hh=hh: nc.tensor.transpose(out=pTr[:, hh * 128:(hh + 1) * 128], in_=oT_w[:, hh * 128:(hh + 1) * 128], identity=identf[:]),
                          r=[oT_w, identf], w=[pTr])
                    for hh in range(4):
                        I(V, lambda hh=hh: nc.vector.scalar_tensor_tensor(out=oa[:, hh, :], in0=pTr[:, hh * 128:(hh + 1) * 128], scalar=cfs[:, 4 + hh:5 + hh],
                                                                          in1=oa[:, hh, :], op0=ALU.mult, op1=ALU.add),
                          r=[pTr, cfs, oa], w=[oa])
                    yb_ = yat[t % 2]
                    I(G, lambda: nc.gpsimd.tensor_tensor(out=yb_[:], in0=oa[:].rearrange("p h d -> p (h d)"), in1=bga[b][:], op=ALU.mult),
                      r=[oa, bga[b]], w=[yb_])
                    dma(Ya[tk, 4 * g * 128:(4 * g + 4) * 128], yb_[:], r=[yb_])

                n_e = len(entries)
                pend = []
                stA(0, entries[0])
                for i in range(n_e):
                    while pend and pend[0][0] <= i:
                        _, fn_, t_ = pend.pop(0)
                        fn_(t_)
                    if i + 1 < n_e:
                        stA(i + 1, entries[i + 1])
                    stB(i, entries[i])
                    if entries[i][4]:
                        pend.append((i + 2, finA, entries[i][0]))
                        pend.append((i + 4, finB, entries[i][0]))
                while pend:
                    _, fn_, t_ = pend.pop(0)
                    fn_(t_)
            fw.barrier()
        if upto == 4:
            return nc

        NQB = S // 512
        NBLK = 8 * NQB
        with contextlib.ExitStack() as st:
            KN = [sb(st, f"KN{i}", [128, S], BF16) for i in range(2)]
            KR = [sb(st, f"KR{i}", [128, S], BF16) for i in range(2)]
            VB = [sb(st, f"VB{i}", [128, NT, 129], BF16) for i in range(2)]
            QN = [sb(st, f"QN{i}", [128, 512], BF16) for i in range(4)]
            QR = [sb(st, f"QR{i}", [128, 512], BF16) for i in range(4)]
            PT = [sb(st, f"PTm{i}", [128, 512], BF16) for i in range(3)]
            bgb = [sb(st, f"bgb{i}", [128, 4, 128], BF16) for i in range(4)]
            yal = [sb(st, f"yal{i}", [128, 4, 128], F32) for i in range(4)]
            yo = [sb(st, f"yo{i}", [128, 4, 128], BF16) for i in range(4)]
            obf = sb(st, "obf", [128, 4, 128], F32)
            rz5 = sb(st, "rz5", [128, 4], F32)
            pS = [ps(st, f"mS{i}", [128, 512]) for i in range(2)]
            pOT5 = [ps(st, f"mOT{j}", [128, 512]) for j in range(2)]
            pTr5 = ps(st, "mTr", [128, 512])
            pZ5 = ps(st, "mZ", [128, 512])
            acc5 = [sb(st, f"acc5{j}", [128, 512], F32) for j in range(2)]
            oT5 = sb(st, "oT5", [128, 512], F32)
            for i in range(2):
                I(V, lambda i=i: nc.vector.memset(VB[i][:, :, 128:129], 1.0), w=[VB[i]])
                I(G, lambda i=i: nc.gpsimd.memset(KR[i][64:128, :], 0.0), w=[KR[i]])
            for i in range(4):
                I(G, lambda i=i: nc.gpsimd.memset(QR[i][64:128, :], 0.0), w=[QR[i]])

            def ldh(h):
                b = h % 2
                dma(KN[b][:], KbN[h, :, :], w=[KN[b]])
                dma(KR[b][0:64, :], KbR[h, :, :], w=[KR[b]])
                dma(VB[b][:, :, 0:128], Vb[:, h, :].rearrange("(t p) d -> p t d", p=128), w=[VB[b]])

            def ldq(n):
                h, qb = n // NQB, n % NQB
                bq = n % 4
                rows = slice(qb * 512, (qb + 1) * 512)
                dma(QN[bq][:], QbN[h, :, rows], w=[QN[bq]])
                dma(QR[bq][0:64, :], QbR[h, :, rows], w=[QR[bq]])
                dma(bgb[bq][:], BG[rows, 1024 + h * 128:1024 + (h + 1) * 128].rearrange("(s p) c -> p s c", p=128), w=[bgb[bq]])
                dma(yal[bq][:], Ya[rows, h * 128:(h + 1) * 128].rearrange("(s p) c -> p s c", p=128), w=[yal[bq]])

            entries = []
            for n in range(NBLK):
                qb = n % NQB
                nk = 4 * qb + 4
                for kt in range(nk):
                    entries.append((n, kt, kt == 0, kt == nk - 1))
            ldh(0)
            ldq(0)
            if NBLK > 1:
                ldq(1)

            def stA(i, e):
                n, kt, first, last = e
                h, qb = n // NQB, n % NQB
                if first and n + 2 < NBLK:
                    ldq(n + 2)
                if qb == 0 and kt == 3 and h + 1 < 8:
                    ldh(h + 1)
                b, bq = h % 2, n % 4
                p = pS[i % 2]
                pt = PT[i % 3]
                j = kt - 4 * qb
                c0 = 128 * max(j, 0)
                ksl = slice(kt * 128, (kt + 1) * 128)
                I(PE, lambda: nc.tensor.matmul(p[:, c0:512], lhsT=KN[b][:, ksl], rhs=QN[bq][:, c0:512], start=True, stop=False), r=[KN[b], QN[bq]], w=[p])
                I(PE, lambda: nc.tensor.matmul(p[:, c0:512], lhsT=KR[b][:, ksl], rhs=QR[bq][:, c0:512], start=False, stop=True), r=[KR[b], QR[bq]], w=[p])
                I(A, lambda: nc.scalar.activation(out=pt[:, c0:512], in_=p[:, c0:512], func=AF.Exp, scale=SC192), r=[p], w=[pt])
                if j >= 0:
                    I(G, lambda: nc.gpsimd.tensor_tensor(out=pt[:, c0:c0 + 128], in0=pt[:, c0:c0 + 128], in1=tri[:], op=ALU.mult), r=[pt, tri], w=[pt])

            def stB(i, e):
                n, kt, first, last = e
                h, qb = n // NQB, n % NQB
                b = h % 2
                pt = PT[i % 3]
                pO_, acc_ = pOT5[n % 2], acc5[n % 2]
                j = kt - 4 * qb
                c0 = 128 * max(j, 0)
                I(PE, lambda: nc.tensor.matmul(pO_[:, c0:512], lhsT=VB[b][:, kt, 0:128], rhs=pt[:, c0:512], start=first, stop=last),
                  r=[pt, VB[b]], w=[pO_])
                if first:
                    I(V, lambda: nc.vector.tensor_copy(out=acc_[:], in_=pt[:, 0:512]), r=[pt], w=[acc_])
                else:
                    I(V, lambda: nc.vector.tensor_tensor(out=acc_[:, c0:512], in0=acc_[:, c0:512], in1=pt[:, c0:512], op=ALU.add), r=[pt, acc_], w=[acc_])

            def fin5(n):
                h, qb = n // NQB, n % NQB
                bq = n % 4
                rows = slice(qb * 512, (qb + 1) * 512)
                pO_, acc_ = pOT5[n % 2], acc5[n % 2]
                for sub in range(4):
                    I(PE, lambda sub=sub: nc.tensor.matmul(pZ5[:, 2 * sub:2 * sub + 2], lhsT=acc_[:, sub * 128:(sub + 1) * 128], rhs=ones_f[:, 0:2],
                                                           start=True, stop=True), r=[acc_, ones_f], w=[pZ5])
                I(A, lambda: nc.scalar.copy(out=oT5[:], in_=pO_[:, :]), r=[pO_], w=[oT5])
                for sub in range(4):
                    I(PE, lambda sub=sub: nc.tensor.transpose(out=pTr5[:, sub * 128:(sub + 1) * 128], in_=oT5[:, sub * 128:(sub + 1) * 128], identity=identf[:]),
                      r=[oT5, identf], w=[pTr5])
                I(V, lambda: nc.vector.reciprocal(out=rz5[:, 0:4], in_=pZ5[:, 0:8:2]), r=[pZ5], w=[rz5])
                for sub in range(4):
                    I(V, lambda sub=sub: nc.vector.tensor_scalar(out=obf[:, sub, :], in0=pTr5[:, sub * 128:(sub + 1) * 128], scalar1=rz5[:, sub:sub + 1],
                                                                 scalar2=None, op0=ALU.mult), r=[pTr5, rz5], w=[obf])
                I(G, lambda: nc.gpsimd.tensor_tensor(out=obf[:], in0=obf[:], in1=bgb[bq][:], op=ALU.mult), r=[obf, bgb[bq]], w=[obf])
                I(G, lambda: nc.gpsimd.tensor_tensor(out=yo[bq][:], in0=obf[:], in1=yal[bq][:], op=ALU.add), r=[obf, yal[bq]], w=[yo[bq]])
                dma(Yb[rows, h * 128:(h + 1) * 128].rearrange("(s p) c -> p s c", p=128), yo[bq][:], r=[yo[bq]])

            n_e = len(entries)
            pend = []
            stA(0, entries[0])
            for i in range(n_e):
                while pend and pend[0][0] <= i:
                    _, n_ = pend.pop(0)
                    fin5(n_)
                if i + 1 < n_e:
                    stA(i + 1, entries[i + 1])
                stB(i, entries[i])
                if entries[i][3]:
                    pend.append((i + 2, entries[i][0]))
            while pend:
                _, n_ = pend.pop(0)
                fin5(n_)
            fw.barrier()
        if upto == 5:
            return nc

        with contextlib.ExitStack() as st:
            wob = sb(st, "wob", [128, 8, 1024], BF16)
            with contextlib.ExitStack() as st2:
                stg = [sb(st2, f"stgo{i}", [128, 1024], F32) for i in range(2)]
                load_cast(stg, wob, lambda i: wob[:, i, :], lambda i: w_o[i * 128:(i + 1) * 128, :], 8, [128, 1024])
                fw.barrier()
            modb = TO(st.enter_context(nc.sbuf_tensor("sb_mod6a", [128, 1024], F32)), 2048)
            dma(modb.t[:], MODB[:, 2048:3072], w=[modb])
            yt = [sb(st, f"yt{i}", [128, 1024], BF16) for i in range(2)]
            xl = [sb(st, f"xla{i}", [128, 1024], F32) for i in range(2)]
            YT = sb(st, "YT", [128, 8, 128], BF16)
            tmp = sb(st, "tmpa", [128, 1024], F32)
            x1t = [sb(st, f"x1t{i}", [128, 1024], F32) for i in range(2)]
            pTa = [ps(st, f"pTa{i}", [128, 1024], BF16) for i in range(2)]
            pA = [ps(st, f"pA{i}", [128, 512]) for i in range(4)]

            def ld6(t):
                b = t % 2
                tk = slice(t * 128, (t + 1) * 128)
                dma(yt[b][:], Yb[tk, :], w=[yt[b]])
                dma(xl[b][:], x[tk, :], w=[xl[b]])
            ld6(0)
            for t in range(NT):
                if t + 1 < NT:
                    ld6(t + 1)
                b = t % 2
                tk = slice(t * 128, (t + 1) * 128)
                tr_generic(pTa[t % 2], yt[b], lambda i: yt[b][:, i * 128:(i + 1) * 128], 8, 128, YT, YT[:])
                for half in range(2):
                    p = pA[(2 * t + half) % 4]
                    hs = slice(half * 512, (half + 1) * 512)
                    for c in range(8):
                        I(PE, lambda c=c, p=p, hs=hs: nc.tensor.matmul(p[:, :], lhsT=YT[:, c, :], rhs=wob[:, c, hs], start=(c == 0), stop=(c == 7)),
                          r=[YT, wob], w=[p])
                    gsl = slice(2048 + half * 512, 2048 + (half + 1) * 512)
                    I(V, lambda p=p, hs=hs, gsl=gsl: nc.vector.tensor_tensor(out=tmp[:, hs], in0=p[:, :], in1=modb[:, gsl], op=ALU.mult), r=[p, modb], w=[tmp])
                I(G, lambda: nc.gpsimd.tensor_tensor(out=x1t[b][:], in0=tmp[:], in1=xl[b][:], op=ALU.add), r=[tmp, xl[b]], w=[x1t[b]])
                dma(out[tk, :], x1t[b][:], r=[x1t[b]])
            fw.barrier()
        if upto == 6:
            return nc

        NB6 = S // 256
        with contextlib.ExitStack() as st:
            wupb = sb(st, "wupb", [128, 8, 5632], BF16)
            wdb = sb(st, "wdb", [128, 22, 1024], BF16)
            with contextlib.ExitStack() as st2:
                stg = [sb(st2, f"stgu{i}", [128, 5632], F32) for i in range(2)]
                load_cast(stg, wupb, lambda i: wupb[:, i, :], lambda i: w_up[i * 128:(i + 1) * 128, :], 8, [128, 5632])
                load_cast(stg, wdb, lambda i: wdb[:, i, :], lambda i: w_down[i * 128:(i + 1) * 128, :], 22, [128, 1024])
                fw.barrier()
            modb = TO(st.enter_context(nc.sbuf_tensor("sb_mod6b", [128, 3072], F32)), 3072)
            dma(modb.t[:], MODB[:, 3072:6144], w=[modb])
            wc = sb(st, "wc", [128, 44, 3], F32)
            bc = sb(st, "bc", [128, 44], F32)
            dma(wc[:], wconv_l, w=[wc])
            dma(bc[:], bconv_l, w=[bc])
            xb = [sb(st, f"xb{i}", [128, 2, 1024], F32) for i in range(2)]
            h2f = sb(st, "h2f", [128, 1024], F32)
            tmpd = sb(st, "tmpd", [128, 1024], F32)
            h2b = sb(st, "h2b", [128, 1024], BF16)
            h2T = [sb(st, f"h2T{i}", [128, 8, 256], BF16) for i in range(2)]
            zb = [sb(st, f"zb{i}", [128, 258], F32) for i in range(3)]
            uv = [sb(st, f"uv{i}", [128, 256], F32) for i in range(2)]
            ug = [sb(st, f"ug{i}", [128, 256], F32) for i in range(2)]
            sgm = [sb(st, f"sgm{i}", [128, 256], F32) for i in range(2)]
            actT = sb(st, "actT", [128, 22, 256], BF16)
            halo = sb(st, "halo", [128, 44, 2], F32)
            ss6 = sb(st, "ss6", [128, 2], F32)
            rs6 = sb(st, "rs6", [128, 2], F32)
            pTb = [ps(st, f"pTb{i}", [128, 1024], BF16) for i in range(2)]
            pU = [ps(st, f"pU{i}", [128, 512]) for i in range(3)]
            pD = [ps(st, f"pD{i}", [128, 512]) for i in range(2)]
            I(V, lambda: nc.vector.memset(halo[:], 0.0), w=[halo])
            nu = [0]

            def load6(blk):
                rows = slice(blk * 256, (blk + 1) * 256)
                dma(xb[blk % 2][:], out[rows, :].rearrange("(s p) c -> p s c", p=128), w=[xb[blk % 2]])

            def prep6(blk):
                xb_ = xb[blk % 2]
                hT_ = h2T[blk % 2]
                for s_ in range(2):
                    I(A, lambda s_=s_: nc.scalar.activation(out=h2b[:], in_=xb_[:, s_, :], func=AF.Square, accum_out=ss6[:, s_:s_ + 1]), r=[xb_], w=[h2b, ss6])
                rsqrt_ms(ss6, ss6[:], rs6, rs6[:], 1.0 / 1024)
                for s_ in range(2):
                    I(V, lambda s_=s_: nc.vector.scalar_tensor_tensor(out=h2f[:], in0=xb_[:, s_, :], scalar=rs6[:, s_:s_ + 1], in1=modb[:, A_FFN],
                                                                      op0=ALU.mult, op1=ALU.mult), r=[xb_, rs6, modb], w=[h2f])
                    I(V, lambda: nc.vector.tensor_tensor(out=h2b[:], in0=h2f[:], in1=modb[:, SH_F], op=ALU.add), r=[h2f, modb], w=[h2b])
                    tr_generic(pTb[s_], h2b, lambda i: h2b[:, i * 128:(i + 1) * 128], 8, 128, hT_, hT_[:, :, s_ * 128:(s_ + 1) * 128])

            def gate6(k):
                sg_ = sgm[k % 2]
                I(A, lambda: nc.scalar.activation(out=sg_[:], in_=ug[k % 2][:], func=AF.Silu), r=[ug[k % 2]], w=[sg_])
                I(G, lambda: nc.gpsimd.tensor_tensor(out=actT[:, k, :], in0=sg_[:], in1=uv[k % 2][:], op=ALU.mult), r=[sg_, uv[k % 2]], w=[actT])

            load6(0)
            prep6(0)
            for blk in range(NB6):
                rows = slice(blk * 256, (blk + 1) * 256)
                xb_ = xb[blk % 2]
                hT_ = h2T[blk % 2]
                if blk + 1 < NB6:
                    load6(blk + 1)
                for k in range(22):
                    for which, fc in ((0, k), (1, 22 + k)):
                        i = nu[0]
                        nu[0] += 1
                        p = pU[i % 3]
                        z = zb[i % 3]
                        u = (uv if which == 0 else ug)[k % 2]
                        for c in range(8):
                            I(PE, lambda c=c, p=p, fc=fc: nc.tensor.matmul(p[:, 0:256], lhsT=wupb[:, c, fc * 128:(fc + 1) * 128], rhs=hT_[:, c, :],
                                                                           start=(c == 0), stop=(c == 7)), r=[wupb, hT_], w=[p])
                        I(G, lambda z=z, fc=fc: nc.gpsimd.tensor_copy(out=z[:, 0:2], in_=halo[:, fc, :]), r=[halo], w=[z])
                        I(A, lambda z=z, p=p: nc.scalar.copy(out=z[:, 2:258], in_=p[:, 0:256]), r=[p], w=[z])
                        I(G, lambda z=z, fc=fc: nc.gpsimd.tensor_copy(out=halo[:, fc, :], in_=z[:, 256:258]), r=[z], w=[halo])
                        I(A, lambda u=u, p=p, fc=fc: nc.scalar.activation(out=u[:], in_=p[:, 0:256], func=AF.Identity, scale=wc[:, fc, 2:3], bias=bc[:, fc:fc + 1]),
                          r=[p, wc, bc], w=[u])
                        I(V, lambda u=u, z=z, fc=fc: nc.vector.scalar_tensor_tensor(out=u[:], in0=z[:, 1:257], scalar=wc[:, fc, 1:2], in1=u[:],
                                                                                    op0=ALU.mult, op1=ALU.add), r=[z, wc, u], w=[u])
                        I(V, lambda u=u, z=z, fc=fc: nc.vector.scalar_tensor_tensor(out=u[:], in0=z[:, 0:256], scalar=wc[:, fc, 0:1], in1=u[:],
                                                                                    op0=ALU.mult, op1=ALU.add), r=[z, wc, u], w=[u])
                    if k >= 1:
                        gate6(k - 1)
                gate6(21)
                if blk + 1 < NB6:
                    prep6(blk + 1)
                for s_ in range(2):
                    for half in range(2):
                        p = pD[half]
                        hs = slice(half * 512, (half + 1) * 512)
                        for k in range(22):
                            I(PE, lambda k=k, p=p, hs=hs, s_=s_: nc.tensor.matmul(p[:, :], lhsT=actT[:, k, s_ * 128:(s_ + 1) * 128], rhs=wdb[:, k, hs],
                                                                                  start=(k == 0), stop=(k == 21)), r=[actT, wdb], w=[p])
                        gsl = slice(5120 + half * 512, 5120 + (half + 1) * 512)
                        I(V, lambda p=p, hs=hs, gsl=gsl: nc.vector.tensor_tensor(out=tmpd[:, hs], in0=p[:, :], in1=modb[:, gsl], op=ALU.mult), r=[p, modb], w=[tmpd])
                    I(G, lambda s_=s_: nc.gpsimd.tensor_tensor(out=xb_[:, s_, :], in0=tmpd[:], in1=xb_[:, s_, :], op=ALU.add), r=[tmpd, xb_], w=[xb_])
                dma(out[rows, :].rearrange("(s p) c -> p s c", p=128), xb_[:], r=[xb_])
            fw.barrier()
    return nc


_PARAM_NAMES = ["w_ada", "b_ada", "attn_norm", "ffn_norm", "w_in", "nsa_q_norm", "nsa_kc_norm", "nsa_ks_norm", "nsa_kw_norm",
                "cmp_k_w1", "cmp_k_w2", "cmp_v_w1", "cmp_v_w2", "mla_cq_norm", "mla_ckv_norm", "w_uq", "w_ukv",
                "mla_q_norm", "mla_k_norm", "w_o", "w_up", "w_down"]
_CONSTS = {}


def make_in_map(inp, b, S):
    if S not in _CONSTS:
        _CONSTS[S] = host_consts(S)
    m = {}
    m["x"] = np.ascontiguousarray(np.asarray(inp["x"])[b, :S], dtype=np.float32)
    m["ccol"] = np.ascontiguousarray(np.asarray(inp["c"])[b].reshape(8, 128).T, dtype=np.float32)
    for k in _PARAM_NAMES:
        m[k] = np.ascontiguousarray(np.asarray(inp[k])[0], dtype=np.float32)
    m["pe_kT"] = np.ascontiguousarray(np.asarray(inp["cmp_k_pe"])[0].T, dtype=np.float32)
    m["pe_vT"] = np.ascontiguousarray(np.asarray(inp["cmp_v_pe"])[0].T, dtype=np.float32)
    m["wconv_l"] = np.ascontiguousarray(np.asarray(inp["w_conv"])[0].reshape(3, 44, 128).transpose(2, 1, 0), dtype=np.float32)
    m["bconv_l"] = np.ascontiguousarray(np.asarray(inp["b_conv"])[0].reshape(44, 128).T, dtype=np.float32)
    m.update(_CONSTS[S])
    return m


_NC = {}


def kernel(**inputs):
    S = 8192
    if S not in _NC:
        _NC[S] = build(S)
    nc = _NC[S]
    in_maps = [make_in_map(inputs, b, S) for b in range(8)]
    res = run_bass_kernel_spmd(nc, in_maps, core_ids=list(range(8)))
    return np.stack([np.asarray(r["out"], dtype=np.float32) for r in res.results], axis=0)
```

```python
import contextlib
import numpy as np
import ml_dtypes
import concourse.bass as bass
import concourse.mybir as mybir
from concourse.bass_utils import run_bass_kernel_spmd

F32 = mybir.dt.float32
BF16 = mybir.dt.bfloat16
AF = mybir.ActivationFunctionType
ALU = mybir.AluOpType
AX = mybir.AxisListType

EPS = 1e-6
NEGB = -30000.0
EXPB = -4.0


class Buf:
    __slots__ = ("w", "r")

    def __init__(self):
        self.w = {}
        self.r = {}


class T:
    def __init__(self, t):
        self.t = t
        self.b = Buf()

    def __getitem__(self, k):
        return self.t[k]


class TO(T):
    def __init__(self, t, off):
        super().__init__(t)
        self.off = off

    def __getitem__(self, k):
        p, c = k
        return self.t[p, slice(c.start - self.off, c.stop - self.off)]


def _b(x):
    return x.b if isinstance(x, T) else x


class FW:
    ROT = 12000
    NQ = 24

    def __init__(self, nc, es):
        self.nc, self.es = nc, es
        self.E = {"pe": nc.tensor, "act": nc.scalar, "dve": nc.vector, "pool": nc.gpsimd, "sp": nc.sync}
        self.sem, self.cnt = {}, {}
        self.nsem = 0
        for e in self.E:
            self._newsem(e)
        self.waited = {e: {} for e in self.E}
        self.dq = {}
        self.n = 0
        self.rec = None

    def _newsem(self, e):
        s = self.es.enter_context(self.nc.semaphore(f"s{e}{self.nsem}"))
        self.nsem += 1
        self.sem[e] = s
        self.cnt[e] = 0

    def _wait(self, e, deps):
        for s, (v, pe) in deps.items():
            if self.waited[e].get(s, 0) >= v:
                continue
            self.E[e].wait_ge(s, v)
            self.waited[e][s] = v
            self.n += 1

    def _deps(self, e, r, w):
        deps = {}

        def add(d, raw):
            for s, (v, pe) in d.items():
                if pe == e and e != "dma":
                    if e == "pe" or not raw:
                        continue
                if deps.get(s, (0,))[0] < v:
                    deps[s] = (v, pe)
        for b in r:
            add(_b(b).w, True)
        for b in w:
            add(_b(b).w, False)
            add(_b(b).r, False)
        return deps

    def I(self, e, fn, r=(), w=()):
        if self.rec is not None:
            r, w = list(r), list(w)
            self.rec.append(lambda: self._I(e, fn, r, w))
            return None
        return self._I(e, fn, r, w)

    def _I(self, e, fn, r=(), w=()):
        deps = self._deps(e, r, w)
        pend = [(s_, v_) for s_, (v_, pe_) in deps.items() if self.waited[e].get(s_, 0) < v_]
        if len(pend) > 1:
            self._wait(e, {s_: (v_, "x") for s_, v_ in pend[:-1]})
        if self.cnt[e] >= self.ROT:
            self._newsem(e)
        inst = fn()
        if pend:
            s_, v_ = pend[-1]
            inst._wait_ge(s_, v_)
            self.waited[e][s_] = v_
        s = self.sem[e]
        inst.then_inc(s, 1)
        self.cnt[e] += 1
        self.n += 1
        tok = (self.cnt[e], e)
        for b in w:
            b = _b(b)
            b.w = {s: tok}
            b.r = {}
        for b in r:
            _b(b).r[s] = tok
        return inst

    def dma(self, out, in_, r=(), w=(), q="sp"):
        if self.rec is not None:
            r, w = list(r), list(w)
            self.rec.append(lambda: self._dma(out, in_, r, w, q))
            return None
        return self._dma(out, in_, r, w, q)

    def record(self, fn):
        self.rec = []
        fn()
        lst, self.rec = self.rec, None
        return lst

    @staticmethod
    def interleave_skewed(streams):
        n = len(streams)
        L = max(len(st_) for st_ in streams)
        pos = [0] * n
        start = [0] + [0] * (n - 1)
        i = 0
        while any(pos[k] < len(streams[k]) for k in range(n)):
            for k in range(n):
                if i >= start[k] and pos[k] < len(streams[k]):
                    streams[k][pos[k]]()
                    pos[k] += 1
            i += 1

    @staticmethod
    def interleave(lists):
        for i in range(max(len(l) for l in lists)):
            for l in lists:
                if i < len(l):
                    l[i]()

    def _dma(self, out, in_, r=(), w=(), q="sp"):
        self._wait(q, self._deps("dma", r, w))
        d = self.dq.setdefault(q, {"sems": [], "i": 0})
        if len(d["sems"]) < self.NQ:
            s = self.es.enter_context(self.nc.semaphore(f"d{q}{len(d['sems'])}"))
            ent = [s, 0]
            d["sems"].append(ent)
        else:
            ent = d["sems"][d["i"] % self.NQ]
            d["i"] += 1
            self._wait(q, {ent[0]: (16 * ent[1], "dma")})
        inst = self.E[q].dma_start(out=out, in_=in_)
        inst.then_inc(ent[0], 16)
        ent[1] += 1
        self.n += 1
        tok = (16 * ent[1], "dma")
        for b in w:
            b = _b(b)
            b.w = {ent[0]: tok}
            b.r = {}
        for b in r:
            _b(b).r[ent[0]] = tok

    def barrier(self):
        toks = {}
        for e in self.E:
            if self.cnt[e] > 0:
                toks[self.sem[e]] = (self.cnt[e], "x")
        for q, d in self.dq.items():
            for s, c in d["sems"]:
                if c:
                    toks[s] = (16 * c, "dma")
        for e in self.E:
            self._wait(e, toks)


O_NQ, O_NKC, O_NVC, O_NKS, O_NVS, O_NKW, O_NVW, O_NG, O_CQ, O_CKV, O_KR, O_BG = (
    0, 1024, 1280, 1536, 1792, 2048, 2304, 2560, 2584, 2968, 3224, 3288)
IN_W = 5336


def host_consts(S):
    NT = S // 128
    NSEL = S // 64
    n_cmp = S // 16 - 1
    pos = np.arange(S, dtype=np.float32)
    inv128 = (10000.0 ** (-np.arange(64, dtype=np.float32) * 2.0 / 128)).astype(np.float32)
    inv64 = (10000.0 ** (-np.arange(32, dtype=np.float32) * 2.0 / 64)).astype(np.float32)
    a128 = pos[:, None] * inv128[None, :]
    a64 = pos[:, None] * inv64[None, :]
    c = {}
    c["cs128"] = np.concatenate([np.cos(a128), np.sin(a128)], axis=1).astype(np.float32)
    c["cs64"] = np.concatenate([np.cos(a64), np.sin(a64)], axis=1).astype(np.float32)
    p = np.arange(128)[:, None]
    f = np.arange(128)[None, :]
    c["ident"] = (p == f).astype(ml_dtypes.bfloat16)
    c["identf"] = (p == f).astype(np.float32)
    c["tri"] = (p <= f).astype(ml_dtypes.bfloat16)
    c["atri"] = (p > f).astype(ml_dtypes.bfloat16)
    W0 = 512 + 8 * (NT - 1)
    cc = np.arange(W0)[None, :]
    m = cc - 8 * (NT - 1)
    c["m0ext"] = ((16 * m + 31) <= p).astype(np.float32)
    CO = 2 * (NT - 1)
    W1 = NSEL + CO
    cc = np.arange(W1)[None, :]
    d = cc - CO
    hi = (p >= 64).astype(np.int64)
    c["aext"] = (d <= hi - 2).astype(np.float32)
    forced = (d == hi) | (d == hi - 1)
    c["fext"] = np.where(forced, 1e9, np.where(d > hi, -1.0, 0.0)).astype(np.float32)
    KJ = min(128, NSEL)
    E = np.zeros((KJ, NT, 128), dtype=np.float32)
    for kt in range(NT):
        E[2 * kt, kt, :64] = 1.0
        E[2 * kt + 1, kt, 64:] = 1.0
    c["esel"] = E.astype(ml_dtypes.bfloat16)
    return c


def build(S, dbg=False, upto=99):
    NT = S // 128
    NSEL = S // 64
    KJ = min(128, NSEL)
    n_cmp = S // 16 - 1
    NCT = (n_cmp + 127) // 128
    NCP = NCT * 128
    CO = 2 * (NT - 1)
    nc = bass.Bass("TRN2", target_bir_lowering=False)
    okind = "ExternalOutput"

    def din(name, shape, dt=F32):
        return nc.dram_tensor(name, list(shape), dt, kind="ExternalInput").ap()

    def dscr(name, shape, dt):
        return nc.dram_tensor(name, list(shape), dt, kind=okind).ap()

    x = din("x", [S, 1024])
    ccol = din("ccol", [128, 8])
    w_ada = din("w_ada", [1024, 6144])
    b_ada = din("b_ada", [6144])
    attn_norm = din("attn_norm", [1024])
    ffn_norm = din("ffn_norm", [1024])
    w_in = din("w_in", [1024, IN_W])
    g_q = din("nsa_q_norm", [128])
    g_kc = din("nsa_kc_norm", [128])
    g_ks = din("nsa_ks_norm", [128])
    g_kw = din("nsa_kw_norm", [128])
    pe_kT = din("pe_kT", [128, 32])
    k_w1 = din("cmp_k_w1", [4096, 256])
    k_w2 = din("cmp_k_w2", [256, 128])
    pe_vT = din("pe_vT", [128, 32])
    v_w1 = din("cmp_v_w1", [4096, 256])
    v_w2 = din("cmp_v_w2", [256, 128])
    g_cq = din("mla_cq_norm", [384])
    g_ckv = din("mla_ckv_norm", [256])
    w_uq = din("w_uq", [384, 1536])
    w_ukv = din("w_ukv", [256, 2048])
    g_mq = din("mla_q_norm", [192])
    g_mk = din("mla_k_norm", [192])
    w_o = din("w_o", [1024, 1024])
    w_up = din("w_up", [1024, 5632])
    wconv_l = din("wconv_l", [128, 44, 3])
    bconv_l = din("bconv_l", [128, 44])
    w_down = din("w_down", [2816, 1024])
    cs128 = din("cs128", [S, 128])
    cs64 = din("cs64", [S, 64])
    c_ident = din("ident", [128, 128], BF16)
    c_identf = din("identf", [128, 128])
    c_tri = din("tri", [128, 128], BF16)
    c_atri = din("atri", [128, 128], BF16)
    c_m0 = din("m0ext", [128, 512 + 8 * (NT - 1)])
    c_aext = din("aext", [128, NSEL + CO])
    c_fext = din("fext", [128, NSEL + CO])
    c_esel = din("esel", [KJ, NT, 128], BF16)
    out = nc.dram_tensor("out", [S, 1024], F32, kind="ExternalOutput").ap()

    QaT = dscr("QaT", [8, 128, S], BF16)
    KcT = dscr("KcT", [2, 128, S], BF16)
    VcT = dscr("VcT", [2, 128, S], BF16)
    KsT = dscr("KsT", [2, 128, S], BF16)
    KwT = dscr("KwT", [2, 128, S], BF16)
    Vs = dscr("Vs", [S, 2, 128], BF16)
    Vw = dscr("Vw", [S, 2, 128], BF16)
    Gn = dscr("Gn", [S, 24], F32)
    BG = dscr("BG", [S, 2048], BF16)
    QbN = dscr("QbN", [8, 128, S], BF16)
    QbR = dscr("QbR", [8, 64, S], BF16)
    KbN = dscr("KbN", [8, 128, S], BF16)
    KbR = dscr("KbR", [8, 64, S], BF16)
    Vb = dscr("Vb", [S, 8, 128], BF16)
    Oc = dscr("Oc", [S, 1024], F32)
    BT = dscr("BT", [2, KJ, S], BF16)
    Ya = dscr("Ya", [S, 1024], F32)
    Yb = dscr("Yb", [S, 1024], BF16)
    MODB = dscr("MODB", [128, 6144], F32)

    with contextlib.ExitStack() as es:
        fw = FW(nc, es)
        global LASTFW
        LASTFW = fw
        I = fw.I
        dma = fw.dma
        V, G, A, PE = "dve", "pool", "act", "pe"

        def sb(st, name, shape, dt):
            return T(st.enter_context(nc.sbuf_tensor("sb_" + name, list(shape), dt)))

        def ps(st, name, shape, dt=F32):
            return T(st.enter_context(nc.psum_tensor("ps_" + name, list(shape), dt)))

        ident = sb(es, "ident", [128, 128], BF16)
        tri = sb(es, "tri", [128, 128], BF16)
        atri = sb(es, "atri", [128, 128], BF16)
        dma(ident[:], c_ident, w=[ident])
        dma(tri[:], c_tri, w=[tri])
        identf = sb(es, "identf", [128, 128], F32)
        dma(identf[:], c_identf, w=[identf])
        ones_f = sb(es, "ones_f", [128, 2], F32)
        I(V, lambda: nc.vector.memset(ones_f[:], 1.0), w=[ones_f])
        dma(atri[:], c_atri, w=[atri])
        SH_A, A_ATT, G_A, SH_F, A_FFN, G_F = [slice(i * 1024, (i + 1) * 1024) for i in range(6)]

        def rsqrt_ms(ssT, ss_ap, rsT, rs_ap, inv_n, rows=128):
            I(A, lambda: nc.scalar.activation(out=rs_ap, in_=ss_ap, func=AF.Sqrt, bias=eps_t[0:rows, 0:1], scale=inv_n),
              r=[ssT, eps_t], w=[rsT])
            I(V, lambda: nc.vector.reciprocal(out=rs_ap, in_=rs_ap), r=[rsT], w=[rsT])

        eps_t = sb(es, "eps_t", [128, 2], F32)
        I(V, lambda: nc.vector.memset(eps_t[:, 0:1], EPS), w=[eps_t])
        I(V, lambda: nc.vector.memset(eps_t[:, 1:2], EXPB), w=[eps_t])

        with contextlib.ExitStack() as st:
            modb = sb(st, "modb", [128, 6144], F32)
            cs_t = sb(st, "cs_t", [128, 8], F32)
            sc_t = sb(st, "sc_t", [128, 8], F32)
            scb = sb(st, "scb", [128, 8, 128], F32)
            wst = [sb(st, f"wst{i}", [128, 3072], F32) for i in range(2)]
            gtmp = sb(st, "gtmp", [128, 1024], F32)
            pm = [ps(st, f"pm{i}", [128, 512]) for i in range(6)]
            dma(cs_t[:], ccol, w=[cs_t])
            dma(modb[:], b_ada.partition_broadcast(128), w=[modb])
            I(A, lambda: nc.scalar.activation(out=sc_t[:], in_=cs_t[:], func=AF.Silu), r=[cs_t], w=[sc_t])
            I(V, lambda: nc.vector.tensor_copy(out=scb[:], in_=sc_t[:].unsqueeze(2).broadcast_to([128, 8, 128])),
              r=[sc_t], w=[scb])
            for half in range(2):
                for kc in range(8):
                    wb = wst[kc % 2]
                    dma(wb[:], w_ada[kc * 128:(kc + 1) * 128, half * 3072:(half + 1) * 3072], w=[wb])
                    for j in range(6):
                        I(PE, lambda j=j, wb=wb, kc=kc: nc.tensor.matmul(
                            pm[j][:], lhsT=scb[:, kc, :], rhs=wb[:, j * 512:(j + 1) * 512],
                            start=(kc == 0), stop=(kc == 7)), r=[scb, wb], w=[pm[j]])
                for j in range(6):
                    cs = slice(half * 3072 + j * 512, half * 3072 + (j + 1) * 512)
                    I(V, lambda j=j, cs=cs: nc.vector.tensor_tensor(out=modb[:, cs], in0=pm[j][:], in1=modb[:, cs],
                                                                    op=ALU.add), r=[pm[j], modb], w=[modb])
            for gsrc, sl in ((attn_norm, A_ATT), (ffn_norm, A_FFN)):
                dma(gtmp[:], gsrc.partition_broadcast(128), w=[gtmp])
                I(V, lambda sl=sl: nc.vector.scalar_tensor_tensor(out=modb[:, sl], in0=modb[:, sl], scalar=1.0,
                                                                  in1=gtmp[:], op0=ALU.add, op1=ALU.mult),
                  r=[modb, gtmp], w=[modb])
            dma(MODB, modb[:], r=[modb])
            fw.barrier()

        def load_cast(st_scratch, dst, dst_ap_fn, src_ap_fn, nchunks, shape, engs=(V, G, A)):
            for i in range(nchunks):
                stg = st_scratch[i % len(st_scratch)]
                dma(stg[tuple(slice(0, s_) for s_ in shape)] if False else stg_view(stg, shape), src_ap_fn(i), w=[stg])
                e = engs[i % len(engs)]
                if e == A:
                    I(A, lambda i=i, stg=stg: nc.scalar.copy(out=dst_ap_fn(i), in_=stg_view(stg, shape)), r=[stg], w=[dst])
                elif e == V:
                    I(V, lambda i=i, stg=stg: nc.vector.tensor_copy(out=dst_ap_fn(i), in_=stg_view(stg, shape)), r=[stg], w=[dst])
                else:
                    I(G, lambda i=i, stg=stg: nc.gpsimd.tensor_copy(out=dst_ap_fn(i), in_=stg_view(stg, shape)), r=[stg], w=[dst])

        def stg_view(stg, shape):
            n = 1
            for s_ in shape[1:]:
                n *= s_
            v = stg[0:shape[0], 0:n]
            if len(shape) == 3:
                v = v.rearrange("p (a b) -> p a b", a=shape[1])
            return v

        def bcast_load(st, name, src, n):
            t = sb(st, name, [128, n], F32)
            dma(t[:], src.partition_broadcast(128), w=[t])
            return t

        with contextlib.ExitStack() as st:
            winb = sb(st, "winb", [128, 8, IN_W], BF16)
            wuqb = sb(st, "wuqb", [128, 3, 1536], BF16)
            wukvb = sb(st, "wukvb", [128, 2, 2048], BF16)
            with contextlib.ExitStack() as st2:
                stg = [sb(st2, f"stg{i}", [128, IN_W], F32) for i in range(2)]
                load_cast(stg, winb, lambda i: winb[:, i, :], lambda i: w_in[i * 128:(i + 1) * 128, :], 8, [128, IN_W])
                load_cast(stg, wuqb, lambda i: wuqb[:, i, :], lambda i: w_uq[i * 128:(i + 1) * 128, :], 3, [128, 1536])
                load_cast(stg, wukvb, lambda i: wukvb[:, i, :], lambda i: w_ukv[i * 128:(i + 1) * 128, :], 2, [128, 2048])
                fw.barrier()
            modb = TO(st.enter_context(nc.sbuf_tensor("sb_mod1", [128, 2048], F32)), 0)
            dma(modb.t[:], MODB[:, 0:2048], w=[modb])
            gq_t = bcast_load(st, "gq_t", g_q, 128)
            gks_t = bcast_load(st, "gks_t", g_ks, 128)
            gkw_t = bcast_load(st, "gkw_t", g_kw, 128)
            gcq_t = bcast_load(st, "gcq_t", g_cq, 384)
            gckv_t = bcast_load(st, "gckv_t", g_ckv, 256)
            gmq_t = bcast_load(st, "gmq_t", g_mq, 192)
            gmk_t = bcast_load(st, "gmk_t", g_mk, 192)

            def mk1(u):
                B = {}
                B["cst2"] = [sb(st, f"cst_{u}{i}", [128, 192], F32) for i in range(2)]
                for nm, shp, dt in (("xt", [128, 1024], F32), ("ss1", [128, 1], F32), ("rs1", [128, 1], F32),
                                    ("nb", [128, 8, 192], BF16), ("hT", [128, 8, 128], BF16), ("f_a", [128, 1536], F32),
                                    ("f_b", [128, 1536], F32), ("f_d", [128, 1024], F32), ("ssn", [128, 8], F32), ("rsn", [128, 8], F32),
                                    ("vbt", [128, 8, 128], BF16), ("gnt", [128, 24], F32), ("cqT", [128, 3, 128], BF16),
                                    ("ckvT", [128, 2, 128], BF16), ("krf", [128, 64], F32)):
                    B[nm] = sb(st, f"{nm}_{u}", shp, dt)
                B["bgt"] = [sb(st, f"bgt_{u}{i}", [128, 512], BF16) for i in range(2)]
                B["sgA"] = [sb(st, f"sgA_{u}{i}", [128, 8, 128], BF16) for i in range(2)]
                B["sgR"] = [sb(st, f"sgR_{u}{i}", [64, 8, 128], BF16) for i in range(2)]
                B["pp"] = [ps(st, f"pp_{u}{i}", [128, 512]) for i in range(2)]
                B["pT"] = ps(st, f"pT_{u}", [128, 1024], BF16)
                B["npp"] = 0
                B["nA"] = 0
                B["nR"] = 0
                return B
            sets1 = [mk1(0), mk1(1)]

            def load_tile(t):
                B = sets1[t % 2]
                dma(B["xt"][:], x[t * 128:(t + 1) * 128, :], w=[B["xt"]])
                c_ = B["cst2"][(t // 2) % 2]
                dma(c_[:, 0:128], cs128[t * 128:(t + 1) * 128, :], w=[c_])
                dma(c_[:, 128:192], cs64[t * 128:(t + 1) * 128, :], w=[c_])

            def p1_tile(t):
                B = sets1[t % 2]
                xtt, ss1, rs1, nb, hT, f_a, f_b, f_d = (B[k] for k in ("xt", "ss1", "rs1", "nb", "hT", "f_a", "f_b", "f_d"))
                cs_ = B["cst2"][(t // 2) % 2]
                ssn, rsn, vbt, gnt, cqT, ckvT, krf, pT_ = (B[k] for k in ("ssn", "rsn", "vbt", "gnt", "cqT", "ckvT", "krf", "pT"))
                tok = slice(t * 128, (t + 1) * 128)
                nbf = nb[:].rearrange("p h d -> p (h d)")

                def proj(lhsT_t, lhs_fn, nk, w_t, c0, c1):
                    p = B["pp"][B["npp"] % 2]
                    B["npp"] += 1
                    for kc in range(nk):
                        I(PE, lambda kc=kc: nc.tensor.matmul(p[:, 0:c1 - c0], lhsT=lhs_fn(kc), rhs=w_t[:, kc, c0:c1],
                                                             start=(kc == 0), stop=(kc == nk - 1)), r=[lhsT_t, w_t], w=[p])
                    return p

                def evac(p, c, dstT, dst_ap, eng=A):
                    if eng == A:
                        I(A, lambda: nc.scalar.copy(out=dst_ap, in_=p[:, 0:c]), r=[p], w=[dstT])
                    else:
                        I(V, lambda: nc.vector.tensor_copy(out=dst_ap, in_=p[:, 0:c]), r=[p], w=[dstT])

                def norm_rope(src, src_ap, nh, hd, gain_t, do_norm, rope_off, rope_half, cs_off, dst_ap, dstT):
                    if do_norm:
                        sq = f_b[:, 0:nh * hd].rearrange("p (h d) -> p h d", h=nh)
                        I(A, lambda: nc.scalar.activation(out=sq, in_=src_ap, func=AF.Square), r=[src], w=[f_b])
                        I(V, lambda: nc.vector.tensor_reduce(out=ssn[:, 0:nh], in_=sq, axis=AX.X, op=ALU.add), r=[f_b], w=[ssn])
                        rsqrt_ms(ssn, ssn[:, 0:nh], rsn, rsn[:, 0:nh], 1.0 / hd)
                        I(V, lambda: nc.vector.tensor_tensor(out=src_ap, in0=src_ap, in1=rsn[:, 0:nh].unsqueeze(2).broadcast_to([128, nh, hd]),
                                                             op=ALU.mult), r=[src, rsn], w=[src])
                        I(G, lambda: nc.gpsimd.tensor_tensor(out=src_ap, in0=src_ap, in1=gain_t[:, 0:hd].unsqueeze(1).broadcast_to([128, nh, hd]),
                                                             op=ALU.mult), r=[src, gain_t], w=[src])
                    if rope_half == 0:
                        I(V, lambda: nc.vector.tensor_copy(out=dst_ap, in_=src_ap), r=[src], w=[dstT])
                        return
                    if rope_off > 0:
                        I(G, lambda: nc.gpsimd.tensor_copy(out=dst_ap[:, :, 0:rope_off], in_=src_ap[:, :, 0:rope_off]), r=[src], w=[dstT])
                    hh_ = rope_half
                    x1 = src_ap[:, :, rope_off:rope_off + hh_]
                    x2 = src_ap[:, :, rope_off + hh_:rope_off + 2 * hh_]
                    cb = cs_[:, cs_off:cs_off + hh_].unsqueeze(1).broadcast_to([128, nh, hh_])
                    sbb = cs_[:, cs_off + hh_:cs_off + 2 * hh_].unsqueeze(1).broadcast_to([128, nh, hh_])
                    t1 = f_d[:, 0:nh * hh_].rearrange("p (h d) -> p h d", h=nh)
                    t2 = f_d[:, 512:512 + nh * hh_].rearrange("p (h d) -> p h d", h=nh)
                    t3 = f_b[:, 0:nh * hh_].rearrange("p (h d) -> p h d", h=nh)
                    t4 = f_b[:, 512:512 + nh * hh_].rearrange("p (h d) -> p h d", h=nh)
                    I(V, lambda: nc.vector.tensor_tensor(out=t1, in0=x1, in1=cb, op=ALU.mult), r=[src, cs_], w=[f_d])
                    I(V, lambda: nc.vector.tensor_tensor(out=t2, in0=x2, in1=sbb, op=ALU.mult), r=[src, cs_], w=[f_d])
                    I(G, lambda: nc.gpsimd.tensor_tensor(out=t3, in0=x1, in1=sbb, op=ALU.mult), r=[src, cs_], w=[f_b])
                    I(G, lambda: nc.gpsimd.tensor_tensor(out=t4, in0=x2, in1=cb, op=ALU.mult), r=[src, cs_], w=[f_b])
                    I(V, lambda: nc.vector.tensor_tensor(out=dst_ap[:, :, rope_off:rope_off + hh_], in0=t1, in1=t2, op=ALU.subtract),
                      r=[f_d], w=[dstT])
                    I(G, lambda: nc.gpsimd.tensor_tensor(out=dst_ap[:, :, rope_off + hh_:rope_off + 2 * hh_], in0=t3, in1=t4, op=ALU.add),
                      r=[f_b], w=[dstT])

                def transposes(src_t, src_ap_fn, n, rows, dstT, dst_ap):
                    for i in range(n):
                        I(PE, lambda i=i: nc.tensor.transpose(out=pT_[0:rows, i * 128:(i + 1) * 128], in_=src_ap_fn(i), identity=ident[:]),
                          r=[src_t, ident], w=[pT_])
                    I(A, lambda: nc.scalar.copy(out=dst_ap, in_=pT_[0:rows, 0:n * 128].rearrange("p (a b) -> p a b", a=n)),
                      r=[pT_], w=[dstT])

                def slotA():
                    sg = B["sgA"][B["nA"] % 2]
                    B["nA"] += 1
                    return sg

                def slotR():
                    sg = B["sgR"][B["nR"] % 2]
                    B["nR"] += 1
                    return sg

                def outT(dst, sg, b0, nblk):
                    dma(dst[:, :, tok].rearrange("h d t -> d h t"), sg[:, b0:b0 + nblk, :], r=[sg])

                if t + 2 < NT:
                    pass
                I(A, lambda: nc.scalar.activation(out=nbf[:, 0:1024], in_=xtt[:], func=AF.Square, accum_out=ss1[:, 0:1]), r=[xtt], w=[nb, ss1])
                rsqrt_ms(ss1, ss1[:, 0:1], rs1, rs1[:, 0:1], 1.0 / 1024)
                I(V, lambda: nc.vector.scalar_tensor_tensor(out=f_b[:, 0:1024], in0=xtt[:], scalar=rs1[:, 0:1], in1=modb[:, A_ATT],
                                                            op0=ALU.mult, op1=ALU.mult), r=[xtt, rs1, modb], w=[f_b])
                I(G, lambda: nc.gpsimd.tensor_tensor(out=nbf[:, 0:1024], in0=f_b[:, 0:1024], in1=modb[:, SH_A], op=ALU.add), r=[f_b, modb], w=[nb])
                transposes(nb, lambda i: nbf[:, i * 128:(i + 1) * 128], 8, 128, hT, hT[:])
                if t + 2 < NT:
                    load_tile(t + 2)
                lh = lambda kc: hT[:, kc, :]
                for gq in range(2):
                    p = proj(hT, lh, 8, winb, O_NQ + gq * 512, O_NQ + (gq + 1) * 512)
                    evac(p, 512, f_a, f_a[:, gq * 512:(gq + 1) * 512], eng=(A if gq else V))
                norm_rope(f_a, f_a[:, 0:1024].rearrange("p (h d) -> p h d", h=8), 8, 128, gq_t, True, 0, 64, 0, nb[:, :, 0:128], nb)
                sg = slotA()
                transposes(nb, lambda i: nb[:, i, 0:128], 8, 128, sg, sg[:, 0:8, :])
                outT(QaT, sg, 0, 8)
                p = proj(hT, lh, 8, winb, O_NKC, O_NKC + 512)
                evac(p, 512, f_a, f_a[:, 0:512])
                norm_rope(f_a, f_a[:, 0:256].rearrange("p (h d) -> p h d", h=2), 2, 128, None, False, 0, 64, 0, nb[:, 0:2, 0:128], nb)
                I(V, lambda: nc.vector.tensor_copy(out=nb[:, 2:4, 0:128], in_=f_a[:, 256:512].rearrange("p (h d) -> p h d", h=2)),
                  r=[f_a], w=[nb])
                sg = slotA()
                transposes(nb, lambda i: nb[:, i, 0:128], 4, 128, sg, sg[:, 0:4, :])
                outT(KcT, sg, 0, 2)
                outT(VcT, sg, 2, 2)
                sg = slotA()
                for wi, (off, gt_) in enumerate(((O_NKS, gks_t), (O_NKW, gkw_t))):
                    p = proj(hT, lh, 8, winb, off, off + 512)
                    evac(p, 512, f_a, f_a[:, 0:512])
                    norm_rope(f_a, f_a[:, 0:256].rearrange("p (h d) -> p h d", h=2), 2, 128, gt_, True, 0, 64, 0, nb[:, 0:2, 0:128], nb)
                    I(V, lambda wi=wi: nc.vector.tensor_copy(out=vbt[:, 2 * wi:2 * wi + 2, :], in_=f_a[:, 256:512].rearrange("p (h d) -> p h d", h=2)),
                      r=[f_a], w=[vbt])
                    transposes(nb, lambda i: nb[:, i, 0:128], 2, 128, sg, sg[:, 2 * wi:2 * wi + 2, :])
                outT(KsT, sg, 0, 2)
                outT(KwT, sg, 2, 2)
                dma(Vs[tok, :, :], vbt[:, 0:2, :], r=[vbt])
                dma(Vw[tok, :, :], vbt[:, 2:4, :], r=[vbt])
                p = proj(hT, lh, 8, winb, O_NG, O_CKV)
                I(A, lambda: nc.scalar.activation(out=gnt[:], in_=p[:, 0:24], func=AF.Sigmoid), r=[p], w=[gnt])
                evac(p, 408, f_a, f_a[:, 0:408], eng=V)
                dma(Gn[tok, :], gnt[:], r=[gnt])
                norm_rope(f_a, f_a[:, 24:408].rearrange("p (h d) -> p h d", h=1), 1, 384, gcq_t, True, 0, 0, 0,
                          nbf[:, 0:384].rearrange("p (h d) -> p h d", h=1), nb)
                transposes(nb, lambda i: nbf[:, i * 128:(i + 1) * 128], 3, 128, cqT, cqT[:])
                p = proj(hT, lh, 8, winb, O_CKV, O_BG)
                evac(p, 320, f_a, f_a[:, 0:320], eng=V)
                I(G, lambda: nc.gpsimd.tensor_copy(out=krf[:], in_=f_a[:, 256:320]), r=[f_a], w=[krf])
                norm_rope(f_a, f_a[:, 0:256].rearrange("p (h d) -> p h d", h=1), 1, 256, gckv_t, True, 0, 0, 0,
                          nbf[:, 512:768].rearrange("p (h d) -> p h d", h=1), nb)
                transposes(nb, lambda i: nbf[:, 512 + i * 128:512 + (i + 1) * 128], 2, 128, ckvT, ckvT[:])
                for j in range(4):
                    p = proj(hT, lh, 8, winb, O_BG + j * 512, O_BG + (j + 1) * 512)
                    bg_ = B["bgt"][j % 2]
                    I(A, lambda bg_=bg_, p=p: nc.scalar.activation(out=bg_[:], in_=p[:, 0:512], func=AF.Sigmoid), r=[p], w=[bg_])
                    dma(BG[tok, j * 512:(j + 1) * 512], bg_[:], r=[bg_])
                for j in range(3):
                    p = proj(cqT, lambda kc: cqT[:, kc, :], 3, wuqb, j * 512, (j + 1) * 512)
                    evac(p, 512, f_a, f_a[:, j * 512:(j + 1) * 512], eng=(A if j % 2 else V))
                norm_rope(f_a, f_a[:, 0:1536].rearrange("p (h d) -> p h d", h=8), 8, 192, gmq_t, True, 128, 32, 128, nb[:, :, :], nb)
                sg = slotA()
                transposes(nb, lambda i: nb[:, i, 0:128], 8, 128, sg, sg[:, 0:8, :])
                outT(QbN, sg, 0, 8)
                sg = slotA()
                transposes(nb, lambda i: nb[:, i, 64:192], 8, 128, sg, sg[:, 0:8, :])
                dma(QbR[:, :, tok].rearrange("h d t -> d h t"), sg[64:128, 0:8, :], r=[sg])
                for half in range(2):
                    for j in range(2):
                        c0 = half * 1024 + j * 512
                        p = proj(ckvT, lambda kc: ckvT[:, kc, :], 2, wukvb, c0, c0 + 512)
                        evac(p, 512, f_a, f_a[:, j * 512:(j + 1) * 512], eng=(A if j % 2 else V))
                    kvv = f_a[:, 0:1024].rearrange("p (h d) -> p h d", h=4)
                    I(G, lambda half=half, kvv=kvv: nc.gpsimd.tensor_copy(out=vbt[:, 4 * half:4 * half + 4, :], in_=kvv[:, :, 128:256]), r=[f_a], w=[vbt])
                    I(V, lambda kvv=kvv: nc.vector.tensor_copy(out=kvv[:, :, 128:192], in_=krf[:].unsqueeze(1).broadcast_to([128, 4, 64])),
                      r=[krf, f_a], w=[f_a])
                    norm_rope(f_a, kvv[:, :, 0:192], 4, 192, gmk_t, True, 128, 32, 128, nb[:, 4 * half:4 * half + 4, :], nb)
                dma(Vb[tok, :, :], vbt[:], r=[vbt])
                sg = slotA()
                transposes(nb, lambda i: nb[:, i, 0:128], 8, 128, sg, sg[:, 0:8, :])
                outT(KbN, sg, 0, 8)
                sg = slotA()
                transposes(nb, lambda i: nb[:, i, 64:192], 8, 128, sg, sg[:, 0:8, :])
                dma(KbR[:, :, tok].rearrange("h d t -> d h t"), sg[64:128, 0:8, :], r=[sg])

            load_tile(0)
            if NT > 1:
                load_tile(1)
            str0, str1 = [], []
            for t in range(0, NT, 2):
                str0 += fw.record(lambda t=t: p1_tile(t))
                if t + 1 < NT:
                    str1 += fw.record(lambda t=t: p1_tile(t + 1))
            skew = (len(str0) // max(1, (NT + 1) // 2)) // 2
            fw.interleave([str0, [(lambda: None)] * skew + str1])
            fw.barrier()
            if upto == 1:
                return nc

        SC128 = 128.0 ** -0.5
        SC192 = 192.0 ** -0.5

        def tr_generic(pbuf, src_t, src_ap_fn, n, rows, dstT, dst_ap, eng=A):
            for i in range(n):
                I(PE, lambda i=i: nc.tensor.transpose(out=pbuf[0:rows, i * 128:(i + 1) * 128], in_=src_ap_fn(i), identity=ident[:]),
                  r=[src_t, ident], w=[pbuf])
            if eng == A:
                I(A, lambda: nc.scalar.copy(out=dst_ap, in_=pbuf[0:rows, 0:n * 128].rearrange("p (a b) -> p a b", a=n)), r=[pbuf], w=[dstT])
            else:
                I(V, lambda: nc.vector.tensor_copy(out=dst_ap, in_=pbuf[0:rows, 0:n * 128].rearrange("p (a b) -> p a b", a=n)), r=[pbuf], w=[dstT])

        with contextlib.ExitStack() as st23:
            kcT = [sb(st23, f"kcT{g}", [128, NCP], BF16) for g in range(2)]
            vcb = [sb(st23, f"vcb{g}", [128, NCT, 128], BF16) for g in range(2)]
            for g in range(2):
                I(V, lambda g=g: nc.vector.memset(kcT[g][:], 0.0), w=[kcT[g]])
                I(G, lambda g=g: nc.gpsimd.memset(vcb[g][:], 0.0), w=[vcb[g]])
            with contextlib.ExitStack() as st:
                XT = sb(st, "XT", [128, S], BF16)
                Xl = sb(st, "Xl", [128, 32, 512], BF16)
                w1s = sb(st, "w1s", [128, 32 * 256], F32)
                w1b = sb(st, "w1b", [128, 32, 256], BF16)
                w2s = sb(st, "w2s", [128, 256], F32)
                w2b = sb(st, "w2b", [128, 2, 128], BF16)
                peT = sb(st, "peT", [128, 32], F32)
                hTc = sb(st, "hTc", [128, 2, 512], BF16)
                gkc_t = bcast_load(st, "gkc_t", g_kc, 128)
                kf = sb(st, "kf", [128, 128], F32)
                kf2 = sb(st, "kf2", [128, 128], F32)
                kb = sb(st, "kb", [128, 128], BF16)
                ssk = sb(st, "ssk", [128, 1], F32)
                rsk = sb(st, "rsk", [128, 1], F32)
                pc = [ps(st, f"pc{i}", [128, 512]) for i in range(2)]
                pk = ps(st, "pk", [128, 128])
                pkT = ps(st, "pkT", [128, 128], BF16)
                for kv in range(2):
                    w1, w2, pe_ = (k_w1, k_w2, pe_kT) if kv == 0 else (v_w1, v_w2, pe_vT)
                    dma(w1s[:].rearrange("p (l c) -> p l c", l=32), w1.rearrange("(l d) c -> d l c", d=128), w=[w1s])
                    I(V, lambda: nc.vector.tensor_copy(out=w1b[:], in_=w1s[:].rearrange("p (l c) -> p l c", l=32)), r=[w1s], w=[w1b])
                    dma(w2s[:].rearrange("p (k d) -> p k d", k=2), w2.rearrange("(k c) d -> c k d", c=128), w=[w2s])
                    I(G, lambda: nc.gpsimd.tensor_copy(out=w2b[:], in_=w2s[:].rearrange("p (k d) -> p k d", k=2)), r=[w2s], w=[w2b])
                    dma(peT[:], pe_, w=[peT])
                    for g in range(2):
                        src = KcT if kv == 0 else VcT
                        dma(XT[:], src[g, :, :], w=[XT])
                        for l in range(32):
                            xin = XT[:, l:l + 16 * (n_cmp - 1) + 1:16]
                            if l % 2 == 0:
                                I(V, lambda l=l, xin=xin: nc.vector.tensor_scalar(out=Xl[:, l, 0:n_cmp], in0=xin, scalar1=peT[:, l:l + 1],
                                                                                  scalar2=None, op0=ALU.add), r=[XT, peT], w=[Xl])
                            else:
                                I(A, lambda l=l, xin=xin: nc.scalar.activation(out=Xl[:, l, 0:n_cmp], in_=xin, func=AF.Identity,
                                                                               bias=peT[:, l:l + 1], scale=1.0), r=[XT, peT], w=[Xl])
                        for ch in range(2):
                            p = pc[ch]
                            for l in range(32):
                                I(PE, lambda l=l, p=p, ch=ch: nc.tensor.matmul(p[:, 0:n_cmp], lhsT=w1b[:, l, ch * 128:(ch + 1) * 128],
                                                                               rhs=Xl[:, l, 0:n_cmp], start=(l == 0), stop=(l == 31)),
                                  r=[w1b, Xl], w=[p])
                            I(A, lambda p=p, ch=ch: nc.scalar.activation(out=hTc[:, ch, 0:n_cmp], in_=p[:, 0:n_cmp], func=AF.Silu),
                              r=[p], w=[hTc])
                        for nt in range(NCT):
                            rows = min(128, n_cmp - nt * 128)
                            for ch in range(2):
                                I(PE, lambda ch=ch, nt=nt, rows=rows: nc.tensor.matmul(pk[0:rows, :], lhsT=hTc[:, ch, nt * 128:nt * 128 + rows],
                                                                                       rhs=w2b[:, ch, :], start=(ch == 0), stop=(ch == 1)),
                                  r=[hTc, w2b], w=[pk])
                            if kv == 0:
                                I(A, lambda rows=rows: nc.scalar.copy(out=kf[0:rows, :], in_=pk[0:rows, :]), r=[pk], w=[kf])
                                I(V, lambda rows=rows: nc.vector.tensor_tensor(out=kf2[0:rows, :], in0=kf[0:rows, :], in1=kf[0:rows, :], op=ALU.mult),
                                  r=[kf], w=[kf2])
                                I(V, lambda rows=rows: nc.vector.tensor_reduce(out=ssk[0:rows, :], in_=kf2[0:rows, :], axis=AX.X, op=ALU.add),
                                  r=[kf2], w=[ssk])
                                rsqrt_ms(ssk, ssk[0:rows, :], rsk, rsk[0:rows, :], 1.0 / 128, rows=rows)
                                I(V, lambda rows=rows: nc.vector.scalar_tensor_tensor(out=kb[0:rows, :], in0=kf[0:rows, :], scalar=rsk[0:rows, 0:1],
                                                                                      in1=gkc_t[0:rows, :], op0=ALU.mult, op1=ALU.mult),
                                  r=[kf, rsk, gkc_t], w=[kb])
                                I(PE, lambda rows=rows: nc.tensor.transpose(out=pkT[:, 0:rows], in_=kb[0:rows, :], identity=ident[0:rows, 0:rows]),
                                  r=[kb, ident], w=[pkT])
                                I(A, lambda rows=rows, nt=nt, g=g: nc.scalar.copy(out=kcT[g][:, nt * 128:nt * 128 + rows], in_=pkT[:, 0:rows]),
                                  r=[pkT], w=[kcT[g]])
                            else:
                                I(A, lambda rows=rows, nt=nt, g=g: nc.scalar.copy(out=vcb[g][0:rows, nt, :], in_=pk[0:rows, :]), r=[pk], w=[vcb[g]])
                fw.barrier()
            if upto == 2:
                return nc
            with contextlib.ExitStack() as st:
                W0 = 512 + 8 * (NT - 1)
                m0 = sb(st, "m0", [128, W0], F32)
                aext = sb(st, "aext", [128, NSEL + CO], F32)
                fext = sb(st, "fext", [128, NSEL + CO], F32)
                dma(m0[:], c_m0, w=[m0])
                dma(aext[:], c_aext, w=[aext])
                dma(fext[:], c_fext, w=[fext])
                PW = 4 * NSEL + 8

                def mkset(u):
                    B = {}
                    B["qT"] = [sb(st, f"qT{u}{i}", [128, 8, 128], BF16) for i in range(2)]
                    B["gn3"] = [sb(st, f"gn3{u}{i}", [128, 24], F32) for i in range(2)]
                    B["Ef"] = [sb(st, f"Ef{u}{i}", [128, 512], F32) for i in range(2)]
                    B["Pm"] = sb(st, f"Pm{u}", [128, 8, 512], F32)
                    B["Pb"] = [sb(st, f"Pb{u}{i}", [128, 512], BF16) for i in range(2)]
                    B["PbT"] = [sb(st, f"PbT{u}{i}", [128, 4, 128], BF16) for i in range(2)]
                    for nm, shp, dt in (("Z", [128, 8], F32), ("rz", [128, 8], F32), ("gz", [128, 8], F32), ("ppad", [128, 2, PW], F32),
                                        ("imp", [128, 2, NSEL], F32), ("score", [128, 2, NSEL], F32), ("sc2", [128, 2, NSEL], F32),
                                        ("m1", [128, 2, 8], F32), ("m2", [128, 2, 8], F32), ("Btb", [128, 2, NSEL], BF16),
                                        ("BtT", [KJ, 2, 128], BF16), ("ocm", [128, 8, 128], F32)):
                        B[nm] = sb(st, f"{nm}{u}", shp, dt)
                    B["pS"] = ps(st, f"pS{u}", [128, 512])
                    B["pTB"] = ps(st, f"pTB{u}", [128, 1024], BF16)
                    B["pO"] = [ps(st, f"pO{u}{i}", [128, 512]) for i in range(2)]
                    I(V, lambda: nc.vector.memset(B["ppad"][:], 0.0), w=[B["ppad"]])
                    return B
                sets = [mkset(0), mkset(1)]

                def ld3(t):
                    B = sets[t % 2]
                    b = (t // 2) % 2
                    tk = slice(t * 128, (t + 1) * 128)
                    dma(B["qT"][b][:], QaT[:, :, tk].rearrange("h d t -> d h t"), w=[B["qT"][b]])
                    dma(B["gn3"][b][:], Gn[tk, :], w=[B["gn3"][b]])

                def p3_tile(t):
                    B = sets[t % 2]
                    b = (t // 2) % 2
                    if t + 2 < NT:
                        ld3(t + 2)
                    tk = slice(t * 128, (t + 1) * 128)
                    NW = min(n_cmp, 8 * t + 7)
                    off = 8 * (NT - 1) - 8 * t
                    nkt = (NW + 127) // 128
                    q_, gn_, oc_ = B["qT"][b], B["gn3"][b], B["ocm"]
                    Ef, Pm, Pb, PbT, Z, rz, gz, ppad = B["Ef"], B["Pm"], B["Pb"], B["PbT"], B["Z"], B["rz"], B["gz"], B["ppad"]
                    imp, score, sc2, m1, m2, Btb, BtT = B["imp"], B["score"], B["sc2"], B["m1"], B["m2"], B["Btb"], B["BtT"]
                    pS_, pTB, pO = B["pS"], B["pTB"], B["pO"]
                    for h in range(8):
                        g = h // 4
                        hb_ = h % 2
                        I(PE, lambda h=h, g=g: nc.tensor.matmul(pS_[:, 0:NW], lhsT=q_[:, h, :], rhs=kcT[g][:, 0:NW], start=True, stop=True),
                          r=[q_, kcT[g]], w=[pS_])
                        I(A, lambda hb_=hb_: nc.scalar.activation(out=Ef[hb_][:, 0:NW], in_=pS_[:, 0:NW], func=AF.Exp, scale=SC128),
                          r=[pS_, eps_t], w=[Ef[hb_]])
                        I(V, lambda h=h, hb_=hb_: nc.vector.scalar_tensor_tensor(out=Pm[:, h, 0:NW], in0=Ef[hb_][:, 0:NW], scalar=1.0,
                                                                                  in1=m0[:, off:off + NW], op0=ALU.mult, op1=ALU.mult,
                                                                                  accum_out=Z[:, h:h + 1]), r=[Ef[hb_], m0], w=[Pm, Z])
                        I(G, lambda h=h, hb_=hb_: nc.gpsimd.tensor_copy(out=Pb[hb_][:, 0:NW], in_=Pm[:, h, 0:NW]), r=[Pm], w=[Pb[hb_]])
                        for kt in range(nkt):
                            rows = min(128, NW - kt * 128)
                            I(PE, lambda kt=kt, rows=rows, hb_=hb_: nc.tensor.transpose(out=pTB[0:rows, kt * 128:(kt + 1) * 128],
                                                                                         in_=Pb[hb_][:, kt * 128:kt * 128 + rows], identity=ident[:]),
                              r=[Pb[hb_], ident], w=[pTB])
                        for kt in range(nkt):
                            rows = min(128, NW - kt * 128)
                            I(A, lambda kt=kt, rows=rows, hb_=hb_: nc.scalar.copy(out=PbT[hb_][0:rows, kt, :], in_=pTB[0:rows, kt * 128:(kt + 1) * 128]),
                              r=[pTB], w=[PbT[hb_]])
                        for kt in range(nkt):
                            rows = min(128, NW - kt * 128)
                            I(PE, lambda kt=kt, rows=rows, h=h, g=g, hb_=hb_: nc.tensor.matmul(
                                pO[g][:, (h % 4) * 128:(h % 4 + 1) * 128], lhsT=PbT[hb_][0:rows, kt, :], rhs=vcb[g][0:rows, kt, :],
                                start=(h % 4 == 0 and kt == 0), stop=(kt == nkt - 1)), r=[PbT[hb_], vcb[g]], w=[pO[g]])
                    I(V, lambda: nc.vector.tensor_scalar(out=rz[:], in0=Z[:], scalar1=1e-30, scalar2=None, op0=ALU.max), r=[Z], w=[rz])
                    I(V, lambda: nc.vector.reciprocal(out=rz[:], in_=rz[:]), r=[rz], w=[rz])
                    I(V, lambda: nc.vector.tensor_tensor(out=gz[:], in0=rz[:], in1=gn_[:].rearrange("p (h j) -> p h j", j=3)[:, :, 0], op=ALU.mult),
                      r=[rz, gn_], w=[gz])
                    for g in range(2):
                        I(V, lambda g=g: nc.vector.tensor_tensor(out=oc_[:, 4 * g:4 * g + 4, :], in0=pO[g][:, 0:512].rearrange("p (h d) -> p h d", h=4),
                                                                 in1=gz[:, 4 * g:4 * g + 4].unsqueeze(2).broadcast_to([128, 4, 128]), op=ALU.mult),
                          r=[pO[g], gz], w=[oc_])
                    dma(Oc[tk, :], oc_[:].rearrange("p h d -> p (h d)"), r=[oc_])
                    for g in range(2):
                        for r_ in range(4):
                            h = 4 * g + r_
                            if r_ == 0:
                                I(V, lambda g=g, h=h: nc.vector.tensor_scalar(out=ppad[:, g, 4:4 + NW], in0=Pm[:, h, 0:NW], scalar1=rz[:, h:h + 1],
                                                                              scalar2=None, op0=ALU.mult), r=[Pm, rz], w=[ppad])
                            else:
                                I(V, lambda g=g, h=h: nc.vector.scalar_tensor_tensor(out=ppad[:, g, 4:4 + NW], in0=Pm[:, h, 0:NW], scalar=rz[:, h:h + 1],
                                                                                     in1=ppad[:, g, 4:4 + NW], op0=ALU.mult, op1=ALU.add),
                                  r=[Pm, rz, ppad], w=[ppad])
                    a_t = aext[:, CO - 2 * t:CO - 2 * t + NSEL]
                    f_t = fext[:, CO - 2 * t:CO - 2 * t + NSEL]
                    for g in range(2):
                        I(V, lambda g=g: nc.vector.tensor_reduce(out=imp[:, g, :], in_=ppad[:, g, 4:4 + 4 * NSEL].rearrange("p (j f) -> p j f", f=4),
                                                                 axis=AX.X, op=ALU.add), r=[ppad], w=[imp])
                        I(V, lambda g=g: nc.vector.tensor_tensor(out=imp[:, g, :], in0=imp[:, g, :],
                                                                 in1=ppad[:, g, 0:4 * NSEL].rearrange("p (j f) -> p j f", f=4)[:, :, 3], op=ALU.add),
                          r=[ppad, imp], w=[imp])
                        I(V, lambda g=g: nc.vector.tensor_tensor(out=score[:, g, :], in0=imp[:, g, :], in1=a_t, op=ALU.mult), r=[imp, aext], w=[score])
                        I(V, lambda g=g: nc.vector.tensor_tensor(out=score[:, g, :], in0=score[:, g, :], in1=f_t, op=ALU.add), r=[score, fext], w=[score])
                        I(V, lambda g=g: nc.vector.memset(score[:, g, 0:1], 1e9), r=[score], w=[score])
                        if NSEL > 16:
                            I(V, lambda g=g: nc.vector.max(out=m1[:, g, :], in_=score[:, g, :]), r=[score], w=[m1])
                            I(V, lambda g=g: nc.vector.match_replace(out=sc2[:, g, :], in_to_replace=m1[:, g, :], in_values=score[:, g, :], imm_value=-2.0),
                              r=[score, m1], w=[sc2])
                            I(V, lambda g=g: nc.vector.max(out=m2[:, g, :], in_=sc2[:, g, :]), r=[sc2], w=[m2])
                            I(V, lambda g=g: nc.vector.tensor_scalar(out=Btb[:, g, :], in0=score[:, g, :], scalar1=m2[:, g, 7:8], scalar2=NEGB,
                                                                     op0=ALU.is_lt, op1=ALU.mult), r=[score, m2], w=[Btb])
                        else:
                            I(V, lambda g=g: nc.vector.memset(Btb[:, g, :], 0.0), w=[Btb])
                        I(PE, lambda g=g: nc.tensor.transpose(out=pTB[0:KJ, 512 + g * 128:512 + (g + 1) * 128], in_=Btb[:, g, 0:KJ], identity=ident[:]),
                          r=[Btb, ident], w=[pTB])
                    I(A, lambda: nc.scalar.copy(out=BtT[:], in_=pTB[0:KJ, 512:768].rearrange("p (g t) -> p g t", g=2)), r=[pTB], w=[BtT])
                    dma(BT[:, :, tk].rearrange("g j t -> j g t"), BtT[:], r=[BtT])

                ld3(0)
                if NT > 1:
                    ld3(1)
                str0, str1 = [], []
                for t in range(0, NT, 2):
                    str0 += fw.record(lambda t=t: p3_tile(t))
                    if t + 1 < NT:
                        str1 += fw.record(lambda t=t: p3_tile(t + 1))
                skew = (len(str0) // max(1, (NT + 1) // 2)) // 2
                fw.interleave([str0, [(lambda: None)] * skew + str1])
                fw.barrier()
        if upto == 3:
            return nc

        def attn_pipeline(pairs, stageA, stageB):
            n = len(pairs)
            if n == 0:
                return
            stageA(0, pairs[0])
            for i in range(n):
                if i + 1 < n:
                    stageA(i + 1, pairs[i + 1])
                stageB(i, pairs[i])

        with contextlib.ExitStack() as st:
            esel = sb(st, "esel", [KJ, NT, 128], BF16)
            dma(esel[:], c_esel, w=[esel])
            Ks_s = sb(st, "Ks_s", [128, S], BF16)
            Kw_s = sb(st, "Kw_s", [128, S], BF16)
            Vs_s = sb(st, "Vs_s", [128, NT, 129], BF16)
            Vw_s = sb(st, "Vw_s", [128, NT, 129], BF16)
            qT4 = [sb(st, f"qT4{i}", [128, 4, 128], BF16) for i in range(4)]
            btl = [sb(st, f"btl{i}", [KJ, 128], BF16) for i in range(4)]
            BT4 = [sb(st, f"BT4{i}", [KJ, 4, 128], BF16) for i in range(4)]
            PT = [sb(st, f"PT{i}", [128, 4, 128], BF16) for i in range(4)]
            gn4 = [sb(st, f"gn4{i}", [128, 24], F32) for i in range(4)]
            ocl = [sb(st, f"ocl{i}", [128, 4, 128], F32) for i in range(4)]
            bga = [sb(st, f"bga{i}", [128, 512], BF16) for i in range(4)]
            oa = sb(st, "oa", [128, 4, 128], F32)
            yat = [sb(st, f"yat{i}", [128, 512], F32) for i in range(2)]
            rzs = sb(st, "rzs", [128, 8], F32)
            cfs = sb(st, "cfs", [128, 8], F32)
            pS = [ps(st, f"qS{i}", [128, 512]) for i in range(2)]
            pOTs = [ps(st, f"pOTs{i}", [128, 512]) for i in range(2)]
            pOTw = [ps(st, f"pOTw{i}", [128, 512]) for i in range(2)]
            pTr = ps(st, "pTr4", [128, 512])
            pZ4 = ps(st, "pZ4", [128, 512])
            acc_s = [sb(st, f"acc_s{i}", [128, 512], F32) for i in range(2)]
            acc_w = [sb(st, f"acc_w{i}", [128, 512], F32) for i in range(2)]
            oT_s = sb(st, "oT_s", [128, 512], F32)
            oT_w = sb(st, "oT_w", [128, 512], F32)
            I(V, lambda: nc.vector.memset(Vs_s[:, :, 128:129], 1.0), w=[Vs_s])
            I(V, lambda: nc.vector.memset(Vw_s[:, :, 128:129], 1.0), w=[Vw_s])
            cnt4 = [0]
            for g in range(2):
                dma(Ks_s[:], KsT[g, :, :], w=[Ks_s])
                dma(Kw_s[:], KwT[g, :, :], w=[Kw_s])
                dma(Vs_s[:, :, 0:128], Vs[:, g, :].rearrange("(t p) d -> p t d", p=128), w=[Vs_s])
                dma(Vw_s[:, :, 0:128], Vw[:, g, :].rearrange("(t p) d -> p t d", p=128), w=[Vw_s])

                def ld4(t, g=g):
                    b = t % 4
                    tk = slice(t * 128, (t + 1) * 128)
                    dma(qT4[b][:], QaT[4 * g:4 * g + 4, :, tk].rearrange("h d t -> d h t"), w=[qT4[b]])
                    dma(btl[b][:], BT[g, :, tk], w=[btl[b]])
                    dma(gn4[b][:], Gn[tk, :], w=[gn4[b]])
                    dma(ocl[b][:], Oc[tk, 4 * g * 128:(4 * g + 4) * 128].rearrange("p (h d) -> p h d", h=4), w=[ocl[b]])
                    dma(bga[b][:], BG[tk, 4 * g * 128:(4 * g + 4) * 128], w=[bga[b]])

                def bt4(t):
                    b = t % 4
                    I(G, lambda: nc.gpsimd.tensor_copy(out=BT4[b][:], in_=btl[b][:].unsqueeze(1).broadcast_to([KJ, 4, 128])), r=[btl[b]], w=[BT4[b]])

                entries = []
                for t in range(NT):
                    prs = [("s", kt) for kt in range(t + 1)] + [("w", kt) for kt in range(max(0, t - 4), t + 1)]
                    for idx, (kind, kt) in enumerate(prs):
                        entries.append((t, kind, kt, idx == 0, idx == len(prs) - 1))
                base = cnt4[0]
                cnt4[0] += len(entries)
                ld4(0)
                if NT > 1:
                    ld4(1)
                bt4(0)

                def stA(i, e):
                    t, kind, kt, first_of_tile, last_of_tile = e
                    if first_of_tile:
                        if t + 2 < NT:
                            ld4(t + 2)
                        if t + 1 < NT:
                            bt4(t + 1)
                    b = t % 4
                    q_ = qT4[b]
                    k = base + i
                    p = pS[k % 2]
                    pt = PT[k % 4]
                    ksl = slice(kt * 128, (kt + 1) * 128)
                    if kind == "s":
                        I(PE, lambda: nc.tensor.matmul(p[:, :], lhsT=Ks_s[:, ksl], rhs=q_[:].rearrange("p h t -> p (h t)"), start=True, stop=False),
                          r=[Ks_s, q_], w=[p])
                        I(PE, lambda: nc.tensor.matmul(p[:, :], lhsT=esel[:, kt, :], rhs=BT4[b][:].rearrange("p h t -> p (h t)"), start=False, stop=True),
                          r=[esel, BT4[b]], w=[p])
                    else:
                        I(PE, lambda: nc.tensor.matmul(p[:, :], lhsT=Kw_s[:, ksl], rhs=q_[:].rearrange("p h t -> p (h t)"), start=True, stop=True),
                          r=[Kw_s, q_], w=[p])
                    I(A, lambda: nc.scalar.activation(out=pt[:].rearrange("p h t -> p (h t)"), in_=p[:, :], func=AF.Exp, scale=SC128),
                      r=[p], w=[pt])
                    mk = None
                    if kt == t:
                        mk = tri
                    elif kind == "w" and kt == t - 4:
                        mk = atri
                    if mk is not None:
                        I(G, lambda: nc.gpsimd.tensor_tensor(out=pt[:], in0=pt[:], in1=mk[:].unsqueeze(1).broadcast_to([128, 4, 128]), op=ALU.mult),
                          r=[pt, mk], w=[pt])

                def stB(i, e):
                    t, kind, kt, first_of_tile, last_of_tile = e
                    k = base + i
                    pt = PT[k % 4]
                    ptf = pt[:].rearrange("p h t -> p (h t)")
                    if kind == "s":
                        pO_, vv, acc_, first, last = pOTs[t % 2], Vs_s, acc_s[t % 2], (kt == 0), (kt == t)
                    else:
                        pO_, vv, acc_, first, last = pOTw[t % 2], Vw_s, acc_w[t % 2], (kt == max(0, t - 4)), (kt == t)
                    I(PE, lambda: nc.tensor.matmul(pO_[:, :], lhsT=vv[:, kt, 0:128], rhs=ptf, start=first, stop=last), r=[pt, vv], w=[pO_])
                    if kind == "s":
                        if first:
                            I(V, lambda: nc.vector.tensor_copy(out=acc_[:], in_=ptf), r=[pt], w=[acc_])
                        else:
                            I(V, lambda: nc.vector.tensor_tensor(out=acc_[:], in0=acc_[:], in1=ptf, op=ALU.add), r=[pt, acc_], w=[acc_])
                    else:
                        if first:
                            I(G, lambda: nc.gpsimd.tensor_copy(out=acc_[:], in_=ptf), r=[pt], w=[acc_])
                        else:
                            I(G, lambda: nc.gpsimd.tensor_tensor(out=acc_[:], in0=acc_[:], in1=ptf, op=ALU.add), r=[pt, acc_], w=[acc_])

                def finA(t, g=g):
                    b = t % 4
                    for ki, (pO_, acc_, oT_) in enumerate(((pOTs[t % 2], acc_s[t % 2], oT_s), (pOTw[t % 2], acc_w[t % 2], oT_w))):
                        for hh in range(4):
                            I(PE, lambda hh=hh, ki=ki, acc_=acc_: nc.tensor.matmul(pZ4[:, 8 * ki + 2 * hh:8 * ki + 2 * hh + 2], lhsT=acc_[:, hh * 128:(hh + 1) * 128],
                                                                                    rhs=ones_f[:, 0:2], start=True, stop=True), r=[acc_, ones_f], w=[pZ4])
                        I(A, lambda pO_=pO_, oT_=oT_: nc.scalar.copy(out=oT_[:], in_=pO_[:, :]), r=[pO_], w=[oT_])
                    for hh in range(4):
                        I(PE, lambda hh=hh: nc.tensor.transpose(out=pTr[:, hh * 128:(hh + 1) * 128], in_=oT_s[:, hh * 128:(hh + 1) * 128], identity=identf[:]),
                          r=[oT_s, identf], w=[pTr])
                    I(V, lambda: nc.vector.reciprocal(out=rzs[:, 0:8], in_=pZ4[:, 0:16:2]), r=[pZ4], w=[rzs])
                    gv = gn4[b][:].rearrange("p (h j) -> p h j", j=3)
                    I(V, lambda: nc.vector.tensor_tensor(out=cfs[:, 0:4], in0=rzs[:, 0:4], in1=gv[:, 4 * g:4 * g + 4, 1], op=ALU.mult), r=[rzs, gn4[b]], w=[cfs])
                    I(V, lambda: nc.vector.tensor_tensor(out=cfs[:, 4:8], in0=rzs[:, 4:8], in1=gv[:, 4 * g:4 * g + 4, 2], op=ALU.mult), r=[rzs, gn4[b]], w=[cfs])
                    for hh in range(4):
                        I(V, lambda hh=hh: nc.vector.scalar_tensor_tensor(out=oa[:, hh, :], in0=pTr[:, hh * 128:(hh + 1) * 128], scalar=cfs[:, hh:hh + 1],
                                                                          in1=ocl[b][:, hh, :], op0=ALU.mult, op1=ALU.add),
                          r=[pTr, cfs, ocl[b]], w=[oa])

                def finB(t, g=g):
                    b = t % 4
                    tk = slice(t * 128, (t + 1) * 128)
                    for hh in range(4):
                        I(PE, lambda hh=hh: nc.tensor.transpose(out=pTr[:, hh * 128:(hh + 1) * 128], in_=oT_w[:, hh * 128:(hh + 1) * 128], identity=identf[:]),
                          r=[oT_w, identf], w=[pTr])
                    for hh in range(4):
                        I(V, lambda hh=hh: nc.vector.scalar_tensor_tensor(out=oa[:, hh, :], in0=pTr[:, hh * 128:(hh + 1) * 128], scalar=cfs[:, 4 + hh:5 + hh],
                                                                          in1=oa[:, hh, :], op0=ALU.mult, op1=ALU.add),
                          r=[pTr, cfs, oa], w=[oa])
                    yb_ = yat[t % 2]
                    I(G, lambda: nc.gpsimd.tensor_tensor(out=yb_[:], in0=oa[:].rearrange("p h d -> p (h d)"), in1=bga[b][:], op=ALU.mult),
                      r=[oa, bga[b]], w=[yb_])
                    dma(Ya[tk, 4 * g * 128:(4 * g + 4) * 128], yb_[:], r=[yb_])

                n_e = len(entries)
                pend = []
                stA(0, entries[0])
                for i in range(n_e):
                    while pend and pend[0][0] <= i:
                        _, fn_, t_ = pend.pop(0)
                        fn_(t_)
                    if i + 1 < n_e:
                        stA(i + 1, entries[i + 1])
                    stB(i, entries[i])
                    if entries[i][4]:
                        pend.append((i + 2, finA, entries[i][0]))
                        pend.append((i + 4, finB, entries[i][0]))
                while pend:
                    _, fn_, t_ = pend.pop(0)
                    fn_(t_)
            fw.barrier()
        if upto == 4:
            return nc

        NQB = S // 512
        NBLK = 8 * NQB
        with contextlib.ExitStack() as st:
            KN = [sb(st, f"KN{i}", [128, S], BF16) for i in range(2)]
            KR = [sb(st, f"KR{i}", [128, S], BF16) for i in range(2)]
            VB = [sb(st, f"VB{i}", [128, NT, 129], BF16) for i in range(2)]
            QN = [sb(st, f"QN{i}", [128, 512], BF16) for i in range(4)]
            QR = [sb(st, f"QR{i}", [128, 512], BF16) for i in range(4)]
            PT = [sb(st, f"PTm{i}", [128, 512], BF16) for i in range(4)]
            bgb = [sb(st, f"bgb{i}", [128, 4, 128], BF16) for i in range(4)]
            yal = [sb(st, f"yal{i}", [128, 4, 128], F32) for i in range(4)]
            yo = [sb(st, f"yo{i}", [128, 4, 128], BF16) for i in range(4)]
            obf = sb(st, "obf", [128, 4, 128], F32)
            rz5 = sb(st, "rz5", [128, 4], F32)
            pS = [ps(st, f"mS{i}", [128, 512]) for i in range(2)]
            pOT5 = [ps(st, f"mOT{j}", [128, 512]) for j in range(2)]
            pTr5 = ps(st, "mTr", [128, 512])
            pZ5 = ps(st, "mZ", [128, 512])
            acc5 = [sb(st, f"acc5{j}", [128, 512], F32) for j in range(2)]
            oT5 = sb(st, "oT5", [128, 512], F32)
            for i in range(2):
                I(V, lambda i=i: nc.vector.memset(VB[i][:, :, 128:129], 1.0), w=[VB[i]])
                I(G, lambda i=i: nc.gpsimd.memset(KR[i][64:128, :], 0.0), w=[KR[i]])
            for i in range(4):
                I(G, lambda i=i: nc.gpsimd.memset(QR[i][64:128, :], 0.0), w=[QR[i]])

            def ldh(h):
                b = h % 2
                dma(KN[b][:], KbN[h, :, :], w=[KN[b]])
                dma(KR[b][0:64, :], KbR[h, :, :], w=[KR[b]])
                dma(VB[b][:, :, 0:128], Vb[:, h, :].rearrange("(t p) d -> p t d", p=128), w=[VB[b]])

            def ldq(n):
                h, qb = n // NQB, n % NQB
                bq = n % 4
                rows = slice(qb * 512, (qb + 1) * 512)
                dma(QN[bq][:], QbN[h, :, rows], w=[QN[bq]])
                dma(QR[bq][0:64, :], QbR[h, :, rows], w=[QR[bq]])
                dma(bgb[bq][:], BG[rows, 1024 + h * 128:1024 + (h + 1) * 128].rearrange("(s p) c -> p s c", p=128), w=[bgb[bq]])
                dma(yal[bq][:], Ya[rows, h * 128:(h + 1) * 128].rearrange("(s p) c -> p s c", p=128), w=[yal[bq]])

            entries = []
            for n in range(NBLK):
                qb = n % NQB
                nk = 4 * qb + 4
                for kt in range(nk):
                    entries.append((n, kt, kt == 0, kt == nk - 1))
            ldh(0)
            ldq(0)
            if NBLK > 1:
                ldq(1)

            def stA(i, e):
                n, kt, first, last = e
                h, qb = n // NQB, n % NQB
                if first and n + 2 < NBLK:
                    ldq(n + 2)
                if qb == 0 and kt == 3 and h + 1 < 8:
                    ldh(h + 1)
                b, bq = h % 2, n % 4
                p = pS[i % 2]
                pt = PT[i % 4]
                j = kt - 4 * qb
                c0 = 128 * max(j, 0)
                ksl = slice(kt * 128, (kt + 1) * 128)
                I(PE, lambda: nc.tensor.matmul(p[:, c0:512], lhsT=KN[b][:, ksl], rhs=QN[bq][:, c0:512], start=True, stop=False), r=[KN[b], QN[bq]], w=[p])
                I(PE, lambda: nc.tensor.matmul(p[:, c0:512], lhsT=KR[b][:, ksl], rhs=QR[bq][:, c0:512], start=False, stop=True), r=[KR[b], QR[bq]], w=[p])
                I(A, lambda: nc.scalar.activation(out=pt[:, c0:512], in_=p[:, c0:512], func=AF.Exp, scale=SC192), r=[p], w=[pt])
                if j >= 0:
                    I(G, lambda: nc.gpsimd.tensor_tensor(out=pt[:, c0:c0 + 128], in0=pt[:, c0:c0 + 128], in1=tri[:], op=ALU.mult), r=[pt, tri], w=[pt])

            def stB(i, e):
                n, kt, first, last = e
                h, qb = n // NQB, n % NQB
                b = h % 2
                pt = PT[i % 4]
                pO_, acc_ = pOT5[n % 2], acc5[n % 2]
                j = kt - 4 * qb
                c0 = 128 * max(j, 0)
                I(PE, lambda: nc.tensor.matmul(pO_[:, c0:512], lhsT=VB[b][:, kt, 0:128], rhs=pt[:, c0:512], start=first, stop=last),
                  r=[pt, VB[b]], w=[pO_])
                if first:
                    I(V, lambda: nc.vector.tensor_copy(out=acc_[:], in_=pt[:, 0:512]), r=[pt], w=[acc_])
                else:
                    I(V, lambda: nc.vector.tensor_tensor(out=acc_[:, c0:512], in0=acc_[:, c0:512], in1=pt[:, c0:512], op=ALU.add), r=[pt, acc_], w=[acc_])

            def fin5(n):
                h, qb = n // NQB, n % NQB
                bq = n % 4
                rows = slice(qb * 512, (qb + 1) * 512)
                pO_, acc_ = pOT5[n % 2], acc5[n % 2]
                for sub in range(4):
                    I(PE, lambda sub=sub: nc.tensor.matmul(pZ5[:, 2 * sub:2 * sub + 2], lhsT=acc_[:, sub * 128:(sub + 1) * 128], rhs=ones_f[:, 0:2],
                                                           start=True, stop=True), r=[acc_, ones_f], w=[pZ5])
                I(A, lambda: nc.scalar.copy(out=oT5[:], in_=pO_[:, :]), r=[pO_], w=[oT5])
                for sub in range(4):
                    I(PE, lambda sub=sub: nc.tensor.transpose(out=pTr5[:, sub * 128:(sub + 1) * 128], in_=oT5[:, sub * 128:(sub + 1) * 128], identity=identf[:]),
                      r=[oT5, identf], w=[pTr5])
                I(V, lambda: nc.vector.reciprocal(out=rz5[:, 0:4], in_=pZ5[:, 0:8:2]), r=[pZ5], w=[rz5])
                for sub in range(4):
                    I(V, lambda sub=sub: nc.vector.tensor_scalar(out=obf[:, sub, :], in0=pTr5[:, sub * 128:(sub + 1) * 128], scalar1=rz5[:, sub:sub + 1],
                                                                 scalar2=None, op0=ALU.mult), r=[pTr5, rz5], w=[obf])
                I(G, lambda: nc.gpsimd.tensor_tensor(out=obf[:], in0=obf[:], in1=bgb[bq][:], op=ALU.mult), r=[obf, bgb[bq]], w=[obf])
                I(G, lambda: nc.gpsimd.tensor_tensor(out=yo[bq][:], in0=obf[:], in1=yal[bq][:], op=ALU.add), r=[obf, yal[bq]], w=[yo[bq]])
                dma(Yb[rows, h * 128:(h + 1) * 128].rearrange("(s p) c -> p s c", p=128), yo[bq][:], r=[yo[bq]])

            n_e = len(entries)
            pend = []
            stA(0, entries[0])
            for i in range(n_e):
                while pend and pend[0][0] <= i:
                    _, n_ = pend.pop(0)
                    fin5(n_)
                if i + 1 < n_e:
                    stA(i + 1, entries[i + 1])
                stB(i, entries[i])
                if entries[i][3]:
                    pend.append((i + 2, entries[i][0]))
            while pend:
                _, n_ = pend.pop(0)
                fin5(n_)
            fw.barrier()
        if upto == 5:
            return nc

        with contextlib.ExitStack() as st:
            wob = sb(st, "wob", [128, 8, 1024], BF16)
            with contextlib.ExitStack() as st2:
                stg = [sb(st2, f"stgo{i}", [128, 1024], F32) for i in range(2)]
                load_cast(stg, wob, lambda i: wob[:, i, :], lambda i: w_o[i * 128:(i + 1) * 128, :], 8, [128, 1024])
                fw.barrier()
            modb = TO(st.enter_context(nc.sbuf_tensor("sb_mod6a", [128, 1024], F32)), 2048)
            dma(modb.t[:], MODB[:, 2048:3072], w=[modb])
            yt = [sb(st, f"yt{i}", [128, 1024], BF16) for i in range(2)]
            xl = [sb(st, f"xla{i}", [128, 1024], F32) for i in range(2)]
            YT = sb(st, "YT", [128, 8, 128], BF16)
            tmp = sb(st, "tmpa", [128, 1024], F32)
            x1t = [sb(st, f"x1t{i}", [128, 1024], F32) for i in range(2)]
            pTa = [ps(st, f"pTa{i}", [128, 1024], BF16) for i in range(2)]
            pA = [ps(st, f"pA{i}", [128, 512]) for i in range(4)]

            def ld6(t):
                b = t % 2
                tk = slice(t * 128, (t + 1) * 128)
                dma(yt[b][:], Yb[tk, :], w=[yt[b]])
                dma(xl[b][:], x[tk, :], w=[xl[b]])
            ld6(0)
            for t in range(NT):
                if t + 1 < NT:
                    ld6(t + 1)
                b = t % 2
                tk = slice(t * 128, (t + 1) * 128)
                tr_generic(pTa[t % 2], yt[b], lambda i: yt[b][:, i * 128:(i + 1) * 128], 8, 128, YT, YT[:])
                for half in range(2):
                    p = pA[(2 * t + half) % 4]
                    hs = slice(half * 512, (half + 1) * 512)
                    for c in range(8):
                        I(PE, lambda c=c, p=p, hs=hs: nc.tensor.matmul(p[:, :], lhsT=YT[:, c, :], rhs=wob[:, c, hs], start=(c == 0), stop=(c == 7)),
                          r=[YT, wob], w=[p])
                    gsl = slice(2048 + half * 512, 2048 + (half + 1) * 512)
                    I(V, lambda p=p, hs=hs, gsl=gsl: nc.vector.tensor_tensor(out=tmp[:, hs], in0=p[:, :], in1=modb[:, gsl], op=ALU.mult), r=[p, modb], w=[tmp])
                I(G, lambda: nc.gpsimd.tensor_tensor(out=x1t[b][:], in0=tmp[:], in1=xl[b][:], op=ALU.add), r=[tmp, xl[b]], w=[x1t[b]])
                dma(out[tk, :], x1t[b][:], r=[x1t[b]])
            fw.barrier()
        if upto == 6:
            return nc

        NB6 = S // 256
        with contextlib.ExitStack() as st:
            wupb = sb(st, "wupb", [128, 8, 5632], BF16)
            wdb = sb(st, "wdb", [128, 22, 1024], BF16)
            with contextlib.ExitStack() as st2:
                stg = [sb(st2, f"stgu{i}", [128, 5632], F32) for i in range(2)]
                load_cast(stg, wupb, lambda i: wupb[:, i, :], lambda i: w_up[i * 128:(i + 1) * 128, :], 8, [128, 5632])
                load_cast(stg, wdb, lambda i: wdb[:, i, :], lambda i: w_down[i * 128:(i + 1) * 128, :], 22, [128, 1024])
                fw.barrier()
            modb = TO(st.enter_context(nc.sbuf_tensor("sb_mod6b", [128, 3072], F32)), 3072)
            dma(modb.t[:], MODB[:, 3072:6144], w=[modb])
            wc = sb(st, "wc", [128, 44, 3], F32)
            bc = sb(st, "bc", [128, 44], F32)
            dma(wc[:], wconv_l, w=[wc])
            dma(bc[:], bconv_l, w=[bc])
            xb = [sb(st, f"xb{i}", [128, 2, 1024], F32) for i in range(2)]
            h2f = sb(st, "h2f", [128, 1024], F32)
            tmpd = sb(st, "tmpd", [128, 1024], F32)
            h2b = sb(st, "h2b", [128, 1024], BF16)
            h2T = [sb(st, f"h2T{i}", [128, 8, 256], BF16) for i in range(2)]
            zb = [sb(st, f"zb{i}", [128, 258], F32) for i in range(4)]
            uv = [sb(st, f"uv{i}", [128, 256], F32) for i in range(2)]
            ug = [sb(st, f"ug{i}", [128, 256], F32) for i in range(2)]
            sgm = [sb(st, f"sgm{i}", [128, 256], F32) for i in range(2)]
            actT = sb(st, "actT", [128, 22, 256], BF16)
            halo = sb(st, "halo", [128, 44, 2], F32)
            ss6 = sb(st, "ss6", [128, 2], F32)
            rs6 = sb(st, "rs6", [128, 2], F32)
            pTb = [ps(st, f"pTb{i}", [128, 1024], BF16) for i in range(2)]
            pU = [ps(st, f"pU{i}", [128, 512]) for i in range(3)]
            pD = [ps(st, f"pD{i}", [128, 512]) for i in range(2)]
            I(V, lambda: nc.vector.memset(halo[:], 0.0), w=[halo])
            nu = [0]

            def load6(blk):
                rows = slice(blk * 256, (blk + 1) * 256)
                dma(xb[blk % 2][:], out[rows, :].rearrange("(s p) c -> p s c", p=128), w=[xb[blk % 2]])

            def prep6(blk):
                xb_ = xb[blk % 2]
                hT_ = h2T[blk % 2]
                for s_ in range(2):
                    I(A, lambda s_=s_: nc.scalar.activation(out=h2b[:], in_=xb_[:, s_, :], func=AF.Square, accum_out=ss6[:, s_:s_ + 1]), r=[xb_], w=[h2b, ss6])
                rsqrt_ms(ss6, ss6[:], rs6, rs6[:], 1.0 / 1024)
                for s_ in range(2):
                    I(V, lambda s_=s_: nc.vector.scalar_tensor_tensor(out=h2f[:], in0=xb_[:, s_, :], scalar=rs6[:, s_:s_ + 1], in1=modb[:, A_FFN],
                                                                      op0=ALU.mult, op1=ALU.mult), r=[xb_, rs6, modb], w=[h2f])
                    I(V, lambda: nc.vector.tensor_tensor(out=h2b[:], in0=h2f[:], in1=modb[:, SH_F], op=ALU.add), r=[h2f, modb], w=[h2b])
                    tr_generic(pTb[s_], h2b, lambda i: h2b[:, i * 128:(i + 1) * 128], 8, 128, hT_, hT_[:, :, s_ * 128:(s_ + 1) * 128])

            def gate6(k):
                sg_ = sgm[k % 2]
                I(A, lambda: nc.scalar.activation(out=sg_[:], in_=ug[k % 2][:], func=AF.Silu), r=[ug[k % 2]], w=[sg_])
                I(G, lambda: nc.gpsimd.tensor_tensor(out=actT[:, k, :], in0=sg_[:], in1=uv[k % 2][:], op=ALU.mult), r=[sg_, uv[k % 2]], w=[actT])

            load6(0)
            prep6(0)
            for blk in range(NB6):
                rows = slice(blk * 256, (blk + 1) * 256)
                xb_ = xb[blk % 2]
                hT_ = h2T[blk % 2]
                if blk + 1 < NB6:
                    load6(blk + 1)
                for k in range(22):
                    for which, fc in ((0, k), (1, 22 + k)):
                        i = nu[0]
                        nu[0] += 1
                        p = pU[i % 3]
                        z = zb[i % 4]
                        u = (uv if which == 0 else ug)[k % 2]
                        for c in range(8):
                            I(PE, lambda c=c, p=p, fc=fc: nc.tensor.matmul(p[:, 0:256], lhsT=wupb[:, c, fc * 128:(fc + 1) * 128], rhs=hT_[:, c, :],
                                                                           start=(c == 0), stop=(c == 7)), r=[wupb, hT_], w=[p])
                        I(G, lambda z=z, fc=fc: nc.gpsimd.tensor_copy(out=z[:, 0:2], in_=halo[:, fc, :]), r=[halo], w=[z])
                        I(A, lambda z=z, p=p: nc.scalar.copy(out=z[:, 2:258], in_=p[:, 0:256]), r=[p], w=[z])
                        I(G, lambda z=z, fc=fc: nc.gpsimd.tensor_copy(out=halo[:, fc, :], in_=z[:, 256:258]), r=[z], w=[halo])
                        I(A, lambda u=u, p=p, fc=fc: nc.scalar.activation(out=u[:], in_=p[:, 0:256], func=AF.Identity, scale=wc[:, fc, 2:3], bias=bc[:, fc:fc + 1]),
                          r=[p, wc, bc], w=[u])
                        I(V, lambda u=u, z=z, fc=fc: nc.vector.scalar_tensor_tensor(out=u[:], in0=z[:, 1:257], scalar=wc[:, fc, 1:2], in1=u[:],
                                                                                    op0=ALU.mult, op1=ALU.add), r=[z, wc, u], w=[u])
                        I(V, lambda u=u, z=z, fc=fc: nc.vector.scalar_tensor_tensor(out=u[:], in0=z[:, 0:256], scalar=wc[:, fc, 0:1], in1=u[:],
                                                                                    op0=ALU.mult, op1=ALU.add), r=[z, wc, u], w=[u])
                    if k >= 1:
                        gate6(k - 1)
                gate6(21)
                if blk + 1 < NB6:
                    prep6(blk + 1)
                for s_ in range(2):
                    for half in range(2):
                        p = pD[half]
                        hs = slice(half * 512, (half + 1) * 512)
                        for k in range(22):
                            I(PE, lambda k=k, p=p, hs=hs, s_=s_: nc.tensor.matmul(p[:, :], lhsT=actT[:, k, s_ * 128:(s_ + 1) * 128], rhs=wdb[:, k, hs],
                                                                                  start=(k == 0), stop=(k == 21)), r=[actT, wdb], w=[p])
                        gsl = slice(5120 + half * 512, 5120 + (half + 1) * 512)
                        I(V, lambda p=p, hs=hs, gsl=gsl: nc.vector.tensor_tensor(out=tmpd[:, hs], in0=p[:, :], in1=modb[:, gsl], op=ALU.mult), r=[p, modb], w=[tmpd])
                    I(G, lambda s_=s_: nc.gpsimd.tensor_tensor(out=xb_[:, s_, :], in0=tmpd[:], in1=xb_[:, s_, :], op=ALU.add), r=[tmpd, xb_], w=[xb_])
                dma(out[rows, :].rearrange("(s p) c -> p s c", p=128), xb_[:], r=[xb_])
            fw.barrier()
    return nc


_PARAM_NAMES = ["w_ada", "b_ada", "attn_norm", "ffn_norm", "w_in", "nsa_q_norm", "nsa_kc_norm", "nsa_ks_norm", "nsa_kw_norm",
                "cmp_k_w1", "cmp_k_w2", "cmp_v_w1", "cmp_v_w2", "mla_cq_norm", "mla_ckv_norm", "w_uq", "w_ukv",
                "mla_q_norm", "mla_k_norm", "w_o", "w_up", "w_down"]
_CONSTS = {}


def make_in_map(inp, b, S):
    if S not in _CONSTS:
        _CONSTS[S] = host_consts(S)
    m = {}
    m["x"] = np.ascontiguousarray(np.asarray(inp["x"])[b, :S], dtype=np.float32)
    m["ccol"] = np.ascontiguousarray(np.asarray(inp["c"])[b].reshape(8, 128).T, dtype=np.float32)
    for k in _PARAM_NAMES:
        m[k] = np.ascontiguousarray(np.asarray(inp[k])[0], dtype=np.float32)
    m["pe_kT"] = np.ascontiguousarray(np.asarray(inp["cmp_k_pe"])[0].T, dtype=np.float32)
    m["pe_vT"] = np.ascontiguousarray(np.asarray(inp["cmp_v_pe"])[0].T, dtype=np.float32)
    m["wconv_l"] = np.ascontiguousarray(np.asarray(inp["w_conv"])[0].reshape(3, 44, 128).transpose(2, 1, 0), dtype=np.float32)
    m["bconv_l"] = np.ascontiguousarray(np.asarray(inp["b_conv"])[0].reshape(44, 128).T, dtype=np.float32)
    m.update(_CONSTS[S])
    return m


_NC = {}


def kernel(**inputs):
    S = 8192
    if S not in _NC:
        _NC[S] = build(S)
    nc = _NC[S]
    in_maps = [make_in_map(inputs, b, S) for b in range(8)]
    res = run_bass_kernel_spmd(nc, in_maps, core_ids=list(range(8)))
    return np.stack([np.asarray(r["out"], dtype=np.float32) for r in res.results], axis=0)
```
